# Optimizing a Trainium2 kernel written in Bass

```python
import jax, jax.numpy as jnp
from jax import lax
import numpy as np

D_MODEL = 1024
BATCH = 1
SEQ = 16384
DEPTH = 1

GRID_W = 64
NA_W = D_MODEL // 2
NA_HEAD_DIM = 64
NA_HEADS = NA_W // NA_HEAD_DIM
NA_KH = 8
NA_KW = 16
ML_W = D_MODEL // 2
ML_HEADS = 4
ML_HEAD_DIM = ML_W // ML_HEADS
MIX_W = NA_W + ML_W
ML_CHUNK = 64
CONV_W = 5
EPS = 1e-6
IN_SPLITS = (NA_W,) * 4 + (ML_W,) * 5 + (ML_HEADS,) * 4
IN_W = 4 * NA_W + 5 * ML_W + 4 * ML_HEADS

kernel_name = 'hybrid_natten2d_bimlstm_block'


def rmsnorm(x, w):
    x32 = x.astype(jnp.float32)
    return x32 * lax.rsqrt(jnp.mean(x32 * x32, axis=-1, keepdims=True) + EPS) * w


def centred_short_conv(u, w, b):
    T = u.shape[1]
    pad = CONV_W // 2
    up = jnp.pad(u, ((0, 0), (pad, pad), (0, 0)))
    out = up[:, 0:T] * w[0]
    for j in range(1, CONV_W):
        out = out + up[:, j:j + T] * w[j]
    return out + b


def neighbourhood_attention(q, k, v, rpb):
    B, T, _ = q.shape
    rows = T // GRID_W
    kh = min(NA_KH, rows)

    def to_grid(t):
        return t.reshape(B, rows, GRID_W, NA_HEADS, NA_HEAD_DIM).transpose(0, 3, 1, 2, 4)

    qg = to_grid(q) * (NA_HEAD_DIM ** -0.5)
    kgrid, vgrid = to_grid(k), to_grid(v)
    r = jnp.arange(rows)
    row_idx = jnp.clip(r - kh // 2, 0, rows - kh)[:, None] + jnp.arange(kh)[None, :]
    kg = kgrid[:, :, row_idx]
    vg = vgrid[:, :, row_idx]
    col = jnp.arange(GRID_W)
    col_start = jnp.clip(col - NA_KW // 2, 0, GRID_W - NA_KW)
    col_mask = (col[None, :] >= col_start[:, None]) & (col[None, :] < col_start[:, None] + NA_KW)
    dy = row_idx - r[:, None] + NA_KH - 1
    dx = jnp.clip(col[None, :] - col[:, None], -(NA_KW - 1), NA_KW - 1) + NA_KW - 1
    bias = rpb[:, dy[:, None, :, None], dx[None, :, None, :]]
    s = jnp.einsum('bhrcd,bhrjkd->bhrcjk', qg, kg).astype(jnp.float32) + bias[None]
    s = jnp.where(col_mask[:, None, :], s, -jnp.inf)
    p = jax.nn.softmax(s.reshape(B, NA_HEADS, rows, GRID_W, kh * GRID_W), axis=-1).reshape(s.shape)
    o = jnp.einsum('bhrcjk,bhrjkd->bhrcd', p, vg)
    return o.transpose(0, 2, 3, 1, 4).reshape(B, T, NA_W)


def mlstm_chunkwise(q, k, v, i_pre, f_pre):
    B, H, T, d = q.shape
    L = ML_CHUNK
    nc = T // L
    q = q.reshape(B, H, nc, L, d)
    k = (k * (d ** -0.5)).reshape(B, H, nc, L, d)
    v = v.reshape(B, H, nc, L, d)
    i_pre = i_pre.reshape(B, H, nc, L)
    b = jnp.cumsum(jax.nn.log_sigmoid(f_pre).reshape(B, H, nc, L), axis=-1)
    b_last = b[..., -1]
    a = b_last[..., None] - b + i_pre

    def step(carry, inp):
        C, n, m = carry
        k_c, v_c, a_c, bl_c = inp
        m_new = jnp.maximum(bl_c + m, jnp.max(a_c, axis=-1))
        decay = jnp.exp(bl_c + m - m_new)
        w = jnp.exp(a_c - m_new[..., None])
        C_new = decay[..., None, None] * C + jnp.einsum('bhs,bhse,bhsd->bhed', w, v_c, k_c)
        n_new = decay[..., None] * n + jnp.einsum('bhs,bhsd->bhd', w, k_c)
        return (C_new, n_new, m_new), (C, n, m)

    init = (jnp.zeros((B, H, d, d), q.dtype), jnp.zeros((B, H, d), q.dtype), jnp.zeros((B, H), q.dtype))
    xs = (jnp.moveaxis(k, 2, 0), jnp.moveaxis(v, 2, 0), jnp.moveaxis(a, 2, 0), jnp.moveaxis(b_last, 2, 0))
    _, (C_prev, n_prev, m_prev) = lax.scan(step, init, xs)
    C_prev = jnp.moveaxis(C_prev, 0, 2)
    n_prev = jnp.moveaxis(n_prev, 0, 2)
    m_prev = jnp.moveaxis(m_prev, 0, 2)

    lower = jnp.tril(jnp.ones((L, L), dtype=bool))
    Dlog = jnp.where(lower, b[..., :, None] - b[..., None, :] + i_pre[..., None, :], -jnp.inf)
    m_inter = b + m_prev[..., None]
    m_t = jnp.maximum(m_inter, jnp.max(Dlog, axis=-1))
    S = jnp.einsum('bhntd,bhnsd->bhnts', q, k) * jnp.exp(Dlog - m_t[..., None])
    inter = jnp.exp(m_inter - m_t)
    num = jnp.einsum('bhnts,bhnse->bhnte', S, v) + inter[..., None] * jnp.einsum('bhned,bhntd->bhnte', C_prev, q)
    den = jnp.sum(S, axis=-1) + inter * jnp.einsum('bhnd,bhntd->bhnt', n_prev, q)
    h = num / jnp.maximum(jnp.abs(den), jnp.exp(-m_t))[..., None]
    return h.reshape(B, H, T, d)


def hybrid_mixer(h, w_in, b_in, conv_w, conv_b, rpb, ml_norm_w, w_out):
    B, T, _ = h.shape
    proj = h @ w_in + b_in
    split_idx = np.cumsum(IN_SPLITS)[:-1].tolist()
    (na_q, na_k, na_v, na_z, ml_q, ml_k, ml_v, ml_o, ml_z,
     i_f, f_f, i_b, f_b) = jnp.split(proj, split_idx, axis=-1)

    na_out = neighbourhood_attention(na_q, na_k, na_v, rpb) * jax.nn.silu(na_z)

    qk = jax.nn.silu(centred_short_conv(jnp.concatenate([ml_q, ml_k], axis=-1), conv_w, conv_b))
    mq, mk = jnp.split(qk, 2, axis=-1)

    def heads(t):
        return t.reshape(B, T, ML_HEADS, ML_HEAD_DIM).transpose(0, 2, 1, 3)

    mq, mk, mv = heads(mq), heads(mk), heads(ml_v)
    i_f, f_f, i_b, f_b = (g.transpose(0, 2, 1) for g in (i_f, f_f, i_b, f_b))
    h_fwd = mlstm_chunkwise(mq, mk, mv, i_f, f_f)
    flip = lambda t: jnp.flip(t, axis=2)
    h_bwd = flip(mlstm_chunkwise(flip(mq), flip(mk), flip(mv), flip(i_b), flip(f_b)))
    hm = (h_fwd + h_bwd).transpose(0, 2, 1, 3) * jax.nn.sigmoid(ml_o).reshape(B, T, ML_HEADS, ML_HEAD_DIM)
    mu = jnp.mean(hm, axis=-1, keepdims=True)
    var = jnp.mean(jnp.square(hm - mu), axis=-1, keepdims=True)
    hm = ((hm - mu) * lax.rsqrt(var + EPS)).reshape(B, T, ML_W) * ml_norm_w
    ml_out = hm * jax.nn.silu(ml_z)

    return jnp.concatenate([na_out, ml_out], axis=-1) @ w_out


def setup_inputs(seed: int = 0) -> dict:
    key = jax.random.key(seed)
    ks = jax.random.split(key, 14)
    D = D_MODEL
    x = jax.random.normal(ks[0], (BATCH, SEQ, D), jnp.float32)
    c = jax.random.normal(ks[1], (BATCH, D), jnp.float32)
    w_ada = jax.random.normal(ks[2], (DEPTH, D, 3 * D), jnp.float32) * (0.5 * D ** -0.5)
    b_ada = jax.random.normal(ks[3], (DEPTH, 3 * D), jnp.float32) * 0.02
    norm_w = 1.0 + 0.1 * jax.random.normal(ks[4], (DEPTH, D), jnp.float32)
    w_in = jax.random.normal(ks[5], (DEPTH, D, IN_W), jnp.float32) * (D ** -0.5)
    n_main = 4 * NA_W + 5 * ML_W
    fgate_base = jnp.linspace(3.0, 6.0, ML_HEADS, dtype=jnp.float32)
    gk = jax.random.split(ks[6], 5)
    b_in = jnp.concatenate([
        0.02 * jax.random.normal(gk[0], (DEPTH, n_main), jnp.float32),
        0.1 * jax.random.normal(gk[1], (DEPTH, ML_HEADS), jnp.float32),
        fgate_base + 0.1 * jax.random.normal(gk[2], (DEPTH, ML_HEADS), jnp.float32),
        0.1 * jax.random.normal(gk[3], (DEPTH, ML_HEADS), jnp.float32),
        fgate_base + 0.1 * jax.random.normal(gk[4], (DEPTH, ML_HEADS), jnp.float32),
    ], axis=-1)
    conv_w = jax.random.normal(ks[7], (DEPTH, CONV_W, 2 * ML_W), jnp.float32) * (CONV_W ** -0.5)
    conv_b = 0.02 * jax.random.normal(ks[8], (DEPTH, 2 * ML_W), jnp.float32)
    rpb = 0.1 * jax.random.normal(ks[9], (DEPTH, NA_HEADS, 2 * NA_KH - 1, 2 * NA_KW - 1), jnp.float32)
    ml_norm_w = 1.0 + 0.1 * jax.random.normal(ks[10], (DEPTH, ML_W), jnp.float32)
    w_out = jax.random.normal(ks[11], (DEPTH, MIX_W, D), jnp.float32) * (MIX_W ** -0.5)
    final_norm_w = 1.0 + 0.1 * jax.random.normal(ks[12], (D,), jnp.float32)
    return {'x': x, 'c': c, 'w_ada': w_ada, 'b_ada': b_ada, 'norm_w': norm_w, 'w_in': w_in,
            'b_in': b_in, 'conv_w': conv_w, 'conv_b': conv_b, 'rpb': rpb, 'ml_norm_w': ml_norm_w,
            'w_out': w_out, 'final_norm_w': final_norm_w}


def reference(x, c, w_ada, b_ada, norm_w, w_in, b_in, conv_w, conv_b, rpb, ml_norm_w, w_out, final_norm_w):
    h_res = x.astype(jnp.float32)
    c_act = jax.nn.silu(c.astype(jnp.float32))
    for l in range(DEPTH):
        mod = c_act @ w_ada[l] + b_ada[l]
        shift, scale, gate = jnp.split(mod, 3, axis=-1)
        h = rmsnorm(h_res, norm_w[l]) * (1.0 + scale[:, None, :]) + shift[:, None, :]
        y = hybrid_mixer(h, w_in[l], b_in[l], conv_w[l], conv_b[l], rpb[l], ml_norm_w[l], w_out[l])
        h_res = h_res + gate[:, None, :] * y
    return rmsnorm(h_res, final_norm_w).astype(x.dtype)
```

```python
import numpy as np
from contextlib import ExitStack
import concourse.bass as bass
import concourse.mybir as mybir
from concourse.bass_utils import run_bass_kernel_spmd

F32 = mybir.dt.float32
BF16 = mybir.dt.bfloat16
AF = mybir.ActivationFunctionType
ALU = mybir.AluOpType

NCORES = 8
D = 1024
T = 16384
TOK = T // NCORES
NT = TOK // 128
HT = NT + 4
NA_HEADS = 8
ML_HEADS = 4
EPS = 1e-6
MLW = 644
NA_COLS = 2048
IN_WP = NA_COLS + ML_HEADS * MLW
UNI_WORDS = 22592
NG = 28


class _Stop(Exception):
    pass


class Lane:
    def __init__(self, nc, name):
        self.sem = nc.alloc_semaphore(name=name)
        self.count = 0
        self.name = name


class Trk:
    def __init__(self, nc):
        self.nc = nc
        self.engs = {"pe": nc.tensor, "act": nc.scalar, "dve": nc.vector, "pool": nc.gpsimd, "sp": nc.sync}
        self.lanes = {k: Lane(nc, "sem_" + k) for k in ("pe", "act", "dve", "pool")}
        self.seen = {k: {} for k in self.engs}
        self.last_w = {}
        self.reads = {}
        self.dma_lanes = {}
        self.n_inst = 0
        self.n_wait = 0

    def dma_lane(self, name):
        if name not in self.dma_lanes:
            self.dma_lanes[name] = Lane(self.nc, "dsem_" + name)
        return self.dma_lanes[name]

    def _wait(self, eng, lane, val):
        s = self.seen[eng]
        if s.get(lane.name, 0) >= val:
            return
        if eng == "pe" and lane is self.lanes["pe"]:
            return
        self.engs[eng].wait_ge(lane.sem, val)
        s[lane.name] = val
        self.n_wait += 1

    def _deps(self, eng, reads, writes):
        for k in reads:
            if k in self.last_w:
                self._wait(eng, *self.last_w[k])
        for k in writes:
            if k in self.last_w:
                self._wait(eng, *self.last_w[k])
            for (l, v) in self.reads.get(k, ()):
                self._wait(eng, l, v)

    def _record(self, lane, reads, writes):
        v = lane.count
        for k in reads:
            if k in writes:
                continue
            lst = self.reads.setdefault(k, [])
            lst[:] = [(l, x) for (l, x) in lst if l is not lane]
            lst.append((lane, v))
        for k in writes:
            self.last_w[k] = (lane, v)
            self.reads[k] = []

    def op(self, eng, fn, reads=(), writes=()):
        lane = self.lanes[eng]
        self._deps(eng, reads, writes)
        ins = fn()
        lane.count += 1
        ins.then_inc(lane.sem, 1)
        self._record(lane, reads, writes)
        self.n_inst += 1
        return ins

    def dma(self, q, lane_name, out, in_, reads=(), writes=(), **kw):
        if lane_name in ("c0", "c1"):
            self._oneshot = getattr(self, "_oneshot", 0) + 1
            lane_name = f"os{self._oneshot % 24}"
            lane = self.dma_lane(lane_name)
            if lane.count > 0:
                self._wait(q, lane, lane.count)
        lane = self.dma_lane(lane_name)
        self._deps(q, reads, writes)
        ins = self.engs[q].dma_start(out=out, in_=in_, **kw)
        lane.count += 16
        ins.then_inc(lane.sem, 16)
        self._record(lane, reads, writes)
        self.n_inst += 1
        return ins

    def wait_keys(self, eng, keys):
        for k in keys:
            if k in self.last_w:
                self._wait(eng, *self.last_w[k])
            for (l, v) in self.reads.get(k, ()):
                self._wait(eng, l, v)

    def barrier(self):
        all_lanes = list(self.lanes.values()) + list(self.dma_lanes.values())
        for eng in self.engs:
            for l in all_lanes:
                if l.count > 0:
                    self._wait(eng, l, l.count)


def _col_perm():
    NAW = 512
    cols = list(range(0, 4 * NAW))
    base_ml = 4 * NAW
    gate0 = 4 * NAW + 5 * 512
    for h in range(ML_HEADS):
        for blk in range(5):
            s = base_ml + blk * 512 + h * 128
            cols += list(range(s, s + 128))
        cols += [gate0 + 0 + h, gate0 + 4 + h, gate0 + 8 + h, gate0 + 12 + h]
    return np.array(cols, dtype=np.int64)


def _consts():
    s = np.arange(128)[:, None]
    t = np.arange(128)[None, :]
    same = (s // 64) == (t // 64)
    triF = (same & (s <= t)).astype(np.float32)
    triB = (same & (s >= t)).astype(np.float32)
    blk = same.astype(np.float32)
    ones = np.ones((128, 128), np.float32)
    ident = np.eye(128, dtype=np.float32)
    return np.stack([ident, triF, triB, blk, ones], 0)


def _tri2():
    s = np.arange(128)[:, None]
    t = np.arange(128)[None, :]
    return np.stack([(s > t).astype(np.float32), (s < t).astype(np.float32)], 0)


def _rpb_tables(rpb):
    p = np.arange(128)
    a = p // 64
    k = p % 64
    c = np.arange(64)
    e = np.arange(14)
    dyi = 13 - e
    dy = dyi[None, :] + a[:, None]
    dyv = (dy >= 0) & (dy <= 14)
    dx = np.clip(k[:, None] - c[None, :], -15, 15) + 15
    cs = np.clip(c - 8, 0, 48)
    cv = (k[:, None] >= cs[None, :]) & (k[:, None] < cs[None, :] + 16)
    valid = dyv[:, :, None] & cv[:, None, :]
    dyc = np.clip(dy, 0, 14)
    g = rpb[:, dyc[:, :, None], dx[:, None, :]]
    g = np.where(valid[None], g, np.float32(0.0)).astype(np.float32)
    return np.ascontiguousarray(g), valid.astype(np.float32)


def _na_base(m):
    return 14 if m == 15 else m


def _na_nj(m):
    return 6 if m in (0, 15) else 5


def _mcol(core):
    out = np.zeros((128, 16, 6, 2), np.float32)
    for m in range(16):
        for j in range(_na_nj(m)):
            for b in range(2):
                r = 32 * core + 2 * m + b
                start = min(max(r - 4, 0), 248)
                for a in range(2):
                    kr = 32 * core - 4 + 2 * (_na_base(m) + j) + a
                    ok = (start <= kr < start + 8)
                    out[a * 64:(a + 1) * 64, m, j, b] = 1.0 if ok else 0.0
    return out.reshape(128, 192)


def build_program(dbg=None):
    nc = bass.Bass("TRN2", target_bir_lowering=False)
    try:
        _build_body(nc, dbg)
    except _Stop:
        pass
    return nc


def _build_body(nc, dbg):

    def din(name, shape):
        return nc.dram_tensor(name, list(shape), F32, kind="ExternalInput").ap()

    xh = din("xh", [HT * 128, D])
    w_ada = din("w_ada", [D, 3 * D])
    w_in = din("w_in", [D, IN_WP])
    w_out = din("w_out", [D, D])
    consts = din("consts", [5, 128, 128])
    c_col = din("c_col", [128, 8])
    bada_col = din("bada_col", [128, 24])
    bada_gate = din("bada_gate", [D])
    normw_col = din("normw_col", [128, 8])
    fnw_row = din("fnw_row", [D])
    mlnw_row = din("mlnw_row", [512])
    bin_col = din("bin_col", [128, 16])
    bin_row = din("bin_row", [IN_WP])
    convw = din("convw", [128, 8, 5])
    convb = din("convb", [128, 8])
    rpbA = din("rpbA", [8, 128, 14 * 64])
    cmask = din("cmask", [128, 14 * 64])
    mcol_d = din("mcol", [128, 192])
    flags = din("flags", [128, 18])
    xf = din("xf", [NG * 512, D])
    xfh = din("xfh", [NG * 4, D])
    fflags = din("fflags", [128, NG * 4])
    tri2 = din("tri2", [2, 128, 128])
    w_gates = din("w_gates", [D, 16])
    b_gates = din("b_gates", [16])
    y = nc.dram_tensor("y", [TOK, D], F32, kind="ExternalOutput").ap()
    dbg_out = None
    if dbg:
        dbg_out = nc.dram_tensor("dbg", [128, 8 * TOK], F32, kind="ExternalOutput").ap()

    es = ExitStack()
    with es:
        def sb(name, shape, dt=F32):
            return es.enter_context(nc.sbuf_tensor(name, list(shape), dt))

        def ps(name, shape, dt=F32):
            return es.enter_context(nc.psum_tensor(name, list(shape), dt))

        t = Trk(nc)
        V, A_, P_, G_ = nc.vector, nc.scalar, nc.tensor, nc.gpsimd

        xT = sb("xT", [128, 8, TOK], BF16)
        xTh = sb("xTh", [128, 8, 512], BF16)
        mixT = sb("mixT", [128, 4, TOK], BF16)
        gate_bc = sb("gate_bc", [128, D])
        cst = sb("cst", [128, 5, 128])
        identb = sb("identb", [128, 128], BF16)
        gT = sb("gT", [128, 8])
        shiftT = sb("shiftT", [128, 8])
        bcol = sb("bcol", [128, 16])
        bcolq = sb("bcolq", [128, 4])
        flg = sb("flg", [128, 18])
        cw = sb("cw", [128, 8, 5])
        cb = sb("cb", [128, 8])
        kcol = sb("kcol", [128, 4])
        wst = [sb(f"wst{i}", [128, 8, 256]) for i in range(2)]
        wbf = [sb(f"wbf{i}", [128, 8, 512], BF16) for i in range(2)]
        Cacc = sb("Cacc", [128, 2, 4, 129])
        UNI = sb("UNI", [128, UNI_WORDS])

        PA = ps("PA", [128, 1024])
        PB = ps("PB", [128, 1024])
        P4 = ps("P4", [128, 512])
        P5 = ps("P5", [128, 512])
        P6 = ps("P6", [128, 512])
        P7 = ps("P7", [128, 512])
        IDENT, TRIF, TRIB, BLK, ONES = range(5)

        class Carver:
            def __init__(self):
                self.off = 0

            def take(self, shape, dt=F32):
                n = int(np.prod(shape[1:]))
                words = n if dt == F32 else (n + 1) // 2
                ap = UNI[:, self.off:self.off + words]
                self.off += words
                assert self.off <= UNI_WORDS, self.off
                if dt != F32:
                    ap = ap.bitcast(BF16)[:, 0:n]
                if len(shape) > 2:
                    names = " ".join(f"d{i}" for i in range(1, len(shape)))
                    kw = {f"d{i}": shape[i] for i in range(1, len(shape))}
                    ap = ap.rearrange(f"p ({names}) -> p {names}", **kw)
                return ap

        def dbg_dump(stage, items):
            if dbg != stage:
                return
            t.barrier()
            off = 0
            dt_ = UNI[:, UNI_WORDS - 2048:UNI_WORDS]
            for ap, n in items:
                o2 = 0
                while o2 < n:
                    w = min(2048, n - o2)
                    t.op("dve", lambda ap=ap, o2=o2, w=w: V.tensor_copy(out=dt_[:, 0:w], in_=ap[:, o2:o2 + w]), reads=[], writes=["dbgtmp"])
                    t.dma("sp", "dbg", dbg_out[:, off:off + w], dt_[:, 0:w], reads=["dbgtmp"], writes=["dbgout"])
                    off += w
                    o2 += w
            nc.sync.wait_ge(t.dma_lanes["dbg"].sem, t.dma_lanes["dbg"].count)
            print("dbg stop at", stage, "instructions", t.n_inst, "waits", t.n_wait)
            raise _Stop()

        t.dma("sp", "c0", cst[:], consts.rearrange("n p f -> p n f"), writes=["cst"])
        t.dma("sp", "c0", flg[:], flags, writes=["flg"])
        t.dma("sp", "c0", bcol[:], bin_col, writes=["bcol"])
        t.dma("sp", "c0", cw[:], convw, writes=["cw"])
        t.dma("sp", "c0", cb[:], convb, writes=["cb"])
        t.op("dve", lambda: V.tensor_copy(out=identb[:], in_=cst[:, IDENT, :]), reads=["cst"], writes=["identb"])
        t.op("pool", lambda: G_.memset(kcol[:, 0:1], 1.0), writes=["kcol"])
        t.op("pool", lambda: G_.memset(kcol[:, 1:2], float(np.log(128.0 ** -0.5))), writes=["kcol"])
        t.op("pool", lambda: G_.memset(kcol[:, 2:3], EPS), writes=["kcol"])
        t.op("pool", lambda: G_.memset(kcol[:, 3:4], 0.0), writes=["kcol"])
        t.op("dve", lambda: V.tensor_scalar(out=bcolq[:], in0=bcol[:, 0:4], scalar1=0.125, scalar2=None, op0=ALU.mult),
             reads=["bcol"], writes=["bcolq"])

        cv = Carver()
        ccol = cv.take([128, 8])
        cact = cv.take([128, 8])
        cbc = cv.take([128, 8, 128])
        badac = cv.take([128, 24])
        nwc = cv.take([128, 8])
        bgate = cv.take([128, D])
        modT = cv.take([128, 16])
        wada_sb = [cv.take([128, 8, 512]) for _ in range(2)]
        t.dma("sp", "c0", ccol, c_col, writes=["ccol"])
        t.dma("sp", "c0", badac, bada_col, writes=["badac"])
        t.dma("sp", "c0", nwc, normw_col, writes=["nwc"])
        t.dma("pool", "c1", bgate, bada_gate.partition_broadcast(128), writes=["bgate"])
        t.op("act", lambda: A_.activation(out=cact, in_=ccol, func=AF.Silu), reads=["ccol"], writes=["cact"])
        for k in range(8):
            t.op("dve", lambda k=k: V.tensor_copy(out=cbc[:, k, :], in_=cact[:, k:k + 1].broadcast_to([128, 128])),
                 reads=["cact"], writes=[f"cbc{k}"])
        wada_v = w_ada.rearrange("(k p) n -> p k n", p=128)
        for ch in range(6):
            slot = ch % 2
            t.dma("sp", f"wada{slot}", wada_sb[slot], wada_v[:, :, ch * 512:(ch + 1) * 512], writes=[f"wada{slot}"])
            if ch < 4:
                for jn in range(4):
                    col = ch * 4 + jn
                    for k in range(8):
                        t.op("pe", lambda k=k, jn=jn, col=col, slot=slot: P_.matmul(
                            P4[:, col:col + 1], lhsT=wada_sb[slot][:, k, jn * 128:(jn + 1) * 128],
                            rhs=cact[:, k:k + 1], start=(k == 0), stop=(k == 7)),
                            reads=[f"wada{slot}", "cact"], writes=["P4"])
            else:
                nb = ch - 4
                pt = P5 if nb == 0 else P6
                for k in range(8):
                    t.op("pe", lambda k=k, pt=pt, slot=slot: P_.matmul(
                        pt[:, :], lhsT=cbc[:, k, :], rhs=wada_sb[slot][:, k, :], start=(k == 0), stop=(k == 7)),
                        reads=[f"wada{slot}", f"cbc{k}"], writes=["P5" if nb == 0 else "P6"])
                t.op("dve", lambda nb=nb, pt=pt: V.tensor_tensor(out=gate_bc[:, nb * 512:(nb + 1) * 512], in0=pt[:, :],
                                                                in1=bgate[:, nb * 512:(nb + 1) * 512], op=ALU.add),
                     reads=["P5" if nb == 0 else "P6", "bgate"], writes=["gate_bc"])
        t.op("dve", lambda: V.tensor_tensor(out=modT, in0=P4[:, 0:16], in1=badac[:, 0:16], op=ALU.add),
             reads=["P4", "badac"], writes=["modT"])
        t.op("dve", lambda: V.tensor_copy(out=shiftT[:], in_=modT[:, 0:8]), reads=["modT"], writes=["shiftT"])
        t.op("dve", lambda: V.scalar_tensor_tensor(out=gT[:], in0=modT[:, 8:16], scalar=1.0, in1=nwc, op0=ALU.add, op1=ALU.mult),
             reads=["modT", "nwc"], writes=["gT"])
        t.barrier()

        cv = Carver()
        xin = [cv.take([128, D]) for _ in range(3)]
        xnb = [cv.take([128, D], BF16) for _ in range(2)]
        xtmp = [cv.take([128, D]) for _ in range(2)]
        sq_junk = cv.take([128, D])
        PT_x = [P4, P5]

        def xT_tile(ht):
            if 2 <= ht < 18:
                return xT[:, :, (ht - 2) * 128:(ht - 1) * 128]
            hi = ht if ht < 2 else ht - 18 + 2
            return xTh[:, :, hi * 128:(hi + 1) * 128]

        nstate = {"n": 0}
        NRS = 64
        ssq = cv.take([128, NRS])
        rstd = cv.take([128, NRS])

        def norm_tile(src, nr, dst, dkey):
            n = nstate["n"]
            nstate["n"] += 1
            s3, s2, c = n % 3, n % 2, n % NRS
            t.dma("sp", f"xin{s3}", xin[s3][0:nr, :], src, writes=[f"xin{s3}"])
            t.op("act", lambda: A_.activation(out=sq_junk[0:nr, :], in_=xin[s3][0:nr, :], func=AF.Square, accum_out=ssq[0:nr, c:c + 1]),
                 reads=[f"xin{s3}"], writes=["sq_junk", f"ssq{c}"])
            t.op("dve", lambda: V.tensor_scalar(out=rstd[0:nr, c:c + 1], in0=ssq[0:nr, c:c + 1], scalar1=1.0 / D, scalar2=EPS,
                                                op0=ALU.mult, op1=ALU.add), reads=[f"ssq{c}"], writes=[f"rstd{c}"])
            t.op("act", lambda: A_.activation(out=rstd[0:nr, c:c + 1], in_=rstd[0:nr, c:c + 1], func=AF.Sqrt),
                 reads=[f"rstd{c}"], writes=[f"rstd{c}"])
            t.op("dve", lambda: V.reciprocal(out=rstd[0:nr, c:c + 1], in_=rstd[0:nr, c:c + 1]), reads=[f"rstd{c}"], writes=[f"rstd{c}"])
            t.op("act", lambda: A_.activation(out=xnb[s2][0:nr, :], in_=xin[s3][0:nr, :], func=AF.Copy, scale=rstd[0:nr, c:c + 1]),
                 reads=[f"xin{s3}", f"rstd{c}"], writes=[f"xnb{s2}"])
            ptb = PT_x[s2].bitcast(BF16).rearrange("p (k f) -> p k f", k=8)
            pkey = "P4" if s2 == 0 else "P5"
            for k in range(8):
                t.op("pe", lambda k=k: P_.transpose(out=ptb[:, k, 0:nr], in_=xnb[s2][0:nr, k * 128:(k + 1) * 128], identity=identb[0:nr, 0:nr]),
                     reads=[f"xnb{s2}", "identb"], writes=[pkey])
            xt3 = xtmp[s2].rearrange("p (k f) -> p k f", k=8)
            t.op("dve", lambda: V.tensor_tensor(out=xt3[:, :, 0:nr], in0=ptb[:, :, 0:nr], in1=gT[:, :].unsqueeze(2).broadcast_to([128, 8, nr]), op=ALU.mult),
                 reads=[pkey, "gT"], writes=[f"xtmp{s2}"])
            t.op("pool", lambda: G_.tensor_tensor(out=dst, in0=xt3[:, :, 0:nr], in1=shiftT[:, :].unsqueeze(2).broadcast_to([128, 8, nr]), op=ALU.add),
                 reads=[f"xtmp{s2}", "shiftT"], writes=[dkey])

        for ht in range(HT):
            norm_tile(xh[ht * 128:(ht + 1) * 128, :], 128, xT_tile(ht), f"xT{ht}")

        pbanks = [(PA[:, 0:512], "PA0"), (PA[:, 512:1024], "PA1"), (PB[:, 0:512], "PB0"), (PB[:, 512:1024], "PB1")]
        pst = {"n": 0}

        def next_bank():
            b = pbanks[pst["n"] % 4]
            pst["n"] += 1
            return b

        win_v = w_in.rearrange("(k p) n -> p k n", p=128)
        wout_v = w_out.rearrange("(k p) n -> p k n", p=128)
        wstate = {"n": 0}

        def load_w(src_v, c0, ncols, slot):
            off = 0
            while off < ncols:
                n = min(256, ncols - off)
                s = wstate["n"] % 2
                wstate["n"] += 1
                t.dma("sp", f"wst{s}", wst[s][:, :, 0:n], src_v[:, :, c0 + off:c0 + off + n], writes=[f"wst{s}"])
                t.op("pool", lambda s=s, n=n, off=off: G_.tensor_copy(out=wbf[slot][:, :, off:off + n], in_=wst[s][:, :, 0:n]),
                     reads=[f"wst{s}"], writes=[f"wbf{slot}"])
                off += n

        def tok_blocks(with_halo):
            blks = []
            if with_halo:
                blks.append((xTh[:, :, 0:256], 256, 0))
            for b in range(4):
                blks.append((xT[:, :, b * 512:(b + 1) * 512], 512, 256 + b * 512))
            if with_halo:
                blks.append((xTh[:, :, 256:512], 256, 2304))
            return blks

        def xkeys(c0, n):
            return [f"xT{c0 // 128 + i}" for i in range(n // 128)]

        wFk = cv.take([128, 8, 512], BF16)
        wFv = cv.take([128, 8, 512], BF16)
        wFg = cv.take([128, 8, 16], BF16)
        xg = [cv.take([128, 8, 516], BF16) for _ in range(2)]
        preF = cv.take([128, 516])
        accF = cv.take([128, 512])
        kTF = cv.take([128, 512], BF16)
        KtokF = cv.take([128, 4, 4, 128], BF16)
        Vf = cv.take([128, 4, 4, 129], BF16)
        Gf = cv.take([128, 4, 16])
        nlfF = cv.take([128, 4, 2, 4])
        tmpF = cv.take([128, 4, 2, 4])
        Wt = cv.take([128, 4, 2, 4])
        accg = cv.take([128, 2, 4])
        tmpa = cv.take([128, 2, 4])
        wVf = cv.take([128, 4, 2, 4, 129], BF16)
        bFv = cv.take([128, 512])
        bFg = cv.take([128, 16])
        ffl = cv.take([128, NG * 4])
        sgt = cv.take([128, 2, 128])
        print("phase F union words", cv.off)
        t.dma("sp", "c0", ffl, fflags, writes=["ffl"])
        t.dma("sp", "c0", sgt, tri2.rearrange("n p f -> p n f"), writes=["sgt"])
        t.dma("pool", "c1", bFg, b_gates.partition_broadcast(128), writes=["bFg"])
        for hh in range(ML_HEADS):
            c0 = NA_COLS + hh * MLW
            t.dma("pool", "c1", bFv[:, hh * 128:(hh + 1) * 128], bin_row[c0 + 256:c0 + 384].partition_broadcast(128), writes=["bFv"])
            for (dstw, cc) in ((wFk, c0 + 128), (wFv, c0 + 256)):
                sidx = wstate["n"] % 2
                wstate["n"] += 1
                t.dma("sp", f"wst{sidx}", wst[sidx][:, :, 0:128], win_v[:, :, cc:cc + 128], writes=[f"wst{sidx}"])
                t.op("pool", lambda sidx=sidx, dstw=dstw, hh=hh: G_.tensor_copy(out=dstw[:, :, hh * 128:(hh + 1) * 128], in_=wst[sidx][:, :, 0:128]),
                     reads=[f"wst{sidx}"], writes=["wF"])
        sidx = wstate["n"] % 2
        wstate["n"] += 1
        t.dma("sp", f"wst{sidx}", wst[sidx][:, :, 0:16], w_gates.rearrange("(k p) n -> p k n", p=128), writes=[f"wst{sidx}"])
        t.op("pool", lambda: G_.tensor_copy(out=wFg, in_=wst[sidx][:, :, 0:16]), reads=[f"wst{sidx}"], writes=["wF"])
        t.op("pool", lambda: G_.memset(Vf[:, :, :, 128:129], 1.0), writes=["Vfones"])
        t.op("pool", lambda: G_.memset(accg, 0.0), writes=["accg"])
        t.op("pool", lambda: G_.memset(Cacc[:], 0.0), writes=["Cacc"])
        Gv = Gf.rearrange("p t (d x h) -> p t d x h", d=2, x=2)
        SGT, SLT = 0, 1
        for gi in range(NG):
            xs = gi % 2
            xga = xg[xs]
            norm_tile(xfh[gi * 4:gi * 4 + 2, :], 2, xga[:, :, 0:2], f"xg{xs}")
            for j in range(4):
                norm_tile(xf[(gi * 4 + j) * 128:(gi * 4 + j + 1) * 128, :], 128, xga[:, :, 2 + j * 128:2 + (j + 1) * 128], f"xg{xs}")
            norm_tile(xfh[gi * 4 + 2:gi * 4 + 4, :], 2, xga[:, :, 514:516], f"xg{xs}")
            for j in range(4):
                pb, pk = next_bank()
                for k in range(8):
                    t.op("pe", lambda k=k, j=j, pb=pb: P_.matmul(pb[:, :], lhsT=xga[:, k, 2 + j * 128:2 + (j + 1) * 128], rhs=wFv[:, k, :], start=(k == 0), stop=(k == 7)),
                         reads=["wF", f"xg{xs}"], writes=[pk])
                t.op("dve", lambda j=j, pb=pb: V.tensor_tensor(out=Vf[:, j, :, 0:128], in0=pb.rearrange("p (h d) -> p h d", h=4),
                                                              in1=bFv.rearrange("p (h d) -> p h d", h=4), op=ALU.add), reads=[pk, "bFv"], writes=["Vf"])
                for k in range(8):
                    t.op("pe", lambda k=k, j=j: P_.matmul(P6[:, j * 16:(j + 1) * 16], lhsT=xga[:, k, 2 + j * 128:2 + (j + 1) * 128], rhs=wFg[:, k, :], start=(k == 0), stop=(k == 7)),
                         reads=["wF", f"xg{xs}"], writes=["P6"])
            t.op("dve", lambda: V.tensor_tensor(out=Gf, in0=P6[:, 0:64].rearrange("p (t g) -> p t g", t=4), in1=bFg.unsqueeze(1).broadcast_to([128, 4, 16]), op=ALU.add),
                 reads=["P6", "bFg"], writes=["Gf"])
            t.op("act", lambda: A_.activation(out=nlfF, in_=Gv[:, :, :, 1, :], func=AF.Exp, scale=-1.0), reads=["Gf"], writes=["nlfF"])
            t.op("act", lambda: A_.activation(out=nlfF, in_=nlfF, func=AF.Ln, bias=kcol[:, 0:1], scale=1.0), reads=["nlfF", "kcol"], writes=["nlfF"])
            P7c = P7[:, 0:32].rearrange("p (t d h) -> p t d h", t=4, d=2)
            for j in range(4):
                for d in range(2):
                    others = [jj for jj in range(4) if (jj > j if d == 0 else jj < j)]
                    seq = [(j, sgt[:, SGT if d == 0 else SLT, :])] + [(jj, cst[:, ONES, :]) for jj in others]
                    for qi, (jj, lh) in enumerate(seq):
                        t.op("pe", lambda j=j, d=d, jj=jj, lh=lh, qi=qi, nq=len(seq): P_.matmul(P7c[:, j, d, :], lhsT=lh, rhs=nlfF[:, jj, d, :], start=(qi == 0), stop=(qi == nq - 1)),
                             reads=["nlfF", "sgt", "cst"], writes=["P7"])
            P7t = P7[:, 32:40].rearrange("p (d h) -> p d h", d=2)
            for d in range(2):
                for jj in range(4):
                    t.op("pe", lambda d=d, jj=jj: P_.matmul(P7t[:, d, :], lhsT=cst[:, ONES, :], rhs=nlfF[:, jj, d, :], start=(jj == 0), stop=(jj == 3)),
                         reads=["nlfF", "cst"], writes=["P7"])
            fl = ffl[:, gi * 4:gi * 4 + 2]
            t.op("dve", lambda: V.tensor_tensor(out=tmpF, in0=P7c, in1=accg.unsqueeze(1).broadcast_to([128, 4, 2, 4]), op=ALU.add), reads=["P7", "accg"], writes=["tmpF"])
            t.op("dve", lambda: V.tensor_tensor(out=tmpF, in0=Gv[:, :, :, 0, :], in1=tmpF, op=ALU.subtract), reads=["Gf", "tmpF"], writes=["tmpF"])
            t.op("act", lambda: A_.activation(out=Wt, in_=tmpF, func=AF.Exp), reads=["tmpF"], writes=["Wt"])
            t.op("dve", lambda fl=fl: V.tensor_tensor(out=Wt, in0=Wt, in1=fl.unsqueeze(1).unsqueeze(3).broadcast_to([128, 4, 2, 4]), op=ALU.mult), reads=["Wt", "ffl"], writes=["Wt"])
            t.op("dve", lambda fl=fl: V.tensor_tensor(out=tmpa, in0=P7t, in1=fl.unsqueeze(2).broadcast_to([128, 2, 4]), op=ALU.mult), reads=["P7", "ffl"], writes=["tmpa"])
            t.op("dve", lambda: V.tensor_tensor(out=accg, in0=accg, in1=tmpa, op=ALU.add), reads=["accg", "tmpa"], writes=["accg"])
            for d in range(2):
                eng = "dve" if d == 0 else "pool"
                E = V if d == 0 else G_
                t.op(eng, lambda E=E, d=d: E.tensor_tensor(out=wVf[:, :, d, :, :], in0=Vf, in1=Wt[:, :, d, :].unsqueeze(3).broadcast_to([128, 4, 4, 129]), op=ALU.mult),
                     reads=["Vf", "Vfones", "Wt"], writes=[f"wVf{d}"])
            for hh in range(ML_HEADS):
                for (a0, n) in ((0, 512), (512, 4)):
                    pb, pk = next_bank()
                    for k in range(8):
                        t.op("pe", lambda k=k, hh=hh, a0=a0, n=n, pb=pb: P_.matmul(pb[:, 0:n], lhsT=wFk[:, k, hh * 128:(hh + 1) * 128], rhs=xga[:, k, a0:a0 + n], start=(k == 0), stop=(k == 7)),
                             reads=["wF", f"xg{xs}"], writes=[pk])
                    t.op("act", lambda hh=hh, a0=a0, n=n, pb=pb: A_.activation(out=preF[:, a0:a0 + n], in_=pb[:, 0:n], func=AF.Identity, scale=1.0, bias=bcol[:, 9 + 2 * hh:10 + 2 * hh]),
                         reads=[pk, "bcol"], writes=["preF"])
                t.op("dve", lambda gi=gi: V.tensor_scalar(out=preF[:, 0:2], in0=preF[:, 0:2], scalar1=ffl[:, gi * 4 + 2:gi * 4 + 3], scalar2=None, op0=ALU.mult), reads=["preF", "ffl"], writes=["preF"])
                t.op("dve", lambda gi=gi: V.tensor_scalar(out=preF[:, 514:516], in0=preF[:, 514:516], scalar1=ffl[:, gi * 4 + 3:gi * 4 + 4], scalar2=None, op0=ALU.mult), reads=["preF", "ffl"], writes=["preF"])
                gidx = 4 + hh
                t.op("dve", lambda gidx=gidx: V.tensor_scalar(out=accF, in0=preF[:, 0:512], scalar1=cw[:, gidx, 0:1], scalar2=cb[:, gidx:gidx + 1], op0=ALU.mult, op1=ALU.add),
                     reads=["preF", "cw", "cb"], writes=["accF"])
                for jc in range(1, 5):
                    t.op("dve", lambda gidx=gidx, jc=jc: V.scalar_tensor_tensor(out=accF, in0=preF[:, jc:jc + 512], scalar=cw[:, gidx, jc:jc + 1], in1=accF, op0=ALU.mult, op1=ALU.add),
                         reads=["preF", "cw", "accF"], writes=["accF"])
                t.op("act", lambda: A_.activation(out=kTF, in_=accF, func=AF.Silu), reads=["accF"], writes=["kTF"])
                p4b = P4.bitcast(BF16).rearrange("p (k f) -> p k f", k=8)
                for j in range(4):
                    t.op("pe", lambda j=j, p4b=p4b: P_.transpose(out=p4b[:, j, :], in_=kTF[:, j * 128:(j + 1) * 128], identity=identb[:]), reads=["kTF", "identb"], writes=["P4"])
                t.op("act", lambda hh=hh, p4b=p4b: A_.activation(out=KtokF[:, hh, :, :], in_=p4b[:, 0:4, :], func=AF.Copy), reads=["P4"], writes=[f"KtokF{hh}"])
                for d in range(2):
                    pb, pk = next_bank()
                    for j in range(4):
                        t.op("pe", lambda j=j, d=d, hh=hh, pb=pb: P_.matmul(pb[:, 0:129], lhsT=KtokF[:, hh, j, :], rhs=wVf[:, j, d, hh, :], start=(j == 0), stop=(j == 3)),
                             reads=[f"KtokF{hh}", f"wVf{d}"], writes=[pk])
                    t.op("dve", lambda d=d, hh=hh, pb=pb: V.tensor_tensor(out=Cacc[:, d, hh, :], in0=Cacc[:, d, hh, :], in1=pb[:, 0:129], op=ALU.add),
                         reads=[pk, "Cacc"], writes=["Cacc"])
        dbg_dump("f_c", [(Cacc[:].rearrange("p a b c -> p (a b c)"), 8 * 129)])
        t.barrier()

        cv = Carver()
        KT = cv.take([128, 4, HT * 128], BF16)
        QT = cv.take([128, 4, TOK], BF16)
        Vaug = cv.take([128, HT, 8, 65], BF16)
        Ar = cv.take([128, 8, 14, 64], BF16)
        mcol = cv.take([128, 192])
        bias_bc = cv.take([128, 512])
        mark = cv.off
        cmk = cv.take([128, 14 * 64])
        rtmp = [cv.take([128, 14 * 64]) for _ in range(2)]
        t.dma("sp", "c0", mcol, mcol_d, writes=["mcol"])
        t.dma("sp", "c0", cmk, cmask, writes=["cmk"])
        for h in range(8):
            s = h % 2
            t.dma("sp", f"rtmp{s}", rtmp[s], rpbA[h], writes=[f"rtmp{s}"])
            t.op("act", lambda s=s: A_.activation(out=rtmp[s], in_=rtmp[s], func=AF.Exp), reads=[f"rtmp{s}"], writes=[f"rtmp{s}"])
            t.op("dve", lambda s=s, h=h: V.tensor_tensor(out=Ar[:, h, :, :].rearrange("p e c -> p (e c)"), in0=rtmp[s], in1=cmk, op=ALU.mult),
                 reads=[f"rtmp{s}", "cmk"], writes=[f"Ar{h}"])
        t.barrier()
        cv.off = mark
        expS = [cv.take([128, 6, 128]) for _ in range(2)]
        PTt = [cv.take([128, 6, 128], BF16) for _ in range(2)]
        sz_t = cv.take([128, 512])
        sz_b = cv.take([128, 512], BF16)
        rden = cv.take([128, 8])
        otmp = cv.take([128, 512])
        ob = cv.take([128, 512], BF16)
        print("NA union words", cv.off)
        t.op("pool", lambda: G_.memset(Vaug[:, :, :, 64:65], 1.0), writes=["Vones"])

        for which in range(2):
            slot = which
            load_w(win_v, which * 512, 512, slot)
            for g in range(4):
                for (xap, n, c0) in tok_blocks(with_halo=(which == 1)):
                    pb, pk = next_bank()
                    for k in range(8):
                        t.op("pe", lambda k=k, g=g, xap=xap, n=n, pb=pb, slot=slot: P_.matmul(
                            pb[:, 0:n], lhsT=wbf[slot][:, k, g * 128:(g + 1) * 128], rhs=xap[:, k, :], start=(k == 0), stop=(k == 7)),
                            reads=[f"wbf{slot}"] + xkeys(c0, n), writes=[pk])
                    if which == 0:
                        q0 = c0 - 256
                        t.op("act", lambda g=g, pb=pb, n=n, q0=q0: A_.activation(out=QT[:, g, q0:q0 + n], in_=pb[:, 0:n], func=AF.Identity,
                                                                                 scale=0.125, bias=bcolq[:, g:g + 1]),
                             reads=[pk, "bcolq"], writes=[f"QT{g}_{q0 // 128 + i}" for i in range(n // 128)])
                    else:
                        t.op("act", lambda g=g, pb=pb, n=n, c0=c0: A_.activation(out=KT[:, g, c0:c0 + n], in_=pb[:, 0:n], func=AF.Identity,
                                                                                 scale=1.0, bias=bcol[:, 4 + g:5 + g]),
                             reads=[pk, "bcol"], writes=[f"KT{g}_{c0 // 128 + i}" for i in range(n // 128)])
        load_w(win_v, 1024, 512, 0)
        t.dma("pool", "c1", bias_bc, bin_row[1024:1536].partition_broadcast(128), reads=[], writes=["bias_bc"])
        for ht in range(HT):
            pb, pk = next_bank()
            xa = xT_tile(ht)
            for k in range(8):
                t.op("pe", lambda k=k, xa=xa, pb=pb: P_.matmul(pb[:, :], lhsT=xa[:, k, :], rhs=wbf[0][:, k, :], start=(k == 0), stop=(k == 7)),
                     reads=["wbf0", f"xT{ht}"], writes=[pk])
            t.op("dve", lambda ht=ht, pb=pb: V.tensor_tensor(out=Vaug[:, ht, :, 0:64], in0=pb.rearrange("p (h d) -> p h d", h=8),
                                                            in1=bias_bc.rearrange("p (h d) -> p h d", h=8), op=ALU.add),
                 reads=[pk, "bias_bc"], writes=[f"V{ht}"])
        load_w(win_v, 1536, 512, 1)
        t.wait_keys("pool", ["bias_bc"])
        t.dma("pool", "c1", bias_bc, bin_row[1536:2048].partition_broadcast(128), reads=[], writes=["bias_bc"])

        for m in range(NT):
            base, nj = _na_base(m), _na_nj(m)
            qt = m
            for k in range(8):
                t.op("pe", lambda k=k, m=m: P_.matmul(P7[:, :], lhsT=xT[:, k, m * 128:(m + 1) * 128], rhs=wbf[1][:, k, :], start=(k == 0), stop=(k == 7)),
                     reads=["wbf1", f"xT{m + 2}"], writes=["P7"])
            t.op("dve", lambda: V.tensor_tensor(out=sz_t, in0=P7[:, :], in1=bias_bc, op=ALU.add), reads=["P7", "bias_bc"], writes=["sz_t"])
            t.op("act", lambda: A_.activation(out=sz_b, in_=sz_t, func=AF.Silu), reads=["sz_t"], writes=["sz_b"])
            for h in range(8):
                g, hh = h // 2, h % 2
                sl = h % 2
                PS = PA if sl == 0 else PB
                pskeys = ["PA0", "PA1"] if sl == 0 else ["PB0", "PB1"]
                PS3 = PS.rearrange("p (j q) -> p j q", q=128)
                for j in range(nj):
                    kt = base + j
                    t.op("pe", lambda j=j, kt=kt, g=g, hh=hh, PS3=PS3, qt=qt: P_.matmul(
                        PS3[:, j, :], lhsT=KT[hh * 64:(hh + 1) * 64, g, kt * 128:(kt + 1) * 128],
                        rhs=QT[hh * 64:(hh + 1) * 64, g, qt * 128:(qt + 1) * 128], start=True, stop=True),
                        reads=[f"KT{g}_{kt}", f"QT{g}_{qt}"], writes=pskeys)
                t.op("act", lambda sl=sl, nj=nj, PS3=PS3: A_.activation(out=expS[sl][:, 0:nj, :], in_=PS3[:, 0:nj, :], func=AF.Exp),
                     reads=pskeys, writes=[f"expS{sl}"])
                eng = "dve" if h % 2 == 0 else "pool"
                E = V if eng == "dve" else G_
                interior = 2 <= m <= 13
                for j in range(nj):
                    dyi0 = 2 * (base - m + j) + 3
                    e0 = 13 - dyi0
                    if interior and 1 <= j <= 3:
                        t.op(eng, lambda E=E, sl=sl, j=j, h=h, e0=e0: E.tensor_tensor(
                            out=PTt[sl][:, j, :], in0=expS[sl][:, j, :], in1=Ar[:, h, e0:e0 + 2, :].rearrange("p e c -> p (e c)"), op=ALU.mult),
                            reads=[f"expS{sl}", f"Ar{h}"], writes=[f"PT{sl}"])
                    else:
                        for b in range(2):
                            mc = (m * 6 + j) * 2 + b
                            t.op("dve", lambda E=V, sl=sl, j=j, h=h, e0=e0, b=b, mc=mc: E.scalar_tensor_tensor(
                                out=PTt[sl][:, j, b * 64:(b + 1) * 64], in0=expS[sl][:, j, b * 64:(b + 1) * 64], scalar=mcol[:, mc:mc + 1],
                                in1=Ar[:, h, e0 + b, :], op0=ALU.mult, op1=ALU.mult),
                                reads=[f"expS{sl}", f"Ar{h}", "mcol"], writes=[f"PT{sl}"])
                PO = P4 if h < 4 else P5
                pok = "P4" if h < 4 else "P5"
                PO3 = PO[:, 0:260].rearrange("p (h d) -> p h d", d=65)
                for j in range(nj):
                    kt = base + j
                    t.op("pe", lambda j=j, kt=kt, h=h, sl=sl, PO3=PO3, nj=nj: P_.matmul(
                        PO3[:, h % 4, :], lhsT=PTt[sl][:, j, :], rhs=Vaug[:, kt, h, :], start=(j == 0), stop=(j == nj - 1)),
                        reads=[f"PT{sl}", f"V{kt}", "Vones"], writes=[pok])
            for half in range(2):
                PO = P4 if half == 0 else P5
                pok = "P4" if half == 0 else "P5"
                PO3 = PO[:, 0:260].rearrange("p (h d) -> p h d", d=65)
                t.op("dve", lambda half=half, PO3=PO3: V.reciprocal(out=rden[:, half * 4:(half + 1) * 4], in_=PO3[:, :, 64]),
                     reads=[pok], writes=["rden"])
                t.op("dve", lambda half=half, PO3=PO3: V.tensor_tensor(
                    out=otmp[:, half * 256:(half + 1) * 256].rearrange("p (h d) -> p h d", d=64), in0=PO3[:, :, 0:64],
                    in1=rden[:, half * 4:(half + 1) * 4].unsqueeze(2).broadcast_to([128, 4, 64]), op=ALU.mult),
                    reads=[pok, "rden"], writes=["otmp"])
            t.op("dve", lambda: V.tensor_tensor(out=ob, in0=otmp, in1=sz_b, op=ALU.mult), reads=["otmp", "sz_b"], writes=["ob"])
            p6b = P6.bitcast(BF16).rearrange("p (k f) -> p k f", k=8)
            for c4 in range(4):
                t.op("pe", lambda c4=c4, p6b=p6b: P_.transpose(out=p6b[:, c4, :], in_=ob[:, c4 * 128:(c4 + 1) * 128], identity=identb[:]),
                     reads=["ob", "identb"], writes=["P6"])
            t.op("act", lambda m=m, p6b=p6b: A_.activation(out=mixT[:, 0:4, m * 128:(m + 1) * 128], in_=p6b[:, 0:4, :], func=AF.Copy),
                 reads=["P6"], writes=[f"mixNA{m}"])
        t.barrier()

        if dbg == "na":
            cv = Carver()
            dtmp = cv.take([128, TOK])
            for c8 in range(4):
                t.op("dve", lambda c8=c8: V.tensor_copy(out=dtmp, in_=mixT[:, c8, :]), reads=[f"mixNA{m}" for m in range(NT)], writes=["dtmp"])
                t.dma("sp", "dbg", dbg_out[:, c8 * TOK:(c8 + 1) * TOK], dtmp, reads=["dtmp"], writes=["dbgout"])
            t.wait_keys("sp", ["dbgout"])
            nc.sync.wait_ge(t.dma_lanes["dbg"].sem, t.dma_lanes["dbg"].count)
            print("instructions", t.n_inst, "waits", t.n_wait)
            return nc


        cv = Carver()
        mixML = cv.take([128, 4, TOK], BF16)
        mqT = cv.take([128, TOK], BF16)
        mkT = cv.take([128, TOK], BF16)
        QsT = [cv.take([128, TOK], BF16) for _ in range(2)]
        Ktok = cv.take([128, NT, 128], BF16)
        Vg = cv.take([128, NT, 129], BF16)
        wV = [cv.take([128, NT, 129], BF16) for _ in range(2)]
        so = cv.take([128, NT, 128], BF16)
        szm = cv.take([128, NT, 128], BF16)
        hF = cv.take([128, NT, 128])
        Gt = cv.take([128, NT, 4])
        nlf = cv.take([128, NT, 2])
        gw = cv.take([128, NT, 4])
        ebl = cv.take([128, NT, 2, 2])
        bias_ml = cv.take([128, 388])
        mlnw_bc = cv.take([128, 512])
        Cst = [cv.take([128, 129]) for _ in range(2)]
        Cbf = [cv.take([128, 129], BF16) for _ in range(2)]
        small = cv.take([128, 64])
        mark = cv.off
        t.dma("pool", "c1", mlnw_bc, mlnw_row.partition_broadcast(128), writes=["mlnw_bc"])
        t.op("pool", lambda: G_.memset(Vg[:, :, 128:129], 1.0), writes=["Vgones"])
        PNs = [(P6, "P6"), (P7, "P7")]

        for h in range(ML_HEADS):
            c0 = NA_COLS + h * MLW
            cv.off = mark
            pre = cv.take([128, TOK + 4])
            acc = cv.take([128, TOK])
            load_w(win_v, c0, 256, 0)
            load_w(win_v, c0 + 256, 388, 1)
            t.dma("pool", "c1", bias_ml, bin_row[c0 + 256:c0 + 644].partition_broadcast(128), writes=["bias_ml"])
            for g in range(2):
                blks = [(xTh[:, :, 254:256], 2, 0, ["xT1"])]
                for b in range(4):
                    blks.append((xT[:, :, b * 512:(b + 1) * 512], 512, 2 + b * 512, [f"xT{2 + 4 * b + i}" for i in range(4)]))
                blks.append((xTh[:, :, 256:258], 2, 2050, ["xT18"]))
                for (xap, n, p0, xk) in blks:
                    pb, pk = next_bank()
                    for k in range(8):
                        t.op("pe", lambda k=k, g=g, xap=xap, n=n, pb=pb: P_.matmul(
                            pb[:, 0:n], lhsT=wbf[0][:, k, g * 128:(g + 1) * 128], rhs=xap[:, k, :], start=(k == 0), stop=(k == 7)),
                            reads=["wbf0"] + xk, writes=[pk])
                    t.op("act", lambda g=g, pb=pb, n=n, p0=p0, h=h: A_.activation(out=pre[:, p0:p0 + n], in_=pb[:, 0:n], func=AF.Identity,
                                                                             scale=1.0, bias=bcol[:, 8 + 2 * h + g:9 + 2 * h + g]),
                         reads=[pk, "bcol"], writes=["pre"])
                t.op("dve", lambda: V.tensor_scalar(out=pre[:, 0:2], in0=pre[:, 0:2], scalar1=flg[:, 0:1], scalar2=None, op0=ALU.mult),
                     reads=["pre", "flg"], writes=["pre"])
                t.op("dve", lambda: V.tensor_scalar(out=pre[:, TOK + 2:TOK + 4], in0=pre[:, TOK + 2:TOK + 4], scalar1=flg[:, 1:2], scalar2=None, op0=ALU.mult),
                     reads=["pre", "flg"], writes=["pre"])
                gi = g * 4 + h
                t.op("dve", lambda gi=gi: V.tensor_scalar(out=acc, in0=pre[:, 0:TOK], scalar1=cw[:, gi, 0:1], scalar2=cb[:, gi:gi + 1],
                                                          op0=ALU.mult, op1=ALU.add), reads=["pre", "cw", "cb"], writes=["acc"])
                for j in range(1, 5):
                    t.op("dve", lambda gi=gi, j=j: V.scalar_tensor_tensor(out=acc, in0=pre[:, j:j + TOK], scalar=cw[:, gi, j:j + 1], in1=acc,
                                                                          op0=ALU.mult, op1=ALU.add), reads=["pre", "cw", "acc"], writes=["acc"])
                dstT = mqT if g == 0 else mkT
                t.op("act", lambda dstT=dstT: A_.activation(out=dstT, in_=acc, func=AF.Silu), reads=["acc"], writes=["mqT" if g == 0 else "mkT"])
            t.barrier()
            if h == 0:
                dbg_dump("ml_a", [(mqT, TOK), (mkT, TOK)])
            cv.off = mark
            tmpv = [cv.take([128, 388]) for _ in range(2)]
            Rfb = [cv.take([128, 2, 128]) for _ in range(2)]
            ebt = [cv.take([128, 2, 128]) for _ in range(2)]
            tmp4 = [cv.take([128, 4]) for _ in range(2)]
            sdT = [cv.take([128, 128], BF16) for _ in range(2)]
            hm = [cv.take([128, 128]) for _ in range(2)]
            hjunk = cv.take([128, 128])
            mo = [cv.take([128, 128], BF16) for _ in range(2)]
            st = [cv.take([128, 8]) for _ in range(2)]
            for tl in range(NT):
                pb, pk = next_bank()
                s2 = tl % 2
                for k in range(8):
                    t.op("pe", lambda k=k, tl=tl, pb=pb: P_.matmul(pb[:, 0:388], lhsT=xT[:, k, tl * 128:(tl + 1) * 128], rhs=wbf[1][:, k, 0:388],
                                                                  start=(k == 0), stop=(k == 7)), reads=["wbf1", f"xT{tl + 2}"], writes=[pk])
                t.op("dve", lambda pb=pb, s2=s2: V.tensor_tensor(out=tmpv[s2], in0=pb[:, 0:388], in1=bias_ml, op=ALU.add),
                     reads=[pk, "bias_ml"], writes=[f"tmpv{s2}"])
                t.op("pool", lambda tl=tl, s2=s2: G_.tensor_copy(out=Vg[:, tl, 0:128], in_=tmpv[s2][:, 0:128]), reads=[f"tmpv{s2}"], writes=[f"Vg{tl}"])
                t.op("act", lambda tl=tl, s2=s2: A_.activation(out=so[:, tl, :], in_=tmpv[s2][:, 128:256], func=AF.Sigmoid), reads=[f"tmpv{s2}"], writes=[f"so{tl}"])
                t.op("act", lambda tl=tl, s2=s2: A_.activation(out=szm[:, tl, :], in_=tmpv[s2][:, 256:384], func=AF.Silu), reads=[f"tmpv{s2}"], writes=[f"szm{tl}"])
                t.op("pool", lambda tl=tl, s2=s2: G_.tensor_copy(out=Gt[:, tl, :], in_=tmpv[s2][:, 384:388]), reads=[f"tmpv{s2}"], writes=["Gt"])
            Gf = Gt.rearrange("p t (d two) -> p t d two", two=2)[:, :, :, 1]
            Gi = Gt.rearrange("p t (d two) -> p t d two", two=2)[:, :, :, 0]
            t.op("act", lambda: A_.activation(out=nlf, in_=Gf, func=AF.Exp, scale=-1.0), reads=["Gt"], writes=["nlf"])
            t.op("act", lambda: A_.activation(out=nlf, in_=nlf, func=AF.Ln, bias=kcol[:, 0:1], scale=1.0), reads=["nlf", "kcol"], writes=["nlf"])
            for tl in range(NT):
                s2 = tl % 2
                t.op("pe", lambda tl=tl: P_.matmul(P6[:, 0:1], lhsT=cst[:, TRIF, :], rhs=nlf[:, tl, 0:1], start=True, stop=True), reads=["cst", "nlf"], writes=["P6"])
                t.op("pe", lambda tl=tl: P_.matmul(P6[:, 1:2], lhsT=cst[:, TRIB, :], rhs=nlf[:, tl, 1:2], start=True, stop=True), reads=["cst", "nlf"], writes=["P6"])
                t.op("pe", lambda tl=tl: P_.matmul(P6[:, 2:4], lhsT=cst[:, BLK, :], rhs=nlf[:, tl, 0:2], start=True, stop=True), reads=["cst", "nlf"], writes=["P6"])
                t.op("dve", lambda tl=tl, s2=s2: V.tensor_tensor(out=tmp4[s2][:, 0:2], in0=Gi[:, tl, :], in1=P6[:, 0:2], op=ALU.add),
                     reads=["Gt", "P6"], writes=[f"tmp4{s2}"])
                t.op("dve", lambda s2=s2: V.tensor_tensor(out=tmp4[s2][:, 2:4], in0=tmp4[s2][:, 0:2], in1=P6[:, 2:4], op=ALU.subtract),
                     reads=["P6", f"tmp4{s2}"], writes=[f"tmp4{s2}"])
                t.op("act", lambda tl=tl, s2=s2: A_.activation(out=gw[:, tl, :], in_=tmp4[s2], func=AF.Exp), reads=[f"tmp4{s2}"], writes=[f"gw{tl}"])
                t.op("dve", lambda tl=tl, s2=s2: V.tensor_scalar(out=Rfb[s2][:, 0, :], in0=cst[:, TRIF, :], scalar1=nlf[:, tl, 0:1], scalar2=None, op0=ALU.mult),
                     reads=["cst", "nlf"], writes=[f"Rfb{s2}"])
                t.op("dve", lambda tl=tl, s2=s2: V.tensor_scalar(out=Rfb[s2][:, 1, :], in0=cst[:, TRIB, :], scalar1=nlf[:, tl, 1:2], scalar2=None, op0=ALU.mult),
                     reads=["cst", "nlf"], writes=[f"Rfb{s2}"])
                t.op("pe", lambda s2=s2: P_.matmul(P7[:, 0:256], lhsT=cst[:, ONES, :], rhs=Rfb[s2].rearrange("p d t -> p (d t)"), start=True, stop=True),
                     reads=["cst", f"Rfb{s2}"], writes=["P7"])
                P7v = P7[:, 0:256].rearrange("p (d t) -> p d t", d=2)
                t.op("act", lambda s2=s2, P7v=P7v: A_.activation(out=ebt[s2], in_=P7v, func=AF.Exp, scale=-1.0, bias=kcol[:, 1:2]),
                     reads=["P7", "kcol"], writes=[f"ebt{s2}"])
                t.op("act", lambda tl=tl: A_.activation(out=ebl[:, tl, 0, :], in_=P7[:, 63:128:64], func=AF.Exp, scale=-1.0), reads=["P7"], writes=[f"ebl{tl}"])
                t.op("act", lambda tl=tl: A_.activation(out=ebl[:, tl, 1, :], in_=P7[:, 128:256:64], func=AF.Exp, scale=-1.0), reads=["P7"], writes=[f"ebl{tl}"])
                for d in range(2):
                    eng = "dve" if d == 0 else "pool"
                    E = V if d == 0 else G_
                    t.op(eng, lambda E=E, d=d, tl=tl, s2=s2: E.tensor_tensor(out=QsT[d][:, tl * 128:(tl + 1) * 128], in0=mqT[:, tl * 128:(tl + 1) * 128],
                                                                           in1=ebt[s2][:, d, :], op=ALU.mult), reads=["mqT", f"ebt{s2}"], writes=[f"Qs{d}_{tl}"])
                    t.op("pool", lambda d=d, tl=tl: G_.tensor_scalar(out=wV[d][:, tl, :], in0=Vg[:, tl, :], scalar1=gw[:, tl, 2 + d:3 + d], scalar2=None, op0=ALU.mult),
                         reads=[f"Vg{tl}", "Vgones", f"gw{tl}"], writes=[f"wV{d}_{tl}"])
                p4b = P4.bitcast(BF16)
                t.op("pe", lambda tl=tl, p4b=p4b, s2=s2: P_.transpose(out=p4b[:, s2 * 128:(s2 + 1) * 128], in_=mkT[:, tl * 128:(tl + 1) * 128], identity=identb[:]),
                     reads=["mkT", "identb"], writes=["P4"])
                t.op("act", lambda tl=tl, p4b=p4b, s2=s2: A_.activation(out=Ktok[:, tl, :], in_=p4b[:, s2 * 128:(s2 + 1) * 128], func=AF.Copy),
                     reads=["P4"], writes=[f"Ktok{tl}"])

            if h == 0:
                dbg_dump("ml_b", [(gw.rearrange("p a b -> p (a b)"), NT * 4), (ebl.rearrange("p a b c -> p (a b c)"), NT * 4),
                                  (nlf.rearrange("p a b -> p (a b)"), NT * 2), (QsT[0], TOK), (QsT[1], TOK),
                                  (Ktok.rearrange("p a b -> p (a b)"), TOK), (Vg.rearrange("p a b -> p (a b)"), NT * 129),
                                  (wV[0].rearrange("p a b -> p (a b)"), NT * 129)])
            def chunks(d):
                order = list(range(2 * NT))
                return order if d == 0 else order[::-1]

            def state_update(d, c, to_bf):
                tl, c2 = c // 2, c % 2
                pb, pk = next_bank()
                lo, hi = c2 * 64, c2 * 64 + 64
                t.op("pe", lambda: P_.matmul(pb[:, 0:129], lhsT=Ktok[lo:hi, tl, :], rhs=wV[d][lo:hi, tl, :], start=True, stop=True),
                     reads=[f"Ktok{tl}", f"wV{d}_{tl}"], writes=[pk])
                t.op("dve", lambda: V.scalar_tensor_tensor(out=Cst[d], in0=Cst[d], scalar=ebl[:, tl, d, c2:c2 + 1], in1=pb[:, 0:129],
                                                           op0=ALU.mult, op1=ALU.add), reads=[pk, f"ebl{tl}", f"C{d}"], writes=[f"C{d}"])
                if to_bf:
                    t.op("act", lambda: A_.activation(out=Cbf[d], in_=Cst[d], func=AF.Copy), reads=[f"C{d}"], writes=[f"Cb{d}"])

            for d in range(2):
                t.op("dve", lambda d=d, h=h: V.tensor_copy(out=Cst[d], in_=Cacc[:, d, h, :]), reads=["Cacc"], writes=[f"C{d}"])
                t.op("act", lambda d=d: A_.activation(out=Cbf[d], in_=Cst[d], func=AF.Copy), reads=[f"C{d}"], writes=[f"Cb{d}"])
            if h == 0:
                dbg_dump("ml_d", [(Cst[0], 129), (Cst[1], 129)])
            def scan_tile(d, tl):
                PN, pnk = PNs[d]
                pb, pk = next_bank()
                tok = slice(tl * 128, (tl + 1) * 128)
                t.op("pe", lambda: P_.matmul(pb[:, 0:128], lhsT=mkT[:, tok], rhs=QsT[d][:, tok], start=True, stop=True),
                     reads=["mkT", f"Qs{d}_{tl}"], writes=[pk])
                msk = cst[:, TRIF, :] if d == 0 else cst[:, TRIB, :]
                t.op("dve", lambda: V.scalar_tensor_tensor(out=sdT[d], in0=pb[:, 0:128], scalar=gw[:, tl, d:d + 1], in1=msk, op0=ALU.mult, op1=ALU.mult),
                     reads=[pk, f"gw{tl}", "cst"], writes=[f"sdT{d}"])
                t.op("pe", lambda: P_.matmul(PN[:, 0:129], lhsT=sdT[d], rhs=Vg[:, tl, :], start=True, stop=False),
                     reads=[f"sdT{d}", f"Vg{tl}", "Vgones"], writes=[pnk])
                order = (0, 1) if d == 0 else (1, 0)
                for ci, c2 in enumerate(order):
                    lo = tl * 128 + c2 * 64
                    t.op("pe", lambda c2=c2, lo=lo, ci=ci: P_.matmul(PN[c2 * 64:(c2 + 1) * 64, 0:129], lhsT=QsT[d][:, lo:lo + 64], rhs=Cbf[d],
                                                                 start=False, stop=True), reads=[f"Qs{d}_{tl}", f"Cb{d}"], writes=[pnk])
                    c = tl * 2 + c2
                    last = (c == (2 * NT - 1 if d == 0 else 0))
                    if not last:
                        state_update(d, c, True)
                s2 = d
                t.op("act", lambda: A_.activation(out=st[s2][:, 0:1], in_=PN[:, 128:129], func=AF.Abs), reads=[pnk], writes=[f"st{s2}"])
                t.op("dve", lambda: V.tensor_scalar(out=st[s2][:, 0:1], in0=st[s2][:, 0:1], scalar1=1.0, scalar2=None, op0=ALU.max),
                     reads=[f"st{s2}"], writes=[f"st{s2}"])
                t.op("dve", lambda: V.reciprocal(out=st[s2][:, 1:2], in_=st[s2][:, 0:1]), reads=[f"st{s2}"], writes=[f"st{s2}"])
                return PN, pnk

            for i in range(NT):
                PN, pnk = scan_tile(0, i)
                t.op("dve", lambda PN=PN, i=i: V.tensor_scalar(out=hF[:, i, :], in0=PN[:, 0:128], scalar1=st[0][:, 1:2], scalar2=None, op0=ALU.mult),
                     reads=[pnk, "st0"], writes=[f"hF{i}"])
            if h == 0:
                dbg_dump("ml_e", [(hF.rearrange("p a b -> p (a b)"), TOK)])
            for i in range(NT - 1, -1, -1):
                PN, pnk = scan_tile(1, i)
                s2 = i % 2
                t.op("dve", lambda PN=PN, i=i, s2=s2: V.scalar_tensor_tensor(out=hm[s2], in0=PN[:, 0:128], scalar=st[1][:, 1:2], in1=hF[:, i, :], op0=ALU.mult, op1=ALU.add),
                     reads=[pnk, "st1", f"hF{i}"], writes=[f"hm{s2}"])
                t.op("dve", lambda i=i, s2=s2: V.tensor_tensor(out=hm[s2], in0=hm[s2], in1=so[:, i, :], op=ALU.mult), reads=[f"hm{s2}", f"so{i}"], writes=[f"hm{s2}"])
                if h == 0 and i == NT - 1:
                    dbg_dump("ml_g", [(hm[s2], 128)])
                t.op("act", lambda s2=s2: A_.activation(out=hjunk, in_=hm[s2], func=AF.Identity, accum_out=st[1][:, 2:3]), reads=[f"hm{s2}"], writes=["hjunk", "st1b"])
                t.op("act", lambda s2=s2: A_.activation(out=hjunk, in_=hm[s2], func=AF.Square, accum_out=st[1][:, 3:4]), reads=[f"hm{s2}"], writes=["hjunk", "st1c"])
                t.op("dve", lambda: V.tensor_scalar(out=st[1][:, 4:5], in0=st[1][:, 2:3], scalar1=1.0 / 128, scalar2=None, op0=ALU.mult), reads=["st1b"], writes=["st1d"])
                t.op("dve", lambda: V.tensor_tensor(out=st[1][:, 5:6], in0=st[1][:, 4:5], in1=st[1][:, 4:5], op=ALU.mult), reads=["st1d"], writes=["st1e"])
                t.op("dve", lambda: V.scalar_tensor_tensor(out=st[1][:, 6:7], in0=st[1][:, 3:4], scalar=1.0 / 128, in1=st[1][:, 5:6], op0=ALU.mult, op1=ALU.subtract),
                     reads=["st1c", "st1e"], writes=["st1f"])
                t.op("dve", lambda: V.tensor_scalar(out=st[1][:, 6:7], in0=st[1][:, 6:7], scalar1=EPS, scalar2=None, op0=ALU.add), reads=["st1f"], writes=["st1f"])
                t.op("act", lambda: A_.activation(out=st[1][:, 6:7], in_=st[1][:, 6:7], func=AF.Sqrt), reads=["st1f"], writes=["st1f"])
                t.op("dve", lambda: V.reciprocal(out=st[1][:, 7:8], in_=st[1][:, 6:7]), reads=["st1f"], writes=["st1g"])
                t.op("dve", lambda s2=s2: V.tensor_scalar(out=hm[s2], in0=hm[s2], scalar1=st[1][:, 4:5], scalar2=st[1][:, 7:8], op0=ALU.subtract, op1=ALU.mult),
                     reads=[f"hm{s2}", "st1d", "st1g"], writes=[f"hm{s2}"])
                t.op("pool", lambda s2=s2, h=h: G_.tensor_tensor(out=hm[s2], in0=hm[s2], in1=mlnw_bc[:, h * 128:(h + 1) * 128], op=ALU.mult),
                     reads=[f"hm{s2}", "mlnw_bc"], writes=[f"hm{s2}"])
                t.op("pool", lambda s2=s2, i=i: G_.tensor_tensor(out=mo[s2], in0=hm[s2], in1=szm[:, i, :], op=ALU.mult), reads=[f"hm{s2}", f"szm{i}"], writes=[f"mo{s2}"])
                p4b = P4.bitcast(BF16)
                if h == 0 and i == NT - 1:
                    dbg_dump("ml_h", [(mo[s2], 128)])
                t.op("pe", lambda s2=s2, p4b=p4b: P_.transpose(out=p4b[:, s2 * 128:(s2 + 1) * 128], in_=mo[s2], identity=identb[:]),
                     reads=[f"mo{s2}", "identb"], writes=["P4"])
                t.op("act", lambda s2=s2, p4b=p4b, i=i, h=h: A_.activation(out=mixML[:, h, i * 128:(i + 1) * 128], in_=p4b[:, s2 * 128:(s2 + 1) * 128], func=AF.Copy),
                     reads=["P4"], writes=[f"mixML{h}_{i}"])
            t.barrier()
            if h == 0:
                dbg_dump("ml_f", [(mixML[:, 0, :], TOK)])

        if dbg == "ml":
            cv.off = mark
            dtmp = cv.take([128, TOK])
            for c8 in range(4):
                t.op("dve", lambda c8=c8: V.tensor_copy(out=dtmp, in_=mixML[:, c8, :]), reads=[], writes=["dtmp"])
                t.dma("sp", "dbg", dbg_out[:, (4 + c8) * TOK:(5 + c8) * TOK], dtmp, reads=["dtmp"], writes=["dbgout"])
            nc.sync.wait_ge(t.dma_lanes["dbg"].sem, t.dma_lanes["dbg"].count)
            print("instructions", t.n_inst, "waits", t.n_wait)
            return nc

        cv.off = 4 * TOK // 2
        fnw_bc = cv.take([128, D])
        woutb = cv.take([128, 8, D], BF16)
        xres = [cv.take([128, D]) for _ in range(2)]
        hres = [cv.take([128, D]) for _ in range(2)]
        ojunk = cv.take([128, D])
        ost = cv.take([128, 2 * NT])
        t.dma("pool", "c1", fnw_bc, fnw_row.partition_broadcast(128), writes=["fnw_bc"])
        for q4 in range(4):
            s = wstate["n"] % 2
            wstate["n"] += 1
            t.dma("sp", f"wst{s}", wst[s][:, :, 0:256], wout_v[:, :, q4 * 256:(q4 + 1) * 256], writes=[f"wst{s}"])
            t.op("pool", lambda s=s, q4=q4: G_.tensor_copy(out=woutb[:, :, q4 * 256:(q4 + 1) * 256], in_=wst[s][:, :, 0:256]), reads=[f"wst{s}"], writes=["woutb"])
        for tl in range(NT):
            s2 = tl % 2
            t.dma("sp", f"xres{s2}", xres[s2], xh[(tl + 2) * 128:(tl + 3) * 128, :], writes=[f"xres{s2}"])
            for nb in range(2):
                pb, pk = next_bank()
                for mc in range(8):
                    src = mixT[:, mc, tl * 128:(tl + 1) * 128] if mc < 4 else mixML[:, mc - 4, tl * 128:(tl + 1) * 128]
                    t.op("pe", lambda mc=mc, src=src, pb=pb, nb=nb: P_.matmul(pb[:, :], lhsT=src, rhs=woutb[:, mc, nb * 512:(nb + 1) * 512], start=(mc == 0), stop=(mc == 7)),
                         reads=["woutb"], writes=[pk])
                t.op("dve", lambda pb=pb, nb=nb, s2=s2: V.tensor_tensor(out=hres[s2][:, nb * 512:(nb + 1) * 512], in0=pb[:, :], in1=gate_bc[:, nb * 512:(nb + 1) * 512], op=ALU.mult),
                     reads=[pk, "gate_bc"], writes=[f"hres{s2}_{nb}"])
                t.op("pool", lambda nb=nb, s2=s2: G_.tensor_tensor(out=hres[s2][:, nb * 512:(nb + 1) * 512], in0=hres[s2][:, nb * 512:(nb + 1) * 512],
                                                                 in1=xres[s2][:, nb * 512:(nb + 1) * 512], op=ALU.add),
                     reads=[f"hres{s2}_{nb}", f"xres{s2}"], writes=[f"hres{s2}_{nb}"])
            hk = [f"hres{s2}_0", f"hres{s2}_1"]
            t.op("act", lambda s2=s2, tl=tl: A_.activation(out=ojunk, in_=hres[s2], func=AF.Square, accum_out=ost[:, tl:tl + 1]), reads=hk, writes=["ojunk", f"ost{tl}"])
            t.op("dve", lambda tl=tl: V.tensor_scalar(out=ost[:, tl:tl + 1], in0=ost[:, tl:tl + 1], scalar1=1.0 / D, scalar2=EPS, op0=ALU.mult, op1=ALU.add),
                 reads=[f"ost{tl}"], writes=[f"ost{tl}"])
            t.op("act", lambda tl=tl: A_.activation(out=ost[:, tl:tl + 1], in_=ost[:, tl:tl + 1], func=AF.Sqrt), reads=[f"ost{tl}"], writes=[f"ost{tl}"])
            t.op("dve", lambda tl=tl: V.reciprocal(out=ost[:, NT + tl:NT + tl + 1], in_=ost[:, tl:tl + 1]), reads=[f"ost{tl}"], writes=[f"ost{tl}"])
            t.op("dve", lambda tl=tl, s2=s2: V.scalar_tensor_tensor(out=hres[s2], in0=hres[s2], scalar=ost[:, NT + tl:NT + tl + 1], in1=fnw_bc, op0=ALU.mult, op1=ALU.mult),
                 reads=hk + [f"ost{tl}", "fnw_bc"], writes=hk)
            t.dma("sp", "yout", y[tl * 128:(tl + 1) * 128, :], hres[s2], reads=hk, writes=["yout"])
        yl = t.dma_lanes["yout"]
        nc.sync.wait_ge(yl.sem, yl.count)
        print("instructions", t.n_inst, "waits", t.n_wait)

    return nc


def make_in_maps(x, c, w_ada, b_ada, norm_w, w_in, b_in, conv_w, conv_b, rpb, ml_norm_w, w_out, final_norm_w):
    f = lambda a: np.ascontiguousarray(np.asarray(a, dtype=np.float32))
    x = f(x)[0]
    perm = _col_perm()
    w_in_p = f(f(w_in)[0][:, perm])
    b_in_p = f(f(b_in)[0][perm])
    groups = [b_in_p[g * 128:(g + 1) * 128] for g in range(8)]
    for h in range(ML_HEADS):
        o = NA_COLS + h * MLW
        groups.append(b_in_p[o:o + 128])
        groups.append(b_in_p[o + 128:o + 256])
    bin_col = f(np.stack(groups, 1))
    col8 = lambda v: f(np.asarray(v, np.float32).reshape(-1, 128).T)
    rpbA, cmask = _rpb_tables(f(rpb)[0])
    cw = f(conv_w)[0]
    convw = f(cw.T.reshape(8, 128, 5).transpose(1, 0, 2))
    shared = {
        "w_ada": f(w_ada)[0], "w_in": w_in_p, "w_out": f(w_out)[0], "consts": _consts(),
        "c_col": col8(f(c)[0]), "bada_col": col8(f(b_ada)[0]), "bada_gate": f(f(b_ada)[0][2 * D:]),
        "normw_col": col8(f(norm_w)[0]), "fnw_row": f(final_norm_w), "mlnw_row": f(ml_norm_w)[0],
        "bin_col": bin_col, "bin_row": b_in_p, "convw": convw, "convb": col8(f(conv_b)[0]),
        "rpbA": f(rpbA.reshape(8, 128, 14 * 64)), "cmask": f(cmask.reshape(128, 14 * 64)),
        "tri2": _tri2(), "w_gates": f(f(w_in)[0][:, 4608:4624]), "b_gates": f(f(b_in)[0][4608:4624]),
    }
    in_maps = []
    for i in range(NCORES):
        xhh = np.zeros((HT * 128, D), np.float32)
        lo, hi = i * TOK - 256, i * TOK + TOK + 256
        slo, shi = max(lo, 0), min(hi, T)
        xhh[slo - lo:shi - lo] = x[slo:shi]
        fl = np.zeros((128, 18), np.float32)
        fl[:, 0] = 1.0 if i > 0 else 0.0
        fl[:, 1] = 1.0 if i < NCORES - 1 else 0.0
        for j in range(NCORES):
            fl[:, 2 + j] = 1.0 if j < i else 0.0
            fl[:, 10 + j] = 1.0 if j > i else 0.0
        gb = list(range(4 * i - 1, -1, -1))
        ga = list(range(4 * i + 4, T // 512))
        order = gb + ga
        assert len(order) == NG
        xf = np.concatenate([x[g * 512:(g + 1) * 512] for g in order], 0)
        xfh = np.zeros((NG * 4, D), np.float32)
        ffl = np.zeros((128, NG * 4), np.float32)
        for p, g in enumerate(order):
            if g > 0:
                xfh[p * 4:p * 4 + 2] = x[g * 512 - 2:g * 512]
                ffl[:, p * 4 + 2] = 1.0
            if g < T // 512 - 1:
                xfh[p * 4 + 2:p * 4 + 4] = x[(g + 1) * 512:(g + 1) * 512 + 2]
                ffl[:, p * 4 + 3] = 1.0
            ffl[:, p * 4 + 0] = 1.0 if g < 4 * i else 0.0
            ffl[:, p * 4 + 1] = 1.0 if g > 4 * i else 0.0
        m = dict(shared)
        m.update({"xh": xhh, "mcol": _mcol(i), "flags": fl, "xf": xf, "xfh": xfh, "fflags": ffl})
        in_maps.append(m)
    return in_maps


_NC_CACHE = {}


def kernel(**inputs):
    in_maps = make_in_maps(**inputs)
    if "nc" not in _NC_CACHE:
        _NC_CACHE["nc"] = build_program()
    res = run_bass_kernel_spmd(_NC_CACHE["nc"], in_maps, core_ids=list(range(NCORES)))
    out = np.concatenate([r["y"] for r in res.results], axis=0)
    return out.reshape(1, T, D).astype(np.float32)
```

```python
import numpy as np
from contextlib import ExitStack
import concourse.bass as bass
import concourse.mybir as mybir
from concourse.bass_utils import run_bass_kernel_spmd

F32 = mybir.dt.float32
BF16 = mybir.dt.bfloat16
AF = mybir.ActivationFunctionType
ALU = mybir.AluOpType

NCORES = 8
D = 1024
T = 16384
TOK = T // NCORES
NT = TOK // 128
HT = NT + 4
NA_HEADS = 8
ML_HEADS = 4
EPS = 1e-6
MLW = 644
NA_COLS = 2048
IN_WP = NA_COLS + ML_HEADS * MLW
UNI_WORDS = 23424
NG = 28


class _Stop(Exception):
    pass


class Lane:
    def __init__(self, nc, name):
        self.sem = nc.alloc_semaphore(name=name)
        self.count = 0
        self.name = name


class Trk:
    def __init__(self, nc):
        self.nc = nc
        self.engs = {"pe": nc.tensor, "act": nc.scalar, "dve": nc.vector, "pool": nc.gpsimd, "sp": nc.sync}
        self.lanes = {k: Lane(nc, "sem_" + k) for k in ("pe", "act", "dve", "pool")}
        self.seen = {k: {} for k in self.engs}
        self.last_w = {}
        self.reads = {}
        self.dma_lanes = {}
        self.n_inst = 0
        self.n_wait = 0

    def dma_lane(self, name):
        if name not in self.dma_lanes:
            self.dma_lanes[name] = Lane(self.nc, "dsem_" + name)
        return self.dma_lanes[name]

    def _wait(self, eng, lane, val):
        s = self.seen[eng]
        if s.get(lane.name, 0) >= val:
            return
        if eng == "pe" and lane is self.lanes["pe"]:
            return
        self.engs[eng].wait_ge(lane.sem, val)
        s[lane.name] = val
        self.n_wait += 1

    def _deps(self, eng, reads, writes):
        for k in reads:
            if k in self.last_w:
                self._wait(eng, *self.last_w[k])
        for k in writes:
            if k in self.last_w:
                self._wait(eng, *self.last_w[k])
            for (l, v) in self.reads.get(k, ()):
                self._wait(eng, l, v)

    def _record(self, lane, reads, writes):
        v = lane.count
        for k in reads:
            if k in writes:
                continue
            lst = self.reads.setdefault(k, [])
            lst[:] = [(l, x) for (l, x) in lst if l is not lane]
            lst.append((lane, v))
        for k in writes:
            self.last_w[k] = (lane, v)
            self.reads[k] = []

    def op(self, eng, fn, reads=(), writes=()):
        lane = self.lanes[eng]
        self._deps(eng, reads, writes)
        ins = fn()
        lane.count += 1
        ins.then_inc(lane.sem, 1)
        self._record(lane, reads, writes)
        self.n_inst += 1
        return ins

    def dma(self, q, lane_name, out, in_, reads=(), writes=(), **kw):
        if lane_name in ("c0", "c1"):
            self._oneshot = getattr(self, "_oneshot", 0) + 1
            lane_name = f"os{self._oneshot % 24}"
            lane = self.dma_lane(lane_name)
            if lane.count > 0:
                self._wait(q, lane, lane.count)
        lane = self.dma_lane(lane_name)
        self._deps(q, reads, writes)
        ins = self.engs[q].dma_start(out=out, in_=in_, **kw)
        lane.count += 16
        ins.then_inc(lane.sem, 16)
        self._record(lane, reads, writes)
        self.n_inst += 1
        return ins

    def wait_keys(self, eng, keys):
        for k in keys:
            if k in self.last_w:
                self._wait(eng, *self.last_w[k])
            for (l, v) in self.reads.get(k, ()):
                self._wait(eng, l, v)

    def barrier(self):
        all_lanes = list(self.lanes.values()) + list(self.dma_lanes.values())
        for eng in self.engs:
            for l in all_lanes:
                if l.count > 0:
                    self._wait(eng, l, l.count)


def _col_perm():
    NAW = 512
    cols = list(range(0, 4 * NAW))
    base_ml = 4 * NAW
    gate0 = 4 * NAW + 5 * 512
    for h in range(ML_HEADS):
        for blk in range(5):
            s = base_ml + blk * 512 + h * 128
            cols += list(range(s, s + 128))
        cols += [gate0 + 0 + h, gate0 + 4 + h, gate0 + 8 + h, gate0 + 12 + h]
    return np.array(cols, dtype=np.int64)


def _consts():
    s = np.arange(128)[:, None]
    t = np.arange(128)[None, :]
    same = (s // 64) == (t // 64)
    triF = (same & (s <= t)).astype(np.float32)
    triB = (same & (s >= t)).astype(np.float32)
    blk = same.astype(np.float32)
    ones = np.ones((128, 128), np.float32)
    ident = np.eye(128, dtype=np.float32)
    return np.stack([ident, triF, triB, blk, ones], 0)


def _tri2():
    s = np.arange(128)[:, None]
    t = np.arange(128)[None, :]
    return np.stack([(s > t).astype(np.float32), (s < t).astype(np.float32)], 0)


def _rpb_tables(rpb):
    p = np.arange(128)
    a = p // 64
    k = p % 64
    c = np.arange(64)
    e = np.arange(14)
    dyi = 13 - e
    dy = dyi[None, :] + a[:, None]
    dyv = (dy >= 0) & (dy <= 14)
    dx = np.clip(k[:, None] - c[None, :], -15, 15) + 15
    cs = np.clip(c - 8, 0, 48)
    cv = (k[:, None] >= cs[None, :]) & (k[:, None] < cs[None, :] + 16)
    valid = dyv[:, :, None] & cv[:, None, :]
    dyc = np.clip(dy, 0, 14)
    g = rpb[:, dyc[:, :, None], dx[:, None, :]]
    g = np.where(valid[None], g, np.float32(0.0)).astype(np.float32)
    return np.ascontiguousarray(g), valid.astype(np.float32)


def _na_base(m):
    return 14 if m == 15 else m


def _na_nj(m):
    return 6 if m in (0, 15) else 5


def _mcol(core):
    out = np.zeros((128, 16, 6, 2), np.float32)
    for m in range(16):
        for j in range(_na_nj(m)):
            for b in range(2):
                r = 32 * core + 2 * m + b
                start = min(max(r - 4, 0), 248)
                for a in range(2):
                    kr = 32 * core - 4 + 2 * (_na_base(m) + j) + a
                    ok = (start <= kr < start + 8)
                    out[a * 64:(a + 1) * 64, m, j, b] = 1.0 if ok else 0.0
    return out.reshape(128, 192)


def build_program(dbg=None):
    nc = bass.Bass("TRN2", target_bir_lowering=False)
    try:
        _build_body(nc, dbg)
    except _Stop:
        pass
    return nc


def _build_body(nc, dbg):

    def din(name, shape):
        return nc.dram_tensor(name, list(shape), F32, kind="ExternalInput").ap()

    xh = din("xh", [HT * 128, D])
    w_ada = din("w_ada", [D, 3 * D])
    w_in = din("w_in", [D, IN_WP])
    w_out = din("w_out", [D, D])
    consts = din("consts", [5, 128, 128])
    c_col = din("c_col", [128, 8])
    bada_col = din("bada_col", [128, 24])
    bada_gate = din("bada_gate", [D])
    normw_col = din("normw_col", [128, 8])
    fnw_row = din("fnw_row", [D])
    mlnw_row = din("mlnw_row", [512])
    bin_col = din("bin_col", [128, 16])
    bin_row = din("bin_row", [IN_WP])
    convw = din("convw", [128, 8, 5])
    convb = din("convb", [128, 8])
    rpbA = din("rpbA", [8, 128, 14 * 64])
    cmask = din("cmask", [128, 14 * 64])
    mcol_d = din("mcol", [128, 192])
    flags = din("flags", [128, 18])
    xf = din("xf", [NG * 512, D])
    xfh = din("xfh", [NG * 4, D])
    fflags = din("fflags", [128, NG * 4])
    tri2 = din("tri2", [2, 128, 128])
    w_gates = din("w_gates", [D, 16])
    b_gates = din("b_gates", [16])
    y = nc.dram_tensor("y", [TOK, D], F32, kind="ExternalOutput").ap()
    dbg_out = None
    if dbg:
        dbg_out = nc.dram_tensor("dbg", [128, 8 * TOK], F32, kind="ExternalOutput").ap()

    es = ExitStack()
    with es:
        def sb(name, shape, dt=F32):
            return es.enter_context(nc.sbuf_tensor(name, list(shape), dt))

        def ps(name, shape, dt=F32):
            return es.enter_context(nc.psum_tensor(name, list(shape), dt))

        t = Trk(nc)
        V, A_, P_, G_ = nc.vector, nc.scalar, nc.tensor, nc.gpsimd

        xT = sb("xT", [128, 8, TOK], BF16)
        xTh = sb("xTh", [128, 8, 512], BF16)
        mixT = sb("mixT", [128, 4, TOK], BF16)
        gate_bc = sb("gate_bc", [128, D])
        cst = sb("cst", [128, 5, 128])
        identb = sb("identb", [128, 128], BF16)
        gT = sb("gT", [128, 8])
        shiftT = sb("shiftT", [128, 8])
        bcol = sb("bcol", [128, 16])
        bcolq = sb("bcolq", [128, 4])
        flg = sb("flg", [128, 18])
        cw = sb("cw", [128, 8, 5])
        cb = sb("cb", [128, 8])
        kcol = sb("kcol", [128, 4])
        wst = [sb(f"wst{i}", [128, 8, 256]) for i in range(2)]
        wbf = [sb(f"wbf{i}", [128, 8, 512], BF16) for i in range(2)]
        Cacc = sb("Cacc", [128, 2, 4, 129])
        UNI = sb("UNI", [128, UNI_WORDS])

        PA = ps("PA", [128, 1024])
        PB = ps("PB", [128, 1024])
        P4 = ps("P4", [128, 512])
        P5 = ps("P5", [128, 512])
        P6 = ps("P6", [128, 512])
        P7 = ps("P7", [128, 512])
        IDENT, TRIF, TRIB, BLK, ONES = range(5)

        class Carver:
            def __init__(self):
                self.off = 0

            def take(self, shape, dt=F32):
                n = int(np.prod(shape[1:]))
                words = n if dt == F32 else (n + 1) // 2
                ap = UNI[:, self.off:self.off + words]
                self.off += words
                assert self.off <= UNI_WORDS, self.off
                if dt != F32:
                    ap = ap.bitcast(BF16)[:, 0:n]
                if len(shape) > 2:
                    names = " ".join(f"d{i}" for i in range(1, len(shape)))
                    kw = {f"d{i}": shape[i] for i in range(1, len(shape))}
                    ap = ap.rearrange(f"p ({names}) -> p {names}", **kw)
                return ap

        def dbg_dump(stage, items):
            if dbg != stage:
                return
            t.barrier()
            off = 0
            dt_ = UNI[:, UNI_WORDS - 2048:UNI_WORDS]
            for ap, n in items:
                o2 = 0
                while o2 < n:
                    w = min(2048, n - o2)
                    t.op("dve", lambda ap=ap, o2=o2, w=w: V.tensor_copy(out=dt_[:, 0:w], in_=ap[:, o2:o2 + w]), reads=[], writes=["dbgtmp"])
                    t.dma("sp", "dbg", dbg_out[:, off:off + w], dt_[:, 0:w], reads=["dbgtmp"], writes=["dbgout"])
                    off += w
                    o2 += w
            nc.sync.wait_ge(t.dma_lanes["dbg"].sem, t.dma_lanes["dbg"].count)
            print("dbg stop at", stage, "instructions", t.n_inst, "waits", t.n_wait)
            raise _Stop()

        t.dma("sp", "c0", cst[:], consts.rearrange("n p f -> p n f"), writes=["cst"])
        t.dma("sp", "c0", flg[:], flags, writes=["flg"])
        t.dma("sp", "c0", bcol[:], bin_col, writes=["bcol"])
        t.dma("sp", "c0", cw[:], convw, writes=["cw"])
        t.dma("sp", "c0", cb[:], convb, writes=["cb"])
        t.op("dve", lambda: V.tensor_copy(out=identb[:], in_=cst[:, IDENT, :]), reads=["cst"], writes=["identb"])
        t.op("pool", lambda: G_.memset(kcol[:, 0:1], 1.0), writes=["kcol"])
        t.op("pool", lambda: G_.memset(kcol[:, 1:2], float(np.log(128.0 ** -0.5))), writes=["kcol"])
        t.op("pool", lambda: G_.memset(kcol[:, 2:3], EPS), writes=["kcol"])
        t.op("pool", lambda: G_.memset(kcol[:, 3:4], 0.0), writes=["kcol"])
        t.op("dve", lambda: V.tensor_scalar(out=bcolq[:], in0=bcol[:, 0:4], scalar1=0.125, scalar2=None, op0=ALU.mult),
             reads=["bcol"], writes=["bcolq"])

        cv = Carver()
        ccol = cv.take([128, 8])
        cact = cv.take([128, 8])
        cbc = cv.take([128, 8, 128])
        badac = cv.take([128, 24])
        nwc = cv.take([128, 8])
        bgate = cv.take([128, D])
        modT = cv.take([128, 16])
        wada_sb = [cv.take([128, 8, 512]) for _ in range(2)]
        t.dma("sp", "c0", ccol, c_col, writes=["ccol"])
        t.dma("sp", "c0", badac, bada_col, writes=["badac"])
        t.dma("sp", "c0", nwc, normw_col, writes=["nwc"])
        t.dma("pool", "c1", bgate, bada_gate.partition_broadcast(128), writes=["bgate"])
        t.op("act", lambda: A_.activation(out=cact, in_=ccol, func=AF.Silu), reads=["ccol"], writes=["cact"])
        for k in range(8):
            t.op("dve", lambda k=k: V.tensor_copy(out=cbc[:, k, :], in_=cact[:, k:k + 1].broadcast_to([128, 128])),
                 reads=["cact"], writes=[f"cbc{k}"])
        wada_v = w_ada.rearrange("(k p) n -> p k n", p=128)
        for ch in range(6):
            slot = ch % 2
            t.dma("sp", f"wada{slot}", wada_sb[slot], wada_v[:, :, ch * 512:(ch + 1) * 512], writes=[f"wada{slot}"])
            if ch < 4:
                for jn in range(4):
                    col = ch * 4 + jn
                    for k in range(8):
                        t.op("pe", lambda k=k, jn=jn, col=col, slot=slot: P_.matmul(
                            P4[:, col:col + 1], lhsT=wada_sb[slot][:, k, jn * 128:(jn + 1) * 128],
                            rhs=cact[:, k:k + 1], start=(k == 0), stop=(k == 7)),
                            reads=[f"wada{slot}", "cact"], writes=["P4"])
            else:
                nb = ch - 4
                pt = P5 if nb == 0 else P6
                for k in range(8):
                    t.op("pe", lambda k=k, pt=pt, slot=slot: P_.matmul(
                        pt[:, :], lhsT=cbc[:, k, :], rhs=wada_sb[slot][:, k, :], start=(k == 0), stop=(k == 7)),
                        reads=[f"wada{slot}", f"cbc{k}"], writes=["P5" if nb == 0 else "P6"])
                t.op("dve", lambda nb=nb, pt=pt: V.tensor_tensor(out=gate_bc[:, nb * 512:(nb + 1) * 512], in0=pt[:, :],
                                                                in1=bgate[:, nb * 512:(nb + 1) * 512], op=ALU.add),
                     reads=["P5" if nb == 0 else "P6", "bgate"], writes=["gate_bc"])
        t.op("dve", lambda: V.tensor_tensor(out=modT, in0=P4[:, 0:16], in1=badac[:, 0:16], op=ALU.add),
             reads=["P4", "badac"], writes=["modT"])
        t.op("dve", lambda: V.tensor_copy(out=shiftT[:], in_=modT[:, 0:8]), reads=["modT"], writes=["shiftT"])
        t.op("dve", lambda: V.scalar_tensor_tensor(out=gT[:], in0=modT[:, 8:16], scalar=1.0, in1=nwc, op0=ALU.add, op1=ALU.mult),
             reads=["modT", "nwc"], writes=["gT"])
        t.barrier()

        cv = Carver()
        xin = [cv.take([128, D]) for _ in range(3)]
        xnb = [cv.take([128, D], BF16) for _ in range(2)]
        xtmp = [cv.take([128, D]) for _ in range(2)]
        sq_junk = cv.take([128, D])
        PT_x = [P4, P5]

        def xT_tile(ht):
            if 2 <= ht < 18:
                return xT[:, :, (ht - 2) * 128:(ht - 1) * 128]
            hi = ht if ht < 2 else ht - 18 + 2
            return xTh[:, :, hi * 128:(hi + 1) * 128]

        nstate = {"n": 0}
        NRS = 64
        ssq = cv.take([128, NRS])
        rstd = cv.take([128, NRS])

        def norm_tile(src, nr, dst, dkey):
            n = nstate["n"]
            nstate["n"] += 1
            s3, s2, c = n % 3, n % 2, n % NRS
            t.dma("sp", f"xin{s3}", xin[s3][0:nr, :], src, writes=[f"xin{s3}"])
            t.op("act", lambda: A_.activation(out=sq_junk[0:nr, :], in_=xin[s3][0:nr, :], func=AF.Square, accum_out=ssq[0:nr, c:c + 1]),
                 reads=[f"xin{s3}"], writes=["sq_junk", f"ssq{c}"])
            t.op("dve", lambda: V.tensor_scalar(out=rstd[0:nr, c:c + 1], in0=ssq[0:nr, c:c + 1], scalar1=1.0 / D, scalar2=EPS,
                                                op0=ALU.mult, op1=ALU.add), reads=[f"ssq{c}"], writes=[f"rstd{c}"])
            t.op("act", lambda: A_.activation(out=rstd[0:nr, c:c + 1], in_=rstd[0:nr, c:c + 1], func=AF.Sqrt),
                 reads=[f"rstd{c}"], writes=[f"rstd{c}"])
            t.op("dve", lambda: V.reciprocal(out=rstd[0:nr, c:c + 1], in_=rstd[0:nr, c:c + 1]), reads=[f"rstd{c}"], writes=[f"rstd{c}"])
            t.op("act", lambda: A_.activation(out=xnb[s2][0:nr, :], in_=xin[s3][0:nr, :], func=AF.Copy, scale=rstd[0:nr, c:c + 1]),
                 reads=[f"xin{s3}", f"rstd{c}"], writes=[f"xnb{s2}"])
            ptb = PT_x[s2].bitcast(BF16).rearrange("p (k f) -> p k f", k=8)
            pkey = "P4" if s2 == 0 else "P5"
            for k in range(8):
                t.op("pe", lambda k=k: P_.transpose(out=ptb[:, k, 0:nr], in_=xnb[s2][0:nr, k * 128:(k + 1) * 128], identity=identb[0:nr, 0:nr]),
                     reads=[f"xnb{s2}", "identb"], writes=[pkey])
            xt3 = xtmp[s2].rearrange("p (k f) -> p k f", k=8)
            t.op("dve", lambda: V.tensor_tensor(out=xt3[:, :, 0:nr], in0=ptb[:, :, 0:nr], in1=gT[:, :].unsqueeze(2).broadcast_to([128, 8, nr]), op=ALU.mult),
                 reads=[pkey, "gT"], writes=[f"xtmp{s2}"])
            t.op("pool", lambda: G_.tensor_tensor(out=dst, in0=xt3[:, :, 0:nr], in1=shiftT[:, :].unsqueeze(2).broadcast_to([128, 8, nr]), op=ALU.add),
                 reads=[f"xtmp{s2}", "shiftT"], writes=[dkey])

        for ht in range(HT):
            norm_tile(xh[ht * 128:(ht + 1) * 128, :], 128, xT_tile(ht), f"xT{ht}")

        pbanks = [(PA[:, 0:512], "PA0"), (PA[:, 512:1024], "PA1"), (PB[:, 0:512], "PB0"), (PB[:, 512:1024], "PB1")]
        pst = {"n": 0}

        def next_bank():
            b = pbanks[pst["n"] % 4]
            pst["n"] += 1
            return b

        win_v = w_in.rearrange("(k p) n -> p k n", p=128)
        wout_v = w_out.rearrange("(k p) n -> p k n", p=128)
        wstate = {"n": 0}

        def load_w(src_v, c0, ncols, slot):
            off = 0
            while off < ncols:
                n = min(256, ncols - off)
                s = wstate["n"] % 2
                wstate["n"] += 1
                t.dma("sp", f"wst{s}", wst[s][:, :, 0:n], src_v[:, :, c0 + off:c0 + off + n], writes=[f"wst{s}"])
                t.op("pool", lambda s=s, n=n, off=off: G_.tensor_copy(out=wbf[slot][:, :, off:off + n], in_=wst[s][:, :, 0:n]),
                     reads=[f"wst{s}"], writes=[f"wbf{slot}"])
                off += n

        def tok_blocks(with_halo):
            blks = []
            if with_halo:
                blks.append((xTh[:, :, 0:256], 256, 0))
            for b in range(4):
                blks.append((xT[:, :, b * 512:(b + 1) * 512], 512, 256 + b * 512))
            if with_halo:
                blks.append((xTh[:, :, 256:512], 256, 2304))
            return blks

        def xkeys(c0, n):
            return [f"xT{c0 // 128 + i}" for i in range(n // 128)]

        wFk = cv.take([128, 8, 512], BF16)
        wFv = cv.take([128, 8, 512], BF16)
        wFg = cv.take([128, 8, 16], BF16)
        xg = [cv.take([128, 8, 516], BF16) for _ in range(2)]
        preF2 = [cv.take([128, 516]) for _ in range(2)]
        accF2 = [cv.take([128, 512]) for _ in range(2)]
        kTF2 = [cv.take([128, 512], BF16) for _ in range(2)]
        xgh = [cv.take([128, 8, 4], BF16) for _ in range(2)]
        KtokF = cv.take([128, 4, 4, 128], BF16)
        Vf = cv.take([128, 4, 4, 129], BF16)
        Gf = cv.take([128, 4, 16])
        nlfF = cv.take([128, 4, 2, 4])
        tmpF = cv.take([128, 4, 2, 4])
        Wt = cv.take([128, 4, 2, 4])
        accg = cv.take([128, 2, 4])
        tmpa = cv.take([128, 2, 4])
        wVf = cv.take([128, 4, 2, 4, 129], BF16)
        bFv = cv.take([128, 512])
        bFg = cv.take([128, 16])
        ffl = cv.take([128, NG * 4])
        sgt = cv.take([128, 2, 128])
        print("phase F union words", cv.off)
        t.dma("sp", "c0", ffl, fflags, writes=["ffl"])
        t.dma("sp", "c0", sgt, tri2.rearrange("n p f -> p n f"), writes=["sgt"])
        t.dma("pool", "c1", bFg, b_gates.partition_broadcast(128), writes=["bFg"])
        for hh in range(ML_HEADS):
            c0 = NA_COLS + hh * MLW
            t.dma("pool", "c1", bFv[:, hh * 128:(hh + 1) * 128], bin_row[c0 + 256:c0 + 384].partition_broadcast(128), writes=["bFv"])
            for (dstw, cc) in ((wFk, c0 + 128), (wFv, c0 + 256)):
                sidx = wstate["n"] % 2
                wstate["n"] += 1
                t.dma("sp", f"wst{sidx}", wst[sidx][:, :, 0:128], win_v[:, :, cc:cc + 128], writes=[f"wst{sidx}"])
                t.op("pool", lambda sidx=sidx, dstw=dstw, hh=hh: G_.tensor_copy(out=dstw[:, :, hh * 128:(hh + 1) * 128], in_=wst[sidx][:, :, 0:128]),
                     reads=[f"wst{sidx}"], writes=["wF"])
        sidx = wstate["n"] % 2
        wstate["n"] += 1
        t.dma("sp", f"wst{sidx}", wst[sidx][:, :, 0:16], w_gates.rearrange("(k p) n -> p k n", p=128), writes=[f"wst{sidx}"])
        t.op("pool", lambda: G_.tensor_copy(out=wFg, in_=wst[sidx][:, :, 0:16]), reads=[f"wst{sidx}"], writes=["wF"])
        t.op("pool", lambda: G_.memset(Vf[:, :, :, 128:129], 1.0), writes=["Vfones"])
        t.op("pool", lambda: G_.memset(accg, 0.0), writes=["accg"])
        t.op("pool", lambda: G_.memset(Cacc[:], 0.0), writes=["Cacc"])
        Gv = Gf.rearrange("p t (d x h) -> p t d x h", d=2, x=2)
        SGT, SLT = 0, 1
        def norm_group(gi):
            xs = gi % 2
            norm_tile(xfh[gi * 4:gi * 4 + 4, :], 4, xgh[xs], f"xgh{xs}")
            for j in range(4):
                norm_tile(xf[(gi * 4 + j) * 128:(gi * 4 + j + 1) * 128, :], 128, xg[xs][:, :, 2 + j * 128:2 + (j + 1) * 128], f"xg{xs}")

        norm_group(0)
        for gi in range(NG):
            xs = gi % 2
            xga = xg[xs]
            for j in range(4):
                pb, pk = next_bank()
                for k in range(8):
                    t.op("pe", lambda k=k, j=j, pb=pb: P_.matmul(pb[:, :], lhsT=xga[:, k, 2 + j * 128:2 + (j + 1) * 128], rhs=wFv[:, k, :], start=(k == 0), stop=(k == 7)),
                         reads=["wF", f"xg{xs}"], writes=[pk])
                t.op("dve", lambda j=j, pb=pb: V.tensor_tensor(out=Vf[:, j, :, 0:128], in0=pb.rearrange("p (h d) -> p h d", h=4),
                                                              in1=bFv.rearrange("p (h d) -> p h d", h=4), op=ALU.add), reads=[pk, "bFv"], writes=["Vf"])
                for k in range(8):
                    t.op("pe", lambda k=k, j=j: P_.matmul(P6[:, j * 16:(j + 1) * 16], lhsT=xga[:, k, 2 + j * 128:2 + (j + 1) * 128], rhs=wFg[:, k, :], start=(k == 0), stop=(k == 7)),
                         reads=["wF", f"xg{xs}"], writes=["P6"])
            t.op("dve", lambda: V.tensor_tensor(out=Gf, in0=P6[:, 0:64].rearrange("p (t g) -> p t g", t=4), in1=bFg.unsqueeze(1).broadcast_to([128, 4, 16]), op=ALU.add),
                 reads=["P6", "bFg"], writes=["Gf"])
            t.op("act", lambda: A_.activation(out=nlfF, in_=Gv[:, :, :, 1, :], func=AF.Exp, scale=-1.0), reads=["Gf"], writes=["nlfF"])
            t.op("act", lambda: A_.activation(out=nlfF, in_=nlfF, func=AF.Ln, bias=kcol[:, 0:1], scale=1.0), reads=["nlfF", "kcol"], writes=["nlfF"])
            P7c = P7[:, 0:32].rearrange("p (t d h) -> p t d h", t=4, d=2)
            for j in range(4):
                for d in range(2):
                    others = [jj for jj in range(4) if (jj > j if d == 0 else jj < j)]
                    seq = [(j, sgt[:, SGT if d == 0 else SLT, :])] + [(jj, cst[:, ONES, :]) for jj in others]
                    for qi, (jj, lh) in enumerate(seq):
                        t.op("pe", lambda j=j, d=d, jj=jj, lh=lh, qi=qi, nq=len(seq): P_.matmul(P7c[:, j, d, :], lhsT=lh, rhs=nlfF[:, jj, d, :], start=(qi == 0), stop=(qi == nq - 1)),
                             reads=["nlfF", "sgt", "cst"], writes=["P7"])
            P7t = P7[:, 32:40].rearrange("p (d h) -> p d h", d=2)
            for d in range(2):
                for jj in range(4):
                    t.op("pe", lambda d=d, jj=jj: P_.matmul(P7t[:, d, :], lhsT=cst[:, ONES, :], rhs=nlfF[:, jj, d, :], start=(jj == 0), stop=(jj == 3)),
                         reads=["nlfF", "cst"], writes=["P7"])
            fl = ffl[:, gi * 4:gi * 4 + 2]
            t.op("dve", lambda: V.tensor_tensor(out=tmpF, in0=P7c, in1=accg.unsqueeze(1).broadcast_to([128, 4, 2, 4]), op=ALU.add), reads=["P7", "accg"], writes=["tmpF"])
            t.op("dve", lambda: V.tensor_tensor(out=tmpF, in0=Gv[:, :, :, 0, :], in1=tmpF, op=ALU.subtract), reads=["Gf", "tmpF"], writes=["tmpF"])
            t.op("act", lambda: A_.activation(out=Wt, in_=tmpF, func=AF.Exp), reads=["tmpF"], writes=["Wt"])
            t.op("dve", lambda fl=fl: V.tensor_tensor(out=Wt, in0=Wt, in1=fl.unsqueeze(1).unsqueeze(3).broadcast_to([128, 4, 2, 4]), op=ALU.mult), reads=["Wt", "ffl"], writes=["Wt"])
            t.op("dve", lambda fl=fl: V.tensor_tensor(out=tmpa, in0=P7t, in1=fl.unsqueeze(2).broadcast_to([128, 2, 4]), op=ALU.mult), reads=["P7", "ffl"], writes=["tmpa"])
            t.op("dve", lambda: V.tensor_tensor(out=accg, in0=accg, in1=tmpa, op=ALU.add), reads=["accg", "tmpa"], writes=["accg"])
            for d in range(2):
                eng = "dve" if d == 0 else "pool"
                E = V if d == 0 else G_
                t.op(eng, lambda E=E, d=d: E.tensor_tensor(out=wVf[:, :, d, :, :], in0=Vf, in1=Wt[:, :, d, :].unsqueeze(3).broadcast_to([128, 4, 4, 129]), op=ALU.mult),
                     reads=["Vf", "Vfones", "Wt"], writes=[f"wVf{d}"])
            if gi + 1 < NG:
                norm_group(gi + 1)
            for hh in range(ML_HEADS):
                preF, accF, kTF = preF2[hh % 2], accF2[hh % 2], kTF2[hh % 2]
                pk_, ak_, kk_ = f"preF{hh % 2}", f"accF{hh % 2}", f"kTF{hh % 2}"
                pb, pk = next_bank()
                for k in range(8):
                    t.op("pe", lambda k=k, hh=hh, pb=pb: P_.matmul(pb[:, 0:512], lhsT=wFk[:, k, hh * 128:(hh + 1) * 128], rhs=xga[:, k, 2:514], start=(k == 0), stop=(k == 7)),
                         reads=["wF", f"xg{xs}"], writes=[pk])
                t.op("act", lambda hh=hh, pb=pb, preF=preF: A_.activation(out=preF[:, 2:514], in_=pb[:, 0:512], func=AF.Identity, scale=1.0, bias=bcol[:, 9 + 2 * hh:10 + 2 * hh]),
                     reads=[pk, "bcol"], writes=[pk_])
                pb, pk = next_bank()
                for k in range(8):
                    t.op("pe", lambda k=k, hh=hh, pb=pb: P_.matmul(pb[:, 0:4], lhsT=wFk[:, k, hh * 128:(hh + 1) * 128], rhs=xgh[xs][:, k, :], start=(k == 0), stop=(k == 7)),
                         reads=["wF", f"xgh{xs}"], writes=[pk])
                t.op("act", lambda hh=hh, pb=pb, preF=preF: A_.activation(out=preF[:, 0:2], in_=pb[:, 0:2], func=AF.Identity, scale=1.0, bias=bcol[:, 9 + 2 * hh:10 + 2 * hh]),
                     reads=[pk, "bcol"], writes=[pk_])
                t.op("act", lambda hh=hh, pb=pb, preF=preF: A_.activation(out=preF[:, 514:516], in_=pb[:, 2:4], func=AF.Identity, scale=1.0, bias=bcol[:, 9 + 2 * hh:10 + 2 * hh]),
                     reads=[pk, "bcol"], writes=[pk_])
                t.op("dve", lambda gi=gi: V.tensor_scalar(out=preF[:, 0:2], in0=preF[:, 0:2], scalar1=ffl[:, gi * 4 + 2:gi * 4 + 3], scalar2=None, op0=ALU.mult), reads=[pk_, "ffl"], writes=[pk_])
                t.op("dve", lambda gi=gi: V.tensor_scalar(out=preF[:, 514:516], in0=preF[:, 514:516], scalar1=ffl[:, gi * 4 + 3:gi * 4 + 4], scalar2=None, op0=ALU.mult), reads=[pk_, "ffl"], writes=[pk_])
                gidx = 4 + hh
                t.op("dve", lambda gidx=gidx: V.tensor_scalar(out=accF, in0=preF[:, 0:512], scalar1=cw[:, gidx, 0:1], scalar2=cb[:, gidx:gidx + 1], op0=ALU.mult, op1=ALU.add),
                     reads=[pk_, "cw", "cb"], writes=[ak_])
                for jc in range(1, 5):
                    t.op("dve", lambda gidx=gidx, jc=jc: V.scalar_tensor_tensor(out=accF, in0=preF[:, jc:jc + 512], scalar=cw[:, gidx, jc:jc + 1], in1=accF, op0=ALU.mult, op1=ALU.add),
                         reads=[pk_, "cw", ak_], writes=[ak_])
                t.op("act", lambda: A_.activation(out=kTF, in_=accF, func=AF.Silu), reads=[ak_], writes=[kk_])
                p4b = P4.bitcast(BF16).rearrange("p (k f) -> p k f", k=8)
                for j in range(4):
                    t.op("pe", lambda j=j, p4b=p4b: P_.transpose(out=p4b[:, j, :], in_=kTF[:, j * 128:(j + 1) * 128], identity=identb[:]), reads=[kk_, "identb"], writes=["P4"])
                t.op("act", lambda hh=hh, p4b=p4b: A_.activation(out=KtokF[:, hh, :, :], in_=p4b[:, 0:4, :], func=AF.Copy), reads=["P4"], writes=[f"KtokF{hh}"])
                for d in range(2):
                    pb, pk = next_bank()
                    for j in range(4):
                        t.op("pe", lambda j=j, d=d, hh=hh, pb=pb: P_.matmul(pb[:, 0:129], lhsT=KtokF[:, hh, j, :], rhs=wVf[:, j, d, hh, :], start=(j == 0), stop=(j == 3)),
                             reads=[f"KtokF{hh}", f"wVf{d}"], writes=[pk])
                    t.op("dve", lambda d=d, hh=hh, pb=pb: V.tensor_tensor(out=Cacc[:, d, hh, :], in0=Cacc[:, d, hh, :], in1=pb[:, 0:129], op=ALU.add),
                         reads=[pk, "Cacc"], writes=["Cacc"])
        dbg_dump("f_c", [(Cacc[:].rearrange("p a b c -> p (a b c)"), 8 * 129)])
        t.barrier()

        cv = Carver()
        KT = cv.take([128, 4, HT * 128], BF16)
        QT = cv.take([128, 4, TOK], BF16)
        Vaug = cv.take([128, HT, 8, 65], BF16)
        Ar = cv.take([128, 8, 14, 64], BF16)
        mcol = cv.take([128, 192])
        bias_bc = cv.take([128, 512])
        mark = cv.off
        cmk = cv.take([128, 14 * 64])
        rtmp = [cv.take([128, 14 * 64]) for _ in range(2)]
        t.dma("sp", "c0", mcol, mcol_d, writes=["mcol"])
        t.dma("sp", "c0", cmk, cmask, writes=["cmk"])
        for h in range(8):
            s = h % 2
            t.dma("sp", f"rtmp{s}", rtmp[s], rpbA[h], writes=[f"rtmp{s}"])
            t.op("act", lambda s=s: A_.activation(out=rtmp[s], in_=rtmp[s], func=AF.Exp), reads=[f"rtmp{s}"], writes=[f"rtmp{s}"])
            t.op("dve", lambda s=s, h=h: V.tensor_tensor(out=Ar[:, h, :, :].rearrange("p e c -> p (e c)"), in0=rtmp[s], in1=cmk, op=ALU.mult),
                 reads=[f"rtmp{s}", "cmk"], writes=[f"Ar{h}"])
        t.barrier()
        cv.off = mark
        expS = [cv.take([128, 6, 128]) for _ in range(2)]
        PTt = [cv.take([128, 6, 128], BF16) for _ in range(2)]
        sz_t = cv.take([128, 512])
        sz_b = cv.take([128, 512], BF16)
        rden = cv.take([128, 8])
        otmp = cv.take([128, 512])
        ob = cv.take([128, 512], BF16)
        print("NA union words", cv.off)
        t.op("pool", lambda: G_.memset(Vaug[:, :, :, 64:65], 1.0), writes=["Vones"])

        for which in range(2):
            slot = which
            load_w(win_v, which * 512, 512, slot)
            for g in range(4):
                for (xap, n, c0) in tok_blocks(with_halo=(which == 1)):
                    pb, pk = next_bank()
                    for k in range(8):
                        t.op("pe", lambda k=k, g=g, xap=xap, n=n, pb=pb, slot=slot: P_.matmul(
                            pb[:, 0:n], lhsT=wbf[slot][:, k, g * 128:(g + 1) * 128], rhs=xap[:, k, :], start=(k == 0), stop=(k == 7)),
                            reads=[f"wbf{slot}"] + xkeys(c0, n), writes=[pk])
                    if which == 0:
                        q0 = c0 - 256
                        t.op("act", lambda g=g, pb=pb, n=n, q0=q0: A_.activation(out=QT[:, g, q0:q0 + n], in_=pb[:, 0:n], func=AF.Identity,
                                                                                 scale=0.125, bias=bcolq[:, g:g + 1]),
                             reads=[pk, "bcolq"], writes=[f"QT{g}_{q0 // 128 + i}" for i in range(n // 128)])
                    else:
                        t.op("act", lambda g=g, pb=pb, n=n, c0=c0: A_.activation(out=KT[:, g, c0:c0 + n], in_=pb[:, 0:n], func=AF.Identity,
                                                                                 scale=1.0, bias=bcol[:, 4 + g:5 + g]),
                             reads=[pk, "bcol"], writes=[f"KT{g}_{c0 // 128 + i}" for i in range(n // 128)])
        load_w(win_v, 1024, 512, 0)
        t.dma("pool", "c1", bias_bc, bin_row[1024:1536].partition_broadcast(128), reads=[], writes=["bias_bc"])
        for ht in range(HT):
            pb, pk = next_bank()
            xa = xT_tile(ht)
            for k in range(8):
                t.op("pe", lambda k=k, xa=xa, pb=pb: P_.matmul(pb[:, :], lhsT=xa[:, k, :], rhs=wbf[0][:, k, :], start=(k == 0), stop=(k == 7)),
                     reads=["wbf0", f"xT{ht}"], writes=[pk])
            t.op("dve", lambda ht=ht, pb=pb: V.tensor_tensor(out=Vaug[:, ht, :, 0:64], in0=pb.rearrange("p (h d) -> p h d", h=8),
                                                            in1=bias_bc.rearrange("p (h d) -> p h d", h=8), op=ALU.add),
                 reads=[pk, "bias_bc"], writes=[f"V{ht}"])
        load_w(win_v, 1536, 512, 1)
        t.wait_keys("pool", ["bias_bc"])
        t.dma("pool", "c1", bias_bc, bin_row[1536:2048].partition_broadcast(128), reads=[], writes=["bias_bc"])

        for m in range(NT):
            base, nj = _na_base(m), _na_nj(m)
            qt = m
            for k in range(8):
                t.op("pe", lambda k=k, m=m: P_.matmul(P7[:, :], lhsT=xT[:, k, m * 128:(m + 1) * 128], rhs=wbf[1][:, k, :], start=(k == 0), stop=(k == 7)),
                     reads=["wbf1", f"xT{m + 2}"], writes=["P7"])
            t.op("dve", lambda: V.tensor_tensor(out=sz_t, in0=P7[:, :], in1=bias_bc, op=ALU.add), reads=["P7", "bias_bc"], writes=["sz_t"])
            t.op("act", lambda: A_.activation(out=sz_b, in_=sz_t, func=AF.Silu), reads=["sz_t"], writes=["sz_b"])
            for h in range(8):
                g, hh = h // 2, h % 2
                sl = h % 2
                PS = PA if sl == 0 else PB
                pskeys = ["PA0", "PA1"] if sl == 0 else ["PB0", "PB1"]
                PS3 = PS.rearrange("p (j q) -> p j q", q=128)
                for j in range(nj):
                    kt = base + j
                    t.op("pe", lambda j=j, kt=kt, g=g, hh=hh, PS3=PS3, qt=qt: P_.matmul(
                        PS3[:, j, :], lhsT=KT[hh * 64:(hh + 1) * 64, g, kt * 128:(kt + 1) * 128],
                        rhs=QT[hh * 64:(hh + 1) * 64, g, qt * 128:(qt + 1) * 128], start=True, stop=True),
                        reads=[f"KT{g}_{kt}", f"QT{g}_{qt}"], writes=pskeys)
                t.op("act", lambda sl=sl, nj=nj, PS3=PS3: A_.activation(out=expS[sl][:, 0:nj, :], in_=PS3[:, 0:nj, :], func=AF.Exp),
                     reads=pskeys, writes=[f"expS{sl}"])
                eng = "dve" if h % 2 == 0 else "pool"
                E = V if eng == "dve" else G_
                interior = 2 <= m <= 13
                for j in range(nj):
                    dyi0 = 2 * (base - m + j) + 3
                    e0 = 13 - dyi0
                    if interior and 1 <= j <= 3:
                        t.op(eng, lambda E=E, sl=sl, j=j, h=h, e0=e0: E.tensor_tensor(
                            out=PTt[sl][:, j, :], in0=expS[sl][:, j, :], in1=Ar[:, h, e0:e0 + 2, :].rearrange("p e c -> p (e c)"), op=ALU.mult),
                            reads=[f"expS{sl}", f"Ar{h}"], writes=[f"PT{sl}"])
                    else:
                        for b in range(2):
                            mc = (m * 6 + j) * 2 + b
                            t.op("dve", lambda E=V, sl=sl, j=j, h=h, e0=e0, b=b, mc=mc: E.scalar_tensor_tensor(
                                out=PTt[sl][:, j, b * 64:(b + 1) * 64], in0=expS[sl][:, j, b * 64:(b + 1) * 64], scalar=mcol[:, mc:mc + 1],
                                in1=Ar[:, h, e0 + b, :], op0=ALU.mult, op1=ALU.mult),
                                reads=[f"expS{sl}", f"Ar{h}", "mcol"], writes=[f"PT{sl}"])
                PO = P4 if h < 4 else P5
                pok = "P4" if h < 4 else "P5"
                PO3 = PO[:, 0:260].rearrange("p (h d) -> p h d", d=65)
                for j in range(nj):
                    kt = base + j
                    t.op("pe", lambda j=j, kt=kt, h=h, sl=sl, PO3=PO3, nj=nj: P_.matmul(
                        PO3[:, h % 4, :], lhsT=PTt[sl][:, j, :], rhs=Vaug[:, kt, h, :], start=(j == 0), stop=(j == nj - 1)),
                        reads=[f"PT{sl}", f"V{kt}", "Vones"], writes=[pok])
            for half in range(2):
                PO = P4 if half == 0 else P5
                pok = "P4" if half == 0 else "P5"
                PO3 = PO[:, 0:260].rearrange("p (h d) -> p h d", d=65)
                t.op("dve", lambda half=half, PO3=PO3: V.reciprocal(out=rden[:, half * 4:(half + 1) * 4], in_=PO3[:, :, 64]),
                     reads=[pok], writes=["rden"])
                t.op("dve", lambda half=half, PO3=PO3: V.tensor_tensor(
                    out=otmp[:, half * 256:(half + 1) * 256].rearrange("p (h d) -> p h d", d=64), in0=PO3[:, :, 0:64],
                    in1=rden[:, half * 4:(half + 1) * 4].unsqueeze(2).broadcast_to([128, 4, 64]), op=ALU.mult),
                    reads=[pok, "rden"], writes=["otmp"])
            t.op("dve", lambda: V.tensor_tensor(out=ob, in0=otmp, in1=sz_b, op=ALU.mult), reads=["otmp", "sz_b"], writes=["ob"])
            p6b = P6.bitcast(BF16).rearrange("p (k f) -> p k f", k=8)
            for c4 in range(4):
                t.op("pe", lambda c4=c4, p6b=p6b: P_.transpose(out=p6b[:, c4, :], in_=ob[:, c4 * 128:(c4 + 1) * 128], identity=identb[:]),
                     reads=["ob", "identb"], writes=["P6"])
            t.op("act", lambda m=m, p6b=p6b: A_.activation(out=mixT[:, 0:4, m * 128:(m + 1) * 128], in_=p6b[:, 0:4, :], func=AF.Copy),
                 reads=["P6"], writes=[f"mixNA{m}"])
        t.barrier()

        if dbg == "na":
            cv = Carver()
            dtmp = cv.take([128, TOK])
            for c8 in range(4):
                t.op("dve", lambda c8=c8: V.tensor_copy(out=dtmp, in_=mixT[:, c8, :]), reads=[f"mixNA{m}" for m in range(NT)], writes=["dtmp"])
                t.dma("sp", "dbg", dbg_out[:, c8 * TOK:(c8 + 1) * TOK], dtmp, reads=["dtmp"], writes=["dbgout"])
            t.wait_keys("sp", ["dbgout"])
            nc.sync.wait_ge(t.dma_lanes["dbg"].sem, t.dma_lanes["dbg"].count)
            print("instructions", t.n_inst, "waits", t.n_wait)
            return nc


        cv = Carver()
        mixML = cv.take([128, 4, TOK], BF16)
        mqT = cv.take([128, TOK], BF16)
        mkT = cv.take([128, TOK], BF16)
        QsT = [cv.take([128, TOK], BF16) for _ in range(2)]
        Ktok = cv.take([128, NT, 128], BF16)
        Vg = cv.take([128, NT, 129], BF16)
        wV = [cv.take([128, NT, 129], BF16) for _ in range(2)]
        so = cv.take([128, NT, 128], BF16)
        szm = cv.take([128, NT, 128], BF16)
        hF = cv.take([128, NT, 128])
        Gt = cv.take([128, NT, 4])
        nlf = cv.take([128, NT, 2])
        gw = cv.take([128, NT, 4])
        ebl = cv.take([128, NT, 2, 2])
        bias_ml = cv.take([128, 388])
        mlnw_bc = cv.take([128, 512])
        Cst = [cv.take([128, 129]) for _ in range(2)]
        Cbf = [cv.take([128, 129], BF16) for _ in range(2)]
        small = cv.take([128, 64])
        mark = cv.off
        t.dma("pool", "c1", mlnw_bc, mlnw_row.partition_broadcast(128), writes=["mlnw_bc"])
        t.op("pool", lambda: G_.memset(Vg[:, :, 128:129], 1.0), writes=["Vgones"])
        PNs = [(P6, "P6"), (P7, "P7")]

        for h in range(ML_HEADS):
            c0 = NA_COLS + h * MLW
            cv.off = mark
            pre = cv.take([128, TOK + 4])
            acc = cv.take([128, TOK])
            load_w(win_v, c0, 256, 0)
            load_w(win_v, c0 + 256, 388, 1)
            t.dma("pool", "c1", bias_ml, bin_row[c0 + 256:c0 + 644].partition_broadcast(128), writes=["bias_ml"])
            for g in range(2):
                blks = [(xTh[:, :, 254:256], 2, 0, ["xT1"])]
                for b in range(4):
                    blks.append((xT[:, :, b * 512:(b + 1) * 512], 512, 2 + b * 512, [f"xT{2 + 4 * b + i}" for i in range(4)]))
                blks.append((xTh[:, :, 256:258], 2, 2050, ["xT18"]))
                for (xap, n, p0, xk) in blks:
                    pb, pk = next_bank()
                    for k in range(8):
                        t.op("pe", lambda k=k, g=g, xap=xap, n=n, pb=pb: P_.matmul(
                            pb[:, 0:n], lhsT=wbf[0][:, k, g * 128:(g + 1) * 128], rhs=xap[:, k, :], start=(k == 0), stop=(k == 7)),
                            reads=["wbf0"] + xk, writes=[pk])
                    t.op("act", lambda g=g, pb=pb, n=n, p0=p0, h=h: A_.activation(out=pre[:, p0:p0 + n], in_=pb[:, 0:n], func=AF.Identity,
                                                                             scale=1.0, bias=bcol[:, 8 + 2 * h + g:9 + 2 * h + g]),
                         reads=[pk, "bcol"], writes=["pre"])
                t.op("dve", lambda: V.tensor_scalar(out=pre[:, 0:2], in0=pre[:, 0:2], scalar1=flg[:, 0:1], scalar2=None, op0=ALU.mult),
                     reads=["pre", "flg"], writes=["pre"])
                t.op("dve", lambda: V.tensor_scalar(out=pre[:, TOK + 2:TOK + 4], in0=pre[:, TOK + 2:TOK + 4], scalar1=flg[:, 1:2], scalar2=None, op0=ALU.mult),
                     reads=["pre", "flg"], writes=["pre"])
                gi = g * 4 + h
                t.op("dve", lambda gi=gi: V.tensor_scalar(out=acc, in0=pre[:, 0:TOK], scalar1=cw[:, gi, 0:1], scalar2=cb[:, gi:gi + 1],
                                                          op0=ALU.mult, op1=ALU.add), reads=["pre", "cw", "cb"], writes=["acc"])
                for j in range(1, 5):
                    t.op("dve", lambda gi=gi, j=j: V.scalar_tensor_tensor(out=acc, in0=pre[:, j:j + TOK], scalar=cw[:, gi, j:j + 1], in1=acc,
                                                                          op0=ALU.mult, op1=ALU.add), reads=["pre", "cw", "acc"], writes=["acc"])
                dstT = mqT if g == 0 else mkT
                t.op("act", lambda dstT=dstT: A_.activation(out=dstT, in_=acc, func=AF.Silu), reads=["acc"], writes=["mqT" if g == 0 else "mkT"])
            t.barrier()
            if h == 0:
                dbg_dump("ml_a", [(mqT, TOK), (mkT, TOK)])
            cv.off = mark
            tmpv = [cv.take([128, 388]) for _ in range(2)]
            Rfb = [cv.take([128, 2, 128]) for _ in range(2)]
            ebt = [cv.take([128, 2, 128]) for _ in range(2)]
            tmp4 = [cv.take([128, 4]) for _ in range(2)]
            sdT = [cv.take([128, 128], BF16) for _ in range(2)]
            hm = [cv.take([128, 128]) for _ in range(2)]
            hjunk = cv.take([128, 128])
            mo = [cv.take([128, 128], BF16) for _ in range(2)]
            st = [cv.take([128, 8]) for _ in range(2)]
            for tl in range(NT):
                pb, pk = next_bank()
                s2 = tl % 2
                for k in range(8):
                    t.op("pe", lambda k=k, tl=tl, pb=pb: P_.matmul(pb[:, 0:388], lhsT=xT[:, k, tl * 128:(tl + 1) * 128], rhs=wbf[1][:, k, 0:388],
                                                                  start=(k == 0), stop=(k == 7)), reads=["wbf1", f"xT{tl + 2}"], writes=[pk])
                t.op("dve", lambda pb=pb, s2=s2: V.tensor_tensor(out=tmpv[s2], in0=pb[:, 0:388], in1=bias_ml, op=ALU.add),
                     reads=[pk, "bias_ml"], writes=[f"tmpv{s2}"])
                t.op("pool", lambda tl=tl, s2=s2: G_.tensor_copy(out=Vg[:, tl, 0:128], in_=tmpv[s2][:, 0:128]), reads=[f"tmpv{s2}"], writes=[f"Vg{tl}"])
                t.op("act", lambda tl=tl, s2=s2: A_.activation(out=so[:, tl, :], in_=tmpv[s2][:, 128:256], func=AF.Sigmoid), reads=[f"tmpv{s2}"], writes=[f"so{tl}"])
                t.op("act", lambda tl=tl, s2=s2: A_.activation(out=szm[:, tl, :], in_=tmpv[s2][:, 256:384], func=AF.Silu), reads=[f"tmpv{s2}"], writes=[f"szm{tl}"])
                t.op("pool", lambda tl=tl, s2=s2: G_.tensor_copy(out=Gt[:, tl, :], in_=tmpv[s2][:, 384:388]), reads=[f"tmpv{s2}"], writes=["Gt"])
            Gf = Gt.rearrange("p t (d two) -> p t d two", two=2)[:, :, :, 1]
            Gi = Gt.rearrange("p t (d two) -> p t d two", two=2)[:, :, :, 0]
            t.op("act", lambda: A_.activation(out=nlf, in_=Gf, func=AF.Exp, scale=-1.0), reads=["Gt"], writes=["nlf"])
            t.op("act", lambda: A_.activation(out=nlf, in_=nlf, func=AF.Ln, bias=kcol[:, 0:1], scale=1.0), reads=["nlf", "kcol"], writes=["nlf"])
            for tl in range(NT):
                s2 = tl % 2
                t.op("pe", lambda tl=tl: P_.matmul(P6[:, 0:1], lhsT=cst[:, TRIF, :], rhs=nlf[:, tl, 0:1], start=True, stop=True), reads=["cst", "nlf"], writes=["P6"])
                t.op("pe", lambda tl=tl: P_.matmul(P6[:, 1:2], lhsT=cst[:, TRIB, :], rhs=nlf[:, tl, 1:2], start=True, stop=True), reads=["cst", "nlf"], writes=["P6"])
                t.op("pe", lambda tl=tl: P_.matmul(P6[:, 2:4], lhsT=cst[:, BLK, :], rhs=nlf[:, tl, 0:2], start=True, stop=True), reads=["cst", "nlf"], writes=["P6"])
                t.op("dve", lambda tl=tl, s2=s2: V.tensor_tensor(out=tmp4[s2][:, 0:2], in0=Gi[:, tl, :], in1=P6[:, 0:2], op=ALU.add),
                     reads=["Gt", "P6"], writes=[f"tmp4{s2}"])
                t.op("dve", lambda s2=s2: V.tensor_tensor(out=tmp4[s2][:, 2:4], in0=tmp4[s2][:, 0:2], in1=P6[:, 2:4], op=ALU.subtract),
                     reads=["P6", f"tmp4{s2}"], writes=[f"tmp4{s2}"])
                t.op("act", lambda tl=tl, s2=s2: A_.activation(out=gw[:, tl, :], in_=tmp4[s2], func=AF.Exp), reads=[f"tmp4{s2}"], writes=[f"gw{tl}"])
                t.op("dve", lambda tl=tl, s2=s2: V.tensor_scalar(out=Rfb[s2][:, 0, :], in0=cst[:, TRIF, :], scalar1=nlf[:, tl, 0:1], scalar2=None, op0=ALU.mult),
                     reads=["cst", "nlf"], writes=[f"Rfb{s2}"])
                t.op("dve", lambda tl=tl, s2=s2: V.tensor_scalar(out=Rfb[s2][:, 1, :], in0=cst[:, TRIB, :], scalar1=nlf[:, tl, 1:2], scalar2=None, op0=ALU.mult),
                     reads=["cst", "nlf"], writes=[f"Rfb{s2}"])
                t.op("pe", lambda s2=s2: P_.matmul(P7[:, 0:256], lhsT=cst[:, ONES, :], rhs=Rfb[s2].rearrange("p d t -> p (d t)"), start=True, stop=True),
                     reads=["cst", f"Rfb{s2}"], writes=["P7"])
                P7v = P7[:, 0:256].rearrange("p (d t) -> p d t", d=2)
                t.op("act", lambda s2=s2, P7v=P7v: A_.activation(out=ebt[s2], in_=P7v, func=AF.Exp, scale=-1.0, bias=kcol[:, 1:2]),
                     reads=["P7", "kcol"], writes=[f"ebt{s2}"])
                t.op("act", lambda tl=tl: A_.activation(out=ebl[:, tl, 0, :], in_=P7[:, 63:128:64], func=AF.Exp, scale=-1.0), reads=["P7"], writes=[f"ebl{tl}"])
                t.op("act", lambda tl=tl: A_.activation(out=ebl[:, tl, 1, :], in_=P7[:, 128:256:64], func=AF.Exp, scale=-1.0), reads=["P7"], writes=[f"ebl{tl}"])
                for d in range(2):
                    eng = "dve" if d == 0 else "pool"
                    E = V if d == 0 else G_
                    t.op(eng, lambda E=E, d=d, tl=tl, s2=s2: E.tensor_tensor(out=QsT[d][:, tl * 128:(tl + 1) * 128], in0=mqT[:, tl * 128:(tl + 1) * 128],
                                                                           in1=ebt[s2][:, d, :], op=ALU.mult), reads=["mqT", f"ebt{s2}"], writes=[f"Qs{d}_{tl}"])
                    t.op("pool", lambda d=d, tl=tl: G_.tensor_scalar(out=wV[d][:, tl, :], in0=Vg[:, tl, :], scalar1=gw[:, tl, 2 + d:3 + d], scalar2=None, op0=ALU.mult),
                         reads=[f"Vg{tl}", "Vgones", f"gw{tl}"], writes=[f"wV{d}_{tl}"])
                p4b = P4.bitcast(BF16)
                t.op("pe", lambda tl=tl, p4b=p4b, s2=s2: P_.transpose(out=p4b[:, s2 * 128:(s2 + 1) * 128], in_=mkT[:, tl * 128:(tl + 1) * 128], identity=identb[:]),
                     reads=["mkT", "identb"], writes=["P4"])
                t.op("act", lambda tl=tl, p4b=p4b, s2=s2: A_.activation(out=Ktok[:, tl, :], in_=p4b[:, s2 * 128:(s2 + 1) * 128], func=AF.Copy),
                     reads=["P4"], writes=[f"Ktok{tl}"])

            if h == 0:
                dbg_dump("ml_b", [(gw.rearrange("p a b -> p (a b)"), NT * 4), (ebl.rearrange("p a b c -> p (a b c)"), NT * 4),
                                  (nlf.rearrange("p a b -> p (a b)"), NT * 2), (QsT[0], TOK), (QsT[1], TOK),
                                  (Ktok.rearrange("p a b -> p (a b)"), TOK), (Vg.rearrange("p a b -> p (a b)"), NT * 129),
                                  (wV[0].rearrange("p a b -> p (a b)"), NT * 129)])
            def chunks(d):
                order = list(range(2 * NT))
                return order if d == 0 else order[::-1]

            def state_update(d, c, to_bf):
                tl, c2 = c // 2, c % 2
                pb, pk = next_bank()
                lo, hi = c2 * 64, c2 * 64 + 64
                t.op("pe", lambda: P_.matmul(pb[:, 0:129], lhsT=Ktok[lo:hi, tl, :], rhs=wV[d][lo:hi, tl, :], start=True, stop=True),
                     reads=[f"Ktok{tl}", f"wV{d}_{tl}"], writes=[pk])
                t.op("dve", lambda: V.scalar_tensor_tensor(out=Cst[d], in0=Cst[d], scalar=ebl[:, tl, d, c2:c2 + 1], in1=pb[:, 0:129],
                                                           op0=ALU.mult, op1=ALU.add), reads=[pk, f"ebl{tl}", f"C{d}"], writes=[f"C{d}"])
                if to_bf:
                    t.op("act", lambda: A_.activation(out=Cbf[d], in_=Cst[d], func=AF.Copy), reads=[f"C{d}"], writes=[f"Cb{d}"])

            for d in range(2):
                t.op("dve", lambda d=d, h=h: V.tensor_copy(out=Cst[d], in_=Cacc[:, d, h, :]), reads=["Cacc"], writes=[f"C{d}"])
                t.op("act", lambda d=d: A_.activation(out=Cbf[d], in_=Cst[d], func=AF.Copy), reads=[f"C{d}"], writes=[f"Cb{d}"])
            if h == 0:
                dbg_dump("ml_d", [(Cst[0], 129), (Cst[1], 129)])
            def scan_tile(d, tl):
                PN, pnk = PNs[d]
                pb, pk = next_bank()
                tok = slice(tl * 128, (tl + 1) * 128)
                t.op("pe", lambda: P_.matmul(pb[:, 0:128], lhsT=mkT[:, tok], rhs=QsT[d][:, tok], start=True, stop=True),
                     reads=["mkT", f"Qs{d}_{tl}"], writes=[pk])
                msk = cst[:, TRIF, :] if d == 0 else cst[:, TRIB, :]
                t.op("dve", lambda: V.scalar_tensor_tensor(out=sdT[d], in0=pb[:, 0:128], scalar=gw[:, tl, d:d + 1], in1=msk, op0=ALU.mult, op1=ALU.mult),
                     reads=[pk, f"gw{tl}", "cst"], writes=[f"sdT{d}"])
                t.op("pe", lambda: P_.matmul(PN[:, 0:129], lhsT=sdT[d], rhs=Vg[:, tl, :], start=True, stop=False),
                     reads=[f"sdT{d}", f"Vg{tl}", "Vgones"], writes=[pnk])
                order = (0, 1) if d == 0 else (1, 0)
                for ci, c2 in enumerate(order):
                    lo = tl * 128 + c2 * 64
                    t.op("pe", lambda c2=c2, lo=lo, ci=ci: P_.matmul(PN[c2 * 64:(c2 + 1) * 64, 0:129], lhsT=QsT[d][:, lo:lo + 64], rhs=Cbf[d],
                                                                 start=False, stop=True), reads=[f"Qs{d}_{tl}", f"Cb{d}"], writes=[pnk])
                    c = tl * 2 + c2
                    last = (c == (2 * NT - 1 if d == 0 else 0))
                    if not last:
                        state_update(d, c, True)
                s2 = d
                t.op("act", lambda: A_.activation(out=st[s2][:, 0:1], in_=PN[:, 128:129], func=AF.Abs), reads=[pnk], writes=[f"st{s2}"])
                t.op("dve", lambda: V.tensor_scalar(out=st[s2][:, 0:1], in0=st[s2][:, 0:1], scalar1=1.0, scalar2=None, op0=ALU.max),
                     reads=[f"st{s2}"], writes=[f"st{s2}"])
                t.op("dve", lambda: V.reciprocal(out=st[s2][:, 1:2], in_=st[s2][:, 0:1]), reads=[f"st{s2}"], writes=[f"st{s2}"])
                return PN, pnk

            for i in range(NT):
                PN, pnk = scan_tile(0, i)
                t.op("dve", lambda PN=PN, i=i: V.tensor_scalar(out=hF[:, i, :], in0=PN[:, 0:128], scalar1=st[0][:, 1:2], scalar2=None, op0=ALU.mult),
                     reads=[pnk, "st0"], writes=[f"hF{i}"])
            if h == 0:
                dbg_dump("ml_e", [(hF.rearrange("p a b -> p (a b)"), TOK)])
            for i in range(NT - 1, -1, -1):
                PN, pnk = scan_tile(1, i)
                s2 = i % 2
                t.op("dve", lambda PN=PN, i=i, s2=s2: V.scalar_tensor_tensor(out=hm[s2], in0=PN[:, 0:128], scalar=st[1][:, 1:2], in1=hF[:, i, :], op0=ALU.mult, op1=ALU.add),
                     reads=[pnk, "st1", f"hF{i}"], writes=[f"hm{s2}"])
                t.op("dve", lambda i=i, s2=s2: V.tensor_tensor(out=hm[s2], in0=hm[s2], in1=so[:, i, :], op=ALU.mult), reads=[f"hm{s2}", f"so{i}"], writes=[f"hm{s2}"])
                if h == 0 and i == NT - 1:
                    dbg_dump("ml_g", [(hm[s2], 128)])
                t.op("act", lambda s2=s2: A_.activation(out=hjunk, in_=hm[s2], func=AF.Identity, accum_out=st[1][:, 2:3]), reads=[f"hm{s2}"], writes=["hjunk", "st1b"])
                t.op("act", lambda s2=s2: A_.activation(out=hjunk, in_=hm[s2], func=AF.Square, accum_out=st[1][:, 3:4]), reads=[f"hm{s2}"], writes=["hjunk", "st1c"])
                t.op("dve", lambda: V.tensor_scalar(out=st[1][:, 4:5], in0=st[1][:, 2:3], scalar1=1.0 / 128, scalar2=None, op0=ALU.mult), reads=["st1b"], writes=["st1d"])
                t.op("dve", lambda: V.tensor_tensor(out=st[1][:, 5:6], in0=st[1][:, 4:5], in1=st[1][:, 4:5], op=ALU.mult), reads=["st1d"], writes=["st1e"])
                t.op("dve", lambda: V.scalar_tensor_tensor(out=st[1][:, 6:7], in0=st[1][:, 3:4], scalar=1.0 / 128, in1=st[1][:, 5:6], op0=ALU.mult, op1=ALU.subtract),
                     reads=["st1c", "st1e"], writes=["st1f"])
                t.op("dve", lambda: V.tensor_scalar(out=st[1][:, 6:7], in0=st[1][:, 6:7], scalar1=EPS, scalar2=None, op0=ALU.add), reads=["st1f"], writes=["st1f"])
                t.op("act", lambda: A_.activation(out=st[1][:, 6:7], in_=st[1][:, 6:7], func=AF.Sqrt), reads=["st1f"], writes=["st1f"])
                t.op("dve", lambda: V.reciprocal(out=st[1][:, 7:8], in_=st[1][:, 6:7]), reads=["st1f"], writes=["st1g"])
                t.op("dve", lambda s2=s2: V.tensor_scalar(out=hm[s2], in0=hm[s2], scalar1=st[1][:, 4:5], scalar2=st[1][:, 7:8], op0=ALU.subtract, op1=ALU.mult),
                     reads=[f"hm{s2}", "st1d", "st1g"], writes=[f"hm{s2}"])
                t.op("pool", lambda s2=s2, h=h: G_.tensor_tensor(out=hm[s2], in0=hm[s2], in1=mlnw_bc[:, h * 128:(h + 1) * 128], op=ALU.mult),
                     reads=[f"hm{s2}", "mlnw_bc"], writes=[f"hm{s2}"])
                t.op("pool", lambda s2=s2, i=i: G_.tensor_tensor(out=mo[s2], in0=hm[s2], in1=szm[:, i, :], op=ALU.mult), reads=[f"hm{s2}", f"szm{i}"], writes=[f"mo{s2}"])
                p4b = P4.bitcast(BF16)
                if h == 0 and i == NT - 1:
                    dbg_dump("ml_h", [(mo[s2], 128)])
                t.op("pe", lambda s2=s2, p4b=p4b: P_.transpose(out=p4b[:, s2 * 128:(s2 + 1) * 128], in_=mo[s2], identity=identb[:]),
                     reads=[f"mo{s2}", "identb"], writes=["P4"])
                t.op("act", lambda s2=s2, p4b=p4b, i=i, h=h: A_.activation(out=mixML[:, h, i * 128:(i + 1) * 128], in_=p4b[:, s2 * 128:(s2 + 1) * 128], func=AF.Copy),
                     reads=["P4"], writes=[f"mixML{h}_{i}"])
            t.barrier()
            if h == 0:
                dbg_dump("ml_f", [(mixML[:, 0, :], TOK)])

        if dbg == "ml":
            cv.off = mark
            dtmp = cv.take([128, TOK])
            for c8 in range(4):
                t.op("dve", lambda c8=c8: V.tensor_copy(out=dtmp, in_=mixML[:, c8, :]), reads=[], writes=["dtmp"])
                t.dma("sp", "dbg", dbg_out[:, (4 + c8) * TOK:(5 + c8) * TOK], dtmp, reads=["dtmp"], writes=["dbgout"])
            nc.sync.wait_ge(t.dma_lanes["dbg"].sem, t.dma_lanes["dbg"].count)
            print("instructions", t.n_inst, "waits", t.n_wait)
            return nc

        cv.off = 4 * TOK // 2
        fnw_bc = cv.take([128, D])
        woutb = cv.take([128, 8, D], BF16)
        xres = [cv.take([128, D]) for _ in range(2)]
        hres = [cv.take([128, D]) for _ in range(2)]
        ojunk = cv.take([128, D])
        ost = cv.take([128, 2 * NT])
        t.dma("pool", "c1", fnw_bc, fnw_row.partition_broadcast(128), writes=["fnw_bc"])
        for q4 in range(4):
            s = wstate["n"] % 2
            wstate["n"] += 1
            t.dma("sp", f"wst{s}", wst[s][:, :, 0:256], wout_v[:, :, q4 * 256:(q4 + 1) * 256], writes=[f"wst{s}"])
            t.op("pool", lambda s=s, q4=q4: G_.tensor_copy(out=woutb[:, :, q4 * 256:(q4 + 1) * 256], in_=wst[s][:, :, 0:256]), reads=[f"wst{s}"], writes=["woutb"])
        for tl in range(NT):
            s2 = tl % 2
            t.dma("sp", f"xres{s2}", xres[s2], xh[(tl + 2) * 128:(tl + 3) * 128, :], writes=[f"xres{s2}"])
            for nb in range(2):
                pb, pk = next_bank()
                for mc in range(8):
                    src = mixT[:, mc, tl * 128:(tl + 1) * 128] if mc < 4 else mixML[:, mc - 4, tl * 128:(tl + 1) * 128]
                    t.op("pe", lambda mc=mc, src=src, pb=pb, nb=nb: P_.matmul(pb[:, :], lhsT=src, rhs=woutb[:, mc, nb * 512:(nb + 1) * 512], start=(mc == 0), stop=(mc == 7)),
                         reads=["woutb"], writes=[pk])
                t.op("dve", lambda pb=pb, nb=nb, s2=s2: V.tensor_tensor(out=hres[s2][:, nb * 512:(nb + 1) * 512], in0=pb[:, :], in1=gate_bc[:, nb * 512:(nb + 1) * 512], op=ALU.mult),
                     reads=[pk, "gate_bc"], writes=[f"hres{s2}_{nb}"])
                t.op("pool", lambda nb=nb, s2=s2: G_.tensor_tensor(out=hres[s2][:, nb * 512:(nb + 1) * 512], in0=hres[s2][:, nb * 512:(nb + 1) * 512],
                                                                 in1=xres[s2][:, nb * 512:(nb + 1) * 512], op=ALU.add),
                     reads=[f"hres{s2}_{nb}", f"xres{s2}"], writes=[f"hres{s2}_{nb}"])
            hk = [f"hres{s2}_0", f"hres{s2}_1"]
            t.op("act", lambda s2=s2, tl=tl: A_.activation(out=ojunk, in_=hres[s2], func=AF.Square, accum_out=ost[:, tl:tl + 1]), reads=hk, writes=["ojunk", f"ost{tl}"])
            t.op("dve", lambda tl=tl: V.tensor_scalar(out=ost[:, tl:tl + 1], in0=ost[:, tl:tl + 1], scalar1=1.0 / D, scalar2=EPS, op0=ALU.mult, op1=ALU.add),
                 reads=[f"ost{tl}"], writes=[f"ost{tl}"])
            t.op("act", lambda tl=tl: A_.activation(out=ost[:, tl:tl + 1], in_=ost[:, tl:tl + 1], func=AF.Sqrt), reads=[f"ost{tl}"], writes=[f"ost{tl}"])
            t.op("dve", lambda tl=tl: V.reciprocal(out=ost[:, NT + tl:NT + tl + 1], in_=ost[:, tl:tl + 1]), reads=[f"ost{tl}"], writes=[f"ost{tl}"])
            t.op("dve", lambda tl=tl, s2=s2: V.scalar_tensor_tensor(out=hres[s2], in0=hres[s2], scalar=ost[:, NT + tl:NT + tl + 1], in1=fnw_bc, op0=ALU.mult, op1=ALU.mult),
                 reads=hk + [f"ost{tl}", "fnw_bc"], writes=hk)
            t.dma("sp", "yout", y[tl * 128:(tl + 1) * 128, :], hres[s2], reads=hk, writes=["yout"])
        yl = t.dma_lanes["yout"]
        nc.sync.wait_ge(yl.sem, yl.count)
        print("instructions", t.n_inst, "waits", t.n_wait)

    return nc


def make_in_maps(x, c, w_ada, b_ada, norm_w, w_in, b_in, conv_w, conv_b, rpb, ml_norm_w, w_out, final_norm_w):
    f = lambda a: np.ascontiguousarray(np.asarray(a, dtype=np.float32))
    x = f(x)[0]
    perm = _col_perm()
    w_in_p = f(f(w_in)[0][:, perm])
    b_in_p = f(f(b_in)[0][perm])
    groups = [b_in_p[g * 128:(g + 1) * 128] for g in range(8)]
    for h in range(ML_HEADS):
        o = NA_COLS + h * MLW
        groups.append(b_in_p[o:o + 128])
        groups.append(b_in_p[o + 128:o + 256])
    bin_col = f(np.stack(groups, 1))
    col8 = lambda v: f(np.asarray(v, np.float32).reshape(-1, 128).T)
    rpbA, cmask = _rpb_tables(f(rpb)[0])
    cw = f(conv_w)[0]
    convw = f(cw.T.reshape(8, 128, 5).transpose(1, 0, 2))
    shared = {
        "w_ada": f(w_ada)[0], "w_in": w_in_p, "w_out": f(w_out)[0], "consts": _consts(),
        "c_col": col8(f(c)[0]), "bada_col": col8(f(b_ada)[0]), "bada_gate": f(f(b_ada)[0][2 * D:]),
        "normw_col": col8(f(norm_w)[0]), "fnw_row": f(final_norm_w), "mlnw_row": f(ml_norm_w)[0],
        "bin_col": bin_col, "bin_row": b_in_p, "convw": convw, "convb": col8(f(conv_b)[0]),
        "rpbA": f(rpbA.reshape(8, 128, 14 * 64)), "cmask": f(cmask.reshape(128, 14 * 64)),
        "tri2": _tri2(), "w_gates": f(f(w_in)[0][:, 4608:4624]), "b_gates": f(f(b_in)[0][4608:4624]),
    }
    in_maps = []
    for i in range(NCORES):
        xhh = np.zeros((HT * 128, D), np.float32)
        lo, hi = i * TOK - 256, i * TOK + TOK + 256
        slo, shi = max(lo, 0), min(hi, T)
        xhh[slo - lo:shi - lo] = x[slo:shi]
        fl = np.zeros((128, 18), np.float32)
        fl[:, 0] = 1.0 if i > 0 else 0.0
        fl[:, 1] = 1.0 if i < NCORES - 1 else 0.0
        for j in range(NCORES):
            fl[:, 2 + j] = 1.0 if j < i else 0.0
            fl[:, 10 + j] = 1.0 if j > i else 0.0
        gb = list(range(4 * i - 1, -1, -1))
        ga = list(range(4 * i + 4, T // 512))
        order = gb + ga
        assert len(order) == NG
        xf = np.concatenate([x[g * 512:(g + 1) * 512] for g in order], 0)
        xfh = np.zeros((NG * 4, D), np.float32)
        ffl = np.zeros((128, NG * 4), np.float32)
        for p, g in enumerate(order):
            if g > 0:
                xfh[p * 4:p * 4 + 2] = x[g * 512 - 2:g * 512]
                ffl[:, p * 4 + 2] = 1.0
            if g < T // 512 - 1:
                xfh[p * 4 + 2:p * 4 + 4] = x[(g + 1) * 512:(g + 1) * 512 + 2]
                ffl[:, p * 4 + 3] = 1.0
            ffl[:, p * 4 + 0] = 1.0 if g < 4 * i else 0.0
            ffl[:, p * 4 + 1] = 1.0 if g > 4 * i else 0.0
        m = dict(shared)
        m.update({"xh": xhh, "mcol": _mcol(i), "flags": fl, "xf": xf, "xfh": xfh, "fflags": ffl})
        in_maps.append(m)
    return in_maps


_NC_CACHE = {}


def kernel(**inputs):
    in_maps = make_in_maps(**inputs)
    if "nc" not in _NC_CACHE:
        _NC_CACHE["nc"] = build_program()
    res = run_bass_kernel_spmd(_NC_CACHE["nc"], in_maps, core_ids=list(range(NCORES)))
    out = np.concatenate([r["y"] for r in res.results], axis=0)
    return out.reshape(1, T, D).astype(np.float32)
```

```python
import numpy as np
from contextlib import ExitStack
import concourse.bass as bass
import concourse.mybir as mybir
from concourse.bass_utils import run_bass_kernel_spmd

F32 = mybir.dt.float32
BF16 = mybir.dt.bfloat16
AF = mybir.ActivationFunctionType
ALU = mybir.AluOpType

NCORES = 8
D = 1024
T = 16384
TOK = T // NCORES
NT = TOK // 128
HT = NT + 4
NA_HEADS = 8
ML_HEADS = 4
EPS = 1e-6
MLW = 644
NA_COLS = 2048
IN_WP = NA_COLS + ML_HEADS * MLW
UNI_WORDS = 23424
NG = 28


class _Stop(Exception):
    pass


class Lane:
    def __init__(self, nc, name):
        self.sem = nc.alloc_semaphore(name=name)
        self.count = 0
        self.name = name


class Trk:
    def __init__(self, nc):
        self.nc = nc
        self.engs = {"pe": nc.tensor, "act": nc.scalar, "dve": nc.vector, "pool": nc.gpsimd, "sp": nc.sync}
        self.lanes = {k: Lane(nc, "sem_" + k) for k in ("pe", "act", "dve", "pool")}
        self.seen = {k: {} for k in self.engs}
        self.last_w = {}
        self.reads = {}
        self.dma_lanes = {}
        self.n_inst = 0
        self.n_wait = 0

    def dma_lane(self, name):
        if name not in self.dma_lanes:
            self.dma_lanes[name] = Lane(self.nc, "dsem_" + name)
        return self.dma_lanes[name]

    def _wait(self, eng, lane, val):
        s = self.seen[eng]
        if s.get(lane.name, 0) >= val:
            return
        if eng == "pe" and lane is self.lanes["pe"]:
            return
        self.engs[eng].wait_ge(lane.sem, val)
        s[lane.name] = val
        self.n_wait += 1

    def _deps(self, eng, reads, writes):
        for k in reads:
            if k in self.last_w:
                self._wait(eng, *self.last_w[k])
        for k in writes:
            if k in self.last_w:
                self._wait(eng, *self.last_w[k])
            for (l, v) in self.reads.get(k, ()):
                self._wait(eng, l, v)

    def _record(self, lane, reads, writes):
        v = lane.count
        for k in reads:
            if k in writes:
                continue
            lst = self.reads.setdefault(k, [])
            lst[:] = [(l, x) for (l, x) in lst if l is not lane]
            lst.append((lane, v))
        for k in writes:
            self.last_w[k] = (lane, v)
            self.reads[k] = []

    def op(self, eng, fn, reads=(), writes=()):
        lane = self.lanes[eng]
        self._deps(eng, reads, writes)
        ins = fn()
        lane.count += 1
        ins.then_inc(lane.sem, 1)
        self._record(lane, reads, writes)
        self.n_inst += 1
        return ins

    def dma(self, q, lane_name, out, in_, reads=(), writes=(), **kw):
        if lane_name in ("c0", "c1"):
            self._oneshot = getattr(self, "_oneshot", 0) + 1
            lane_name = f"os{self._oneshot % 24}"
            lane = self.dma_lane(lane_name)
            if lane.count > 0:
                self._wait(q, lane, lane.count)
        lane = self.dma_lane(lane_name)
        self._deps(q, reads, writes)
        ins = self.engs[q].dma_start(out=out, in_=in_, **kw)
        lane.count += 16
        ins.then_inc(lane.sem, 16)
        self._record(lane, reads, writes)
        self.n_inst += 1
        return ins

    def wait_keys(self, eng, keys):
        for k in keys:
            if k in self.last_w:
                self._wait(eng, *self.last_w[k])
            for (l, v) in self.reads.get(k, ()):
                self._wait(eng, l, v)

    def barrier(self):
        all_lanes = list(self.lanes.values()) + list(self.dma_lanes.values())
        for eng in self.engs:
            for l in all_lanes:
                if l.count > 0:
                    self._wait(eng, l, l.count)


def _col_perm():
    NAW = 512
    cols = list(range(0, 4 * NAW))
    base_ml = 4 * NAW
    gate0 = 4 * NAW + 5 * 512
    for h in range(ML_HEADS):
        for blk in range(5):
            s = base_ml + blk * 512 + h * 128
            cols += list(range(s, s + 128))
        cols += [gate0 + 0 + h, gate0 + 4 + h, gate0 + 8 + h, gate0 + 12 + h]
    return np.array(cols, dtype=np.int64)


def _consts():
    s = np.arange(128)[:, None]
    t = np.arange(128)[None, :]
    same = (s // 64) == (t // 64)
    triF = (same & (s <= t)).astype(np.float32)
    triB = (same & (s >= t)).astype(np.float32)
    blk = same.astype(np.float32)
    ones = np.ones((128, 128), np.float32)
    ident = np.eye(128, dtype=np.float32)
    return np.stack([ident, triF, triB, blk, ones], 0)


def _tri2():
    s = np.arange(128)[:, None]
    t = np.arange(128)[None, :]
    return np.stack([(s > t).astype(np.float32), (s < t).astype(np.float32)], 0)


def _rpb_tables(rpb):
    p = np.arange(128)
    a = p // 64
    k = p % 64
    c = np.arange(64)
    e = np.arange(14)
    dyi = 13 - e
    dy = dyi[None, :] + a[:, None]
    dyv = (dy >= 0) & (dy <= 14)
    dx = np.clip(k[:, None] - c[None, :], -15, 15) + 15
    cs = np.clip(c - 8, 0, 48)
    cv = (k[:, None] >= cs[None, :]) & (k[:, None] < cs[None, :] + 16)
    valid = dyv[:, :, None] & cv[:, None, :]
    dyc = np.clip(dy, 0, 14)
    g = rpb[:, dyc[:, :, None], dx[:, None, :]]
    g = np.where(valid[None], g, np.float32(0.0)).astype(np.float32)
    return np.ascontiguousarray(g), valid.astype(np.float32)


def _na_base(m):
    return 14 if m == 15 else m


def _na_nj(m):
    return 6 if m in (0, 15) else 5


def _mcol(core):
    out = np.zeros((128, 16, 6, 2), np.float32)
    for m in range(16):
        for j in range(_na_nj(m)):
            for b in range(2):
                r = 32 * core + 2 * m + b
                start = min(max(r - 4, 0), 248)
                for a in range(2):
                    kr = 32 * core - 4 + 2 * (_na_base(m) + j) + a
                    ok = (start <= kr < start + 8)
                    out[a * 64:(a + 1) * 64, m, j, b] = 1.0 if ok else 0.0
    return out.reshape(128, 192)


def build_program(dbg=None):
    nc = bass.Bass("TRN2", target_bir_lowering=False)
    try:
        _build_body(nc, dbg)
    except _Stop:
        pass
    return nc


def _build_body(nc, dbg):

    def din(name, shape):
        return nc.dram_tensor(name, list(shape), F32, kind="ExternalInput").ap()

    xh = din("xh", [HT * 128, D])
    w_ada = din("w_ada", [D, 3 * D])
    w_in = din("w_in", [D, IN_WP])
    w_out = din("w_out", [D, D])
    consts = din("consts", [5, 128, 128])
    c_col = din("c_col", [128, 8])
    bada_col = din("bada_col", [128, 24])
    bada_gate = din("bada_gate", [D])
    normw_col = din("normw_col", [128, 8])
    fnw_row = din("fnw_row", [D])
    mlnw_row = din("mlnw_row", [512])
    bin_col = din("bin_col", [128, 16])
    bin_row = din("bin_row", [IN_WP])
    convw = din("convw", [128, 8, 5])
    convb = din("convb", [128, 8])
    rpbA = din("rpbA", [8, 128, 14 * 64])
    cmask = din("cmask", [128, 14 * 64])
    mcol_d = din("mcol", [128, 192])
    flags = din("flags", [128, 18])
    xf = din("xf", [NG * 512, D])
    xfh = din("xfh", [NG * 4, D])
    fflags = din("fflags", [128, NG * 4])
    tri2 = din("tri2", [2, 128, 128])
    w_gates = din("w_gates", [D, 16])
    b_gates = din("b_gates", [16])
    y = nc.dram_tensor("y", [TOK, D], F32, kind="ExternalOutput").ap()
    dbg_out = None
    if dbg:
        dbg_out = nc.dram_tensor("dbg", [128, 8 * TOK], F32, kind="ExternalOutput").ap()

    es = ExitStack()
    with es:
        def sb(name, shape, dt=F32):
            return es.enter_context(nc.sbuf_tensor(name, list(shape), dt))

        def ps(name, shape, dt=F32):
            return es.enter_context(nc.psum_tensor(name, list(shape), dt))

        t = Trk(nc)
        V, A_, P_, G_ = nc.vector, nc.scalar, nc.tensor, nc.gpsimd

        xT = sb("xT", [128, 8, TOK], BF16)
        xTh = sb("xTh", [128, 8, 512], BF16)
        mixT = sb("mixT", [128, 4, TOK], BF16)
        gate_bc = sb("gate_bc", [128, D])
        cst = sb("cst", [128, 5, 128])
        identb = sb("identb", [128, 128], BF16)
        gT = sb("gT", [128, 8])
        shiftT = sb("shiftT", [128, 8])
        bcol = sb("bcol", [128, 16])
        bcolq = sb("bcolq", [128, 4])
        flg = sb("flg", [128, 18])
        cw = sb("cw", [128, 8, 5])
        cb = sb("cb", [128, 8])
        kcol = sb("kcol", [128, 4])
        wst = [sb(f"wst{i}", [128, 8, 256]) for i in range(2)]
        wbf = [sb(f"wbf{i}", [128, 8, 512], BF16) for i in range(2)]
        Cacc = sb("Cacc", [128, 2, 4, 129])
        UNI = sb("UNI", [128, UNI_WORDS])

        PA = ps("PA", [128, 1024])
        PB = ps("PB", [128, 1024])
        P4 = ps("P4", [128, 512])
        P5 = ps("P5", [128, 512])
        P6 = ps("P6", [128, 512])
        P7 = ps("P7", [128, 512])
        IDENT, TRIF, TRIB, BLK, ONES = range(5)

        class Carver:
            def __init__(self):
                self.off = 0

            def take(self, shape, dt=F32):
                n = int(np.prod(shape[1:]))
                words = n if dt == F32 else (n + 1) // 2
                ap = UNI[:, self.off:self.off + words]
                self.off += words
                assert self.off <= UNI_WORDS, self.off
                if dt != F32:
                    ap = ap.bitcast(BF16)[:, 0:n]
                if len(shape) > 2:
                    names = " ".join(f"d{i}" for i in range(1, len(shape)))
                    kw = {f"d{i}": shape[i] for i in range(1, len(shape))}
                    ap = ap.rearrange(f"p ({names}) -> p {names}", **kw)
                return ap

        def dbg_dump(stage, items):
            if dbg != stage:
                return
            t.barrier()
            off = 0
            dt_ = UNI[:, UNI_WORDS - 2048:UNI_WORDS]
            for ap, n in items:
                o2 = 0
                while o2 < n:
                    w = min(2048, n - o2)
                    t.op("dve", lambda ap=ap, o2=o2, w=w: V.tensor_copy(out=dt_[:, 0:w], in_=ap[:, o2:o2 + w]), reads=[], writes=["dbgtmp"])
                    t.dma("sp", "dbg", dbg_out[:, off:off + w], dt_[:, 0:w], reads=["dbgtmp"], writes=["dbgout"])
                    off += w
                    o2 += w
            nc.sync.wait_ge(t.dma_lanes["dbg"].sem, t.dma_lanes["dbg"].count)
            print("dbg stop at", stage, "instructions", t.n_inst, "waits", t.n_wait)
            raise _Stop()

        t.dma("sp", "c0", cst[:], consts.rearrange("n p f -> p n f"), writes=["cst"])
        t.dma("sp", "c0", flg[:], flags, writes=["flg"])
        t.dma("sp", "c0", bcol[:], bin_col, writes=["bcol"])
        t.dma("sp", "c0", cw[:], convw, writes=["cw"])
        t.dma("sp", "c0", cb[:], convb, writes=["cb"])
        t.op("dve", lambda: V.tensor_copy(out=identb[:], in_=cst[:, IDENT, :]), reads=["cst"], writes=["identb"])
        t.op("pool", lambda: G_.memset(kcol[:, 0:1], 1.0), writes=["kcol"])
        t.op("pool", lambda: G_.memset(kcol[:, 1:2], float(np.log(128.0 ** -0.5))), writes=["kcol"])
        t.op("pool", lambda: G_.memset(kcol[:, 2:3], EPS), writes=["kcol"])
        t.op("pool", lambda: G_.memset(kcol[:, 3:4], 0.0), writes=["kcol"])
        t.op("dve", lambda: V.tensor_scalar(out=bcolq[:], in0=bcol[:, 0:4], scalar1=0.125, scalar2=None, op0=ALU.mult),
             reads=["bcol"], writes=["bcolq"])

        cv = Carver()
        ccol = cv.take([128, 8])
        cact = cv.take([128, 8])
        cbc = cv.take([128, 8, 128])
        badac = cv.take([128, 24])
        nwc = cv.take([128, 8])
        bgate = cv.take([128, D])
        modT = cv.take([128, 16])
        wada_sb = [cv.take([128, 8, 512]) for _ in range(2)]
        t.dma("sp", "c0", ccol, c_col, writes=["ccol"])
        t.dma("sp", "c0", badac, bada_col, writes=["badac"])
        t.dma("sp", "c0", nwc, normw_col, writes=["nwc"])
        t.dma("pool", "c1", bgate, bada_gate.partition_broadcast(128), writes=["bgate"])
        t.op("act", lambda: A_.activation(out=cact, in_=ccol, func=AF.Silu), reads=["ccol"], writes=["cact"])
        for k in range(8):
            t.op("dve", lambda k=k: V.tensor_copy(out=cbc[:, k, :], in_=cact[:, k:k + 1].broadcast_to([128, 128])),
                 reads=["cact"], writes=[f"cbc{k}"])
        wada_v = w_ada.rearrange("(k p) n -> p k n", p=128)
        for ch in range(6):
            slot = ch % 2
            t.dma("sp", f"wada{slot}", wada_sb[slot], wada_v[:, :, ch * 512:(ch + 1) * 512], writes=[f"wada{slot}"])
            if ch < 4:
                for jn in range(4):
                    col = ch * 4 + jn
                    for k in range(8):
                        t.op("pe", lambda k=k, jn=jn, col=col, slot=slot: P_.matmul(
                            P4[:, col:col + 1], lhsT=wada_sb[slot][:, k, jn * 128:(jn + 1) * 128],
                            rhs=cact[:, k:k + 1], start=(k == 0), stop=(k == 7)),
                            reads=[f"wada{slot}", "cact"], writes=["P4"])
            else:
                nb = ch - 4
                pt = P5 if nb == 0 else P6
                for k in range(8):
                    t.op("pe", lambda k=k, pt=pt, slot=slot: P_.matmul(
                        pt[:, :], lhsT=cbc[:, k, :], rhs=wada_sb[slot][:, k, :], start=(k == 0), stop=(k == 7)),
                        reads=[f"wada{slot}", f"cbc{k}"], writes=["P5" if nb == 0 else "P6"])
                t.op("dve", lambda nb=nb, pt=pt: V.tensor_tensor(out=gate_bc[:, nb * 512:(nb + 1) * 512], in0=pt[:, :],
                                                                in1=bgate[:, nb * 512:(nb + 1) * 512], op=ALU.add),
                     reads=["P5" if nb == 0 else "P6", "bgate"], writes=["gate_bc"])
        t.op("dve", lambda: V.tensor_tensor(out=modT, in0=P4[:, 0:16], in1=badac[:, 0:16], op=ALU.add),
             reads=["P4", "badac"], writes=["modT"])
        t.op("dve", lambda: V.tensor_copy(out=shiftT[:], in_=modT[:, 0:8]), reads=["modT"], writes=["shiftT"])
        t.op("dve", lambda: V.scalar_tensor_tensor(out=gT[:], in0=modT[:, 8:16], scalar=1.0, in1=nwc, op0=ALU.add, op1=ALU.mult),
             reads=["modT", "nwc"], writes=["gT"])
        t.barrier()

        cv = Carver()
        xin = [cv.take([128, D]) for _ in range(3)]
        xnb = [cv.take([128, D], BF16) for _ in range(2)]
        xtmp = [cv.take([128, D]) for _ in range(2)]
        sq_junk = cv.take([128, D])
        PT_x = [P4, P5]

        def xT_tile(ht):
            if 2 <= ht < 18:
                return xT[:, :, (ht - 2) * 128:(ht - 1) * 128]
            hi = ht if ht < 2 else ht - 18 + 2
            return xTh[:, :, hi * 128:(hi + 1) * 128]

        nstate = {"n": 0}
        NRS = 64
        ssq = cv.take([128, NRS])
        rstd = cv.take([128, NRS])

        def norm_tile(src, nr, dst, dkey):
            n = nstate["n"]
            nstate["n"] += 1
            s3, s2, c = n % 3, n % 2, n % NRS
            t.dma("sp", f"xin{s3}", xin[s3][0:nr, :], src, writes=[f"xin{s3}"])
            t.op("act", lambda: A_.activation(out=sq_junk[0:nr, :], in_=xin[s3][0:nr, :], func=AF.Square, accum_out=ssq[0:nr, c:c + 1]),
                 reads=[f"xin{s3}"], writes=["sq_junk", f"ssq{c}"])
            t.op("dve", lambda: V.tensor_scalar(out=rstd[0:nr, c:c + 1], in0=ssq[0:nr, c:c + 1], scalar1=1.0 / D, scalar2=EPS,
                                                op0=ALU.mult, op1=ALU.add), reads=[f"ssq{c}"], writes=[f"rstd{c}"])
            t.op("act", lambda: A_.activation(out=rstd[0:nr, c:c + 1], in_=rstd[0:nr, c:c + 1], func=AF.Sqrt),
                 reads=[f"rstd{c}"], writes=[f"rstd{c}"])
            t.op("dve", lambda: V.reciprocal(out=rstd[0:nr, c:c + 1], in_=rstd[0:nr, c:c + 1]), reads=[f"rstd{c}"], writes=[f"rstd{c}"])
            t.op("act", lambda: A_.activation(out=xnb[s2][0:nr, :], in_=xin[s3][0:nr, :], func=AF.Copy, scale=rstd[0:nr, c:c + 1]),
                 reads=[f"xin{s3}", f"rstd{c}"], writes=[f"xnb{s2}"])
            ptb = PT_x[s2].bitcast(BF16).rearrange("p (k f) -> p k f", k=8)
            pkey = "P4" if s2 == 0 else "P5"
            for k in range(8):
                t.op("pe", lambda k=k: P_.transpose(out=ptb[:, k, 0:nr], in_=xnb[s2][0:nr, k * 128:(k + 1) * 128], identity=identb[0:nr, 0:nr]),
                     reads=[f"xnb{s2}", "identb"], writes=[pkey])
            xt3 = xtmp[s2].rearrange("p (k f) -> p k f", k=8)
            t.op("dve", lambda: V.tensor_tensor(out=xt3[:, :, 0:nr], in0=ptb[:, :, 0:nr], in1=gT[:, :].unsqueeze(2).broadcast_to([128, 8, nr]), op=ALU.mult),
                 reads=[pkey, "gT"], writes=[f"xtmp{s2}"])
            t.op("pool", lambda: G_.tensor_tensor(out=dst, in0=xt3[:, :, 0:nr], in1=shiftT[:, :].unsqueeze(2).broadcast_to([128, 8, nr]), op=ALU.add),
                 reads=[f"xtmp{s2}", "shiftT"], writes=[dkey])

        for ht in range(HT):
            norm_tile(xh[ht * 128:(ht + 1) * 128, :], 128, xT_tile(ht), f"xT{ht}")

        pbanks = [(PA[:, 0:512], "PA0"), (PA[:, 512:1024], "PA1"), (PB[:, 0:512], "PB0"), (PB[:, 512:1024], "PB1")]
        pst = {"n": 0}

        def next_bank():
            b = pbanks[pst["n"] % 4]
            pst["n"] += 1
            return b

        win_v = w_in.rearrange("(k p) n -> p k n", p=128)
        wout_v = w_out.rearrange("(k p) n -> p k n", p=128)
        wstate = {"n": 0}

        def load_w(src_v, c0, ncols, slot):
            off = 0
            while off < ncols:
                n = min(256, ncols - off)
                s = wstate["n"] % 2
                wstate["n"] += 1
                t.dma("sp", f"wst{s}", wst[s][:, :, 0:n], src_v[:, :, c0 + off:c0 + off + n], writes=[f"wst{s}"])
                t.op("pool", lambda s=s, n=n, off=off: G_.tensor_copy(out=wbf[slot][:, :, off:off + n], in_=wst[s][:, :, 0:n]),
                     reads=[f"wst{s}"], writes=[f"wbf{slot}"])
                off += n

        def tok_blocks(with_halo):
            blks = []
            if with_halo:
                blks.append((xTh[:, :, 0:256], 256, 0))
            for b in range(4):
                blks.append((xT[:, :, b * 512:(b + 1) * 512], 512, 256 + b * 512))
            if with_halo:
                blks.append((xTh[:, :, 256:512], 256, 2304))
            return blks

        def xkeys(c0, n):
            return [f"xT{c0 // 128 + i}" for i in range(n // 128)]

        wFk = cv.take([128, 8, 512], BF16)
        wFv = cv.take([128, 8, 512], BF16)
        wFg = cv.take([128, 8, 16], BF16)
        xg = [cv.take([128, 8, 516], BF16) for _ in range(2)]
        preF2 = [cv.take([128, 516]) for _ in range(2)]
        accF2 = [cv.take([128, 512]) for _ in range(2)]
        kTF2 = [cv.take([128, 512], BF16) for _ in range(2)]
        xgh = [cv.take([128, 8, 4], BF16) for _ in range(2)]
        KtokF = cv.take([128, 4, 4, 128], BF16)
        Vf = cv.take([128, 4, 4, 129], BF16)
        Gf = cv.take([128, 4, 16])
        nlfF = cv.take([128, 4, 2, 4])
        tmpF = cv.take([128, 4, 2, 4])
        Wt = cv.take([128, 4, 2, 4])
        accg = cv.take([128, 2, 4])
        tmpa = cv.take([128, 2, 4])
        wVf = cv.take([128, 4, 2, 4, 129], BF16)
        bFv = cv.take([128, 512])
        bFg = cv.take([128, 16])
        ffl = cv.take([128, NG * 4])
        sgt = cv.take([128, 2, 128])
        print("phase F union words", cv.off)
        t.dma("sp", "c0", ffl, fflags, writes=["ffl"])
        t.dma("sp", "c0", sgt, tri2.rearrange("n p f -> p n f"), writes=["sgt"])
        t.dma("pool", "c1", bFg, b_gates.partition_broadcast(128), writes=["bFg"])
        for hh in range(ML_HEADS):
            c0 = NA_COLS + hh * MLW
            t.dma("pool", "c1", bFv[:, hh * 128:(hh + 1) * 128], bin_row[c0 + 256:c0 + 384].partition_broadcast(128), writes=["bFv"])
            for (dstw, cc) in ((wFk, c0 + 128), (wFv, c0 + 256)):
                sidx = wstate["n"] % 2
                wstate["n"] += 1
                t.dma("sp", f"wst{sidx}", wst[sidx][:, :, 0:128], win_v[:, :, cc:cc + 128], writes=[f"wst{sidx}"])
                t.op("pool", lambda sidx=sidx, dstw=dstw, hh=hh: G_.tensor_copy(out=dstw[:, :, hh * 128:(hh + 1) * 128], in_=wst[sidx][:, :, 0:128]),
                     reads=[f"wst{sidx}"], writes=["wF"])
        sidx = wstate["n"] % 2
        wstate["n"] += 1
        t.dma("sp", f"wst{sidx}", wst[sidx][:, :, 0:16], w_gates.rearrange("(k p) n -> p k n", p=128), writes=[f"wst{sidx}"])
        t.op("pool", lambda: G_.tensor_copy(out=wFg, in_=wst[sidx][:, :, 0:16]), reads=[f"wst{sidx}"], writes=["wF"])
        t.op("pool", lambda: G_.memset(Vf[:, :, :, 128:129], 1.0), writes=["Vfones"])
        t.op("pool", lambda: G_.memset(accg, 0.0), writes=["accg"])
        t.op("pool", lambda: G_.memset(Cacc[:], 0.0), writes=["Cacc"])
        Gv = Gf.rearrange("p t (d x h) -> p t d x h", d=2, x=2)
        SGT, SLT = 0, 1
        def norm_group(gi):
            xs = gi % 2
            norm_tile(xfh[gi * 4:gi * 4 + 4, :], 4, xgh[xs], f"xgh{xs}")
            for j in range(4):
                norm_tile(xf[(gi * 4 + j) * 128:(gi * 4 + j + 1) * 128, :], 128, xg[xs][:, :, 2 + j * 128:2 + (j + 1) * 128], f"xg{xs}")

        norm_group(0)
        for gi in range(NG):
            xs = gi % 2
            xga = xg[xs]
            for j in range(4):
                pb, pk = next_bank()
                for k in range(8):
                    t.op("pe", lambda k=k, j=j, pb=pb: P_.matmul(pb[:, :], lhsT=xga[:, k, 2 + j * 128:2 + (j + 1) * 128], rhs=wFv[:, k, :], start=(k == 0), stop=(k == 7)),
                         reads=["wF", f"xg{xs}"], writes=[pk])
                t.op("dve", lambda j=j, pb=pb: V.tensor_tensor(out=Vf[:, j, :, 0:128], in0=pb.rearrange("p (h d) -> p h d", h=4),
                                                              in1=bFv.rearrange("p (h d) -> p h d", h=4), op=ALU.add), reads=[pk, "bFv"], writes=["Vf"])
                for k in range(8):
                    t.op("pe", lambda k=k, j=j: P_.matmul(P6[:, j * 16:(j + 1) * 16], lhsT=xga[:, k, 2 + j * 128:2 + (j + 1) * 128], rhs=wFg[:, k, :], start=(k == 0), stop=(k == 7)),
                         reads=["wF", f"xg{xs}"], writes=["P6"])
            t.op("dve", lambda: V.tensor_tensor(out=Gf, in0=P6[:, 0:64].rearrange("p (t g) -> p t g", t=4), in1=bFg.unsqueeze(1).broadcast_to([128, 4, 16]), op=ALU.add),
                 reads=["P6", "bFg"], writes=["Gf"])
            t.op("act", lambda: A_.activation(out=nlfF, in_=Gv[:, :, :, 1, :], func=AF.Exp, scale=-1.0), reads=["Gf"], writes=["nlfF"])
            t.op("act", lambda: A_.activation(out=nlfF, in_=nlfF, func=AF.Ln, bias=kcol[:, 0:1], scale=1.0), reads=["nlfF", "kcol"], writes=["nlfF"])
            P7c = P7[:, 0:32].rearrange("p (t d h) -> p t d h", t=4, d=2)
            for j in range(4):
                for d in range(2):
                    others = [jj for jj in range(4) if (jj > j if d == 0 else jj < j)]
                    seq = [(j, sgt[:, SGT if d == 0 else SLT, :])] + [(jj, cst[:, ONES, :]) for jj in others]
                    for qi, (jj, lh) in enumerate(seq):
                        t.op("pe", lambda j=j, d=d, jj=jj, lh=lh, qi=qi, nq=len(seq): P_.matmul(P7c[:, j, d, :], lhsT=lh, rhs=nlfF[:, jj, d, :], start=(qi == 0), stop=(qi == nq - 1)),
                             reads=["nlfF", "sgt", "cst"], writes=["P7"])
            P7t = P7[:, 32:40].rearrange("p (d h) -> p d h", d=2)
            for d in range(2):
                for jj in range(4):
                    t.op("pe", lambda d=d, jj=jj: P_.matmul(P7t[:, d, :], lhsT=cst[:, ONES, :], rhs=nlfF[:, jj, d, :], start=(jj == 0), stop=(jj == 3)),
                         reads=["nlfF", "cst"], writes=["P7"])
            fl = ffl[:, gi * 4:gi * 4 + 2]
            t.op("dve", lambda: V.tensor_tensor(out=tmpF, in0=P7c, in1=accg.unsqueeze(1).broadcast_to([128, 4, 2, 4]), op=ALU.add), reads=["P7", "accg"], writes=["tmpF"])
            t.op("dve", lambda: V.tensor_tensor(out=tmpF, in0=Gv[:, :, :, 0, :], in1=tmpF, op=ALU.subtract), reads=["Gf", "tmpF"], writes=["tmpF"])
            t.op("act", lambda: A_.activation(out=Wt, in_=tmpF, func=AF.Exp), reads=["tmpF"], writes=["Wt"])
            t.op("dve", lambda fl=fl: V.tensor_tensor(out=Wt, in0=Wt, in1=fl.unsqueeze(1).unsqueeze(3).broadcast_to([128, 4, 2, 4]), op=ALU.mult), reads=["Wt", "ffl"], writes=["Wt"])
            t.op("dve", lambda fl=fl: V.tensor_tensor(out=tmpa, in0=P7t, in1=fl.unsqueeze(2).broadcast_to([128, 2, 4]), op=ALU.mult), reads=["P7", "ffl"], writes=["tmpa"])
            t.op("dve", lambda: V.tensor_tensor(out=accg, in0=accg, in1=tmpa, op=ALU.add), reads=["accg", "tmpa"], writes=["accg"])
            for d in range(2):
                eng = "dve" if d == 0 else "pool"
                E = V if d == 0 else G_
                t.op(eng, lambda E=E, d=d: E.tensor_tensor(out=wVf[:, :, d, :, :], in0=Vf, in1=Wt[:, :, d, :].unsqueeze(3).broadcast_to([128, 4, 4, 129]), op=ALU.mult),
                     reads=["Vf", "Vfones", "Wt"], writes=[f"wVf{d}"])
            if gi + 1 < NG:
                norm_group(gi + 1)
            def kproj(hh):
                preF = preF2[hh % 2]
                pk_ = f"preF{hh % 2}"
                pb, pk = next_bank()
                for k in range(8):
                    t.op("pe", lambda k=k: P_.matmul(pb[:, 0:512], lhsT=wFk[:, k, hh * 128:(hh + 1) * 128], rhs=xga[:, k, 2:514], start=(k == 0), stop=(k == 7)),
                         reads=["wF", f"xg{xs}"], writes=[pk])
                t.op("act", lambda: A_.activation(out=preF[:, 2:514], in_=pb[:, 0:512], func=AF.Identity, scale=1.0, bias=bcol[:, 9 + 2 * hh:10 + 2 * hh]),
                     reads=[pk, "bcol"], writes=[pk_])
                pb2, pk2 = next_bank()
                for k in range(8):
                    t.op("pe", lambda k=k: P_.matmul(pb2[:, 0:4], lhsT=wFk[:, k, hh * 128:(hh + 1) * 128], rhs=xgh[xs][:, k, :], start=(k == 0), stop=(k == 7)),
                         reads=["wF", f"xgh{xs}"], writes=[pk2])
                t.op("act", lambda: A_.activation(out=preF[:, 0:2], in_=pb2[:, 0:2], func=AF.Identity, scale=1.0, bias=bcol[:, 9 + 2 * hh:10 + 2 * hh]),
                     reads=[pk2, "bcol"], writes=[pk_])
                t.op("act", lambda: A_.activation(out=preF[:, 514:516], in_=pb2[:, 2:4], func=AF.Identity, scale=1.0, bias=bcol[:, 9 + 2 * hh:10 + 2 * hh]),
                     reads=[pk2, "bcol"], writes=[pk_])

            kproj(0)
            for hh in range(ML_HEADS):
                if hh + 1 < ML_HEADS:
                    kproj(hh + 1)
                preF, accF, kTF = preF2[hh % 2], accF2[hh % 2], kTF2[hh % 2]
                pk_, ak_, kk_ = f"preF{hh % 2}", f"accF{hh % 2}", f"kTF{hh % 2}"
                t.op("dve", lambda: V.tensor_scalar(out=preF[:, 0:2], in0=preF[:, 0:2], scalar1=ffl[:, gi * 4 + 2:gi * 4 + 3], scalar2=None, op0=ALU.mult), reads=[pk_, "ffl"], writes=[pk_])
                t.op("dve", lambda: V.tensor_scalar(out=preF[:, 514:516], in0=preF[:, 514:516], scalar1=ffl[:, gi * 4 + 3:gi * 4 + 4], scalar2=None, op0=ALU.mult), reads=[pk_, "ffl"], writes=[pk_])
                gidx = 4 + hh
                t.op("dve", lambda: V.tensor_scalar(out=accF, in0=preF[:, 0:512], scalar1=cw[:, gidx, 0:1], scalar2=cb[:, gidx:gidx + 1], op0=ALU.mult, op1=ALU.add),
                     reads=[pk_, "cw", "cb"], writes=[ak_])
                for jc in range(1, 5):
                    t.op("dve", lambda jc=jc: V.scalar_tensor_tensor(out=accF, in0=preF[:, jc:jc + 512], scalar=cw[:, gidx, jc:jc + 1], in1=accF, op0=ALU.mult, op1=ALU.add),
                         reads=[pk_, "cw", ak_], writes=[ak_])
                t.op("act", lambda: A_.activation(out=kTF, in_=accF, func=AF.Silu), reads=[ak_], writes=[kk_])
                p4b = P4.bitcast(BF16).rearrange("p (k f) -> p k f", k=8)
                for j in range(4):
                    t.op("pe", lambda j=j: P_.transpose(out=p4b[:, j, :], in_=kTF[:, j * 128:(j + 1) * 128], identity=identb[:]), reads=[kk_, "identb"], writes=["P4"])
                t.op("act", lambda: A_.activation(out=KtokF[:, hh, :, :], in_=p4b[:, 0:4, :], func=AF.Copy), reads=["P4"], writes=[f"KtokF{hh}"])
                for d in range(2):
                    pb, pk = next_bank()
                    for j in range(4):
                        t.op("pe", lambda j=j: P_.matmul(pb[:, 0:129], lhsT=KtokF[:, hh, j, :], rhs=wVf[:, j, d, hh, :], start=(j == 0), stop=(j == 3)),
                             reads=[f"KtokF{hh}", f"wVf{d}"], writes=[pk])
                    t.op("dve", lambda: V.tensor_tensor(out=Cacc[:, d, hh, :], in0=Cacc[:, d, hh, :], in1=pb[:, 0:129], op=ALU.add),
                         reads=[pk, "Cacc"], writes=["Cacc"])
        dbg_dump("f_c", [(Cacc[:].rearrange("p a b c -> p (a b c)"), 8 * 129)])
        t.barrier()

        cv = Carver()
        KT = cv.take([128, 4, HT * 128], BF16)
        QT = cv.take([128, 4, TOK], BF16)
        Vaug = cv.take([128, HT, 8, 65], BF16)
        Ar = cv.take([128, 8, 14, 64], BF16)
        mcol = cv.take([128, 192])
        bias_bc = cv.take([128, 512])
        mark = cv.off
        cmk = cv.take([128, 14 * 64])
        rtmp = [cv.take([128, 14 * 64]) for _ in range(2)]
        t.dma("sp", "c0", mcol, mcol_d, writes=["mcol"])
        t.dma("sp", "c0", cmk, cmask, writes=["cmk"])
        for h in range(8):
            s = h % 2
            t.dma("sp", f"rtmp{s}", rtmp[s], rpbA[h], writes=[f"rtmp{s}"])
            t.op("act", lambda s=s: A_.activation(out=rtmp[s], in_=rtmp[s], func=AF.Exp), reads=[f"rtmp{s}"], writes=[f"rtmp{s}"])
            t.op("dve", lambda s=s, h=h: V.tensor_tensor(out=Ar[:, h, :, :].rearrange("p e c -> p (e c)"), in0=rtmp[s], in1=cmk, op=ALU.mult),
                 reads=[f"rtmp{s}", "cmk"], writes=[f"Ar{h}"])
        t.barrier()
        cv.off = mark
        expS = [cv.take([128, 6, 128]) for _ in range(2)]
        PTt = [cv.take([128, 6, 128], BF16) for _ in range(2)]
        sz_t = cv.take([128, 512])
        sz_b = cv.take([128, 512], BF16)
        rden = cv.take([128, 8])
        otmp = cv.take([128, 512])
        ob = cv.take([128, 512], BF16)
        print("NA union words", cv.off)
        t.op("pool", lambda: G_.memset(Vaug[:, :, :, 64:65], 1.0), writes=["Vones"])

        for which in range(2):
            slot = which
            load_w(win_v, which * 512, 512, slot)
            for g in range(4):
                for (xap, n, c0) in tok_blocks(with_halo=(which == 1)):
                    pb, pk = next_bank()
                    for k in range(8):
                        t.op("pe", lambda k=k, g=g, xap=xap, n=n, pb=pb, slot=slot: P_.matmul(
                            pb[:, 0:n], lhsT=wbf[slot][:, k, g * 128:(g + 1) * 128], rhs=xap[:, k, :], start=(k == 0), stop=(k == 7)),
                            reads=[f"wbf{slot}"] + xkeys(c0, n), writes=[pk])
                    if which == 0:
                        q0 = c0 - 256
                        t.op("act", lambda g=g, pb=pb, n=n, q0=q0: A_.activation(out=QT[:, g, q0:q0 + n], in_=pb[:, 0:n], func=AF.Identity,
                                                                                 scale=0.125, bias=bcolq[:, g:g + 1]),
                             reads=[pk, "bcolq"], writes=[f"QT{g}_{q0 // 128 + i}" for i in range(n // 128)])
                    else:
                        t.op("act", lambda g=g, pb=pb, n=n, c0=c0: A_.activation(out=KT[:, g, c0:c0 + n], in_=pb[:, 0:n], func=AF.Identity,
                                                                                 scale=1.0, bias=bcol[:, 4 + g:5 + g]),
                             reads=[pk, "bcol"], writes=[f"KT{g}_{c0 // 128 + i}" for i in range(n // 128)])
        load_w(win_v, 1024, 512, 0)
        t.dma("pool", "c1", bias_bc, bin_row[1024:1536].partition_broadcast(128), reads=[], writes=["bias_bc"])
        for ht in range(HT):
            pb, pk = next_bank()
            xa = xT_tile(ht)
            for k in range(8):
                t.op("pe", lambda k=k, xa=xa, pb=pb: P_.matmul(pb[:, :], lhsT=xa[:, k, :], rhs=wbf[0][:, k, :], start=(k == 0), stop=(k == 7)),
                     reads=["wbf0", f"xT{ht}"], writes=[pk])
            t.op("dve", lambda ht=ht, pb=pb: V.tensor_tensor(out=Vaug[:, ht, :, 0:64], in0=pb.rearrange("p (h d) -> p h d", h=8),
                                                            in1=bias_bc.rearrange("p (h d) -> p h d", h=8), op=ALU.add),
                 reads=[pk, "bias_bc"], writes=[f"V{ht}"])
        load_w(win_v, 1536, 512, 1)
        t.wait_keys("pool", ["bias_bc"])
        t.dma("pool", "c1", bias_bc, bin_row[1536:2048].partition_broadcast(128), reads=[], writes=["bias_bc"])

        for m in range(NT):
            base, nj = _na_base(m), _na_nj(m)
            qt = m
            for k in range(8):
                t.op("pe", lambda k=k, m=m: P_.matmul(P7[:, :], lhsT=xT[:, k, m * 128:(m + 1) * 128], rhs=wbf[1][:, k, :], start=(k == 0), stop=(k == 7)),
                     reads=["wbf1", f"xT{m + 2}"], writes=["P7"])
            t.op("dve", lambda: V.tensor_tensor(out=sz_t, in0=P7[:, :], in1=bias_bc, op=ALU.add), reads=["P7", "bias_bc"], writes=["sz_t"])
            t.op("act", lambda: A_.activation(out=sz_b, in_=sz_t, func=AF.Silu), reads=["sz_t"], writes=["sz_b"])
            for h in range(8):
                g, hh = h // 2, h % 2
                sl = h % 2
                PS = PA if sl == 0 else PB
                pskeys = ["PA0", "PA1"] if sl == 0 else ["PB0", "PB1"]
                PS3 = PS.rearrange("p (j q) -> p j q", q=128)
                for j in range(nj):
                    kt = base + j
                    t.op("pe", lambda j=j, kt=kt, g=g, hh=hh, PS3=PS3, qt=qt: P_.matmul(
                        PS3[:, j, :], lhsT=KT[hh * 64:(hh + 1) * 64, g, kt * 128:(kt + 1) * 128],
                        rhs=QT[hh * 64:(hh + 1) * 64, g, qt * 128:(qt + 1) * 128], start=True, stop=True),
                        reads=[f"KT{g}_{kt}", f"QT{g}_{qt}"], writes=pskeys)
                t.op("act", lambda sl=sl, nj=nj, PS3=PS3: A_.activation(out=expS[sl][:, 0:nj, :], in_=PS3[:, 0:nj, :], func=AF.Exp),
                     reads=pskeys, writes=[f"expS{sl}"])
                eng = "dve" if h % 2 == 0 else "pool"
                E = V if eng == "dve" else G_
                interior = 2 <= m <= 13
                for j in range(nj):
                    dyi0 = 2 * (base - m + j) + 3
                    e0 = 13 - dyi0
                    if interior and 1 <= j <= 3:
                        t.op(eng, lambda E=E, sl=sl, j=j, h=h, e0=e0: E.tensor_tensor(
                            out=PTt[sl][:, j, :], in0=expS[sl][:, j, :], in1=Ar[:, h, e0:e0 + 2, :].rearrange("p e c -> p (e c)"), op=ALU.mult),
                            reads=[f"expS{sl}", f"Ar{h}"], writes=[f"PT{sl}"])
                    else:
                        for b in range(2):
                            mc = (m * 6 + j) * 2 + b
                            t.op("dve", lambda E=V, sl=sl, j=j, h=h, e0=e0, b=b, mc=mc: E.scalar_tensor_tensor(
                                out=PTt[sl][:, j, b * 64:(b + 1) * 64], in0=expS[sl][:, j, b * 64:(b + 1) * 64], scalar=mcol[:, mc:mc + 1],
                                in1=Ar[:, h, e0 + b, :], op0=ALU.mult, op1=ALU.mult),
                                reads=[f"expS{sl}", f"Ar{h}", "mcol"], writes=[f"PT{sl}"])
                PO = P4 if h < 4 else P5
                pok = "P4" if h < 4 else "P5"
                PO3 = PO[:, 0:260].rearrange("p (h d) -> p h d", d=65)
                for j in range(nj):
                    kt = base + j
                    t.op("pe", lambda j=j, kt=kt, h=h, sl=sl, PO3=PO3, nj=nj: P_.matmul(
                        PO3[:, h % 4, :], lhsT=PTt[sl][:, j, :], rhs=Vaug[:, kt, h, :], start=(j == 0), stop=(j == nj - 1)),
                        reads=[f"PT{sl}", f"V{kt}", "Vones"], writes=[pok])
            for half in range(2):
                PO = P4 if half == 0 else P5
                pok = "P4" if half == 0 else "P5"
                PO3 = PO[:, 0:260].rearrange("p (h d) -> p h d", d=65)
                t.op("dve", lambda half=half, PO3=PO3: V.reciprocal(out=rden[:, half * 4:(half + 1) * 4], in_=PO3[:, :, 64]),
                     reads=[pok], writes=["rden"])
                t.op("dve", lambda half=half, PO3=PO3: V.tensor_tensor(
                    out=otmp[:, half * 256:(half + 1) * 256].rearrange("p (h d) -> p h d", d=64), in0=PO3[:, :, 0:64],
                    in1=rden[:, half * 4:(half + 1) * 4].unsqueeze(2).broadcast_to([128, 4, 64]), op=ALU.mult),
                    reads=[pok, "rden"], writes=["otmp"])
            t.op("dve", lambda: V.tensor_tensor(out=ob, in0=otmp, in1=sz_b, op=ALU.mult), reads=["otmp", "sz_b"], writes=["ob"])
            p6b = P6.bitcast(BF16).rearrange("p (k f) -> p k f", k=8)
            for c4 in range(4):
                t.op("pe", lambda c4=c4, p6b=p6b: P_.transpose(out=p6b[:, c4, :], in_=ob[:, c4 * 128:(c4 + 1) * 128], identity=identb[:]),
                     reads=["ob", "identb"], writes=["P6"])
            t.op("act", lambda m=m, p6b=p6b: A_.activation(out=mixT[:, 0:4, m * 128:(m + 1) * 128], in_=p6b[:, 0:4, :], func=AF.Copy),
                 reads=["P6"], writes=[f"mixNA{m}"])
        t.barrier()

        if dbg == "na":
            cv = Carver()
            dtmp = cv.take([128, TOK])
            for c8 in range(4):
                t.op("dve", lambda c8=c8: V.tensor_copy(out=dtmp, in_=mixT[:, c8, :]), reads=[f"mixNA{m}" for m in range(NT)], writes=["dtmp"])
                t.dma("sp", "dbg", dbg_out[:, c8 * TOK:(c8 + 1) * TOK], dtmp, reads=["dtmp"], writes=["dbgout"])
            t.wait_keys("sp", ["dbgout"])
            nc.sync.wait_ge(t.dma_lanes["dbg"].sem, t.dma_lanes["dbg"].count)
            print("instructions", t.n_inst, "waits", t.n_wait)
            return nc


        cv = Carver()
        mixML = cv.take([128, 4, TOK], BF16)
        mqT = cv.take([128, TOK], BF16)
        mkT = cv.take([128, TOK], BF16)
        QsT = [cv.take([128, TOK], BF16) for _ in range(2)]
        Ktok = cv.take([128, NT, 128], BF16)
        Vg = cv.take([128, NT, 129], BF16)
        wV = [cv.take([128, NT, 129], BF16) for _ in range(2)]
        so = cv.take([128, NT, 128], BF16)
        szm = cv.take([128, NT, 128], BF16)
        hF = cv.take([128, NT, 128])
        Gt = cv.take([128, NT, 4])
        nlf = cv.take([128, NT, 2])
        gw = cv.take([128, NT, 4])
        ebl = cv.take([128, NT, 2, 2])
        bias_ml = cv.take([128, 388])
        mlnw_bc = cv.take([128, 512])
        Cst = [cv.take([128, 129]) for _ in range(2)]
        Cbf = [cv.take([128, 129], BF16) for _ in range(2)]
        small = cv.take([128, 64])
        mark = cv.off
        t.dma("pool", "c1", mlnw_bc, mlnw_row.partition_broadcast(128), writes=["mlnw_bc"])
        t.op("pool", lambda: G_.memset(Vg[:, :, 128:129], 1.0), writes=["Vgones"])
        PNs = [(P6, "P6"), (P7, "P7")]

        for h in range(ML_HEADS):
            c0 = NA_COLS + h * MLW
            cv.off = mark
            pre = cv.take([128, TOK + 4])
            acc = cv.take([128, TOK])
            load_w(win_v, c0, 256, 0)
            load_w(win_v, c0 + 256, 388, 1)
            t.dma("pool", "c1", bias_ml, bin_row[c0 + 256:c0 + 644].partition_broadcast(128), writes=["bias_ml"])
            for g in range(2):
                blks = [(xTh[:, :, 254:256], 2, 0, ["xT1"])]
                for b in range(4):
                    blks.append((xT[:, :, b * 512:(b + 1) * 512], 512, 2 + b * 512, [f"xT{2 + 4 * b + i}" for i in range(4)]))
                blks.append((xTh[:, :, 256:258], 2, 2050, ["xT18"]))
                for (xap, n, p0, xk) in blks:
                    pb, pk = next_bank()
                    for k in range(8):
                        t.op("pe", lambda k=k, g=g, xap=xap, n=n, pb=pb: P_.matmul(
                            pb[:, 0:n], lhsT=wbf[0][:, k, g * 128:(g + 1) * 128], rhs=xap[:, k, :], start=(k == 0), stop=(k == 7)),
                            reads=["wbf0"] + xk, writes=[pk])
                    t.op("act", lambda g=g, pb=pb, n=n, p0=p0, h=h: A_.activation(out=pre[:, p0:p0 + n], in_=pb[:, 0:n], func=AF.Identity,
                                                                             scale=1.0, bias=bcol[:, 8 + 2 * h + g:9 + 2 * h + g]),
                         reads=[pk, "bcol"], writes=["pre"])
                t.op("dve", lambda: V.tensor_scalar(out=pre[:, 0:2], in0=pre[:, 0:2], scalar1=flg[:, 0:1], scalar2=None, op0=ALU.mult),
                     reads=["pre", "flg"], writes=["pre"])
                t.op("dve", lambda: V.tensor_scalar(out=pre[:, TOK + 2:TOK + 4], in0=pre[:, TOK + 2:TOK + 4], scalar1=flg[:, 1:2], scalar2=None, op0=ALU.mult),
                     reads=["pre", "flg"], writes=["pre"])
                gi = g * 4 + h
                t.op("dve", lambda gi=gi: V.tensor_scalar(out=acc, in0=pre[:, 0:TOK], scalar1=cw[:, gi, 0:1], scalar2=cb[:, gi:gi + 1],
                                                          op0=ALU.mult, op1=ALU.add), reads=["pre", "cw", "cb"], writes=["acc"])
                for j in range(1, 5):
                    t.op("dve", lambda gi=gi, j=j: V.scalar_tensor_tensor(out=acc, in0=pre[:, j:j + TOK], scalar=cw[:, gi, j:j + 1], in1=acc,
                                                                          op0=ALU.mult, op1=ALU.add), reads=["pre", "cw", "acc"], writes=["acc"])
                dstT = mqT if g == 0 else mkT
                t.op("act", lambda dstT=dstT: A_.activation(out=dstT, in_=acc, func=AF.Silu), reads=["acc"], writes=["mqT" if g == 0 else "mkT"])
            t.barrier()
            if h == 0:
                dbg_dump("ml_a", [(mqT, TOK), (mkT, TOK)])
            cv.off = mark
            tmpv = [cv.take([128, 388]) for _ in range(2)]
            Rfb = [cv.take([128, 2, 128]) for _ in range(2)]
            ebt = [cv.take([128, 2, 128]) for _ in range(2)]
            tmp4 = [cv.take([128, 4]) for _ in range(2)]
            sdT = [cv.take([128, 128], BF16) for _ in range(2)]
            hm = [cv.take([128, 128]) for _ in range(2)]
            hjunk = cv.take([128, 128])
            mo = [cv.take([128, 128], BF16) for _ in range(2)]
            st = [cv.take([128, 8]) for _ in range(2)]
            for tl in range(NT):
                pb, pk = next_bank()
                s2 = tl % 2
                for k in range(8):
                    t.op("pe", lambda k=k, tl=tl, pb=pb: P_.matmul(pb[:, 0:388], lhsT=xT[:, k, tl * 128:(tl + 1) * 128], rhs=wbf[1][:, k, 0:388],
                                                                  start=(k == 0), stop=(k == 7)), reads=["wbf1", f"xT{tl + 2}"], writes=[pk])
                t.op("dve", lambda pb=pb, s2=s2: V.tensor_tensor(out=tmpv[s2], in0=pb[:, 0:388], in1=bias_ml, op=ALU.add),
                     reads=[pk, "bias_ml"], writes=[f"tmpv{s2}"])
                t.op("pool", lambda tl=tl, s2=s2: G_.tensor_copy(out=Vg[:, tl, 0:128], in_=tmpv[s2][:, 0:128]), reads=[f"tmpv{s2}"], writes=[f"Vg{tl}"])
                t.op("act", lambda tl=tl, s2=s2: A_.activation(out=so[:, tl, :], in_=tmpv[s2][:, 128:256], func=AF.Sigmoid), reads=[f"tmpv{s2}"], writes=[f"so{tl}"])
                t.op("act", lambda tl=tl, s2=s2: A_.activation(out=szm[:, tl, :], in_=tmpv[s2][:, 256:384], func=AF.Silu), reads=[f"tmpv{s2}"], writes=[f"szm{tl}"])
                t.op("pool", lambda tl=tl, s2=s2: G_.tensor_copy(out=Gt[:, tl, :], in_=tmpv[s2][:, 384:388]), reads=[f"tmpv{s2}"], writes=["Gt"])
            Gf = Gt.rearrange("p t (d two) -> p t d two", two=2)[:, :, :, 1]
            Gi = Gt.rearrange("p t (d two) -> p t d two", two=2)[:, :, :, 0]
            t.op("act", lambda: A_.activation(out=nlf, in_=Gf, func=AF.Exp, scale=-1.0), reads=["Gt"], writes=["nlf"])
            t.op("act", lambda: A_.activation(out=nlf, in_=nlf, func=AF.Ln, bias=kcol[:, 0:1], scale=1.0), reads=["nlf", "kcol"], writes=["nlf"])
            for tl in range(NT):
                s2 = tl % 2
                t.op("pe", lambda tl=tl: P_.matmul(P6[:, 0:1], lhsT=cst[:, TRIF, :], rhs=nlf[:, tl, 0:1], start=True, stop=True), reads=["cst", "nlf"], writes=["P6"])
                t.op("pe", lambda tl=tl: P_.matmul(P6[:, 1:2], lhsT=cst[:, TRIB, :], rhs=nlf[:, tl, 1:2], start=True, stop=True), reads=["cst", "nlf"], writes=["P6"])
                t.op("pe", lambda tl=tl: P_.matmul(P6[:, 2:4], lhsT=cst[:, BLK, :], rhs=nlf[:, tl, 0:2], start=True, stop=True), reads=["cst", "nlf"], writes=["P6"])
                t.op("dve", lambda tl=tl, s2=s2: V.tensor_tensor(out=tmp4[s2][:, 0:2], in0=Gi[:, tl, :], in1=P6[:, 0:2], op=ALU.add),
                     reads=["Gt", "P6"], writes=[f"tmp4{s2}"])
                t.op("dve", lambda s2=s2: V.tensor_tensor(out=tmp4[s2][:, 2:4], in0=tmp4[s2][:, 0:2], in1=P6[:, 2:4], op=ALU.subtract),
                     reads=["P6", f"tmp4{s2}"], writes=[f"tmp4{s2}"])
                t.op("act", lambda tl=tl, s2=s2: A_.activation(out=gw[:, tl, :], in_=tmp4[s2], func=AF.Exp), reads=[f"tmp4{s2}"], writes=[f"gw{tl}"])
                t.op("dve", lambda tl=tl, s2=s2: V.tensor_scalar(out=Rfb[s2][:, 0, :], in0=cst[:, TRIF, :], scalar1=nlf[:, tl, 0:1], scalar2=None, op0=ALU.mult),
                     reads=["cst", "nlf"], writes=[f"Rfb{s2}"])
                t.op("dve", lambda tl=tl, s2=s2: V.tensor_scalar(out=Rfb[s2][:, 1, :], in0=cst[:, TRIB, :], scalar1=nlf[:, tl, 1:2], scalar2=None, op0=ALU.mult),
                     reads=["cst", "nlf"], writes=[f"Rfb{s2}"])
                t.op("pe", lambda s2=s2: P_.matmul(P7[:, 0:256], lhsT=cst[:, ONES, :], rhs=Rfb[s2].rearrange("p d t -> p (d t)"), start=True, stop=True),
                     reads=["cst", f"Rfb{s2}"], writes=["P7"])
                P7v = P7[:, 0:256].rearrange("p (d t) -> p d t", d=2)
                t.op("act", lambda s2=s2, P7v=P7v: A_.activation(out=ebt[s2], in_=P7v, func=AF.Exp, scale=-1.0, bias=kcol[:, 1:2]),
                     reads=["P7", "kcol"], writes=[f"ebt{s2}"])
                t.op("act", lambda tl=tl: A_.activation(out=ebl[:, tl, 0, :], in_=P7[:, 63:128:64], func=AF.Exp, scale=-1.0), reads=["P7"], writes=[f"ebl{tl}"])
                t.op("act", lambda tl=tl: A_.activation(out=ebl[:, tl, 1, :], in_=P7[:, 128:256:64], func=AF.Exp, scale=-1.0), reads=["P7"], writes=[f"ebl{tl}"])
                for d in range(2):
                    eng = "dve" if d == 0 else "pool"
                    E = V if d == 0 else G_
                    t.op(eng, lambda E=E, d=d, tl=tl, s2=s2: E.tensor_tensor(out=QsT[d][:, tl * 128:(tl + 1) * 128], in0=mqT[:, tl * 128:(tl + 1) * 128],
                                                                           in1=ebt[s2][:, d, :], op=ALU.mult), reads=["mqT", f"ebt{s2}"], writes=[f"Qs{d}_{tl}"])
                    t.op("pool", lambda d=d, tl=tl: G_.tensor_scalar(out=wV[d][:, tl, :], in0=Vg[:, tl, :], scalar1=gw[:, tl, 2 + d:3 + d], scalar2=None, op0=ALU.mult),
                         reads=[f"Vg{tl}", "Vgones", f"gw{tl}"], writes=[f"wV{d}_{tl}"])
                p4b = P4.bitcast(BF16)
                t.op("pe", lambda tl=tl, p4b=p4b, s2=s2: P_.transpose(out=p4b[:, s2 * 128:(s2 + 1) * 128], in_=mkT[:, tl * 128:(tl + 1) * 128], identity=identb[:]),
                     reads=["mkT", "identb"], writes=["P4"])
                t.op("act", lambda tl=tl, p4b=p4b, s2=s2: A_.activation(out=Ktok[:, tl, :], in_=p4b[:, s2 * 128:(s2 + 1) * 128], func=AF.Copy),
                     reads=["P4"], writes=[f"Ktok{tl}"])

            if h == 0:
                dbg_dump("ml_b", [(gw.rearrange("p a b -> p (a b)"), NT * 4), (ebl.rearrange("p a b c -> p (a b c)"), NT * 4),
                                  (nlf.rearrange("p a b -> p (a b)"), NT * 2), (QsT[0], TOK), (QsT[1], TOK),
                                  (Ktok.rearrange("p a b -> p (a b)"), TOK), (Vg.rearrange("p a b -> p (a b)"), NT * 129),
                                  (wV[0].rearrange("p a b -> p (a b)"), NT * 129)])
            def chunks(d):
                order = list(range(2 * NT))
                return order if d == 0 else order[::-1]

            def state_update(d, c, to_bf):
                tl, c2 = c // 2, c % 2
                pb, pk = next_bank()
                lo, hi = c2 * 64, c2 * 64 + 64
                t.op("pe", lambda: P_.matmul(pb[:, 0:129], lhsT=Ktok[lo:hi, tl, :], rhs=wV[d][lo:hi, tl, :], start=True, stop=True),
                     reads=[f"Ktok{tl}", f"wV{d}_{tl}"], writes=[pk])
                t.op("dve", lambda: V.scalar_tensor_tensor(out=Cst[d], in0=Cst[d], scalar=ebl[:, tl, d, c2:c2 + 1], in1=pb[:, 0:129],
                                                           op0=ALU.mult, op1=ALU.add), reads=[pk, f"ebl{tl}", f"C{d}"], writes=[f"C{d}"])
                if to_bf:
                    t.op("act", lambda: A_.activation(out=Cbf[d], in_=Cst[d], func=AF.Copy), reads=[f"C{d}"], writes=[f"Cb{d}"])

            for d in range(2):
                t.op("dve", lambda d=d, h=h: V.tensor_copy(out=Cst[d], in_=Cacc[:, d, h, :]), reads=["Cacc"], writes=[f"C{d}"])
                t.op("act", lambda d=d: A_.activation(out=Cbf[d], in_=Cst[d], func=AF.Copy), reads=[f"C{d}"], writes=[f"Cb{d}"])
            if h == 0:
                dbg_dump("ml_d", [(Cst[0], 129), (Cst[1], 129)])
            def scan_tile(d, tl):
                PN, pnk = PNs[d]
                pb, pk = next_bank()
                tok = slice(tl * 128, (tl + 1) * 128)
                t.op("pe", lambda: P_.matmul(pb[:, 0:128], lhsT=mkT[:, tok], rhs=QsT[d][:, tok], start=True, stop=True),
                     reads=["mkT", f"Qs{d}_{tl}"], writes=[pk])
                msk = cst[:, TRIF, :] if d == 0 else cst[:, TRIB, :]
                t.op("dve", lambda: V.scalar_tensor_tensor(out=sdT[d], in0=pb[:, 0:128], scalar=gw[:, tl, d:d + 1], in1=msk, op0=ALU.mult, op1=ALU.mult),
                     reads=[pk, f"gw{tl}", "cst"], writes=[f"sdT{d}"])
                t.op("pe", lambda: P_.matmul(PN[:, 0:129], lhsT=sdT[d], rhs=Vg[:, tl, :], start=True, stop=False),
                     reads=[f"sdT{d}", f"Vg{tl}", "Vgones"], writes=[pnk])
                order = (0, 1) if d == 0 else (1, 0)
                for ci, c2 in enumerate(order):
                    lo = tl * 128 + c2 * 64
                    t.op("pe", lambda c2=c2, lo=lo, ci=ci: P_.matmul(PN[c2 * 64:(c2 + 1) * 64, 0:129], lhsT=QsT[d][:, lo:lo + 64], rhs=Cbf[d],
                                                                 start=False, stop=True), reads=[f"Qs{d}_{tl}", f"Cb{d}"], writes=[pnk])
                    c = tl * 2 + c2
                    last = (c == (2 * NT - 1 if d == 0 else 0))
                    if not last:
                        state_update(d, c, True)
                s2 = d
                t.op("act", lambda: A_.activation(out=st[s2][:, 0:1], in_=PN[:, 128:129], func=AF.Abs), reads=[pnk], writes=[f"st{s2}"])
                t.op("dve", lambda: V.tensor_scalar(out=st[s2][:, 0:1], in0=st[s2][:, 0:1], scalar1=1.0, scalar2=None, op0=ALU.max),
                     reads=[f"st{s2}"], writes=[f"st{s2}"])
                t.op("dve", lambda: V.reciprocal(out=st[s2][:, 1:2], in_=st[s2][:, 0:1]), reads=[f"st{s2}"], writes=[f"st{s2}"])
                return PN, pnk

            for i in range(NT):
                PN, pnk = scan_tile(0, i)
                t.op("dve", lambda PN=PN, i=i: V.tensor_scalar(out=hF[:, i, :], in0=PN[:, 0:128], scalar1=st[0][:, 1:2], scalar2=None, op0=ALU.mult),
                     reads=[pnk, "st0"], writes=[f"hF{i}"])
            if h == 0:
                dbg_dump("ml_e", [(hF.rearrange("p a b -> p (a b)"), TOK)])
            for i in range(NT - 1, -1, -1):
                PN, pnk = scan_tile(1, i)
                s2 = i % 2
                t.op("dve", lambda PN=PN, i=i, s2=s2: V.scalar_tensor_tensor(out=hm[s2], in0=PN[:, 0:128], scalar=st[1][:, 1:2], in1=hF[:, i, :], op0=ALU.mult, op1=ALU.add),
                     reads=[pnk, "st1", f"hF{i}"], writes=[f"hm{s2}"])
                t.op("dve", lambda i=i, s2=s2: V.tensor_tensor(out=hm[s2], in0=hm[s2], in1=so[:, i, :], op=ALU.mult), reads=[f"hm{s2}", f"so{i}"], writes=[f"hm{s2}"])
                if h == 0 and i == NT - 1:
                    dbg_dump("ml_g", [(hm[s2], 128)])
                t.op("act", lambda s2=s2: A_.activation(out=hjunk, in_=hm[s2], func=AF.Identity, accum_out=st[1][:, 2:3]), reads=[f"hm{s2}"], writes=["hjunk", "st1b"])
                t.op("act", lambda s2=s2: A_.activation(out=hjunk, in_=hm[s2], func=AF.Square, accum_out=st[1][:, 3:4]), reads=[f"hm{s2}"], writes=["hjunk", "st1c"])
                t.op("dve", lambda: V.tensor_scalar(out=st[1][:, 4:5], in0=st[1][:, 2:3], scalar1=1.0 / 128, scalar2=None, op0=ALU.mult), reads=["st1b"], writes=["st1d"])
                t.op("dve", lambda: V.tensor_tensor(out=st[1][:, 5:6], in0=st[1][:, 4:5], in1=st[1][:, 4:5], op=ALU.mult), reads=["st1d"], writes=["st1e"])
                t.op("dve", lambda: V.scalar_tensor_tensor(out=st[1][:, 6:7], in0=st[1][:, 3:4], scalar=1.0 / 128, in1=st[1][:, 5:6], op0=ALU.mult, op1=ALU.subtract),
                     reads=["st1c", "st1e"], writes=["st1f"])
                t.op("dve", lambda: V.tensor_scalar(out=st[1][:, 6:7], in0=st[1][:, 6:7], scalar1=EPS, scalar2=None, op0=ALU.add), reads=["st1f"], writes=["st1f"])
                t.op("act", lambda: A_.activation(out=st[1][:, 6:7], in_=st[1][:, 6:7], func=AF.Sqrt), reads=["st1f"], writes=["st1f"])
                t.op("dve", lambda: V.reciprocal(out=st[1][:, 7:8], in_=st[1][:, 6:7]), reads=["st1f"], writes=["st1g"])
                t.op("dve", lambda s2=s2: V.tensor_scalar(out=hm[s2], in0=hm[s2], scalar1=st[1][:, 4:5], scalar2=st[1][:, 7:8], op0=ALU.subtract, op1=ALU.mult),
                     reads=[f"hm{s2}", "st1d", "st1g"], writes=[f"hm{s2}"])
                t.op("pool", lambda s2=s2, h=h: G_.tensor_tensor(out=hm[s2], in0=hm[s2], in1=mlnw_bc[:, h * 128:(h + 1) * 128], op=ALU.mult),
                     reads=[f"hm{s2}", "mlnw_bc"], writes=[f"hm{s2}"])
                t.op("pool", lambda s2=s2, i=i: G_.tensor_tensor(out=mo[s2], in0=hm[s2], in1=szm[:, i, :], op=ALU.mult), reads=[f"hm{s2}", f"szm{i}"], writes=[f"mo{s2}"])
                p4b = P4.bitcast(BF16)
                if h == 0 and i == NT - 1:
                    dbg_dump("ml_h", [(mo[s2], 128)])
                t.op("pe", lambda s2=s2, p4b=p4b: P_.transpose(out=p4b[:, s2 * 128:(s2 + 1) * 128], in_=mo[s2], identity=identb[:]),
                     reads=[f"mo{s2}", "identb"], writes=["P4"])
                t.op("act", lambda s2=s2, p4b=p4b, i=i, h=h: A_.activation(out=mixML[:, h, i * 128:(i + 1) * 128], in_=p4b[:, s2 * 128:(s2 + 1) * 128], func=AF.Copy),
                     reads=["P4"], writes=[f"mixML{h}_{i}"])
            t.barrier()
            if h == 0:
                dbg_dump("ml_f", [(mixML[:, 0, :], TOK)])

        if dbg == "ml":
            cv.off = mark
            dtmp = cv.take([128, TOK])
            for c8 in range(4):
                t.op("dve", lambda c8=c8: V.tensor_copy(out=dtmp, in_=mixML[:, c8, :]), reads=[], writes=["dtmp"])
                t.dma("sp", "dbg", dbg_out[:, (4 + c8) * TOK:(5 + c8) * TOK], dtmp, reads=["dtmp"], writes=["dbgout"])
            nc.sync.wait_ge(t.dma_lanes["dbg"].sem, t.dma_lanes["dbg"].count)
            print("instructions", t.n_inst, "waits", t.n_wait)
            return nc

        cv.off = 4 * TOK // 2
        fnw_bc = cv.take([128, D])
        woutb = cv.take([128, 8, D], BF16)
        xres = [cv.take([128, D]) for _ in range(2)]
        hres = [cv.take([128, D]) for _ in range(2)]
        ojunk = cv.take([128, D])
        ost = cv.take([128, 2 * NT])
        t.dma("pool", "c1", fnw_bc, fnw_row.partition_broadcast(128), writes=["fnw_bc"])
        for q4 in range(4):
            s = wstate["n"] % 2
            wstate["n"] += 1
            t.dma("sp", f"wst{s}", wst[s][:, :, 0:256], wout_v[:, :, q4 * 256:(q4 + 1) * 256], writes=[f"wst{s}"])
            t.op("pool", lambda s=s, q4=q4: G_.tensor_copy(out=woutb[:, :, q4 * 256:(q4 + 1) * 256], in_=wst[s][:, :, 0:256]), reads=[f"wst{s}"], writes=["woutb"])
        for tl in range(NT):
            s2 = tl % 2
            t.dma("sp", f"xres{s2}", xres[s2], xh[(tl + 2) * 128:(tl + 3) * 128, :], writes=[f"xres{s2}"])
            for nb in range(2):
                pb, pk = next_bank()
                for mc in range(8):
                    src = mixT[:, mc, tl * 128:(tl + 1) * 128] if mc < 4 else mixML[:, mc - 4, tl * 128:(tl + 1) * 128]
                    t.op("pe", lambda mc=mc, src=src, pb=pb, nb=nb: P_.matmul(pb[:, :], lhsT=src, rhs=woutb[:, mc, nb * 512:(nb + 1) * 512], start=(mc == 0), stop=(mc == 7)),
                         reads=["woutb"], writes=[pk])
                t.op("dve", lambda pb=pb, nb=nb, s2=s2: V.tensor_tensor(out=hres[s2][:, nb * 512:(nb + 1) * 512], in0=pb[:, :], in1=gate_bc[:, nb * 512:(nb + 1) * 512], op=ALU.mult),
                     reads=[pk, "gate_bc"], writes=[f"hres{s2}_{nb}"])
                t.op("pool", lambda nb=nb, s2=s2: G_.tensor_tensor(out=hres[s2][:, nb * 512:(nb + 1) * 512], in0=hres[s2][:, nb * 512:(nb + 1) * 512],
                                                                 in1=xres[s2][:, nb * 512:(nb + 1) * 512], op=ALU.add),
                     reads=[f"hres{s2}_{nb}", f"xres{s2}"], writes=[f"hres{s2}_{nb}"])
            hk = [f"hres{s2}_0", f"hres{s2}_1"]
            t.op("act", lambda s2=s2, tl=tl: A_.activation(out=ojunk, in_=hres[s2], func=AF.Square, accum_out=ost[:, tl:tl + 1]), reads=hk, writes=["ojunk", f"ost{tl}"])
            t.op("dve", lambda tl=tl: V.tensor_scalar(out=ost[:, tl:tl + 1], in0=ost[:, tl:tl + 1], scalar1=1.0 / D, scalar2=EPS, op0=ALU.mult, op1=ALU.add),
                 reads=[f"ost{tl}"], writes=[f"ost{tl}"])
            t.op("act", lambda tl=tl: A_.activation(out=ost[:, tl:tl + 1], in_=ost[:, tl:tl + 1], func=AF.Sqrt), reads=[f"ost{tl}"], writes=[f"ost{tl}"])
            t.op("dve", lambda tl=tl: V.reciprocal(out=ost[:, NT + tl:NT + tl + 1], in_=ost[:, tl:tl + 1]), reads=[f"ost{tl}"], writes=[f"ost{tl}"])
            t.op("dve", lambda tl=tl, s2=s2: V.scalar_tensor_tensor(out=hres[s2], in0=hres[s2], scalar=ost[:, NT + tl:NT + tl + 1], in1=fnw_bc, op0=ALU.mult, op1=ALU.mult),
                 reads=hk + [f"ost{tl}", "fnw_bc"], writes=hk)
            t.dma("sp", "yout", y[tl * 128:(tl + 1) * 128, :], hres[s2], reads=hk, writes=["yout"])
        yl = t.dma_lanes["yout"]
        nc.sync.wait_ge(yl.sem, yl.count)
        print("instructions", t.n_inst, "waits", t.n_wait)

    return nc


def make_in_maps(x, c, w_ada, b_ada, norm_w, w_in, b_in, conv_w, conv_b, rpb, ml_norm_w, w_out, final_norm_w):
    f = lambda a: np.ascontiguousarray(np.asarray(a, dtype=np.float32))
    x = f(x)[0]
    perm = _col_perm()
    w_in_p = f(f(w_in)[0][:, perm])
    b_in_p = f(f(b_in)[0][perm])
    groups = [b_in_p[g * 128:(g + 1) * 128] for g in range(8)]
    for h in range(ML_HEADS):
        o = NA_COLS + h * MLW
        groups.append(b_in_p[o:o + 128])
        groups.append(b_in_p[o + 128:o + 256])
    bin_col = f(np.stack(groups, 1))
    col8 = lambda v: f(np.asarray(v, np.float32).reshape(-1, 128).T)
    rpbA, cmask = _rpb_tables(f(rpb)[0])
    cw = f(conv_w)[0]
    convw = f(cw.T.reshape(8, 128, 5).transpose(1, 0, 2))
    shared = {
        "w_ada": f(w_ada)[0], "w_in": w_in_p, "w_out": f(w_out)[0], "consts": _consts(),
        "c_col": col8(f(c)[0]), "bada_col": col8(f(b_ada)[0]), "bada_gate": f(f(b_ada)[0][2 * D:]),
        "normw_col": col8(f(norm_w)[0]), "fnw_row": f(final_norm_w), "mlnw_row": f(ml_norm_w)[0],
        "bin_col": bin_col, "bin_row": b_in_p, "convw": convw, "convb": col8(f(conv_b)[0]),
        "rpbA": f(rpbA.reshape(8, 128, 14 * 64)), "cmask": f(cmask.reshape(128, 14 * 64)),
        "tri2": _tri2(), "w_gates": f(f(w_in)[0][:, 4608:4624]), "b_gates": f(f(b_in)[0][4608:4624]),
    }
    in_maps = []
    for i in range(NCORES):
        xhh = np.zeros((HT * 128, D), np.float32)
        lo, hi = i * TOK - 256, i * TOK + TOK + 256
        slo, shi = max(lo, 0), min(hi, T)
        xhh[slo - lo:shi - lo] = x[slo:shi]
        fl = np.zeros((128, 18), np.float32)
        fl[:, 0] = 1.0 if i > 0 else 0.0
        fl[:, 1] = 1.0 if i < NCORES - 1 else 0.0
        for j in range(NCORES):
            fl[:, 2 + j] = 1.0 if j < i else 0.0
            fl[:, 10 + j] = 1.0 if j > i else 0.0
        gb = list(range(4 * i - 1, -1, -1))
        ga = list(range(4 * i + 4, T // 512))
        order = gb + ga
        assert len(order) == NG
        xf = np.concatenate([x[g * 512:(g + 1) * 512] for g in order], 0)
        xfh = np.zeros((NG * 4, D), np.float32)
        ffl = np.zeros((128, NG * 4), np.float32)
        for p, g in enumerate(order):
            if g > 0:
                xfh[p * 4:p * 4 + 2] = x[g * 512 - 2:g * 512]
                ffl[:, p * 4 + 2] = 1.0
            if g < T // 512 - 1:
                xfh[p * 4 + 2:p * 4 + 4] = x[(g + 1) * 512:(g + 1) * 512 + 2]
                ffl[:, p * 4 + 3] = 1.0
            ffl[:, p * 4 + 0] = 1.0 if g < 4 * i else 0.0
            ffl[:, p * 4 + 1] = 1.0 if g > 4 * i else 0.0
        m = dict(shared)
        m.update({"xh": xhh, "mcol": _mcol(i), "flags": fl, "xf": xf, "xfh": xfh, "fflags": ffl})
        in_maps.append(m)
    return in_maps


_NC_CACHE = {}


def kernel(**inputs):
    in_maps = make_in_maps(**inputs)
    if "nc" not in _NC_CACHE:
        _NC_CACHE["nc"] = build_program()
    res = run_bass_kernel_spmd(_NC_CACHE["nc"], in_maps, core_ids=list(range(NCORES)))
    out = np.concatenate([r["y"] for r in res.results], axis=0)
    return out.reshape(1, T, D).astype(np.float32)
```

```python
import numpy as np
from contextlib import ExitStack
import concourse.bass as bass
import concourse.mybir as mybir
from concourse.bass_utils import run_bass_kernel_spmd

F32 = mybir.dt.float32
BF16 = mybir.dt.bfloat16
AF = mybir.ActivationFunctionType
ALU = mybir.AluOpType

NCORES = 8
D = 1024
T = 16384
TOK = T // NCORES
NT = TOK // 128
HT = NT + 4
NA_HEADS = 8
ML_HEADS = 4
EPS = 1e-6
MLW = 644
NA_COLS = 2048
IN_WP = NA_COLS + ML_HEADS * MLW
UNI_WORDS = 23424
NG = 28


class _Stop(Exception):
    pass


class Lane:
    def __init__(self, nc, name):
        self.sem = nc.alloc_semaphore(name=name)
        self.count = 0
        self.name = name


class Trk:
    def __init__(self, nc):
        self.nc = nc
        self.engs = {"pe": nc.tensor, "act": nc.scalar, "dve": nc.vector, "pool": nc.gpsimd, "sp": nc.sync}
        self.lanes = {k: Lane(nc, "sem_" + k) for k in ("pe", "act", "dve", "pool")}
        self.seen = {k: {} for k in self.engs}
        self.last_w = {}
        self.reads = {}
        self.dma_lanes = {}
        self.n_inst = 0
        self.n_wait = 0

    def dma_lane(self, name):
        if name not in self.dma_lanes:
            self.dma_lanes[name] = Lane(self.nc, "dsem_" + name)
        return self.dma_lanes[name]

    def _wait(self, eng, lane, val):
        s = self.seen[eng]
        if s.get(lane.name, 0) >= val:
            return
        if eng == "pe" and lane is self.lanes["pe"]:
            return
        self.engs[eng].wait_ge(lane.sem, val)
        s[lane.name] = val
        self.n_wait += 1

    def _deps(self, eng, reads, writes):
        for k in reads:
            if k in self.last_w:
                self._wait(eng, *self.last_w[k])
        for k in writes:
            if k in self.last_w:
                self._wait(eng, *self.last_w[k])
            for (l, v) in self.reads.get(k, ()):
                self._wait(eng, l, v)

    def _record(self, lane, reads, writes):
        v = lane.count
        for k in reads:
            if k in writes:
                continue
            lst = self.reads.setdefault(k, [])
            lst[:] = [(l, x) for (l, x) in lst if l is not lane]
            lst.append((lane, v))
        for k in writes:
            self.last_w[k] = (lane, v)
            self.reads[k] = []

    def op(self, eng, fn, reads=(), writes=()):
        lane = self.lanes[eng]
        self._deps(eng, reads, writes)
        ins = fn()
        lane.count += 1
        ins.then_inc(lane.sem, 1)
        self._record(lane, reads, writes)
        self.n_inst += 1
        return ins

    def dma(self, q, lane_name, out, in_, reads=(), writes=(), **kw):
        if lane_name in ("c0", "c1"):
            self._oneshot = getattr(self, "_oneshot", 0) + 1
            lane_name = f"os{self._oneshot % 24}"
            lane = self.dma_lane(lane_name)
            if lane.count > 0:
                self._wait(q, lane, lane.count)
        lane = self.dma_lane(lane_name)
        self._deps(q, reads, writes)
        ins = self.engs[q].dma_start(out=out, in_=in_, **kw)
        lane.count += 16
        ins.then_inc(lane.sem, 16)
        self._record(lane, reads, writes)
        self.n_inst += 1
        return ins

    def wait_keys(self, eng, keys):
        for k in keys:
            if k in self.last_w:
                self._wait(eng, *self.last_w[k])
            for (l, v) in self.reads.get(k, ()):
                self._wait(eng, l, v)

    def barrier(self):
        all_lanes = list(self.lanes.values()) + list(self.dma_lanes.values())
        for eng in self.engs:
            for l in all_lanes:
                if l.count > 0:
                    self._wait(eng, l, l.count)


def _col_perm():
    NAW = 512
    cols = list(range(0, 4 * NAW))
    base_ml = 4 * NAW
    gate0 = 4 * NAW + 5 * 512
    for h in range(ML_HEADS):
        for blk in range(5):
            s = base_ml + blk * 512 + h * 128
            cols += list(range(s, s + 128))
        cols += [gate0 + 0 + h, gate0 + 4 + h, gate0 + 8 + h, gate0 + 12 + h]
    return np.array(cols, dtype=np.int64)


def _consts():
    s = np.arange(128)[:, None]
    t = np.arange(128)[None, :]
    same = (s // 64) == (t // 64)
    triF = (same & (s <= t)).astype(np.float32)
    triB = (same & (s >= t)).astype(np.float32)
    blk = same.astype(np.float32)
    ones = np.ones((128, 128), np.float32)
    ident = np.eye(128, dtype=np.float32)
    return np.stack([ident, triF, triB, blk, ones], 0)


def _tri2():
    s = np.arange(128)[:, None]
    t = np.arange(128)[None, :]
    return np.stack([(s > t).astype(np.float32), (s < t).astype(np.float32)], 0)


def _rpb_tables(rpb):
    p = np.arange(128)
    a = p // 64
    k = p % 64
    c = np.arange(64)
    e = np.arange(14)
    dyi = 13 - e
    dy = dyi[None, :] + a[:, None]
    dyv = (dy >= 0) & (dy <= 14)
    dx = np.clip(k[:, None] - c[None, :], -15, 15) + 15
    cs = np.clip(c - 8, 0, 48)
    cv = (k[:, None] >= cs[None, :]) & (k[:, None] < cs[None, :] + 16)
    valid = dyv[:, :, None] & cv[:, None, :]
    dyc = np.clip(dy, 0, 14)
    g = rpb[:, dyc[:, :, None], dx[:, None, :]]
    g = np.where(valid[None], g, np.float32(0.0)).astype(np.float32)
    return np.ascontiguousarray(g), valid.astype(np.float32)


def _na_base(m):
    return 14 if m == 15 else m


def _na_nj(m):
    return 6 if m in (0, 15) else 5


def _mcol(core):
    out = np.zeros((128, 16, 6, 2), np.float32)
    for m in range(16):
        for j in range(_na_nj(m)):
            for b in range(2):
                r = 32 * core + 2 * m + b
                start = min(max(r - 4, 0), 248)
                for a in range(2):
                    kr = 32 * core - 4 + 2 * (_na_base(m) + j) + a
                    ok = (start <= kr < start + 8)
                    out[a * 64:(a + 1) * 64, m, j, b] = 1.0 if ok else 0.0
    return out.reshape(128, 192)


def build_program(dbg=None):
    nc = bass.Bass("TRN2", target_bir_lowering=False)
    try:
        _build_body(nc, dbg)
    except _Stop:
        pass
    return nc


def _build_body(nc, dbg):

    def din(name, shape):
        return nc.dram_tensor(name, list(shape), F32, kind="ExternalInput").ap()

    xh = din("xh", [HT * 128, D])
    w_ada = din("w_ada", [D, 3 * D])
    w_in = din("w_in", [D, IN_WP])
    w_out = din("w_out", [D, D])
    consts = din("consts", [5, 128, 128])
    c_col = din("c_col", [128, 8])
    bada_col = din("bada_col", [128, 24])
    bada_gate = din("bada_gate", [D])
    normw_col = din("normw_col", [128, 8])
    fnw_row = din("fnw_row", [D])
    mlnw_row = din("mlnw_row", [512])
    bin_col = din("bin_col", [128, 16])
    bin_row = din("bin_row", [IN_WP])
    convw = din("convw", [128, 8, 5])
    convb = din("convb", [128, 8])
    rpbA = din("rpbA", [8, 128, 14 * 64])
    cmask = din("cmask", [128, 14 * 64])
    mcol_d = din("mcol", [128, 192])
    flags = din("flags", [128, 18])
    xf = din("xf", [NG * 512, D])
    xfh = din("xfh", [NG * 4, D])
    fflags = din("fflags", [128, NG * 4])
    tri2 = din("tri2", [2, 128, 128])
    w_gates = din("w_gates", [D, 16])
    b_gates = din("b_gates", [16])
    y = nc.dram_tensor("y", [TOK, D], F32, kind="ExternalOutput").ap()
    dbg_out = None
    if dbg:
        dbg_out = nc.dram_tensor("dbg", [128, 8 * TOK], F32, kind="ExternalOutput").ap()

    es = ExitStack()
    with es:
        def sb(name, shape, dt=F32):
            return es.enter_context(nc.sbuf_tensor(name, list(shape), dt))

        def ps(name, shape, dt=F32):
            return es.enter_context(nc.psum_tensor(name, list(shape), dt))

        t = Trk(nc)
        V, A_, P_, G_ = nc.vector, nc.scalar, nc.tensor, nc.gpsimd

        xT = sb("xT", [128, 8, TOK], BF16)
        xTh = sb("xTh", [128, 8, 512], BF16)
        mixT = sb("mixT", [128, 4, TOK], BF16)
        gate_bc = sb("gate_bc", [128, D])
        cst = sb("cst", [128, 5, 128])
        identb = sb("identb", [128, 128], BF16)
        gT = sb("gT", [128, 8])
        shiftT = sb("shiftT", [128, 8])
        bcol = sb("bcol", [128, 16])
        bcolq = sb("bcolq", [128, 4])
        flg = sb("flg", [128, 18])
        cw = sb("cw", [128, 8, 5])
        cb = sb("cb", [128, 8])
        kcol = sb("kcol", [128, 4])
        wst = [sb(f"wst{i}", [128, 8, 256]) for i in range(2)]
        wbf = [sb(f"wbf{i}", [128, 8, 512], BF16) for i in range(2)]
        Cacc = sb("Cacc", [128, 2, 4, 129])
        UNI = sb("UNI", [128, UNI_WORDS])

        PA = ps("PA", [128, 1024])
        PB = ps("PB", [128, 1024])
        P4 = ps("P4", [128, 512])
        P5 = ps("P5", [128, 512])
        P6 = ps("P6", [128, 512])
        P7 = ps("P7", [128, 512])
        IDENT, TRIF, TRIB, BLK, ONES = range(5)

        class Carver:
            def __init__(self):
                self.off = 0

            def take(self, shape, dt=F32):
                n = int(np.prod(shape[1:]))
                words = n if dt == F32 else (n + 1) // 2
                ap = UNI[:, self.off:self.off + words]
                self.off += words
                assert self.off <= UNI_WORDS, self.off
                if dt != F32:
                    ap = ap.bitcast(BF16)[:, 0:n]
                if len(shape) > 2:
                    names = " ".join(f"d{i}" for i in range(1, len(shape)))
                    kw = {f"d{i}": shape[i] for i in range(1, len(shape))}
                    ap = ap.rearrange(f"p ({names}) -> p {names}", **kw)
                return ap

        def dbg_dump(stage, items):
            if dbg != stage:
                return
            t.barrier()
            off = 0
            dt_ = UNI[:, UNI_WORDS - 2048:UNI_WORDS]
            for ap, n in items:
                o2 = 0
                while o2 < n:
                    w = min(2048, n - o2)
                    t.op("dve", lambda ap=ap, o2=o2, w=w: V.tensor_copy(out=dt_[:, 0:w], in_=ap[:, o2:o2 + w]), reads=[], writes=["dbgtmp"])
                    t.dma("sp", "dbg", dbg_out[:, off:off + w], dt_[:, 0:w], reads=["dbgtmp"], writes=["dbgout"])
                    off += w
                    o2 += w
            nc.sync.wait_ge(t.dma_lanes["dbg"].sem, t.dma_lanes["dbg"].count)
            print("dbg stop at", stage, "instructions", t.n_inst, "waits", t.n_wait)
            raise _Stop()

        t.dma("sp", "c0", cst[:], consts.rearrange("n p f -> p n f"), writes=["cst"])
        t.dma("sp", "c0", flg[:], flags, writes=["flg"])
        t.dma("sp", "c0", bcol[:], bin_col, writes=["bcol"])
        t.dma("sp", "c0", cw[:], convw, writes=["cw"])
        t.dma("sp", "c0", cb[:], convb, writes=["cb"])
        t.op("dve", lambda: V.tensor_copy(out=identb[:], in_=cst[:, IDENT, :]), reads=["cst"], writes=["identb"])
        t.op("pool", lambda: G_.memset(kcol[:, 0:1], 1.0), writes=["kcol"])
        t.op("pool", lambda: G_.memset(kcol[:, 1:2], float(np.log(128.0 ** -0.5))), writes=["kcol"])
        t.op("pool", lambda: G_.memset(kcol[:, 2:3], EPS), writes=["kcol"])
        t.op("pool", lambda: G_.memset(kcol[:, 3:4], 0.0), writes=["kcol"])
        t.op("dve", lambda: V.tensor_scalar(out=bcolq[:], in0=bcol[:, 0:4], scalar1=0.125, scalar2=None, op0=ALU.mult),
             reads=["bcol"], writes=["bcolq"])

        cv = Carver()
        ccol = cv.take([128, 8])
        cact = cv.take([128, 8])
        cbc = cv.take([128, 8, 128])
        badac = cv.take([128, 24])
        nwc = cv.take([128, 8])
        bgate = cv.take([128, D])
        modT = cv.take([128, 16])
        wada_sb = [cv.take([128, 8, 512]) for _ in range(2)]
        t.dma("sp", "c0", ccol, c_col, writes=["ccol"])
        t.dma("sp", "c0", badac, bada_col, writes=["badac"])
        t.dma("sp", "c0", nwc, normw_col, writes=["nwc"])
        t.dma("pool", "c1", bgate, bada_gate.partition_broadcast(128), writes=["bgate"])
        t.op("act", lambda: A_.activation(out=cact, in_=ccol, func=AF.Silu), reads=["ccol"], writes=["cact"])
        for k in range(8):
            t.op("dve", lambda k=k: V.tensor_copy(out=cbc[:, k, :], in_=cact[:, k:k + 1].broadcast_to([128, 128])),
                 reads=["cact"], writes=[f"cbc{k}"])
        wada_v = w_ada.rearrange("(k p) n -> p k n", p=128)
        for ch in range(6):
            slot = ch % 2
            t.dma("sp", f"wada{slot}", wada_sb[slot], wada_v[:, :, ch * 512:(ch + 1) * 512], writes=[f"wada{slot}"])
            if ch < 4:
                for jn in range(4):
                    col = ch * 4 + jn
                    for k in range(8):
                        t.op("pe", lambda k=k, jn=jn, col=col, slot=slot: P_.matmul(
                            P4[:, col:col + 1], lhsT=wada_sb[slot][:, k, jn * 128:(jn + 1) * 128],
                            rhs=cact[:, k:k + 1], start=(k == 0), stop=(k == 7)),
                            reads=[f"wada{slot}", "cact"], writes=["P4"])
            else:
                nb = ch - 4
                pt = P5 if nb == 0 else P6
                for k in range(8):
                    t.op("pe", lambda k=k, pt=pt, slot=slot: P_.matmul(
                        pt[:, :], lhsT=cbc[:, k, :], rhs=wada_sb[slot][:, k, :], start=(k == 0), stop=(k == 7)),
                        reads=[f"wada{slot}", f"cbc{k}"], writes=["P5" if nb == 0 else "P6"])
                t.op("dve", lambda nb=nb, pt=pt: V.tensor_tensor(out=gate_bc[:, nb * 512:(nb + 1) * 512], in0=pt[:, :],
                                                                in1=bgate[:, nb * 512:(nb + 1) * 512], op=ALU.add),
                     reads=["P5" if nb == 0 else "P6", "bgate"], writes=["gate_bc"])
        t.op("dve", lambda: V.tensor_tensor(out=modT, in0=P4[:, 0:16], in1=badac[:, 0:16], op=ALU.add),
             reads=["P4", "badac"], writes=["modT"])
        t.op("dve", lambda: V.tensor_copy(out=shiftT[:], in_=modT[:, 0:8]), reads=["modT"], writes=["shiftT"])
        t.op("dve", lambda: V.scalar_tensor_tensor(out=gT[:], in0=modT[:, 8:16], scalar=1.0, in1=nwc, op0=ALU.add, op1=ALU.mult),
             reads=["modT", "nwc"], writes=["gT"])
        t.barrier()

        cv = Carver()
        xin = [cv.take([128, D]) for _ in range(3)]
        xnb = [cv.take([128, D], BF16) for _ in range(2)]
        xtmp = [cv.take([128, D]) for _ in range(2)]
        sq_junk = cv.take([128, D])
        PT_x = [P4, P5]

        def xT_tile(ht):
            if 2 <= ht < 18:
                return xT[:, :, (ht - 2) * 128:(ht - 1) * 128]
            hi = ht if ht < 2 else ht - 18 + 2
            return xTh[:, :, hi * 128:(hi + 1) * 128]

        nstate = {"n": 0}
        NRS = 64
        ssq = cv.take([128, NRS])
        rstd = cv.take([128, NRS])

        def norm_tile(src, nr, dst, dkey):
            n = nstate["n"]
            nstate["n"] += 1
            s3, s2, c = n % 3, n % 2, n % NRS
            t.dma("sp", f"xin{s3}", xin[s3][0:nr, :], src, writes=[f"xin{s3}"])
            t.op("act", lambda: A_.activation(out=sq_junk[0:nr, :], in_=xin[s3][0:nr, :], func=AF.Square, accum_out=ssq[0:nr, c:c + 1]),
                 reads=[f"xin{s3}"], writes=["sq_junk", f"ssq{c}"])
            t.op("dve", lambda: V.tensor_scalar(out=rstd[0:nr, c:c + 1], in0=ssq[0:nr, c:c + 1], scalar1=1.0 / D, scalar2=EPS,
                                                op0=ALU.mult, op1=ALU.add), reads=[f"ssq{c}"], writes=[f"rstd{c}"])
            t.op("act", lambda: A_.activation(out=rstd[0:nr, c:c + 1], in_=rstd[0:nr, c:c + 1], func=AF.Sqrt),
                 reads=[f"rstd{c}"], writes=[f"rstd{c}"])
            t.op("dve", lambda: V.reciprocal(out=rstd[0:nr, c:c + 1], in_=rstd[0:nr, c:c + 1]), reads=[f"rstd{c}"], writes=[f"rstd{c}"])
            t.op("act", lambda: A_.activation(out=xnb[s2][0:nr, :], in_=xin[s3][0:nr, :], func=AF.Copy, scale=rstd[0:nr, c:c + 1]),
                 reads=[f"xin{s3}", f"rstd{c}"], writes=[f"xnb{s2}"])
            ptb = PT_x[s2].bitcast(BF16).rearrange("p (k f) -> p k f", k=8)
            pkey = "P4" if s2 == 0 else "P5"
            for k in range(8):
                t.op("pe", lambda k=k: P_.transpose(out=ptb[:, k, 0:nr], in_=xnb[s2][0:nr, k * 128:(k + 1) * 128], identity=identb[0:nr, 0:nr]),
                     reads=[f"xnb{s2}", "identb"], writes=[pkey])
            xt3 = xtmp[s2].rearrange("p (k f) -> p k f", k=8)
            t.op("dve", lambda: V.tensor_tensor(out=xt3[:, :, 0:nr], in0=ptb[:, :, 0:nr], in1=gT[:, :].unsqueeze(2).broadcast_to([128, 8, nr]), op=ALU.mult),
                 reads=[pkey, "gT"], writes=[f"xtmp{s2}"])
            t.op("pool", lambda: G_.tensor_tensor(out=dst, in0=xt3[:, :, 0:nr], in1=shiftT[:, :].unsqueeze(2).broadcast_to([128, 8, nr]), op=ALU.add),
                 reads=[f"xtmp{s2}", "shiftT"], writes=[dkey])

        for ht in range(HT):
            norm_tile(xh[ht * 128:(ht + 1) * 128, :], 128, xT_tile(ht), f"xT{ht}")

        pbanks = [(PA[:, 0:512], "PA0"), (PA[:, 512:1024], "PA1"), (PB[:, 0:512], "PB0"), (PB[:, 512:1024], "PB1")]
        pst = {"n": 0}

        def next_bank():
            b = pbanks[pst["n"] % 4]
            pst["n"] += 1
            return b

        win_v = w_in.rearrange("(k p) n -> p k n", p=128)
        wout_v = w_out.rearrange("(k p) n -> p k n", p=128)
        wstate = {"n": 0}

        def load_w(src_v, c0, ncols, slot):
            off = 0
            while off < ncols:
                n = min(256, ncols - off)
                s = wstate["n"] % 2
                wstate["n"] += 1
                t.dma("sp", f"wst{s}", wst[s][:, :, 0:n], src_v[:, :, c0 + off:c0 + off + n], writes=[f"wst{s}"])
                t.op("pool", lambda s=s, n=n, off=off: G_.tensor_copy(out=wbf[slot][:, :, off:off + n], in_=wst[s][:, :, 0:n]),
                     reads=[f"wst{s}"], writes=[f"wbf{slot}"])
                off += n

        def tok_blocks(with_halo):
            blks = []
            if with_halo:
                blks.append((xTh[:, :, 0:256], 256, 0))
            for b in range(4):
                blks.append((xT[:, :, b * 512:(b + 1) * 512], 512, 256 + b * 512))
            if with_halo:
                blks.append((xTh[:, :, 256:512], 256, 2304))
            return blks

        def xkeys(c0, n):
            return [f"xT{c0 // 128 + i}" for i in range(n // 128)]

        wFk = cv.take([128, 8, 512], BF16)
        wFv = cv.take([128, 8, 512], BF16)
        wFg = cv.take([128, 8, 16], BF16)
        xg = [cv.take([128, 8, 516], BF16) for _ in range(2)]
        preF2 = [cv.take([128, 516]) for _ in range(2)]
        accF2 = [cv.take([128, 512]) for _ in range(2)]
        kTF2 = [cv.take([128, 512], BF16) for _ in range(2)]
        xgh = [cv.take([128, 8, 4], BF16) for _ in range(2)]
        KtokF = cv.take([128, 4, 4, 128], BF16)
        Vf = cv.take([128, 4, 4, 129], BF16)
        Gf = cv.take([128, 4, 16])
        nlfF = cv.take([128, 4, 2, 4])
        tmpF = cv.take([128, 4, 2, 4])
        Wt = cv.take([128, 4, 2, 4])
        accg = cv.take([128, 2, 4])
        tmpa = cv.take([128, 2, 4])
        wVf = cv.take([128, 4, 2, 4, 129], BF16)
        bFv = cv.take([128, 512])
        bFg = cv.take([128, 16])
        ffl = cv.take([128, NG * 4])
        sgt = cv.take([128, 2, 128])
        print("phase F union words", cv.off)
        t.dma("sp", "c0", ffl, fflags, writes=["ffl"])
        t.dma("sp", "c0", sgt, tri2.rearrange("n p f -> p n f"), writes=["sgt"])
        t.dma("pool", "c1", bFg, b_gates.partition_broadcast(128), writes=["bFg"])
        for hh in range(ML_HEADS):
            c0 = NA_COLS + hh * MLW
            t.dma("pool", "c1", bFv[:, hh * 128:(hh + 1) * 128], bin_row[c0 + 256:c0 + 384].partition_broadcast(128), writes=["bFv"])
            for (dstw, cc) in ((wFk, c0 + 128), (wFv, c0 + 256)):
                sidx = wstate["n"] % 2
                wstate["n"] += 1
                t.dma("sp", f"wst{sidx}", wst[sidx][:, :, 0:128], win_v[:, :, cc:cc + 128], writes=[f"wst{sidx}"])
                t.op("pool", lambda sidx=sidx, dstw=dstw, hh=hh: G_.tensor_copy(out=dstw[:, :, hh * 128:(hh + 1) * 128], in_=wst[sidx][:, :, 0:128]),
                     reads=[f"wst{sidx}"], writes=["wF"])
        sidx = wstate["n"] % 2
        wstate["n"] += 1
        t.dma("sp", f"wst{sidx}", wst[sidx][:, :, 0:16], w_gates.rearrange("(k p) n -> p k n", p=128), writes=[f"wst{sidx}"])
        t.op("pool", lambda: G_.tensor_copy(out=wFg, in_=wst[sidx][:, :, 0:16]), reads=[f"wst{sidx}"], writes=["wF"])
        t.op("pool", lambda: G_.memset(Vf[:, :, :, 128:129], 1.0), writes=["Vfones"])
        t.op("pool", lambda: G_.memset(accg, 0.0), writes=["accg"])
        t.op("pool", lambda: G_.memset(Cacc[:], 0.0), writes=["Cacc"])
        Gv = Gf.rearrange("p t (d x h) -> p t d x h", d=2, x=2)
        SGT, SLT = 0, 1
        def norm_group(gi):
            xs = gi % 2
            norm_tile(xfh[gi * 4:gi * 4 + 4, :], 4, xgh[xs], f"xgh{xs}")
            for j in range(4):
                norm_tile(xf[(gi * 4 + j) * 128:(gi * 4 + j + 1) * 128, :], 128, xg[xs][:, :, 2 + j * 128:2 + (j + 1) * 128], f"xg{xs}")

        norm_group(0)
        for gi in range(NG):
            xs = gi % 2
            xga = xg[xs]
            for j in range(4):
                pb, pk = next_bank()
                for k in range(8):
                    t.op("pe", lambda k=k, j=j, pb=pb: P_.matmul(pb[:, :], lhsT=xga[:, k, 2 + j * 128:2 + (j + 1) * 128], rhs=wFv[:, k, :], start=(k == 0), stop=(k == 7)),
                         reads=["wF", f"xg{xs}"], writes=[pk])
                t.op("dve", lambda j=j, pb=pb: V.tensor_tensor(out=Vf[:, j, :, 0:128], in0=pb.rearrange("p (h d) -> p h d", h=4),
                                                              in1=bFv.rearrange("p (h d) -> p h d", h=4), op=ALU.add), reads=[pk, "bFv"], writes=["Vf"])
                for k in range(8):
                    t.op("pe", lambda k=k, j=j: P_.matmul(P6[:, j * 16:(j + 1) * 16], lhsT=xga[:, k, 2 + j * 128:2 + (j + 1) * 128], rhs=wFg[:, k, :], start=(k == 0), stop=(k == 7)),
                         reads=["wF", f"xg{xs}"], writes=["P6"])
            t.op("dve", lambda: V.tensor_tensor(out=Gf, in0=P6[:, 0:64].rearrange("p (t g) -> p t g", t=4), in1=bFg.unsqueeze(1).broadcast_to([128, 4, 16]), op=ALU.add),
                 reads=["P6", "bFg"], writes=["Gf"])
            t.op("act", lambda: A_.activation(out=nlfF, in_=Gv[:, :, :, 1, :], func=AF.Exp, scale=-1.0), reads=["Gf"], writes=["nlfF"])
            t.op("act", lambda: A_.activation(out=nlfF, in_=nlfF, func=AF.Ln, bias=kcol[:, 0:1], scale=1.0), reads=["nlfF", "kcol"], writes=["nlfF"])
            P7c = P7[:, 0:32].rearrange("p (t d h) -> p t d h", t=4, d=2)
            for j in range(4):
                for d in range(2):
                    others = [jj for jj in range(4) if (jj > j if d == 0 else jj < j)]
                    seq = [(j, sgt[:, SGT if d == 0 else SLT, :])] + [(jj, cst[:, ONES, :]) for jj in others]
                    for qi, (jj, lh) in enumerate(seq):
                        t.op("pe", lambda j=j, d=d, jj=jj, lh=lh, qi=qi, nq=len(seq): P_.matmul(P7c[:, j, d, :], lhsT=lh, rhs=nlfF[:, jj, d, :], start=(qi == 0), stop=(qi == nq - 1)),
                             reads=["nlfF", "sgt", "cst"], writes=["P7"])
            P7t = P7[:, 32:40].rearrange("p (d h) -> p d h", d=2)
            for d in range(2):
                for jj in range(4):
                    t.op("pe", lambda d=d, jj=jj: P_.matmul(P7t[:, d, :], lhsT=cst[:, ONES, :], rhs=nlfF[:, jj, d, :], start=(jj == 0), stop=(jj == 3)),
                         reads=["nlfF", "cst"], writes=["P7"])
            fl = ffl[:, gi * 4:gi * 4 + 2]
            t.op("dve", lambda: V.tensor_tensor(out=tmpF, in0=P7c, in1=accg.unsqueeze(1).broadcast_to([128, 4, 2, 4]), op=ALU.add), reads=["P7", "accg"], writes=["tmpF"])
            t.op("dve", lambda: V.tensor_tensor(out=tmpF, in0=Gv[:, :, :, 0, :], in1=tmpF, op=ALU.subtract), reads=["Gf", "tmpF"], writes=["tmpF"])
            t.op("act", lambda: A_.activation(out=Wt, in_=tmpF, func=AF.Exp), reads=["tmpF"], writes=["Wt"])
            t.op("dve", lambda fl=fl: V.tensor_tensor(out=Wt, in0=Wt, in1=fl.unsqueeze(1).unsqueeze(3).broadcast_to([128, 4, 2, 4]), op=ALU.mult), reads=["Wt", "ffl"], writes=["Wt"])
            t.op("dve", lambda fl=fl: V.tensor_tensor(out=tmpa, in0=P7t, in1=fl.unsqueeze(2).broadcast_to([128, 2, 4]), op=ALU.mult), reads=["P7", "ffl"], writes=["tmpa"])
            t.op("dve", lambda: V.tensor_tensor(out=accg, in0=accg, in1=tmpa, op=ALU.add), reads=["accg", "tmpa"], writes=["accg"])
            for d in range(2):
                eng = "dve" if d == 0 else "pool"
                E = V if d == 0 else G_
                t.op(eng, lambda E=E, d=d: E.tensor_tensor(out=wVf[:, :, d, :, :], in0=Vf, in1=Wt[:, :, d, :].unsqueeze(3).broadcast_to([128, 4, 4, 129]), op=ALU.mult),
                     reads=["Vf", "Vfones", "Wt"], writes=[f"wVf{d}"])
            if gi + 1 < NG:
                norm_group(gi + 1)
            def kproj(hh):
                preF = preF2[hh % 2]
                pk_ = f"preF{hh % 2}"
                pb, pk = next_bank()
                for k in range(8):
                    t.op("pe", lambda k=k: P_.matmul(pb[:, 0:512], lhsT=wFk[:, k, hh * 128:(hh + 1) * 128], rhs=xga[:, k, 2:514], start=(k == 0), stop=(k == 7)),
                         reads=["wF", f"xg{xs}"], writes=[pk])
                t.op("act", lambda: A_.activation(out=preF[:, 2:514], in_=pb[:, 0:512], func=AF.Identity, scale=1.0, bias=bcol[:, 9 + 2 * hh:10 + 2 * hh]),
                     reads=[pk, "bcol"], writes=[pk_])
                pb2, pk2 = next_bank()
                for k in range(8):
                    t.op("pe", lambda k=k: P_.matmul(pb2[:, 0:4], lhsT=wFk[:, k, hh * 128:(hh + 1) * 128], rhs=xgh[xs][:, k, :], start=(k == 0), stop=(k == 7)),
                         reads=["wF", f"xgh{xs}"], writes=[pk2])
                t.op("act", lambda: A_.activation(out=preF[:, 0:2], in_=pb2[:, 0:2], func=AF.Identity, scale=1.0, bias=bcol[:, 9 + 2 * hh:10 + 2 * hh]),
                     reads=[pk2, "bcol"], writes=[pk_])
                t.op("act", lambda: A_.activation(out=preF[:, 514:516], in_=pb2[:, 2:4], func=AF.Identity, scale=1.0, bias=bcol[:, 9 + 2 * hh:10 + 2 * hh]),
                     reads=[pk2, "bcol"], writes=[pk_])

            kproj(0)
            for hh in range(ML_HEADS):
                if hh + 1 < ML_HEADS:
                    kproj(hh + 1)
                preF, accF, kTF = preF2[hh % 2], accF2[hh % 2], kTF2[hh % 2]
                pk_, ak_, kk_ = f"preF{hh % 2}", f"accF{hh % 2}", f"kTF{hh % 2}"
                t.op("dve", lambda: V.tensor_scalar(out=preF[:, 0:2], in0=preF[:, 0:2], scalar1=ffl[:, gi * 4 + 2:gi * 4 + 3], scalar2=None, op0=ALU.mult), reads=[pk_, "ffl"], writes=[pk_])
                t.op("dve", lambda: V.tensor_scalar(out=preF[:, 514:516], in0=preF[:, 514:516], scalar1=ffl[:, gi * 4 + 3:gi * 4 + 4], scalar2=None, op0=ALU.mult), reads=[pk_, "ffl"], writes=[pk_])
                gidx = 4 + hh
                t.op("dve", lambda: V.tensor_scalar(out=accF, in0=preF[:, 0:512], scalar1=cw[:, gidx, 0:1], scalar2=cb[:, gidx:gidx + 1], op0=ALU.mult, op1=ALU.add),
                     reads=[pk_, "cw", "cb"], writes=[ak_])
                for jc in range(1, 5):
                    t.op("dve", lambda jc=jc: V.scalar_tensor_tensor(out=accF, in0=preF[:, jc:jc + 512], scalar=cw[:, gidx, jc:jc + 1], in1=accF, op0=ALU.mult, op1=ALU.add),
                         reads=[pk_, "cw", ak_], writes=[ak_])
                t.op("act", lambda: A_.activation(out=kTF, in_=accF, func=AF.Silu), reads=[ak_], writes=[kk_])
                p4b = P4.bitcast(BF16).rearrange("p (k f) -> p k f", k=8)
                for j in range(4):
                    t.op("pe", lambda j=j: P_.transpose(out=p4b[:, j, :], in_=kTF[:, j * 128:(j + 1) * 128], identity=identb[:]), reads=[kk_, "identb"], writes=["P4"])
                t.op("act", lambda: A_.activation(out=KtokF[:, hh, :, :], in_=p4b[:, 0:4, :], func=AF.Copy), reads=["P4"], writes=[f"KtokF{hh}"])
                for d in range(2):
                    pb, pk = next_bank()
                    for j in range(4):
                        t.op("pe", lambda j=j: P_.matmul(pb[:, 0:129], lhsT=KtokF[:, hh, j, :], rhs=wVf[:, j, d, hh, :], start=(j == 0), stop=(j == 3)),
                             reads=[f"KtokF{hh}", f"wVf{d}"], writes=[pk])
                    t.op("dve", lambda: V.tensor_tensor(out=Cacc[:, d, hh, :], in0=Cacc[:, d, hh, :], in1=pb[:, 0:129], op=ALU.add),
                         reads=[pk, "Cacc"], writes=["Cacc"])
        dbg_dump("f_c", [(Cacc[:].rearrange("p a b c -> p (a b c)"), 8 * 129)])
        t.barrier()

        cv = Carver()
        KT = cv.take([128, 4, HT * 128], BF16)
        QT = cv.take([128, 4, TOK], BF16)
        Vaug = cv.take([128, HT, 8, 65], BF16)
        Ar = cv.take([128, 8, 14, 64], BF16)
        mcol = cv.take([128, 192])
        bias_bc = cv.take([128, 512])
        mark = cv.off
        cmk = cv.take([128, 14 * 64])
        rtmp = [cv.take([128, 14 * 64]) for _ in range(2)]
        t.dma("sp", "c0", mcol, mcol_d, writes=["mcol"])
        t.dma("sp", "c0", cmk, cmask, writes=["cmk"])
        for h in range(8):
            s = h % 2
            t.dma("sp", f"rtmp{s}", rtmp[s], rpbA[h], writes=[f"rtmp{s}"])
            t.op("act", lambda s=s: A_.activation(out=rtmp[s], in_=rtmp[s], func=AF.Exp), reads=[f"rtmp{s}"], writes=[f"rtmp{s}"])
            t.op("dve", lambda s=s, h=h: V.tensor_tensor(out=Ar[:, h, :, :].rearrange("p e c -> p (e c)"), in0=rtmp[s], in1=cmk, op=ALU.mult),
                 reads=[f"rtmp{s}", "cmk"], writes=[f"Ar{h}"])
        t.barrier()
        cv.off = mark
        expS = [cv.take([128, 6, 128]) for _ in range(2)]
        PTt = [cv.take([128, 6, 128], BF16) for _ in range(2)]
        sz_t = cv.take([128, 512])
        sz_b = cv.take([128, 512], BF16)
        rden = cv.take([128, 8])
        otmp = cv.take([128, 512])
        ob = cv.take([128, 512], BF16)
        print("NA union words", cv.off)
        t.op("pool", lambda: G_.memset(Vaug[:, :, :, 64:65], 1.0), writes=["Vones"])

        for which in range(2):
            slot = which
            load_w(win_v, which * 512, 512, slot)
            for g in range(4):
                for (xap, n, c0) in tok_blocks(with_halo=(which == 1)):
                    pb, pk = next_bank()
                    for k in range(8):
                        t.op("pe", lambda k=k, g=g, xap=xap, n=n, pb=pb, slot=slot: P_.matmul(
                            pb[:, 0:n], lhsT=wbf[slot][:, k, g * 128:(g + 1) * 128], rhs=xap[:, k, :], start=(k == 0), stop=(k == 7)),
                            reads=[f"wbf{slot}"] + xkeys(c0, n), writes=[pk])
                    if which == 0:
                        q0 = c0 - 256
                        t.op("act", lambda g=g, pb=pb, n=n, q0=q0: A_.activation(out=QT[:, g, q0:q0 + n], in_=pb[:, 0:n], func=AF.Identity,
                                                                                 scale=0.125, bias=bcolq[:, g:g + 1]),
                             reads=[pk, "bcolq"], writes=[f"QT{g}_{q0 // 128 + i}" for i in range(n // 128)])
                    else:
                        t.op("act", lambda g=g, pb=pb, n=n, c0=c0: A_.activation(out=KT[:, g, c0:c0 + n], in_=pb[:, 0:n], func=AF.Identity,
                                                                                 scale=1.0, bias=bcol[:, 4 + g:5 + g]),
                             reads=[pk, "bcol"], writes=[f"KT{g}_{c0 // 128 + i}" for i in range(n // 128)])
        load_w(win_v, 1024, 512, 0)
        t.dma("pool", "c1", bias_bc, bin_row[1024:1536].partition_broadcast(128), reads=[], writes=["bias_bc"])
        for ht in range(HT):
            pb, pk = next_bank()
            xa = xT_tile(ht)
            for k in range(8):
                t.op("pe", lambda k=k, xa=xa, pb=pb: P_.matmul(pb[:, :], lhsT=xa[:, k, :], rhs=wbf[0][:, k, :], start=(k == 0), stop=(k == 7)),
                     reads=["wbf0", f"xT{ht}"], writes=[pk])
            t.op("dve", lambda ht=ht, pb=pb: V.tensor_tensor(out=Vaug[:, ht, :, 0:64], in0=pb.rearrange("p (h d) -> p h d", h=8),
                                                            in1=bias_bc.rearrange("p (h d) -> p h d", h=8), op=ALU.add),
                 reads=[pk, "bias_bc"], writes=[f"V{ht}"])
        load_w(win_v, 1536, 512, 1)
        t.wait_keys("pool", ["bias_bc"])
        t.dma("pool", "c1", bias_bc, bin_row[1536:2048].partition_broadcast(128), reads=[], writes=["bias_bc"])

        for m in range(NT):
            base, nj = _na_base(m), _na_nj(m)
            qt = m
            for k in range(8):
                t.op("pe", lambda k=k, m=m: P_.matmul(P7[:, :], lhsT=xT[:, k, m * 128:(m + 1) * 128], rhs=wbf[1][:, k, :], start=(k == 0), stop=(k == 7)),
                     reads=["wbf1", f"xT{m + 2}"], writes=["P7"])
            t.op("dve", lambda: V.tensor_tensor(out=sz_t, in0=P7[:, :], in1=bias_bc, op=ALU.add), reads=["P7", "bias_bc"], writes=["sz_t"])
            t.op("act", lambda: A_.activation(out=sz_b, in_=sz_t, func=AF.Silu), reads=["sz_t"], writes=["sz_b"])
            def score_mm(h):
                g, hh = h // 2, h % 2
                sl = h % 2
                PS = PA if sl == 0 else PB
                pskeys = ["PA0", "PA1"] if sl == 0 else ["PB0", "PB1"]
                PS3 = PS.rearrange("p (j q) -> p j q", q=128)
                for j in range(nj):
                    kt = base + j
                    t.op("pe", lambda j=j, kt=kt: P_.matmul(
                        PS3[:, j, :], lhsT=KT[hh * 64:(hh + 1) * 64, g, kt * 128:(kt + 1) * 128],
                        rhs=QT[hh * 64:(hh + 1) * 64, g, qt * 128:(qt + 1) * 128], start=True, stop=True),
                        reads=[f"KT{g}_{kt}", f"QT{g}_{qt}"], writes=pskeys)

            score_mm(0)
            for h in range(8):
                if h + 1 < 8:
                    score_mm(h + 1)
                g, hh = h // 2, h % 2
                sl = h % 2
                PS = PA if sl == 0 else PB
                pskeys = ["PA0", "PA1"] if sl == 0 else ["PB0", "PB1"]
                PS3 = PS.rearrange("p (j q) -> p j q", q=128)
                t.op("act", lambda sl=sl, nj=nj, PS3=PS3: A_.activation(out=expS[sl][:, 0:nj, :], in_=PS3[:, 0:nj, :], func=AF.Exp),
                     reads=pskeys, writes=[f"expS{sl}"])
                eng = "dve" if h % 2 == 0 else "pool"
                E = V if eng == "dve" else G_
                interior = 2 <= m <= 13
                for j in range(nj):
                    dyi0 = 2 * (base - m + j) + 3
                    e0 = 13 - dyi0
                    if interior and 1 <= j <= 3:
                        t.op(eng, lambda E=E, sl=sl, j=j, h=h, e0=e0: E.tensor_tensor(
                            out=PTt[sl][:, j, :], in0=expS[sl][:, j, :], in1=Ar[:, h, e0:e0 + 2, :].rearrange("p e c -> p (e c)"), op=ALU.mult),
                            reads=[f"expS{sl}", f"Ar{h}"], writes=[f"PT{sl}"])
                    else:
                        for b in range(2):
                            mc = (m * 6 + j) * 2 + b
                            t.op("dve", lambda E=V, sl=sl, j=j, h=h, e0=e0, b=b, mc=mc: E.scalar_tensor_tensor(
                                out=PTt[sl][:, j, b * 64:(b + 1) * 64], in0=expS[sl][:, j, b * 64:(b + 1) * 64], scalar=mcol[:, mc:mc + 1],
                                in1=Ar[:, h, e0 + b, :], op0=ALU.mult, op1=ALU.mult),
                                reads=[f"expS{sl}", f"Ar{h}", "mcol"], writes=[f"PT{sl}"])
                PO = P4 if h < 4 else P5
                pok = "P4" if h < 4 else "P5"
                PO3 = PO[:, 0:260].rearrange("p (h d) -> p h d", d=65)
                for j in range(nj):
                    kt = base + j
                    t.op("pe", lambda j=j, kt=kt, h=h, sl=sl, PO3=PO3, nj=nj: P_.matmul(
                        PO3[:, h % 4, :], lhsT=PTt[sl][:, j, :], rhs=Vaug[:, kt, h, :], start=(j == 0), stop=(j == nj - 1)),
                        reads=[f"PT{sl}", f"V{kt}", "Vones"], writes=[pok])
            for half in range(2):
                PO = P4 if half == 0 else P5
                pok = "P4" if half == 0 else "P5"
                PO3 = PO[:, 0:260].rearrange("p (h d) -> p h d", d=65)
                t.op("dve", lambda half=half, PO3=PO3: V.reciprocal(out=rden[:, half * 4:(half + 1) * 4], in_=PO3[:, :, 64]),
                     reads=[pok], writes=["rden"])
                t.op("dve", lambda half=half, PO3=PO3: V.tensor_tensor(
                    out=otmp[:, half * 256:(half + 1) * 256].rearrange("p (h d) -> p h d", d=64), in0=PO3[:, :, 0:64],
                    in1=rden[:, half * 4:(half + 1) * 4].unsqueeze(2).broadcast_to([128, 4, 64]), op=ALU.mult),
                    reads=[pok, "rden"], writes=["otmp"])
            t.op("dve", lambda: V.tensor_tensor(out=ob, in0=otmp, in1=sz_b, op=ALU.mult), reads=["otmp", "sz_b"], writes=["ob"])
            p6b = P6.bitcast(BF16).rearrange("p (k f) -> p k f", k=8)
            for c4 in range(4):
                t.op("pe", lambda c4=c4, p6b=p6b: P_.transpose(out=p6b[:, c4, :], in_=ob[:, c4 * 128:(c4 + 1) * 128], identity=identb[:]),
                     reads=["ob", "identb"], writes=["P6"])
            t.op("act", lambda m=m, p6b=p6b: A_.activation(out=mixT[:, 0:4, m * 128:(m + 1) * 128], in_=p6b[:, 0:4, :], func=AF.Copy),
                 reads=["P6"], writes=[f"mixNA{m}"])
        t.barrier()

        if dbg == "na":
            cv = Carver()
            dtmp = cv.take([128, TOK])
            for c8 in range(4):
                t.op("dve", lambda c8=c8: V.tensor_copy(out=dtmp, in_=mixT[:, c8, :]), reads=[f"mixNA{m}" for m in range(NT)], writes=["dtmp"])
                t.dma("sp", "dbg", dbg_out[:, c8 * TOK:(c8 + 1) * TOK], dtmp, reads=["dtmp"], writes=["dbgout"])
            t.wait_keys("sp", ["dbgout"])
            nc.sync.wait_ge(t.dma_lanes["dbg"].sem, t.dma_lanes["dbg"].count)
            print("instructions", t.n_inst, "waits", t.n_wait)
            return nc


        cv = Carver()
        mixML = cv.take([128, 4, TOK], BF16)
        mqT = cv.take([128, TOK], BF16)
        mkT = cv.take([128, TOK], BF16)
        QsT = [cv.take([128, TOK], BF16) for _ in range(2)]
        Ktok = cv.take([128, NT, 128], BF16)
        Vg = cv.take([128, NT, 129], BF16)
        wV = [cv.take([128, NT, 129], BF16) for _ in range(2)]
        so = cv.take([128, NT, 128], BF16)
        szm = cv.take([128, NT, 128], BF16)
        hF = cv.take([128, NT, 128])
        Gt = cv.take([128, NT, 4])
        nlf = cv.take([128, NT, 2])
        gw = cv.take([128, NT, 4])
        ebl = cv.take([128, NT, 2, 2])
        bias_ml = cv.take([128, 388])
        mlnw_bc = cv.take([128, 512])
        Cst = [cv.take([128, 129]) for _ in range(2)]
        Cbf = [cv.take([128, 129], BF16) for _ in range(2)]
        small = cv.take([128, 64])
        mark = cv.off
        t.dma("pool", "c1", mlnw_bc, mlnw_row.partition_broadcast(128), writes=["mlnw_bc"])
        t.op("pool", lambda: G_.memset(Vg[:, :, 128:129], 1.0), writes=["Vgones"])
        PNs = [(P6, "P6"), (P7, "P7")]

        for h in range(ML_HEADS):
            c0 = NA_COLS + h * MLW
            cv.off = mark
            pre = cv.take([128, TOK + 4])
            acc = cv.take([128, TOK])
            load_w(win_v, c0, 256, 0)
            load_w(win_v, c0 + 256, 388, 1)
            t.dma("pool", "c1", bias_ml, bin_row[c0 + 256:c0 + 644].partition_broadcast(128), writes=["bias_ml"])
            for g in range(2):
                blks = [(xTh[:, :, 254:256], 2, 0, ["xT1"])]
                for b in range(4):
                    blks.append((xT[:, :, b * 512:(b + 1) * 512], 512, 2 + b * 512, [f"xT{2 + 4 * b + i}" for i in range(4)]))
                blks.append((xTh[:, :, 256:258], 2, 2050, ["xT18"]))
                for (xap, n, p0, xk) in blks:
                    pb, pk = next_bank()
                    for k in range(8):
                        t.op("pe", lambda k=k, g=g, xap=xap, n=n, pb=pb: P_.matmul(
                            pb[:, 0:n], lhsT=wbf[0][:, k, g * 128:(g + 1) * 128], rhs=xap[:, k, :], start=(k == 0), stop=(k == 7)),
                            reads=["wbf0"] + xk, writes=[pk])
                    t.op("act", lambda g=g, pb=pb, n=n, p0=p0, h=h: A_.activation(out=pre[:, p0:p0 + n], in_=pb[:, 0:n], func=AF.Identity,
                                                                             scale=1.0, bias=bcol[:, 8 + 2 * h + g:9 + 2 * h + g]),
                         reads=[pk, "bcol"], writes=["pre"])
                t.op("dve", lambda: V.tensor_scalar(out=pre[:, 0:2], in0=pre[:, 0:2], scalar1=flg[:, 0:1], scalar2=None, op0=ALU.mult),
                     reads=["pre", "flg"], writes=["pre"])
                t.op("dve", lambda: V.tensor_scalar(out=pre[:, TOK + 2:TOK + 4], in0=pre[:, TOK + 2:TOK + 4], scalar1=flg[:, 1:2], scalar2=None, op0=ALU.mult),
                     reads=["pre", "flg"], writes=["pre"])
                gi = g * 4 + h
                t.op("dve", lambda gi=gi: V.tensor_scalar(out=acc, in0=pre[:, 0:TOK], scalar1=cw[:, gi, 0:1], scalar2=cb[:, gi:gi + 1],
                                                          op0=ALU.mult, op1=ALU.add), reads=["pre", "cw", "cb"], writes=["acc"])
                for j in range(1, 5):
                    t.op("dve", lambda gi=gi, j=j: V.scalar_tensor_tensor(out=acc, in0=pre[:, j:j + TOK], scalar=cw[:, gi, j:j + 1], in1=acc,
                                                                          op0=ALU.mult, op1=ALU.add), reads=["pre", "cw", "acc"], writes=["acc"])
                dstT = mqT if g == 0 else mkT
                t.op("act", lambda dstT=dstT: A_.activation(out=dstT, in_=acc, func=AF.Silu), reads=["acc"], writes=["mqT" if g == 0 else "mkT"])
            t.barrier()
            if h == 0:
                dbg_dump("ml_a", [(mqT, TOK), (mkT, TOK)])
            cv.off = mark
            tmpv = [cv.take([128, 388]) for _ in range(2)]
            Rfb = [cv.take([128, 2, 128]) for _ in range(2)]
            ebt = [cv.take([128, 2, 128]) for _ in range(2)]
            tmp4 = [cv.take([128, 4]) for _ in range(2)]
            sdT = [cv.take([128, 128], BF16) for _ in range(2)]
            hm = [cv.take([128, 128]) for _ in range(2)]
            hjunk = cv.take([128, 128])
            mo = [cv.take([128, 128], BF16) for _ in range(2)]
            st = [cv.take([128, 8]) for _ in range(2)]
            for tl in range(NT):
                pb, pk = next_bank()
                s2 = tl % 2
                for k in range(8):
                    t.op("pe", lambda k=k, tl=tl, pb=pb: P_.matmul(pb[:, 0:388], lhsT=xT[:, k, tl * 128:(tl + 1) * 128], rhs=wbf[1][:, k, 0:388],
                                                                  start=(k == 0), stop=(k == 7)), reads=["wbf1", f"xT{tl + 2}"], writes=[pk])
                t.op("dve", lambda pb=pb, s2=s2: V.tensor_tensor(out=tmpv[s2], in0=pb[:, 0:388], in1=bias_ml, op=ALU.add),
                     reads=[pk, "bias_ml"], writes=[f"tmpv{s2}"])
                t.op("pool", lambda tl=tl, s2=s2: G_.tensor_copy(out=Vg[:, tl, 0:128], in_=tmpv[s2][:, 0:128]), reads=[f"tmpv{s2}"], writes=[f"Vg{tl}"])
                t.op("act", lambda tl=tl, s2=s2: A_.activation(out=so[:, tl, :], in_=tmpv[s2][:, 128:256], func=AF.Sigmoid), reads=[f"tmpv{s2}"], writes=[f"so{tl}"])
                t.op("act", lambda tl=tl, s2=s2: A_.activation(out=szm[:, tl, :], in_=tmpv[s2][:, 256:384], func=AF.Silu), reads=[f"tmpv{s2}"], writes=[f"szm{tl}"])
                t.op("pool", lambda tl=tl, s2=s2: G_.tensor_copy(out=Gt[:, tl, :], in_=tmpv[s2][:, 384:388]), reads=[f"tmpv{s2}"], writes=["Gt"])
            Gf = Gt.rearrange("p t (d two) -> p t d two", two=2)[:, :, :, 1]
            Gi = Gt.rearrange("p t (d two) -> p t d two", two=2)[:, :, :, 0]
            t.op("act", lambda: A_.activation(out=nlf, in_=Gf, func=AF.Exp, scale=-1.0), reads=["Gt"], writes=["nlf"])
            t.op("act", lambda: A_.activation(out=nlf, in_=nlf, func=AF.Ln, bias=kcol[:, 0:1], scale=1.0), reads=["nlf", "kcol"], writes=["nlf"])
            for tl in range(NT):
                s2 = tl % 2
                t.op("pe", lambda tl=tl: P_.matmul(P6[:, 0:1], lhsT=cst[:, TRIF, :], rhs=nlf[:, tl, 0:1], start=True, stop=True), reads=["cst", "nlf"], writes=["P6"])
                t.op("pe", lambda tl=tl: P_.matmul(P6[:, 1:2], lhsT=cst[:, TRIB, :], rhs=nlf[:, tl, 1:2], start=True, stop=True), reads=["cst", "nlf"], writes=["P6"])
                t.op("pe", lambda tl=tl: P_.matmul(P6[:, 2:4], lhsT=cst[:, BLK, :], rhs=nlf[:, tl, 0:2], start=True, stop=True), reads=["cst", "nlf"], writes=["P6"])
                t.op("dve", lambda tl=tl, s2=s2: V.tensor_tensor(out=tmp4[s2][:, 0:2], in0=Gi[:, tl, :], in1=P6[:, 0:2], op=ALU.add),
                     reads=["Gt", "P6"], writes=[f"tmp4{s2}"])
                t.op("dve", lambda s2=s2: V.tensor_tensor(out=tmp4[s2][:, 2:4], in0=tmp4[s2][:, 0:2], in1=P6[:, 2:4], op=ALU.subtract),
                     reads=["P6", f"tmp4{s2}"], writes=[f"tmp4{s2}"])
                t.op("act", lambda tl=tl, s2=s2: A_.activation(out=gw[:, tl, :], in_=tmp4[s2], func=AF.Exp), reads=[f"tmp4{s2}"], writes=[f"gw{tl}"])
                t.op("dve", lambda tl=tl, s2=s2: V.tensor_scalar(out=Rfb[s2][:, 0, :], in0=cst[:, TRIF, :], scalar1=nlf[:, tl, 0:1], scalar2=None, op0=ALU.mult),
                     reads=["cst", "nlf"], writes=[f"Rfb{s2}"])
                t.op("dve", lambda tl=tl, s2=s2: V.tensor_scalar(out=Rfb[s2][:, 1, :], in0=cst[:, TRIB, :], scalar1=nlf[:, tl, 1:2], scalar2=None, op0=ALU.mult),
                     reads=["cst", "nlf"], writes=[f"Rfb{s2}"])
                t.op("pe", lambda s2=s2: P_.matmul(P7[:, 0:256], lhsT=cst[:, ONES, :], rhs=Rfb[s2].rearrange("p d t -> p (d t)"), start=True, stop=True),
                     reads=["cst", f"Rfb{s2}"], writes=["P7"])
                P7v = P7[:, 0:256].rearrange("p (d t) -> p d t", d=2)
                t.op("act", lambda s2=s2, P7v=P7v: A_.activation(out=ebt[s2], in_=P7v, func=AF.Exp, scale=-1.0, bias=kcol[:, 1:2]),
                     reads=["P7", "kcol"], writes=[f"ebt{s2}"])
                t.op("act", lambda tl=tl: A_.activation(out=ebl[:, tl, 0, :], in_=P7[:, 63:128:64], func=AF.Exp, scale=-1.0), reads=["P7"], writes=[f"ebl{tl}"])
                t.op("act", lambda tl=tl: A_.activation(out=ebl[:, tl, 1, :], in_=P7[:, 128:256:64], func=AF.Exp, scale=-1.0), reads=["P7"], writes=[f"ebl{tl}"])
                for d in range(2):
                    eng = "dve" if d == 0 else "pool"
                    E = V if d == 0 else G_
                    t.op(eng, lambda E=E, d=d, tl=tl, s2=s2: E.tensor_tensor(out=QsT[d][:, tl * 128:(tl + 1) * 128], in0=mqT[:, tl * 128:(tl + 1) * 128],
                                                                           in1=ebt[s2][:, d, :], op=ALU.mult), reads=["mqT", f"ebt{s2}"], writes=[f"Qs{d}_{tl}"])
                    t.op("pool", lambda d=d, tl=tl: G_.tensor_scalar(out=wV[d][:, tl, :], in0=Vg[:, tl, :], scalar1=gw[:, tl, 2 + d:3 + d], scalar2=None, op0=ALU.mult),
                         reads=[f"Vg{tl}", "Vgones", f"gw{tl}"], writes=[f"wV{d}_{tl}"])
                p4b = P4.bitcast(BF16)
                t.op("pe", lambda tl=tl, p4b=p4b, s2=s2: P_.transpose(out=p4b[:, s2 * 128:(s2 + 1) * 128], in_=mkT[:, tl * 128:(tl + 1) * 128], identity=identb[:]),
                     reads=["mkT", "identb"], writes=["P4"])
                t.op("act", lambda tl=tl, p4b=p4b, s2=s2: A_.activation(out=Ktok[:, tl, :], in_=p4b[:, s2 * 128:(s2 + 1) * 128], func=AF.Copy),
                     reads=["P4"], writes=[f"Ktok{tl}"])

            if h == 0:
                dbg_dump("ml_b", [(gw.rearrange("p a b -> p (a b)"), NT * 4), (ebl.rearrange("p a b c -> p (a b c)"), NT * 4),
                                  (nlf.rearrange("p a b -> p (a b)"), NT * 2), (QsT[0], TOK), (QsT[1], TOK),
                                  (Ktok.rearrange("p a b -> p (a b)"), TOK), (Vg.rearrange("p a b -> p (a b)"), NT * 129),
                                  (wV[0].rearrange("p a b -> p (a b)"), NT * 129)])
            def chunks(d):
                order = list(range(2 * NT))
                return order if d == 0 else order[::-1]

            def state_update(d, c, to_bf):
                tl, c2 = c // 2, c % 2
                pb, pk = next_bank()
                lo, hi = c2 * 64, c2 * 64 + 64
                t.op("pe", lambda: P_.matmul(pb[:, 0:129], lhsT=Ktok[lo:hi, tl, :], rhs=wV[d][lo:hi, tl, :], start=True, stop=True),
                     reads=[f"Ktok{tl}", f"wV{d}_{tl}"], writes=[pk])
                t.op("dve", lambda: V.scalar_tensor_tensor(out=Cst[d], in0=Cst[d], scalar=ebl[:, tl, d, c2:c2 + 1], in1=pb[:, 0:129],
                                                           op0=ALU.mult, op1=ALU.add), reads=[pk, f"ebl{tl}", f"C{d}"], writes=[f"C{d}"])
                if to_bf:
                    t.op("act", lambda: A_.activation(out=Cbf[d], in_=Cst[d], func=AF.Copy), reads=[f"C{d}"], writes=[f"Cb{d}"])

            for d in range(2):
                t.op("dve", lambda d=d, h=h: V.tensor_copy(out=Cst[d], in_=Cacc[:, d, h, :]), reads=["Cacc"], writes=[f"C{d}"])
                t.op("act", lambda d=d: A_.activation(out=Cbf[d], in_=Cst[d], func=AF.Copy), reads=[f"C{d}"], writes=[f"Cb{d}"])
            if h == 0:
                dbg_dump("ml_d", [(Cst[0], 129), (Cst[1], 129)])
            def scan_tile(d, tl):
                PN, pnk = PNs[d]
                pb, pk = next_bank()
                tok = slice(tl * 128, (tl + 1) * 128)
                t.op("pe", lambda: P_.matmul(pb[:, 0:128], lhsT=mkT[:, tok], rhs=QsT[d][:, tok], start=True, stop=True),
                     reads=["mkT", f"Qs{d}_{tl}"], writes=[pk])
                msk = cst[:, TRIF, :] if d == 0 else cst[:, TRIB, :]
                t.op("dve", lambda: V.scalar_tensor_tensor(out=sdT[d], in0=pb[:, 0:128], scalar=gw[:, tl, d:d + 1], in1=msk, op0=ALU.mult, op1=ALU.mult),
                     reads=[pk, f"gw{tl}", "cst"], writes=[f"sdT{d}"])
                t.op("pe", lambda: P_.matmul(PN[:, 0:129], lhsT=sdT[d], rhs=Vg[:, tl, :], start=True, stop=False),
                     reads=[f"sdT{d}", f"Vg{tl}", "Vgones"], writes=[pnk])
                order = (0, 1) if d == 0 else (1, 0)
                for ci, c2 in enumerate(order):
                    lo = tl * 128 + c2 * 64
                    t.op("pe", lambda c2=c2, lo=lo, ci=ci: P_.matmul(PN[c2 * 64:(c2 + 1) * 64, 0:129], lhsT=QsT[d][:, lo:lo + 64], rhs=Cbf[d],
                                                                 start=False, stop=True), reads=[f"Qs{d}_{tl}", f"Cb{d}"], writes=[pnk])
                    c = tl * 2 + c2
                    last = (c == (2 * NT - 1 if d == 0 else 0))
                    if not last:
                        state_update(d, c, True)
                s2 = d
                t.op("act", lambda: A_.activation(out=st[s2][:, 0:1], in_=PN[:, 128:129], func=AF.Abs), reads=[pnk], writes=[f"st{s2}"])
                t.op("dve", lambda: V.tensor_scalar(out=st[s2][:, 0:1], in0=st[s2][:, 0:1], scalar1=1.0, scalar2=None, op0=ALU.max),
                     reads=[f"st{s2}"], writes=[f"st{s2}"])
                t.op("dve", lambda: V.reciprocal(out=st[s2][:, 1:2], in_=st[s2][:, 0:1]), reads=[f"st{s2}"], writes=[f"st{s2}"])
                return PN, pnk

            for i in range(NT):
                PN, pnk = scan_tile(0, i)
                t.op("dve", lambda PN=PN, i=i: V.tensor_scalar(out=hF[:, i, :], in0=PN[:, 0:128], scalar1=st[0][:, 1:2], scalar2=None, op0=ALU.mult),
                     reads=[pnk, "st0"], writes=[f"hF{i}"])
            if h == 0:
                dbg_dump("ml_e", [(hF.rearrange("p a b -> p (a b)"), TOK)])
            for i in range(NT - 1, -1, -1):
                PN, pnk = scan_tile(1, i)
                s2 = i % 2
                t.op("dve", lambda PN=PN, i=i, s2=s2: V.scalar_tensor_tensor(out=hm[s2], in0=PN[:, 0:128], scalar=st[1][:, 1:2], in1=hF[:, i, :], op0=ALU.mult, op1=ALU.add),
                     reads=[pnk, "st1", f"hF{i}"], writes=[f"hm{s2}"])
                t.op("dve", lambda i=i, s2=s2: V.tensor_tensor(out=hm[s2], in0=hm[s2], in1=so[:, i, :], op=ALU.mult), reads=[f"hm{s2}", f"so{i}"], writes=[f"hm{s2}"])
                if h == 0 and i == NT - 1:
                    dbg_dump("ml_g", [(hm[s2], 128)])
                t.op("act", lambda s2=s2: A_.activation(out=hjunk, in_=hm[s2], func=AF.Identity, accum_out=st[1][:, 2:3]), reads=[f"hm{s2}"], writes=["hjunk", "st1b"])
                t.op("act", lambda s2=s2: A_.activation(out=hjunk, in_=hm[s2], func=AF.Square, accum_out=st[1][:, 3:4]), reads=[f"hm{s2}"], writes=["hjunk", "st1c"])
                t.op("dve", lambda: V.tensor_scalar(out=st[1][:, 4:5], in0=st[1][:, 2:3], scalar1=1.0 / 128, scalar2=None, op0=ALU.mult), reads=["st1b"], writes=["st1d"])
                t.op("dve", lambda: V.tensor_tensor(out=st[1][:, 5:6], in0=st[1][:, 4:5], in1=st[1][:, 4:5], op=ALU.mult), reads=["st1d"], writes=["st1e"])
                t.op("dve", lambda: V.scalar_tensor_tensor(out=st[1][:, 6:7], in0=st[1][:, 3:4], scalar=1.0 / 128, in1=st[1][:, 5:6], op0=ALU.mult, op1=ALU.subtract),
                     reads=["st1c", "st1e"], writes=["st1f"])
                t.op("dve", lambda: V.tensor_scalar(out=st[1][:, 6:7], in0=st[1][:, 6:7], scalar1=EPS, scalar2=None, op0=ALU.add), reads=["st1f"], writes=["st1f"])
                t.op("act", lambda: A_.activation(out=st[1][:, 6:7], in_=st[1][:, 6:7], func=AF.Sqrt), reads=["st1f"], writes=["st1f"])
                t.op("dve", lambda: V.reciprocal(out=st[1][:, 7:8], in_=st[1][:, 6:7]), reads=["st1f"], writes=["st1g"])
                t.op("dve", lambda s2=s2: V.tensor_scalar(out=hm[s2], in0=hm[s2], scalar1=st[1][:, 4:5], scalar2=st[1][:, 7:8], op0=ALU.subtract, op1=ALU.mult),
                     reads=[f"hm{s2}", "st1d", "st1g"], writes=[f"hm{s2}"])
                t.op("pool", lambda s2=s2, h=h: G_.tensor_tensor(out=hm[s2], in0=hm[s2], in1=mlnw_bc[:, h * 128:(h + 1) * 128], op=ALU.mult),
                     reads=[f"hm{s2}", "mlnw_bc"], writes=[f"hm{s2}"])
                t.op("pool", lambda s2=s2, i=i: G_.tensor_tensor(out=mo[s2], in0=hm[s2], in1=szm[:, i, :], op=ALU.mult), reads=[f"hm{s2}", f"szm{i}"], writes=[f"mo{s2}"])
                p4b = P4.bitcast(BF16)
                if h == 0 and i == NT - 1:
                    dbg_dump("ml_h", [(mo[s2], 128)])
                t.op("pe", lambda s2=s2, p4b=p4b: P_.transpose(out=p4b[:, s2 * 128:(s2 + 1) * 128], in_=mo[s2], identity=identb[:]),
                     reads=[f"mo{s2}", "identb"], writes=["P4"])
                t.op("act", lambda s2=s2, p4b=p4b, i=i, h=h: A_.activation(out=mixML[:, h, i * 128:(i + 1) * 128], in_=p4b[:, s2 * 128:(s2 + 1) * 128], func=AF.Copy),
                     reads=["P4"], writes=[f"mixML{h}_{i}"])
            t.barrier()
            if h == 0:
                dbg_dump("ml_f", [(mixML[:, 0, :], TOK)])

        if dbg == "ml":
            cv.off = mark
            dtmp = cv.take([128, TOK])
            for c8 in range(4):
                t.op("dve", lambda c8=c8: V.tensor_copy(out=dtmp, in_=mixML[:, c8, :]), reads=[], writes=["dtmp"])
                t.dma("sp", "dbg", dbg_out[:, (4 + c8) * TOK:(5 + c8) * TOK], dtmp, reads=["dtmp"], writes=["dbgout"])
            nc.sync.wait_ge(t.dma_lanes["dbg"].sem, t.dma_lanes["dbg"].count)
            print("instructions", t.n_inst, "waits", t.n_wait)
            return nc

        cv.off = 4 * TOK // 2
        fnw_bc = cv.take([128, D])
        woutb = cv.take([128, 8, D], BF16)
        xres = [cv.take([128, D]) for _ in range(2)]
        hres = [cv.take([128, D]) for _ in range(2)]
        ojunk = cv.take([128, D])
        ost = cv.take([128, 2 * NT])
        t.dma("pool", "c1", fnw_bc, fnw_row.partition_broadcast(128), writes=["fnw_bc"])
        for q4 in range(4):
            s = wstate["n"] % 2
            wstate["n"] += 1
            t.dma("sp", f"wst{s}", wst[s][:, :, 0:256], wout_v[:, :, q4 * 256:(q4 + 1) * 256], writes=[f"wst{s}"])
            t.op("pool", lambda s=s, q4=q4: G_.tensor_copy(out=woutb[:, :, q4 * 256:(q4 + 1) * 256], in_=wst[s][:, :, 0:256]), reads=[f"wst{s}"], writes=["woutb"])
        for tl in range(NT):
            s2 = tl % 2
            t.dma("sp", f"xres{s2}", xres[s2], xh[(tl + 2) * 128:(tl + 3) * 128, :], writes=[f"xres{s2}"])
            for nb in range(2):
                pb, pk = next_bank()
                for mc in range(8):
                    src = mixT[:, mc, tl * 128:(tl + 1) * 128] if mc < 4 else mixML[:, mc - 4, tl * 128:(tl + 1) * 128]
                    t.op("pe", lambda mc=mc, src=src, pb=pb, nb=nb: P_.matmul(pb[:, :], lhsT=src, rhs=woutb[:, mc, nb * 512:(nb + 1) * 512], start=(mc == 0), stop=(mc == 7)),
                         reads=["woutb"], writes=[pk])
                t.op("dve", lambda pb=pb, nb=nb, s2=s2: V.tensor_tensor(out=hres[s2][:, nb * 512:(nb + 1) * 512], in0=pb[:, :], in1=gate_bc[:, nb * 512:(nb + 1) * 512], op=ALU.mult),
                     reads=[pk, "gate_bc"], writes=[f"hres{s2}_{nb}"])
                t.op("pool", lambda nb=nb, s2=s2: G_.tensor_tensor(out=hres[s2][:, nb * 512:(nb + 1) * 512], in0=hres[s2][:, nb * 512:(nb + 1) * 512],
                                                                 in1=xres[s2][:, nb * 512:(nb + 1) * 512], op=ALU.add),
                     reads=[f"hres{s2}_{nb}", f"xres{s2}"], writes=[f"hres{s2}_{nb}"])
            hk = [f"hres{s2}_0", f"hres{s2}_1"]
            t.op("act", lambda s2=s2, tl=tl: A_.activation(out=ojunk, in_=hres[s2], func=AF.Square, accum_out=ost[:, tl:tl + 1]), reads=hk, writes=["ojunk", f"ost{tl}"])
            t.op("dve", lambda tl=tl: V.tensor_scalar(out=ost[:, tl:tl + 1], in0=ost[:, tl:tl + 1], scalar1=1.0 / D, scalar2=EPS, op0=ALU.mult, op1=ALU.add),
                 reads=[f"ost{tl}"], writes=[f"ost{tl}"])
            t.op("act", lambda tl=tl: A_.activation(out=ost[:, tl:tl + 1], in_=ost[:, tl:tl + 1], func=AF.Sqrt), reads=[f"ost{tl}"], writes=[f"ost{tl}"])
            t.op("dve", lambda tl=tl: V.reciprocal(out=ost[:, NT + tl:NT + tl + 1], in_=ost[:, tl:tl + 1]), reads=[f"ost{tl}"], writes=[f"ost{tl}"])
            t.op("dve", lambda tl=tl, s2=s2: V.scalar_tensor_tensor(out=hres[s2], in0=hres[s2], scalar=ost[:, NT + tl:NT + tl + 1], in1=fnw_bc, op0=ALU.mult, op1=ALU.mult),
                 reads=hk + [f"ost{tl}", "fnw_bc"], writes=hk)
            t.dma("sp", "yout", y[tl * 128:(tl + 1) * 128, :], hres[s2], reads=hk, writes=["yout"])
        yl = t.dma_lanes["yout"]
        nc.sync.wait_ge(yl.sem, yl.count)
        print("instructions", t.n_inst, "waits", t.n_wait)

    return nc


def make_in_maps(x, c, w_ada, b_ada, norm_w, w_in, b_in, conv_w, conv_b, rpb, ml_norm_w, w_out, final_norm_w):
    f = lambda a: np.ascontiguousarray(np.asarray(a, dtype=np.float32))
    x = f(x)[0]
    perm = _col_perm()
    w_in_p = f(f(w_in)[0][:, perm])
    b_in_p = f(f(b_in)[0][perm])
    groups = [b_in_p[g * 128:(g + 1) * 128] for g in range(8)]
    for h in range(ML_HEADS):
        o = NA_COLS + h * MLW
        groups.append(b_in_p[o:o + 128])
        groups.append(b_in_p[o + 128:o + 256])
    bin_col = f(np.stack(groups, 1))
    col8 = lambda v: f(np.asarray(v, np.float32).reshape(-1, 128).T)
    rpbA, cmask = _rpb_tables(f(rpb)[0])
    cw = f(conv_w)[0]
    convw = f(cw.T.reshape(8, 128, 5).transpose(1, 0, 2))
    shared = {
        "w_ada": f(w_ada)[0], "w_in": w_in_p, "w_out": f(w_out)[0], "consts": _consts(),
        "c_col": col8(f(c)[0]), "bada_col": col8(f(b_ada)[0]), "bada_gate": f(f(b_ada)[0][2 * D:]),
        "normw_col": col8(f(norm_w)[0]), "fnw_row": f(final_norm_w), "mlnw_row": f(ml_norm_w)[0],
        "bin_col": bin_col, "bin_row": b_in_p, "convw": convw, "convb": col8(f(conv_b)[0]),
        "rpbA": f(rpbA.reshape(8, 128, 14 * 64)), "cmask": f(cmask.reshape(128, 14 * 64)),
        "tri2": _tri2(), "w_gates": f(f(w_in)[0][:, 4608:4624]), "b_gates": f(f(b_in)[0][4608:4624]),
    }
    in_maps = []
    for i in range(NCORES):
        xhh = np.zeros((HT * 128, D), np.float32)
        lo, hi = i * TOK - 256, i * TOK + TOK + 256
        slo, shi = max(lo, 0), min(hi, T)
        xhh[slo - lo:shi - lo] = x[slo:shi]
        fl = np.zeros((128, 18), np.float32)
        fl[:, 0] = 1.0 if i > 0 else 0.0
        fl[:, 1] = 1.0 if i < NCORES - 1 else 0.0
        for j in range(NCORES):
            fl[:, 2 + j] = 1.0 if j < i else 0.0
            fl[:, 10 + j] = 1.0 if j > i else 0.0
        gb = list(range(4 * i - 1, -1, -1))
        ga = list(range(4 * i + 4, T // 512))
        order = gb + ga
        assert len(order) == NG
        xf = np.concatenate([x[g * 512:(g + 1) * 512] for g in order], 0)
        xfh = np.zeros((NG * 4, D), np.float32)
        ffl = np.zeros((128, NG * 4), np.float32)
        for p, g in enumerate(order):
            if g > 0:
                xfh[p * 4:p * 4 + 2] = x[g * 512 - 2:g * 512]
                ffl[:, p * 4 + 2] = 1.0
            if g < T // 512 - 1:
                xfh[p * 4 + 2:p * 4 + 4] = x[(g + 1) * 512:(g + 1) * 512 + 2]
                ffl[:, p * 4 + 3] = 1.0
            ffl[:, p * 4 + 0] = 1.0 if g < 4 * i else 0.0
            ffl[:, p * 4 + 1] = 1.0 if g > 4 * i else 0.0
        m = dict(shared)
        m.update({"xh": xhh, "mcol": _mcol(i), "flags": fl, "xf": xf, "xfh": xfh, "fflags": ffl})
        in_maps.append(m)
    return in_maps


_NC_CACHE = {}


def kernel(**inputs):
    in_maps = make_in_maps(**inputs)
    if "nc" not in _NC_CACHE:
        _NC_CACHE["nc"] = build_program()
    res = run_bass_kernel_spmd(_NC_CACHE["nc"], in_maps, core_ids=list(range(NCORES)))
    out = np.concatenate([r["y"] for r in res.results], axis=0)
    return out.reshape(1, T, D).astype(np.float32)
```

```python
import numpy as np
from contextlib import ExitStack
import concourse.bass as bass
import concourse.mybir as mybir
from concourse.bass_utils import run_bass_kernel_spmd

F32 = mybir.dt.float32
BF16 = mybir.dt.bfloat16
AF = mybir.ActivationFunctionType
ALU = mybir.AluOpType

NCORES = 8
D = 1024
T = 16384
TOK = T // NCORES
NT = TOK // 128
HT = NT + 4
NA_HEADS = 8
ML_HEADS = 4
EPS = 1e-6
MLW = 644
NA_COLS = 2048
IN_WP = NA_COLS + ML_HEADS * MLW
UNI_WORDS = 23424
NG = 28


class _Stop(Exception):
    pass


class Lane:
    def __init__(self, nc, name):
        self.sem = nc.alloc_semaphore(name=name)
        self.count = 0
        self.name = name


class Trk:
    def __init__(self, nc):
        self.nc = nc
        self.engs = {"pe": nc.tensor, "act": nc.scalar, "dve": nc.vector, "pool": nc.gpsimd, "sp": nc.sync}
        self.lanes = {k: Lane(nc, "sem_" + k) for k in ("pe", "act", "dve", "pool")}
        self.seen = {k: {} for k in self.engs}
        self.last_w = {}
        self.reads = {}
        self.dma_lanes = {}
        self.n_inst = 0
        self.n_wait = 0

    def dma_lane(self, name):
        if name not in self.dma_lanes:
            self.dma_lanes[name] = Lane(self.nc, "dsem_" + name)
        return self.dma_lanes[name]

    def _wait(self, eng, lane, val):
        s = self.seen[eng]
        if s.get(lane.name, 0) >= val:
            return
        if eng == "pe" and lane is self.lanes["pe"]:
            return
        self.engs[eng].wait_ge(lane.sem, val)
        s[lane.name] = val
        self.n_wait += 1

    def _deps(self, eng, reads, writes):
        for k in reads:
            if k in self.last_w:
                self._wait(eng, *self.last_w[k])
        for k in writes:
            if k in self.last_w:
                self._wait(eng, *self.last_w[k])
            for (l, v) in self.reads.get(k, ()):
                self._wait(eng, l, v)

    def _record(self, lane, reads, writes):
        v = lane.count
        for k in reads:
            if k in writes:
                continue
            lst = self.reads.setdefault(k, [])
            lst[:] = [(l, x) for (l, x) in lst if l is not lane]
            lst.append((lane, v))
        for k in writes:
            self.last_w[k] = (lane, v)
            self.reads[k] = []

    def op(self, eng, fn, reads=(), writes=()):
        lane = self.lanes[eng]
        self._deps(eng, reads, writes)
        ins = fn()
        lane.count += 1
        ins.then_inc(lane.sem, 1)
        self._record(lane, reads, writes)
        self.n_inst += 1
        return ins

    def dma(self, q, lane_name, out, in_, reads=(), writes=(), **kw):
        if lane_name in ("c0", "c1"):
            self._oneshot = getattr(self, "_oneshot", 0) + 1
            lane_name = f"os{self._oneshot % 24}"
            lane = self.dma_lane(lane_name)
            if lane.count > 0:
                self._wait(q, lane, lane.count)
        lane = self.dma_lane(lane_name)
        self._deps(q, reads, writes)
        ins = self.engs[q].dma_start(out=out, in_=in_, **kw)
        lane.count += 16
        ins.then_inc(lane.sem, 16)
        self._record(lane, reads, writes)
        self.n_inst += 1
        return ins

    def wait_keys(self, eng, keys):
        for k in keys:
            if k in self.last_w:
                self._wait(eng, *self.last_w[k])
            for (l, v) in self.reads.get(k, ()):
                self._wait(eng, l, v)

    def barrier(self):
        all_lanes = list(self.lanes.values()) + list(self.dma_lanes.values())
        for eng in self.engs:
            for l in all_lanes:
                if l.count > 0:
                    self._wait(eng, l, l.count)


def _col_perm():
    NAW = 512
    cols = list(range(0, 4 * NAW))
    base_ml = 4 * NAW
    gate0 = 4 * NAW + 5 * 512
    for h in range(ML_HEADS):
        for blk in range(5):
            s = base_ml + blk * 512 + h * 128
            cols += list(range(s, s + 128))
        cols += [gate0 + 0 + h, gate0 + 4 + h, gate0 + 8 + h, gate0 + 12 + h]
    return np.array(cols, dtype=np.int64)


def _consts():
    s = np.arange(128)[:, None]
    t = np.arange(128)[None, :]
    same = (s // 64) == (t // 64)
    triF = (same & (s <= t)).astype(np.float32)
    triB = (same & (s >= t)).astype(np.float32)
    blk = same.astype(np.float32)
    ones = np.ones((128, 128), np.float32)
    ident = np.eye(128, dtype=np.float32)
    return np.stack([ident, triF, triB, blk, ones], 0)


def _tri2():
    s = np.arange(128)[:, None]
    t = np.arange(128)[None, :]
    return np.stack([(s > t).astype(np.float32), (s < t).astype(np.float32)], 0)


def _rpb_tables(rpb):
    p = np.arange(128)
    a = p // 64
    k = p % 64
    c = np.arange(64)
    e = np.arange(14)
    dyi = 13 - e
    dy = dyi[None, :] + a[:, None]
    dyv = (dy >= 0) & (dy <= 14)
    dx = np.clip(k[:, None] - c[None, :], -15, 15) + 15
    cs = np.clip(c - 8, 0, 48)
    cv = (k[:, None] >= cs[None, :]) & (k[:, None] < cs[None, :] + 16)
    valid = dyv[:, :, None] & cv[:, None, :]
    dyc = np.clip(dy, 0, 14)
    g = rpb[:, dyc[:, :, None], dx[:, None, :]]
    g = np.where(valid[None], g, np.float32(0.0)).astype(np.float32)
    return np.ascontiguousarray(g), valid.astype(np.float32)


def _na_base(m):
    return 14 if m == 15 else m


def _na_nj(m):
    return 6 if m in (0, 15) else 5


def _mcol(core):
    out = np.zeros((128, 16, 6, 2), np.float32)
    for m in range(16):
        for j in range(_na_nj(m)):
            for b in range(2):
                r = 32 * core + 2 * m + b
                start = min(max(r - 4, 0), 248)
                for a in range(2):
                    kr = 32 * core - 4 + 2 * (_na_base(m) + j) + a
                    ok = (start <= kr < start + 8)
                    out[a * 64:(a + 1) * 64, m, j, b] = 1.0 if ok else 0.0
    return out.reshape(128, 192)


def build_program(dbg=None):
    nc = bass.Bass("TRN2", target_bir_lowering=False)
    try:
        _build_body(nc, dbg)
    except _Stop:
        pass
    return nc


def _build_body(nc, dbg):

    def din(name, shape):
        return nc.dram_tensor(name, list(shape), F32, kind="ExternalInput").ap()

    xh = din("xh", [HT * 128, D])
    w_ada = din("w_ada", [D, 3 * D])
    w_in = din("w_in", [D, IN_WP])
    w_out = din("w_out", [D, D])
    consts = din("consts", [5, 128, 128])
    c_col = din("c_col", [128, 8])
    bada_col = din("bada_col", [128, 24])
    bada_gate = din("bada_gate", [D])
    normw_col = din("normw_col", [128, 8])
    fnw_row = din("fnw_row", [D])
    mlnw_row = din("mlnw_row", [512])
    bin_col = din("bin_col", [128, 16])
    bin_row = din("bin_row", [IN_WP])
    convw = din("convw", [128, 8, 5])
    convb = din("convb", [128, 8])
    rpbA = din("rpbA", [8, 128, 14 * 64])
    cmask = din("cmask", [128, 14 * 64])
    mcol_d = din("mcol", [128, 192])
    flags = din("flags", [128, 18])
    xf = din("xf", [NG * 512, D])
    xfh = din("xfh", [NG * 4, D])
    fflags = din("fflags", [128, NG * 4])
    tri2 = din("tri2", [2, 128, 128])
    w_gates = din("w_gates", [D, 16])
    b_gates = din("b_gates", [16])
    y = nc.dram_tensor("y", [TOK, D], F32, kind="ExternalOutput").ap()
    dbg_out = None
    if dbg:
        dbg_out = nc.dram_tensor("dbg", [128, 8 * TOK], F32, kind="ExternalOutput").ap()

    es = ExitStack()
    with es:
        def sb(name, shape, dt=F32):
            return es.enter_context(nc.sbuf_tensor(name, list(shape), dt))

        def ps(name, shape, dt=F32):
            return es.enter_context(nc.psum_tensor(name, list(shape), dt))

        t = Trk(nc)
        V, A_, P_, G_ = nc.vector, nc.scalar, nc.tensor, nc.gpsimd

        xT = sb("xT", [128, 8, TOK], BF16)
        xTh = sb("xTh", [128, 8, 512], BF16)
        mixT = sb("mixT", [128, 4, TOK], BF16)
        gate_bc = sb("gate_bc", [128, D])
        cst = sb("cst", [128, 5, 128])
        identb = sb("identb", [128, 128], BF16)
        gT = sb("gT", [128, 8])
        shiftT = sb("shiftT", [128, 8])
        bcol = sb("bcol", [128, 16])
        bcolq = sb("bcolq", [128, 4])
        flg = sb("flg", [128, 18])
        cw = sb("cw", [128, 8, 5])
        cb = sb("cb", [128, 8])
        kcol = sb("kcol", [128, 4])
        wst = [sb(f"wst{i}", [128, 8, 256]) for i in range(2)]
        wbf = [sb(f"wbf{i}", [128, 8, 512], BF16) for i in range(2)]
        Cacc = sb("Cacc", [128, 2, 4, 129])
        UNI = sb("UNI", [128, UNI_WORDS])

        PA = ps("PA", [128, 1024])
        PB = ps("PB", [128, 1024])
        P4 = ps("P4", [128, 512])
        P5 = ps("P5", [128, 512])
        P6 = ps("P6", [128, 512])
        P7 = ps("P7", [128, 512])
        IDENT, TRIF, TRIB, BLK, ONES = range(5)

        class Carver:
            def __init__(self):
                self.off = 0

            def take(self, shape, dt=F32):
                n = int(np.prod(shape[1:]))
                words = n if dt == F32 else (n + 1) // 2
                ap = UNI[:, self.off:self.off + words]
                self.off += words
                assert self.off <= UNI_WORDS, self.off
                if dt != F32:
                    ap = ap.bitcast(BF16)[:, 0:n]
                if len(shape) > 2:
                    names = " ".join(f"d{i}" for i in range(1, len(shape)))
                    kw = {f"d{i}": shape[i] for i in range(1, len(shape))}
                    ap = ap.rearrange(f"p ({names}) -> p {names}", **kw)
                return ap

        def dbg_dump(stage, items):
            if dbg != stage:
                return
            t.barrier()
            off = 0
            dt_ = UNI[:, UNI_WORDS - 2048:UNI_WORDS]
            for ap, n in items:
                o2 = 0
                while o2 < n:
                    w = min(2048, n - o2)
                    t.op("dve", lambda ap=ap, o2=o2, w=w: V.tensor_copy(out=dt_[:, 0:w], in_=ap[:, o2:o2 + w]), reads=[], writes=["dbgtmp"])
                    t.dma("sp", "dbg", dbg_out[:, off:off + w], dt_[:, 0:w], reads=["dbgtmp"], writes=["dbgout"])
                    off += w
                    o2 += w
            nc.sync.wait_ge(t.dma_lanes["dbg"].sem, t.dma_lanes["dbg"].count)
            print("dbg stop at", stage, "instructions", t.n_inst, "waits", t.n_wait)
            raise _Stop()

        t.dma("sp", "c0", cst[:], consts.rearrange("n p f -> p n f"), writes=["cst"])
        t.dma("sp", "c0", flg[:], flags, writes=["flg"])
        t.dma("sp", "c0", bcol[:], bin_col, writes=["bcol"])
        t.dma("sp", "c0", cw[:], convw, writes=["cw"])
        t.dma("sp", "c0", cb[:], convb, writes=["cb"])
        t.op("dve", lambda: V.tensor_copy(out=identb[:], in_=cst[:, IDENT, :]), reads=["cst"], writes=["identb"])
        t.op("pool", lambda: G_.memset(kcol[:, 0:1], 1.0), writes=["kcol"])
        t.op("pool", lambda: G_.memset(kcol[:, 1:2], float(np.log(128.0 ** -0.5))), writes=["kcol"])
        t.op("pool", lambda: G_.memset(kcol[:, 2:3], EPS), writes=["kcol"])
        t.op("pool", lambda: G_.memset(kcol[:, 3:4], 0.0), writes=["kcol"])
        t.op("dve", lambda: V.tensor_scalar(out=bcolq[:], in0=bcol[:, 0:4], scalar1=0.125, scalar2=None, op0=ALU.mult),
             reads=["bcol"], writes=["bcolq"])

        cv = Carver()
        ccol = cv.take([128, 8])
        cact = cv.take([128, 8])
        cbc = cv.take([128, 8, 128])
        badac = cv.take([128, 24])
        nwc = cv.take([128, 8])
        bgate = cv.take([128, D])
        modT = cv.take([128, 16])
        wada_sb = [cv.take([128, 8, 512]) for _ in range(2)]
        t.dma("sp", "c0", ccol, c_col, writes=["ccol"])
        t.dma("sp", "c0", badac, bada_col, writes=["badac"])
        t.dma("sp", "c0", nwc, normw_col, writes=["nwc"])
        t.dma("pool", "c1", bgate, bada_gate.partition_broadcast(128), writes=["bgate"])
        t.op("act", lambda: A_.activation(out=cact, in_=ccol, func=AF.Silu), reads=["ccol"], writes=["cact"])
        for k in range(8):
            t.op("dve", lambda k=k: V.tensor_copy(out=cbc[:, k, :], in_=cact[:, k:k + 1].broadcast_to([128, 128])),
                 reads=["cact"], writes=[f"cbc{k}"])
        wada_v = w_ada.rearrange("(k p) n -> p k n", p=128)
        for ch in range(6):
            slot = ch % 2
            t.dma("sp", f"wada{slot}", wada_sb[slot], wada_v[:, :, ch * 512:(ch + 1) * 512], writes=[f"wada{slot}"])
            if ch < 4:
                for jn in range(4):
                    col = ch * 4 + jn
                    for k in range(8):
                        t.op("pe", lambda k=k, jn=jn, col=col, slot=slot: P_.matmul(
                            P4[:, col:col + 1], lhsT=wada_sb[slot][:, k, jn * 128:(jn + 1) * 128],
                            rhs=cact[:, k:k + 1], start=(k == 0), stop=(k == 7)),
                            reads=[f"wada{slot}", "cact"], writes=["P4"])
            else:
                nb = ch - 4
                pt = P5 if nb == 0 else P6
                for k in range(8):
                    t.op("pe", lambda k=k, pt=pt, slot=slot: P_.matmul(
                        pt[:, :], lhsT=cbc[:, k, :], rhs=wada_sb[slot][:, k, :], start=(k == 0), stop=(k == 7)),
                        reads=[f"wada{slot}", f"cbc{k}"], writes=["P5" if nb == 0 else "P6"])
                t.op("dve", lambda nb=nb, pt=pt: V.tensor_tensor(out=gate_bc[:, nb * 512:(nb + 1) * 512], in0=pt[:, :],
                                                                in1=bgate[:, nb * 512:(nb + 1) * 512], op=ALU.add),
                     reads=["P5" if nb == 0 else "P6", "bgate"], writes=["gate_bc"])
        t.op("dve", lambda: V.tensor_tensor(out=modT, in0=P4[:, 0:16], in1=badac[:, 0:16], op=ALU.add),
             reads=["P4", "badac"], writes=["modT"])
        t.op("dve", lambda: V.tensor_copy(out=shiftT[:], in_=modT[:, 0:8]), reads=["modT"], writes=["shiftT"])
        t.op("dve", lambda: V.scalar_tensor_tensor(out=gT[:], in0=modT[:, 8:16], scalar=1.0, in1=nwc, op0=ALU.add, op1=ALU.mult),
             reads=["modT", "nwc"], writes=["gT"])
        t.barrier()

        cv = Carver()
        xin = [cv.take([128, D]) for _ in range(3)]
        xnb = [cv.take([128, D], BF16) for _ in range(2)]
        xtmp = [cv.take([128, D]) for _ in range(2)]
        sq_junk = cv.take([128, D])
        PT_x = [P4, P5]

        def xT_tile(ht):
            if 2 <= ht < 18:
                return xT[:, :, (ht - 2) * 128:(ht - 1) * 128]
            hi = ht if ht < 2 else ht - 18 + 2
            return xTh[:, :, hi * 128:(hi + 1) * 128]

        nstate = {"n": 0}
        NRS = 64
        ssq = cv.take([128, NRS])
        rstd = cv.take([128, NRS])

        def norm_tile(src, nr, dst, dkey):
            n = nstate["n"]
            nstate["n"] += 1
            s3, s2, c = n % 3, n % 2, n % NRS
            t.dma("sp", f"xin{s3}", xin[s3][0:nr, :], src, writes=[f"xin{s3}"])
            t.op("act", lambda: A_.activation(out=sq_junk[0:nr, :], in_=xin[s3][0:nr, :], func=AF.Square, accum_out=ssq[0:nr, c:c + 1]),
                 reads=[f"xin{s3}"], writes=["sq_junk", f"ssq{c}"])
            t.op("dve", lambda: V.tensor_scalar(out=rstd[0:nr, c:c + 1], in0=ssq[0:nr, c:c + 1], scalar1=1.0 / D, scalar2=EPS,
                                                op0=ALU.mult, op1=ALU.add), reads=[f"ssq{c}"], writes=[f"rstd{c}"])
            t.op("act", lambda: A_.activation(out=rstd[0:nr, c:c + 1], in_=rstd[0:nr, c:c + 1], func=AF.Sqrt),
                 reads=[f"rstd{c}"], writes=[f"rstd{c}"])
            t.op("dve", lambda: V.reciprocal(out=rstd[0:nr, c:c + 1], in_=rstd[0:nr, c:c + 1]), reads=[f"rstd{c}"], writes=[f"rstd{c}"])
            t.op("act", lambda: A_.activation(out=xnb[s2][0:nr, :], in_=xin[s3][0:nr, :], func=AF.Copy, scale=rstd[0:nr, c:c + 1]),
                 reads=[f"xin{s3}", f"rstd{c}"], writes=[f"xnb{s2}"])
            ptb = PT_x[s2].bitcast(BF16).rearrange("p (k f) -> p k f", k=8)
            pkey = "P4" if s2 == 0 else "P5"
            for k in range(8):
                t.op("pe", lambda k=k: P_.transpose(out=ptb[:, k, 0:nr], in_=xnb[s2][0:nr, k * 128:(k + 1) * 128], identity=identb[0:nr, 0:nr]),
                     reads=[f"xnb{s2}", "identb"], writes=[pkey])
            xt3 = xtmp[s2].rearrange("p (k f) -> p k f", k=8)
            t.op("dve", lambda: V.tensor_tensor(out=xt3[:, :, 0:nr], in0=ptb[:, :, 0:nr], in1=gT[:, :].unsqueeze(2).broadcast_to([128, 8, nr]), op=ALU.mult),
                 reads=[pkey, "gT"], writes=[f"xtmp{s2}"])
            t.op("pool", lambda: G_.tensor_tensor(out=dst, in0=xt3[:, :, 0:nr], in1=shiftT[:, :].unsqueeze(2).broadcast_to([128, 8, nr]), op=ALU.add),
                 reads=[f"xtmp{s2}", "shiftT"], writes=[dkey])

        for ht in range(HT):
            norm_tile(xh[ht * 128:(ht + 1) * 128, :], 128, xT_tile(ht), f"xT{ht}")

        pbanks = [(PA[:, 0:512], "PA0"), (PA[:, 512:1024], "PA1"), (PB[:, 0:512], "PB0"), (PB[:, 512:1024], "PB1")]
        pst = {"n": 0}

        def next_bank():
            b = pbanks[pst["n"] % 4]
            pst["n"] += 1
            return b

        win_v = w_in.rearrange("(k p) n -> p k n", p=128)
        wout_v = w_out.rearrange("(k p) n -> p k n", p=128)
        wstate = {"n": 0}

        def load_w(src_v, c0, ncols, slot):
            off = 0
            while off < ncols:
                n = min(256, ncols - off)
                s = wstate["n"] % 2
                wstate["n"] += 1
                t.dma("sp", f"wst{s}", wst[s][:, :, 0:n], src_v[:, :, c0 + off:c0 + off + n], writes=[f"wst{s}"])
                t.op("pool", lambda s=s, n=n, off=off: G_.tensor_copy(out=wbf[slot][:, :, off:off + n], in_=wst[s][:, :, 0:n]),
                     reads=[f"wst{s}"], writes=[f"wbf{slot}"])
                off += n

        def tok_blocks(with_halo):
            blks = []
            if with_halo:
                blks.append((xTh[:, :, 0:256], 256, 0))
            for b in range(4):
                blks.append((xT[:, :, b * 512:(b + 1) * 512], 512, 256 + b * 512))
            if with_halo:
                blks.append((xTh[:, :, 256:512], 256, 2304))
            return blks

        def xkeys(c0, n):
            return [f"xT{c0 // 128 + i}" for i in range(n // 128)]

        wFk = cv.take([128, 8, 512], BF16)
        wFv = cv.take([128, 8, 512], BF16)
        wFg = cv.take([128, 8, 16], BF16)
        xg = [cv.take([128, 8, 516], BF16) for _ in range(2)]
        preF2 = [cv.take([128, 516]) for _ in range(2)]
        accF2 = [cv.take([128, 512]) for _ in range(2)]
        kTF2 = [cv.take([128, 512], BF16) for _ in range(2)]
        xgh = [cv.take([128, 8, 4], BF16) for _ in range(2)]
        KtokF = cv.take([128, 4, 4, 128], BF16)
        Vf = cv.take([128, 4, 4, 129], BF16)
        Gf = cv.take([128, 4, 16])
        nlfF = cv.take([128, 4, 2, 4])
        tmpF = cv.take([128, 4, 2, 4])
        Wt = cv.take([128, 4, 2, 4])
        accg = cv.take([128, 2, 4])
        tmpa = cv.take([128, 2, 4])
        wVf = cv.take([128, 4, 2, 4, 129], BF16)
        bFv = cv.take([128, 512])
        bFg = cv.take([128, 16])
        ffl = cv.take([128, NG * 4])
        sgt = cv.take([128, 2, 128])
        print("phase F union words", cv.off)
        t.dma("sp", "c0", ffl, fflags, writes=["ffl"])
        t.dma("sp", "c0", sgt, tri2.rearrange("n p f -> p n f"), writes=["sgt"])
        t.dma("pool", "c1", bFg, b_gates.partition_broadcast(128), writes=["bFg"])
        for hh in range(ML_HEADS):
            c0 = NA_COLS + hh * MLW
            t.dma("pool", "c1", bFv[:, hh * 128:(hh + 1) * 128], bin_row[c0 + 256:c0 + 384].partition_broadcast(128), writes=["bFv"])
            for (dstw, cc) in ((wFk, c0 + 128), (wFv, c0 + 256)):
                sidx = wstate["n"] % 2
                wstate["n"] += 1
                t.dma("sp", f"wst{sidx}", wst[sidx][:, :, 0:128], win_v[:, :, cc:cc + 128], writes=[f"wst{sidx}"])
                t.op("pool", lambda sidx=sidx, dstw=dstw, hh=hh: G_.tensor_copy(out=dstw[:, :, hh * 128:(hh + 1) * 128], in_=wst[sidx][:, :, 0:128]),
                     reads=[f"wst{sidx}"], writes=["wF"])
        sidx = wstate["n"] % 2
        wstate["n"] += 1
        t.dma("sp", f"wst{sidx}", wst[sidx][:, :, 0:16], w_gates.rearrange("(k p) n -> p k n", p=128), writes=[f"wst{sidx}"])
        t.op("pool", lambda: G_.tensor_copy(out=wFg, in_=wst[sidx][:, :, 0:16]), reads=[f"wst{sidx}"], writes=["wF"])
        t.op("pool", lambda: G_.memset(Vf[:, :, :, 128:129], 1.0), writes=["Vfones"])
        t.op("pool", lambda: G_.memset(accg, 0.0), writes=["accg"])
        t.op("pool", lambda: G_.memset(Cacc[:], 0.0), writes=["Cacc"])
        Gv = Gf.rearrange("p t (d x h) -> p t d x h", d=2, x=2)
        SGT, SLT = 0, 1
        def norm_group(gi):
            xs = gi % 2
            norm_tile(xfh[gi * 4:gi * 4 + 4, :], 4, xgh[xs], f"xgh{xs}")
            for j in range(4):
                norm_tile(xf[(gi * 4 + j) * 128:(gi * 4 + j + 1) * 128, :], 128, xg[xs][:, :, 2 + j * 128:2 + (j + 1) * 128], f"xg{xs}")

        norm_group(0)
        for gi in range(NG):
            xs = gi % 2
            xga = xg[xs]
            for j in range(4):
                pb, pk = next_bank()
                for k in range(8):
                    t.op("pe", lambda k=k, j=j, pb=pb: P_.matmul(pb[:, :], lhsT=xga[:, k, 2 + j * 128:2 + (j + 1) * 128], rhs=wFv[:, k, :], start=(k == 0), stop=(k == 7)),
                         reads=["wF", f"xg{xs}"], writes=[pk])
                t.op("dve", lambda j=j, pb=pb: V.tensor_tensor(out=Vf[:, j, :, 0:128], in0=pb.rearrange("p (h d) -> p h d", h=4),
                                                              in1=bFv.rearrange("p (h d) -> p h d", h=4), op=ALU.add), reads=[pk, "bFv"], writes=["Vf"])
                for k in range(8):
                    t.op("pe", lambda k=k, j=j: P_.matmul(P6[:, j * 16:(j + 1) * 16], lhsT=xga[:, k, 2 + j * 128:2 + (j + 1) * 128], rhs=wFg[:, k, :], start=(k == 0), stop=(k == 7)),
                         reads=["wF", f"xg{xs}"], writes=["P6"])
            t.op("dve", lambda: V.tensor_tensor(out=Gf, in0=P6[:, 0:64].rearrange("p (t g) -> p t g", t=4), in1=bFg.unsqueeze(1).broadcast_to([128, 4, 16]), op=ALU.add),
                 reads=["P6", "bFg"], writes=["Gf"])
            t.op("act", lambda: A_.activation(out=nlfF, in_=Gv[:, :, :, 1, :], func=AF.Exp, scale=-1.0), reads=["Gf"], writes=["nlfF"])
            t.op("act", lambda: A_.activation(out=nlfF, in_=nlfF, func=AF.Ln, bias=kcol[:, 0:1], scale=1.0), reads=["nlfF", "kcol"], writes=["nlfF"])
            P7c = P7[:, 0:32].rearrange("p (t d h) -> p t d h", t=4, d=2)
            for j in range(4):
                for d in range(2):
                    others = [jj for jj in range(4) if (jj > j if d == 0 else jj < j)]
                    seq = [(j, sgt[:, SGT if d == 0 else SLT, :])] + [(jj, cst[:, ONES, :]) for jj in others]
                    for qi, (jj, lh) in enumerate(seq):
                        t.op("pe", lambda j=j, d=d, jj=jj, lh=lh, qi=qi, nq=len(seq): P_.matmul(P7c[:, j, d, :], lhsT=lh, rhs=nlfF[:, jj, d, :], start=(qi == 0), stop=(qi == nq - 1)),
                             reads=["nlfF", "sgt", "cst"], writes=["P7"])
            P7t = P7[:, 32:40].rearrange("p (d h) -> p d h", d=2)
            for d in range(2):
                for jj in range(4):
                    t.op("pe", lambda d=d, jj=jj: P_.matmul(P7t[:, d, :], lhsT=cst[:, ONES, :], rhs=nlfF[:, jj, d, :], start=(jj == 0), stop=(jj == 3)),
                         reads=["nlfF", "cst"], writes=["P7"])
            fl = ffl[:, gi * 4:gi * 4 + 2]
            t.op("dve", lambda: V.tensor_tensor(out=tmpF, in0=P7c, in1=accg.unsqueeze(1).broadcast_to([128, 4, 2, 4]), op=ALU.add), reads=["P7", "accg"], writes=["tmpF"])
            t.op("dve", lambda: V.tensor_tensor(out=tmpF, in0=Gv[:, :, :, 0, :], in1=tmpF, op=ALU.subtract), reads=["Gf", "tmpF"], writes=["tmpF"])
            t.op("act", lambda: A_.activation(out=Wt, in_=tmpF, func=AF.Exp), reads=["tmpF"], writes=["Wt"])
            t.op("dve", lambda fl=fl: V.tensor_tensor(out=Wt, in0=Wt, in1=fl.unsqueeze(1).unsqueeze(3).broadcast_to([128, 4, 2, 4]), op=ALU.mult), reads=["Wt", "ffl"], writes=["Wt"])
            t.op("dve", lambda fl=fl: V.tensor_tensor(out=tmpa, in0=P7t, in1=fl.unsqueeze(2).broadcast_to([128, 2, 4]), op=ALU.mult), reads=["P7", "ffl"], writes=["tmpa"])
            t.op("dve", lambda: V.tensor_tensor(out=accg, in0=accg, in1=tmpa, op=ALU.add), reads=["accg", "tmpa"], writes=["accg"])
            for d in range(2):
                eng = "dve" if d == 0 else "pool"
                E = V if d == 0 else G_
                t.op(eng, lambda E=E, d=d: E.tensor_tensor(out=wVf[:, :, d, :, :], in0=Vf, in1=Wt[:, :, d, :].unsqueeze(3).broadcast_to([128, 4, 4, 129]), op=ALU.mult),
                     reads=["Vf", "Vfones", "Wt"], writes=[f"wVf{d}"])
            if gi + 1 < NG:
                norm_group(gi + 1)
            def kproj(hh):
                preF = preF2[hh % 2]
                pk_ = f"preF{hh % 2}"
                pb, pk = next_bank()
                for k in range(8):
                    t.op("pe", lambda k=k: P_.matmul(pb[:, 0:512], lhsT=wFk[:, k, hh * 128:(hh + 1) * 128], rhs=xga[:, k, 2:514], start=(k == 0), stop=(k == 7)),
                         reads=["wF", f"xg{xs}"], writes=[pk])
                t.op("act", lambda: A_.activation(out=preF[:, 2:514], in_=pb[:, 0:512], func=AF.Identity, scale=1.0, bias=bcol[:, 9 + 2 * hh:10 + 2 * hh]),
                     reads=[pk, "bcol"], writes=[pk_])
                pb2, pk2 = next_bank()
                for k in range(8):
                    t.op("pe", lambda k=k: P_.matmul(pb2[:, 0:4], lhsT=wFk[:, k, hh * 128:(hh + 1) * 128], rhs=xgh[xs][:, k, :], start=(k == 0), stop=(k == 7)),
                         reads=["wF", f"xgh{xs}"], writes=[pk2])
                t.op("act", lambda: A_.activation(out=preF[:, 0:2], in_=pb2[:, 0:2], func=AF.Identity, scale=1.0, bias=bcol[:, 9 + 2 * hh:10 + 2 * hh]),
                     reads=[pk2, "bcol"], writes=[pk_])
                t.op("act", lambda: A_.activation(out=preF[:, 514:516], in_=pb2[:, 2:4], func=AF.Identity, scale=1.0, bias=bcol[:, 9 + 2 * hh:10 + 2 * hh]),
                     reads=[pk2, "bcol"], writes=[pk_])

            kproj(0)
            for hh in range(ML_HEADS):
                if hh + 1 < ML_HEADS:
                    kproj(hh + 1)
                preF, accF, kTF = preF2[hh % 2], accF2[hh % 2], kTF2[hh % 2]
                pk_, ak_, kk_ = f"preF{hh % 2}", f"accF{hh % 2}", f"kTF{hh % 2}"
                t.op("dve", lambda: V.tensor_scalar(out=preF[:, 0:2], in0=preF[:, 0:2], scalar1=ffl[:, gi * 4 + 2:gi * 4 + 3], scalar2=None, op0=ALU.mult), reads=[pk_, "ffl"], writes=[pk_])
                t.op("dve", lambda: V.tensor_scalar(out=preF[:, 514:516], in0=preF[:, 514:516], scalar1=ffl[:, gi * 4 + 3:gi * 4 + 4], scalar2=None, op0=ALU.mult), reads=[pk_, "ffl"], writes=[pk_])
                gidx = 4 + hh
                t.op("dve", lambda: V.tensor_scalar(out=accF, in0=preF[:, 0:512], scalar1=cw[:, gidx, 0:1], scalar2=cb[:, gidx:gidx + 1], op0=ALU.mult, op1=ALU.add),
                     reads=[pk_, "cw", "cb"], writes=[ak_])
                for jc in range(1, 5):
                    t.op("dve", lambda jc=jc: V.scalar_tensor_tensor(out=accF, in0=preF[:, jc:jc + 512], scalar=cw[:, gidx, jc:jc + 1], in1=accF, op0=ALU.mult, op1=ALU.add),
                         reads=[pk_, "cw", ak_], writes=[ak_])
                t.op("act", lambda: A_.activation(out=kTF, in_=accF, func=AF.Silu), reads=[ak_], writes=[kk_])
                p4b = P4.bitcast(BF16).rearrange("p (k f) -> p k f", k=8)
                for j in range(4):
                    t.op("pe", lambda j=j: P_.transpose(out=p4b[:, j, :], in_=kTF[:, j * 128:(j + 1) * 128], identity=identb[:]), reads=[kk_, "identb"], writes=["P4"])
                t.op("act", lambda: A_.activation(out=KtokF[:, hh, :, :], in_=p4b[:, 0:4, :], func=AF.Copy), reads=["P4"], writes=[f"KtokF{hh}"])
                for d in range(2):
                    pb, pk = next_bank()
                    for j in range(4):
                        t.op("pe", lambda j=j: P_.matmul(pb[:, 0:129], lhsT=KtokF[:, hh, j, :], rhs=wVf[:, j, d, hh, :], start=(j == 0), stop=(j == 3)),
                             reads=[f"KtokF{hh}", f"wVf{d}"], writes=[pk])
                    t.op("dve", lambda: V.tensor_tensor(out=Cacc[:, d, hh, :], in0=Cacc[:, d, hh, :], in1=pb[:, 0:129], op=ALU.add),
                         reads=[pk, "Cacc"], writes=["Cacc"])
        dbg_dump("f_c", [(Cacc[:].rearrange("p a b c -> p (a b c)"), 8 * 129)])
        t.barrier()

        cv = Carver()
        KT = cv.take([128, 4, HT * 128], BF16)
        QT = cv.take([128, 4, TOK], BF16)
        Vaug = cv.take([128, HT, 8, 65], BF16)
        Ar = cv.take([128, 8, 14, 64], BF16)
        mcol = cv.take([128, 192])
        bias_bc = cv.take([128, 512])
        mark = cv.off
        cmk = cv.take([128, 14 * 64])
        rtmp = [cv.take([128, 14 * 64]) for _ in range(2)]
        t.dma("sp", "c0", mcol, mcol_d, writes=["mcol"])
        t.dma("sp", "c0", cmk, cmask, writes=["cmk"])
        for h in range(8):
            s = h % 2
            t.dma("sp", f"rtmp{s}", rtmp[s], rpbA[h], writes=[f"rtmp{s}"])
            t.op("act", lambda s=s: A_.activation(out=rtmp[s], in_=rtmp[s], func=AF.Exp), reads=[f"rtmp{s}"], writes=[f"rtmp{s}"])
            t.op("dve", lambda s=s, h=h: V.tensor_tensor(out=Ar[:, h, :, :].rearrange("p e c -> p (e c)"), in0=rtmp[s], in1=cmk, op=ALU.mult),
                 reads=[f"rtmp{s}", "cmk"], writes=[f"Ar{h}"])
        t.barrier()
        cv.off = mark
        expS = [cv.take([128, 6, 128]) for _ in range(2)]
        PTt = [cv.take([128, 6, 128], BF16) for _ in range(2)]
        sz_t = cv.take([128, 512])
        sz_b = cv.take([128, 512], BF16)
        rden = cv.take([128, 8])
        otmp = cv.take([128, 512])
        ob = cv.take([128, 512], BF16)
        print("NA union words", cv.off)
        t.op("pool", lambda: G_.memset(Vaug[:, :, :, 64:65], 1.0), writes=["Vones"])

        for which in range(2):
            slot = which
            load_w(win_v, which * 512, 512, slot)
            for g in range(4):
                for (xap, n, c0) in tok_blocks(with_halo=(which == 1)):
                    pb, pk = next_bank()
                    for k in range(8):
                        t.op("pe", lambda k=k, g=g, xap=xap, n=n, pb=pb, slot=slot: P_.matmul(
                            pb[:, 0:n], lhsT=wbf[slot][:, k, g * 128:(g + 1) * 128], rhs=xap[:, k, :], start=(k == 0), stop=(k == 7)),
                            reads=[f"wbf{slot}"] + xkeys(c0, n), writes=[pk])
                    if which == 0:
                        q0 = c0 - 256
                        t.op("act", lambda g=g, pb=pb, n=n, q0=q0: A_.activation(out=QT[:, g, q0:q0 + n], in_=pb[:, 0:n], func=AF.Identity,
                                                                                 scale=0.125, bias=bcolq[:, g:g + 1]),
                             reads=[pk, "bcolq"], writes=[f"QT{g}_{q0 // 128 + i}" for i in range(n // 128)])
                    else:
                        t.op("act", lambda g=g, pb=pb, n=n, c0=c0: A_.activation(out=KT[:, g, c0:c0 + n], in_=pb[:, 0:n], func=AF.Identity,
                                                                                 scale=1.0, bias=bcol[:, 4 + g:5 + g]),
                             reads=[pk, "bcol"], writes=[f"KT{g}_{c0 // 128 + i}" for i in range(n // 128)])
        load_w(win_v, 1024, 512, 0)
        t.dma("pool", "c1", bias_bc, bin_row[1024:1536].partition_broadcast(128), reads=[], writes=["bias_bc"])
        for ht in range(HT):
            pb, pk = next_bank()
            xa = xT_tile(ht)
            for k in range(8):
                t.op("pe", lambda k=k, xa=xa, pb=pb: P_.matmul(pb[:, :], lhsT=xa[:, k, :], rhs=wbf[0][:, k, :], start=(k == 0), stop=(k == 7)),
                     reads=["wbf0", f"xT{ht}"], writes=[pk])
            t.op("dve", lambda ht=ht, pb=pb: V.tensor_tensor(out=Vaug[:, ht, :, 0:64], in0=pb.rearrange("p (h d) -> p h d", h=8),
                                                            in1=bias_bc.rearrange("p (h d) -> p h d", h=8), op=ALU.add),
                 reads=[pk, "bias_bc"], writes=[f"V{ht}"])
        load_w(win_v, 1536, 512, 1)
        t.wait_keys("pool", ["bias_bc"])
        t.dma("pool", "c1", bias_bc, bin_row[1536:2048].partition_broadcast(128), reads=[], writes=["bias_bc"])

        for m in range(NT):
            base, nj = _na_base(m), _na_nj(m)
            qt = m
            for k in range(8):
                t.op("pe", lambda k=k, m=m: P_.matmul(P7[:, :], lhsT=xT[:, k, m * 128:(m + 1) * 128], rhs=wbf[1][:, k, :], start=(k == 0), stop=(k == 7)),
                     reads=["wbf1", f"xT{m + 2}"], writes=["P7"])
            t.op("dve", lambda: V.tensor_tensor(out=sz_t, in0=P7[:, :], in1=bias_bc, op=ALU.add), reads=["P7", "bias_bc"], writes=["sz_t"])
            t.op("act", lambda: A_.activation(out=sz_b, in_=sz_t, func=AF.Silu), reads=["sz_t"], writes=["sz_b"])
            def score_mm(h):
                g, hh = h // 2, h % 2
                sl = h % 2
                PS = PA if sl == 0 else PB
                pskeys = ["PA0", "PA1"] if sl == 0 else ["PB0", "PB1"]
                PS3 = PS.rearrange("p (j q) -> p j q", q=128)
                for j in range(nj):
                    kt = base + j
                    t.op("pe", lambda j=j, kt=kt: P_.matmul(
                        PS3[:, j, :], lhsT=KT[hh * 64:(hh + 1) * 64, g, kt * 128:(kt + 1) * 128],
                        rhs=QT[hh * 64:(hh + 1) * 64, g, qt * 128:(qt + 1) * 128], start=True, stop=True),
                        reads=[f"KT{g}_{kt}", f"QT{g}_{qt}"], writes=pskeys)

            score_mm(0)
            for h in range(8):
                if h + 1 < 8:
                    score_mm(h + 1)
                g, hh = h // 2, h % 2
                sl = h % 2
                PS = PA if sl == 0 else PB
                pskeys = ["PA0", "PA1"] if sl == 0 else ["PB0", "PB1"]
                PS3 = PS.rearrange("p (j q) -> p j q", q=128)
                t.op("act", lambda sl=sl, nj=nj, PS3=PS3: A_.activation(out=expS[sl][:, 0:nj, :], in_=PS3[:, 0:nj, :], func=AF.Exp),
                     reads=pskeys, writes=[f"expS{sl}"])
                eng = "dve" if h % 2 == 0 else "pool"
                E = V if eng == "dve" else G_
                interior = 2 <= m <= 13
                for j in range(nj):
                    dyi0 = 2 * (base - m + j) + 3
                    e0 = 13 - dyi0
                    if interior and 1 <= j <= 3:
                        t.op(eng, lambda E=E, sl=sl, j=j, h=h, e0=e0: E.tensor_tensor(
                            out=PTt[sl][:, j, :], in0=expS[sl][:, j, :], in1=Ar[:, h, e0:e0 + 2, :].rearrange("p e c -> p (e c)"), op=ALU.mult),
                            reads=[f"expS{sl}", f"Ar{h}"], writes=[f"PT{sl}"])
                    else:
                        for b in range(2):
                            mc = (m * 6 + j) * 2 + b
                            t.op("dve", lambda E=V, sl=sl, j=j, h=h, e0=e0, b=b, mc=mc: E.scalar_tensor_tensor(
                                out=PTt[sl][:, j, b * 64:(b + 1) * 64], in0=expS[sl][:, j, b * 64:(b + 1) * 64], scalar=mcol[:, mc:mc + 1],
                                in1=Ar[:, h, e0 + b, :], op0=ALU.mult, op1=ALU.mult),
                                reads=[f"expS{sl}", f"Ar{h}", "mcol"], writes=[f"PT{sl}"])
                PO = P4 if h < 4 else P5
                pok = "P4" if h < 4 else "P5"
                PO3 = PO[:, 0:260].rearrange("p (h d) -> p h d", d=65)
                for j in range(nj):
                    kt = base + j
                    t.op("pe", lambda j=j, kt=kt, h=h, sl=sl, PO3=PO3, nj=nj: P_.matmul(
                        PO3[:, h % 4, :], lhsT=PTt[sl][:, j, :], rhs=Vaug[:, kt, h, :], start=(j == 0), stop=(j == nj - 1)),
                        reads=[f"PT{sl}", f"V{kt}", "Vones"], writes=[pok])
            for half in range(2):
                PO = P4 if half == 0 else P5
                pok = "P4" if half == 0 else "P5"
                PO3 = PO[:, 0:260].rearrange("p (h d) -> p h d", d=65)
                t.op("dve", lambda half=half, PO3=PO3: V.reciprocal(out=rden[:, half * 4:(half + 1) * 4], in_=PO3[:, :, 64]),
                     reads=[pok], writes=["rden"])
                t.op("dve", lambda half=half, PO3=PO3: V.tensor_tensor(
                    out=otmp[:, half * 256:(half + 1) * 256].rearrange("p (h d) -> p h d", d=64), in0=PO3[:, :, 0:64],
                    in1=rden[:, half * 4:(half + 1) * 4].unsqueeze(2).broadcast_to([128, 4, 64]), op=ALU.mult),
                    reads=[pok, "rden"], writes=["otmp"])
            t.op("dve", lambda: V.tensor_tensor(out=ob, in0=otmp, in1=sz_b, op=ALU.mult), reads=["otmp", "sz_b"], writes=["ob"])
            p6b = P6.bitcast(BF16).rearrange("p (k f) -> p k f", k=8)
            for c4 in range(4):
                t.op("pe", lambda c4=c4, p6b=p6b: P_.transpose(out=p6b[:, c4, :], in_=ob[:, c4 * 128:(c4 + 1) * 128], identity=identb[:]),
                     reads=["ob", "identb"], writes=["P6"])
            t.op("act", lambda m=m, p6b=p6b: A_.activation(out=mixT[:, 0:4, m * 128:(m + 1) * 128], in_=p6b[:, 0:4, :], func=AF.Copy),
                 reads=["P6"], writes=[f"mixNA{m}"])
        t.barrier()

        if dbg == "na":
            cv = Carver()
            dtmp = cv.take([128, TOK])
            for c8 in range(4):
                t.op("dve", lambda c8=c8: V.tensor_copy(out=dtmp, in_=mixT[:, c8, :]), reads=[f"mixNA{m}" for m in range(NT)], writes=["dtmp"])
                t.dma("sp", "dbg", dbg_out[:, c8 * TOK:(c8 + 1) * TOK], dtmp, reads=["dtmp"], writes=["dbgout"])
            t.wait_keys("sp", ["dbgout"])
            nc.sync.wait_ge(t.dma_lanes["dbg"].sem, t.dma_lanes["dbg"].count)
            print("instructions", t.n_inst, "waits", t.n_wait)
            return nc


        cv = Carver()
        mixML = cv.take([128, 4, TOK], BF16)
        mqT = cv.take([128, TOK], BF16)
        mkT = cv.take([128, TOK], BF16)
        QsT = [cv.take([128, TOK], BF16) for _ in range(2)]
        Ktok = cv.take([128, NT, 128], BF16)
        Vg = cv.take([128, NT, 129], BF16)
        wV = [cv.take([128, NT, 129], BF16) for _ in range(2)]
        so = cv.take([128, NT, 128], BF16)
        szm = cv.take([128, NT, 128], BF16)
        hF = cv.take([128, NT, 128])
        Gt = cv.take([128, NT, 4])
        nlf = cv.take([128, NT, 2])
        gw = cv.take([128, NT, 4])
        ebl = cv.take([128, NT, 2, 2])
        bias_ml = cv.take([128, 388])
        mlnw_bc = cv.take([128, 512])
        Cst = [cv.take([128, 129]) for _ in range(2)]
        Cbf = [cv.take([128, 129], BF16) for _ in range(2)]
        small = cv.take([128, 64])
        mark = cv.off
        t.dma("pool", "c1", mlnw_bc, mlnw_row.partition_broadcast(128), writes=["mlnw_bc"])
        t.op("pool", lambda: G_.memset(Vg[:, :, 128:129], 1.0), writes=["Vgones"])
        PNs = [(P6, "P6"), (P7, "P7")]

        for h in range(ML_HEADS):
            c0 = NA_COLS + h * MLW
            cv.off = mark
            pre = cv.take([128, TOK + 4])
            acc = cv.take([128, TOK])
            load_w(win_v, c0, 256, 0)
            load_w(win_v, c0 + 256, 388, 1)
            t.dma("pool", "c1", bias_ml, bin_row[c0 + 256:c0 + 644].partition_broadcast(128), writes=["bias_ml"])
            for g in range(2):
                blks = [(xTh[:, :, 254:256], 2, 0, ["xT1"])]
                for b in range(4):
                    blks.append((xT[:, :, b * 512:(b + 1) * 512], 512, 2 + b * 512, [f"xT{2 + 4 * b + i}" for i in range(4)]))
                blks.append((xTh[:, :, 256:258], 2, 2050, ["xT18"]))
                for (xap, n, p0, xk) in blks:
                    pb, pk = next_bank()
                    for k in range(8):
                        t.op("pe", lambda k=k, g=g, xap=xap, n=n, pb=pb: P_.matmul(
                            pb[:, 0:n], lhsT=wbf[0][:, k, g * 128:(g + 1) * 128], rhs=xap[:, k, :], start=(k == 0), stop=(k == 7)),
                            reads=["wbf0"] + xk, writes=[pk])
                    t.op("act", lambda g=g, pb=pb, n=n, p0=p0, h=h: A_.activation(out=pre[:, p0:p0 + n], in_=pb[:, 0:n], func=AF.Identity,
                                                                             scale=1.0, bias=bcol[:, 8 + 2 * h + g:9 + 2 * h + g]),
                         reads=[pk, "bcol"], writes=["pre"])
                t.op("dve", lambda: V.tensor_scalar(out=pre[:, 0:2], in0=pre[:, 0:2], scalar1=flg[:, 0:1], scalar2=None, op0=ALU.mult),
                     reads=["pre", "flg"], writes=["pre"])
                t.op("dve", lambda: V.tensor_scalar(out=pre[:, TOK + 2:TOK + 4], in0=pre[:, TOK + 2:TOK + 4], scalar1=flg[:, 1:2], scalar2=None, op0=ALU.mult),
                     reads=["pre", "flg"], writes=["pre"])
                gi = g * 4 + h
                t.op("dve", lambda gi=gi: V.tensor_scalar(out=acc, in0=pre[:, 0:TOK], scalar1=cw[:, gi, 0:1], scalar2=cb[:, gi:gi + 1],
                                                          op0=ALU.mult, op1=ALU.add), reads=["pre", "cw", "cb"], writes=["acc"])
                for j in range(1, 5):
                    t.op("dve", lambda gi=gi, j=j: V.scalar_tensor_tensor(out=acc, in0=pre[:, j:j + TOK], scalar=cw[:, gi, j:j + 1], in1=acc,
                                                                          op0=ALU.mult, op1=ALU.add), reads=["pre", "cw", "acc"], writes=["acc"])
                dstT = mqT if g == 0 else mkT
                t.op("act", lambda dstT=dstT: A_.activation(out=dstT, in_=acc, func=AF.Silu), reads=["acc"], writes=["mqT" if g == 0 else "mkT"])
            t.barrier()
            if h == 0:
                dbg_dump("ml_a", [(mqT, TOK), (mkT, TOK)])
            cv.off = mark
            tmpv = [cv.take([128, 388]) for _ in range(2)]
            Rfb = [cv.take([128, 2, 128]) for _ in range(2)]
            ebt = [cv.take([128, 2, 128]) for _ in range(2)]
            tmp4 = [cv.take([128, 4]) for _ in range(2)]
            sdT = [cv.take([128, 128], BF16) for _ in range(2)]
            hm = [cv.take([128, 128]) for _ in range(2)]
            hjunk = cv.take([128, 128])
            hB = cv.take([128, NT, 128])
            mo = [cv.take([128, 128], BF16) for _ in range(2)]
            st = [cv.take([128, 8]) for _ in range(2)]
            for tl in range(NT):
                pb, pk = next_bank()
                s2 = tl % 2
                for k in range(8):
                    t.op("pe", lambda k=k, tl=tl, pb=pb: P_.matmul(pb[:, 0:388], lhsT=xT[:, k, tl * 128:(tl + 1) * 128], rhs=wbf[1][:, k, 0:388],
                                                                  start=(k == 0), stop=(k == 7)), reads=["wbf1", f"xT{tl + 2}"], writes=[pk])
                t.op("dve", lambda pb=pb, s2=s2: V.tensor_tensor(out=tmpv[s2], in0=pb[:, 0:388], in1=bias_ml, op=ALU.add),
                     reads=[pk, "bias_ml"], writes=[f"tmpv{s2}"])
                t.op("pool", lambda tl=tl, s2=s2: G_.tensor_copy(out=Vg[:, tl, 0:128], in_=tmpv[s2][:, 0:128]), reads=[f"tmpv{s2}"], writes=[f"Vg{tl}"])
                t.op("act", lambda tl=tl, s2=s2: A_.activation(out=so[:, tl, :], in_=tmpv[s2][:, 128:256], func=AF.Sigmoid), reads=[f"tmpv{s2}"], writes=[f"so{tl}"])
                t.op("act", lambda tl=tl, s2=s2: A_.activation(out=szm[:, tl, :], in_=tmpv[s2][:, 256:384], func=AF.Silu), reads=[f"tmpv{s2}"], writes=[f"szm{tl}"])
                t.op("pool", lambda tl=tl, s2=s2: G_.tensor_copy(out=Gt[:, tl, :], in_=tmpv[s2][:, 384:388]), reads=[f"tmpv{s2}"], writes=["Gt"])
            Gf = Gt.rearrange("p t (d two) -> p t d two", two=2)[:, :, :, 1]
            Gi = Gt.rearrange("p t (d two) -> p t d two", two=2)[:, :, :, 0]
            t.op("act", lambda: A_.activation(out=nlf, in_=Gf, func=AF.Exp, scale=-1.0), reads=["Gt"], writes=["nlf"])
            t.op("act", lambda: A_.activation(out=nlf, in_=nlf, func=AF.Ln, bias=kcol[:, 0:1], scale=1.0), reads=["nlf", "kcol"], writes=["nlf"])
            for tl in range(NT):
                s2 = tl % 2
                t.op("pe", lambda tl=tl: P_.matmul(P6[:, 0:1], lhsT=cst[:, TRIF, :], rhs=nlf[:, tl, 0:1], start=True, stop=True), reads=["cst", "nlf"], writes=["P6"])
                t.op("pe", lambda tl=tl: P_.matmul(P6[:, 1:2], lhsT=cst[:, TRIB, :], rhs=nlf[:, tl, 1:2], start=True, stop=True), reads=["cst", "nlf"], writes=["P6"])
                t.op("pe", lambda tl=tl: P_.matmul(P6[:, 2:4], lhsT=cst[:, BLK, :], rhs=nlf[:, tl, 0:2], start=True, stop=True), reads=["cst", "nlf"], writes=["P6"])
                t.op("dve", lambda tl=tl, s2=s2: V.tensor_tensor(out=tmp4[s2][:, 0:2], in0=Gi[:, tl, :], in1=P6[:, 0:2], op=ALU.add),
                     reads=["Gt", "P6"], writes=[f"tmp4{s2}"])
                t.op("dve", lambda s2=s2: V.tensor_tensor(out=tmp4[s2][:, 2:4], in0=tmp4[s2][:, 0:2], in1=P6[:, 2:4], op=ALU.subtract),
                     reads=["P6", f"tmp4{s2}"], writes=[f"tmp4{s2}"])
                t.op("act", lambda tl=tl, s2=s2: A_.activation(out=gw[:, tl, :], in_=tmp4[s2], func=AF.Exp), reads=[f"tmp4{s2}"], writes=[f"gw{tl}"])
                t.op("dve", lambda tl=tl, s2=s2: V.tensor_scalar(out=Rfb[s2][:, 0, :], in0=cst[:, TRIF, :], scalar1=nlf[:, tl, 0:1], scalar2=None, op0=ALU.mult),
                     reads=["cst", "nlf"], writes=[f"Rfb{s2}"])
                t.op("dve", lambda tl=tl, s2=s2: V.tensor_scalar(out=Rfb[s2][:, 1, :], in0=cst[:, TRIB, :], scalar1=nlf[:, tl, 1:2], scalar2=None, op0=ALU.mult),
                     reads=["cst", "nlf"], writes=[f"Rfb{s2}"])
                t.op("pe", lambda s2=s2: P_.matmul(P7[:, 0:256], lhsT=cst[:, ONES, :], rhs=Rfb[s2].rearrange("p d t -> p (d t)"), start=True, stop=True),
                     reads=["cst", f"Rfb{s2}"], writes=["P7"])
                P7v = P7[:, 0:256].rearrange("p (d t) -> p d t", d=2)
                t.op("act", lambda s2=s2, P7v=P7v: A_.activation(out=ebt[s2], in_=P7v, func=AF.Exp, scale=-1.0, bias=kcol[:, 1:2]),
                     reads=["P7", "kcol"], writes=[f"ebt{s2}"])
                t.op("act", lambda tl=tl: A_.activation(out=ebl[:, tl, 0, :], in_=P7[:, 63:128:64], func=AF.Exp, scale=-1.0), reads=["P7"], writes=[f"ebl{tl}"])
                t.op("act", lambda tl=tl: A_.activation(out=ebl[:, tl, 1, :], in_=P7[:, 128:256:64], func=AF.Exp, scale=-1.0), reads=["P7"], writes=[f"ebl{tl}"])
                for d in range(2):
                    eng = "dve" if d == 0 else "pool"
                    E = V if d == 0 else G_
                    t.op(eng, lambda E=E, d=d, tl=tl, s2=s2: E.tensor_tensor(out=QsT[d][:, tl * 128:(tl + 1) * 128], in0=mqT[:, tl * 128:(tl + 1) * 128],
                                                                           in1=ebt[s2][:, d, :], op=ALU.mult), reads=["mqT", f"ebt{s2}"], writes=[f"Qs{d}_{tl}"])
                    t.op("pool", lambda d=d, tl=tl: G_.tensor_scalar(out=wV[d][:, tl, :], in0=Vg[:, tl, :], scalar1=gw[:, tl, 2 + d:3 + d], scalar2=None, op0=ALU.mult),
                         reads=[f"Vg{tl}", "Vgones", f"gw{tl}"], writes=[f"wV{d}_{tl}"])
                p4b = P4.bitcast(BF16)
                t.op("pe", lambda tl=tl, p4b=p4b, s2=s2: P_.transpose(out=p4b[:, s2 * 128:(s2 + 1) * 128], in_=mkT[:, tl * 128:(tl + 1) * 128], identity=identb[:]),
                     reads=["mkT", "identb"], writes=["P4"])
                t.op("act", lambda tl=tl, p4b=p4b, s2=s2: A_.activation(out=Ktok[:, tl, :], in_=p4b[:, s2 * 128:(s2 + 1) * 128], func=AF.Copy),
                     reads=["P4"], writes=[f"Ktok{tl}"])

            if h == 0:
                dbg_dump("ml_b", [(gw.rearrange("p a b -> p (a b)"), NT * 4), (ebl.rearrange("p a b c -> p (a b c)"), NT * 4),
                                  (nlf.rearrange("p a b -> p (a b)"), NT * 2), (QsT[0], TOK), (QsT[1], TOK),
                                  (Ktok.rearrange("p a b -> p (a b)"), TOK), (Vg.rearrange("p a b -> p (a b)"), NT * 129),
                                  (wV[0].rearrange("p a b -> p (a b)"), NT * 129)])
            def chunks(d):
                order = list(range(2 * NT))
                return order if d == 0 else order[::-1]

            def state_update(d, c, to_bf):
                tl, c2 = c // 2, c % 2
                pb, pk = next_bank()
                lo, hi = c2 * 64, c2 * 64 + 64
                t.op("pe", lambda: P_.matmul(pb[:, 0:129], lhsT=Ktok[lo:hi, tl, :], rhs=wV[d][lo:hi, tl, :], start=True, stop=True),
                     reads=[f"Ktok{tl}", f"wV{d}_{tl}"], writes=[pk])
                t.op("dve", lambda: V.scalar_tensor_tensor(out=Cst[d], in0=Cst[d], scalar=ebl[:, tl, d, c2:c2 + 1], in1=pb[:, 0:129],
                                                           op0=ALU.mult, op1=ALU.add), reads=[pk, f"ebl{tl}", f"C{d}"], writes=[f"C{d}"])
                if to_bf:
                    t.op("act", lambda: A_.activation(out=Cbf[d], in_=Cst[d], func=AF.Copy), reads=[f"C{d}"], writes=[f"Cb{d}"])

            for d in range(2):
                t.op("dve", lambda d=d, h=h: V.tensor_copy(out=Cst[d], in_=Cacc[:, d, h, :]), reads=["Cacc"], writes=[f"C{d}"])
                t.op("act", lambda d=d: A_.activation(out=Cbf[d], in_=Cst[d], func=AF.Copy), reads=[f"C{d}"], writes=[f"Cb{d}"])
            if h == 0:
                dbg_dump("ml_d", [(Cst[0], 129), (Cst[1], 129)])
            def scan_tile(d, tl):
                PN, pnk = PNs[d]
                pb, pk = next_bank()
                tok = slice(tl * 128, (tl + 1) * 128)
                t.op("pe", lambda: P_.matmul(pb[:, 0:128], lhsT=mkT[:, tok], rhs=QsT[d][:, tok], start=True, stop=True),
                     reads=["mkT", f"Qs{d}_{tl}"], writes=[pk])
                msk = cst[:, TRIF, :] if d == 0 else cst[:, TRIB, :]
                t.op("dve", lambda: V.scalar_tensor_tensor(out=sdT[d], in0=pb[:, 0:128], scalar=gw[:, tl, d:d + 1], in1=msk, op0=ALU.mult, op1=ALU.mult),
                     reads=[pk, f"gw{tl}", "cst"], writes=[f"sdT{d}"])
                t.op("pe", lambda: P_.matmul(PN[:, 0:129], lhsT=sdT[d], rhs=Vg[:, tl, :], start=True, stop=False),
                     reads=[f"sdT{d}", f"Vg{tl}", "Vgones"], writes=[pnk])
                order = (0, 1) if d == 0 else (1, 0)
                for ci, c2 in enumerate(order):
                    lo = tl * 128 + c2 * 64
                    t.op("pe", lambda c2=c2, lo=lo, ci=ci: P_.matmul(PN[c2 * 64:(c2 + 1) * 64, 0:129], lhsT=QsT[d][:, lo:lo + 64], rhs=Cbf[d],
                                                                 start=False, stop=True), reads=[f"Qs{d}_{tl}", f"Cb{d}"], writes=[pnk])
                    c = tl * 2 + c2
                    last = (c == (2 * NT - 1 if d == 0 else 0))
                    if not last:
                        state_update(d, c, True)
                s2 = d
                t.op("act", lambda: A_.activation(out=st[s2][:, 0:1], in_=PN[:, 128:129], func=AF.Abs), reads=[pnk], writes=[f"st{s2}"])
                t.op("dve", lambda: V.tensor_scalar(out=st[s2][:, 0:1], in0=st[s2][:, 0:1], scalar1=1.0, scalar2=None, op0=ALU.max),
                     reads=[f"st{s2}"], writes=[f"st{s2}"])
                t.op("dve", lambda: V.reciprocal(out=st[s2][:, 1:2], in_=st[s2][:, 0:1]), reads=[f"st{s2}"], writes=[f"st{s2}"])
                return PN, pnk

            def finalize(i):
                s2 = i % 2
                t.op("pool", lambda: G_.tensor_tensor(out=hm[s2], in0=hF[:, i, :], in1=hB[:, i, :], op=ALU.add),
                     reads=[f"hF{i}", f"hB{i}"], writes=[f"hm{s2}"])
                fin_rest(i, s2)

            def fin_rest(i, s2):
                t.op("dve", lambda i=i, s2=s2: V.tensor_tensor(out=hm[s2], in0=hm[s2], in1=so[:, i, :], op=ALU.mult), reads=[f"hm{s2}", f"so{i}"], writes=[f"hm{s2}"])
                t.op("act", lambda s2=s2: A_.activation(out=hjunk, in_=hm[s2], func=AF.Identity, accum_out=st[1][:, 2:3]), reads=[f"hm{s2}"], writes=["hjunk", "st1b"])
                t.op("act", lambda s2=s2: A_.activation(out=hjunk, in_=hm[s2], func=AF.Square, accum_out=st[1][:, 3:4]), reads=[f"hm{s2}"], writes=["hjunk", "st1c"])
                t.op("dve", lambda: V.tensor_scalar(out=st[1][:, 4:5], in0=st[1][:, 2:3], scalar1=1.0 / 128, scalar2=None, op0=ALU.mult), reads=["st1b"], writes=["st1d"])
                t.op("dve", lambda: V.tensor_tensor(out=st[1][:, 5:6], in0=st[1][:, 4:5], in1=st[1][:, 4:5], op=ALU.mult), reads=["st1d"], writes=["st1e"])
                t.op("dve", lambda: V.scalar_tensor_tensor(out=st[1][:, 6:7], in0=st[1][:, 3:4], scalar=1.0 / 128, in1=st[1][:, 5:6], op0=ALU.mult, op1=ALU.subtract),
                     reads=["st1c", "st1e"], writes=["st1f"])
                t.op("dve", lambda: V.tensor_scalar(out=st[1][:, 6:7], in0=st[1][:, 6:7], scalar1=EPS, scalar2=None, op0=ALU.add), reads=["st1f"], writes=["st1f"])
                t.op("act", lambda: A_.activation(out=st[1][:, 6:7], in_=st[1][:, 6:7], func=AF.Sqrt), reads=["st1f"], writes=["st1f"])
                t.op("dve", lambda: V.reciprocal(out=st[1][:, 7:8], in_=st[1][:, 6:7]), reads=["st1f"], writes=["st1g"])
                t.op("dve", lambda s2=s2: V.tensor_scalar(out=hm[s2], in0=hm[s2], scalar1=st[1][:, 4:5], scalar2=st[1][:, 7:8], op0=ALU.subtract, op1=ALU.mult),
                     reads=[f"hm{s2}", "st1d", "st1g"], writes=[f"hm{s2}"])
                t.op("pool", lambda s2=s2, h=h: G_.tensor_tensor(out=hm[s2], in0=hm[s2], in1=mlnw_bc[:, h * 128:(h + 1) * 128], op=ALU.mult),
                     reads=[f"hm{s2}", "mlnw_bc"], writes=[f"hm{s2}"])
                t.op("pool", lambda s2=s2, i=i: G_.tensor_tensor(out=mo[s2], in0=hm[s2], in1=szm[:, i, :], op=ALU.mult), reads=[f"hm{s2}", f"szm{i}"], writes=[f"mo{s2}"])
                p4b = P4.bitcast(BF16)
                t.op("pe", lambda s2=s2, p4b=p4b: P_.transpose(out=p4b[:, s2 * 128:(s2 + 1) * 128], in_=mo[s2], identity=identb[:]),
                     reads=[f"mo{s2}", "identb"], writes=["P4"])
                t.op("act", lambda s2=s2, p4b=p4b, i=i, h=h: A_.activation(out=mixML[:, h, i * 128:(i + 1) * 128], in_=p4b[:, s2 * 128:(s2 + 1) * 128], func=AF.Copy),
                     reads=["P4"], writes=[f"mixML{h}_{i}"])

            for i in range(NT):
                PN, pnk = scan_tile(0, i)
                t.op("dve", lambda PN=PN, i=i: V.tensor_scalar(out=hF[:, i, :], in0=PN[:, 0:128], scalar1=st[0][:, 1:2], scalar2=None, op0=ALU.mult),
                     reads=[pnk, "st0"], writes=[f"hF{i}"])
                j = NT - 1 - i
                PN, pnk = scan_tile(1, j)
                t.op("dve", lambda PN=PN, j=j: V.tensor_scalar(out=hB[:, j, :], in0=PN[:, 0:128], scalar1=st[1][:, 1:2], scalar2=None, op0=ALU.mult),
                     reads=[pnk, "st1"], writes=[f"hB{j}"])
                if i >= NT // 2:
                    finalize(i)
                    finalize(j)
            t.barrier()
            if h == 0:
                dbg_dump("ml_f", [(mixML[:, 0, :], TOK)])

        if dbg == "ml":
            cv.off = mark
            dtmp = cv.take([128, TOK])
            for c8 in range(4):
                t.op("dve", lambda c8=c8: V.tensor_copy(out=dtmp, in_=mixML[:, c8, :]), reads=[], writes=["dtmp"])
                t.dma("sp", "dbg", dbg_out[:, (4 + c8) * TOK:(5 + c8) * TOK], dtmp, reads=["dtmp"], writes=["dbgout"])
            nc.sync.wait_ge(t.dma_lanes["dbg"].sem, t.dma_lanes["dbg"].count)
            print("instructions", t.n_inst, "waits", t.n_wait)
            return nc

        cv.off = 4 * TOK // 2
        fnw_bc = cv.take([128, D])
        woutb = cv.take([128, 8, D], BF16)
        xres = [cv.take([128, D]) for _ in range(2)]
        hres = [cv.take([128, D]) for _ in range(2)]
        ojunk = cv.take([128, D])
        ost = cv.take([128, 2 * NT])
        t.dma("pool", "c1", fnw_bc, fnw_row.partition_broadcast(128), writes=["fnw_bc"])
        for q4 in range(4):
            s = wstate["n"] % 2
            wstate["n"] += 1
            t.dma("sp", f"wst{s}", wst[s][:, :, 0:256], wout_v[:, :, q4 * 256:(q4 + 1) * 256], writes=[f"wst{s}"])
            t.op("pool", lambda s=s, q4=q4: G_.tensor_copy(out=woutb[:, :, q4 * 256:(q4 + 1) * 256], in_=wst[s][:, :, 0:256]), reads=[f"wst{s}"], writes=["woutb"])
        for tl in range(NT):
            s2 = tl % 2
            t.dma("sp", f"xres{s2}", xres[s2], xh[(tl + 2) * 128:(tl + 3) * 128, :], writes=[f"xres{s2}"])
            for nb in range(2):
                pb, pk = next_bank()
                for mc in range(8):
                    src = mixT[:, mc, tl * 128:(tl + 1) * 128] if mc < 4 else mixML[:, mc - 4, tl * 128:(tl + 1) * 128]
                    t.op("pe", lambda mc=mc, src=src, pb=pb, nb=nb: P_.matmul(pb[:, :], lhsT=src, rhs=woutb[:, mc, nb * 512:(nb + 1) * 512], start=(mc == 0), stop=(mc == 7)),
                         reads=["woutb"], writes=[pk])
                t.op("dve", lambda pb=pb, nb=nb, s2=s2: V.tensor_tensor(out=hres[s2][:, nb * 512:(nb + 1) * 512], in0=pb[:, :], in1=gate_bc[:, nb * 512:(nb + 1) * 512], op=ALU.mult),
                     reads=[pk, "gate_bc"], writes=[f"hres{s2}_{nb}"])
                t.op("pool", lambda nb=nb, s2=s2: G_.tensor_tensor(out=hres[s2][:, nb * 512:(nb + 1) * 512], in0=hres[s2][:, nb * 512:(nb + 1) * 512],
                                                                 in1=xres[s2][:, nb * 512:(nb + 1) * 512], op=ALU.add),
                     reads=[f"hres{s2}_{nb}", f"xres{s2}"], writes=[f"hres{s2}_{nb}"])
            hk = [f"hres{s2}_0", f"hres{s2}_1"]
            t.op("act", lambda s2=s2, tl=tl: A_.activation(out=ojunk, in_=hres[s2], func=AF.Square, accum_out=ost[:, tl:tl + 1]), reads=hk, writes=["ojunk", f"ost{tl}"])
            t.op("dve", lambda tl=tl: V.tensor_scalar(out=ost[:, tl:tl + 1], in0=ost[:, tl:tl + 1], scalar1=1.0 / D, scalar2=EPS, op0=ALU.mult, op1=ALU.add),
                 reads=[f"ost{tl}"], writes=[f"ost{tl}"])
            t.op("act", lambda tl=tl: A_.activation(out=ost[:, tl:tl + 1], in_=ost[:, tl:tl + 1], func=AF.Sqrt), reads=[f"ost{tl}"], writes=[f"ost{tl}"])
            t.op("dve", lambda tl=tl: V.reciprocal(out=ost[:, NT + tl:NT + tl + 1], in_=ost[:, tl:tl + 1]), reads=[f"ost{tl}"], writes=[f"ost{tl}"])
            t.op("dve", lambda tl=tl, s2=s2: V.scalar_tensor_tensor(out=hres[s2], in0=hres[s2], scalar=ost[:, NT + tl:NT + tl + 1], in1=fnw_bc, op0=ALU.mult, op1=ALU.mult),
                 reads=hk + [f"ost{tl}", "fnw_bc"], writes=hk)
            t.dma("sp", "yout", y[tl * 128:(tl + 1) * 128, :], hres[s2], reads=hk, writes=["yout"])
        yl = t.dma_lanes["yout"]
        nc.sync.wait_ge(yl.sem, yl.count)
        print("instructions", t.n_inst, "waits", t.n_wait)

    return nc


def make_in_maps(x, c, w_ada, b_ada, norm_w, w_in, b_in, conv_w, conv_b, rpb, ml_norm_w, w_out, final_norm_w):
    f = lambda a: np.ascontiguousarray(np.asarray(a, dtype=np.float32))
    x = f(x)[0]
    perm = _col_perm()
    w_in_p = f(f(w_in)[0][:, perm])
    b_in_p = f(f(b_in)[0][perm])
    groups = [b_in_p[g * 128:(g + 1) * 128] for g in range(8)]
    for h in range(ML_HEADS):
        o = NA_COLS + h * MLW
        groups.append(b_in_p[o:o + 128])
        groups.append(b_in_p[o + 128:o + 256])
    bin_col = f(np.stack(groups, 1))
    col8 = lambda v: f(np.asarray(v, np.float32).reshape(-1, 128).T)
    rpbA, cmask = _rpb_tables(f(rpb)[0])
    cw = f(conv_w)[0]
    convw = f(cw.T.reshape(8, 128, 5).transpose(1, 0, 2))
    shared = {
        "w_ada": f(w_ada)[0], "w_in": w_in_p, "w_out": f(w_out)[0], "consts": _consts(),
        "c_col": col8(f(c)[0]), "bada_col": col8(f(b_ada)[0]), "bada_gate": f(f(b_ada)[0][2 * D:]),
        "normw_col": col8(f(norm_w)[0]), "fnw_row": f(final_norm_w), "mlnw_row": f(ml_norm_w)[0],
        "bin_col": bin_col, "bin_row": b_in_p, "convw": convw, "convb": col8(f(conv_b)[0]),
        "rpbA": f(rpbA.reshape(8, 128, 14 * 64)), "cmask": f(cmask.reshape(128, 14 * 64)),
        "tri2": _tri2(), "w_gates": f(f(w_in)[0][:, 4608:4624]), "b_gates": f(f(b_in)[0][4608:4624]),
    }
    in_maps = []
    for i in range(NCORES):
        xhh = np.zeros((HT * 128, D), np.float32)
        lo, hi = i * TOK - 256, i * TOK + TOK + 256
        slo, shi = max(lo, 0), min(hi, T)
        xhh[slo - lo:shi - lo] = x[slo:shi]
        fl = np.zeros((128, 18), np.float32)
        fl[:, 0] = 1.0 if i > 0 else 0.0
        fl[:, 1] = 1.0 if i < NCORES - 1 else 0.0
        for j in range(NCORES):
            fl[:, 2 + j] = 1.0 if j < i else 0.0
            fl[:, 10 + j] = 1.0 if j > i else 0.0
        gb = list(range(4 * i - 1, -1, -1))
        ga = list(range(4 * i + 4, T // 512))
        order = gb + ga
        assert len(order) == NG
        xf = np.concatenate([x[g * 512:(g + 1) * 512] for g in order], 0)
        xfh = np.zeros((NG * 4, D), np.float32)
        ffl = np.zeros((128, NG * 4), np.float32)
        for p, g in enumerate(order):
            if g > 0:
                xfh[p * 4:p * 4 + 2] = x[g * 512 - 2:g * 512]
                ffl[:, p * 4 + 2] = 1.0
            if g < T // 512 - 1:
                xfh[p * 4 + 2:p * 4 + 4] = x[(g + 1) * 512:(g + 1) * 512 + 2]
                ffl[:, p * 4 + 3] = 1.0
            ffl[:, p * 4 + 0] = 1.0 if g < 4 * i else 0.0
            ffl[:, p * 4 + 1] = 1.0 if g > 4 * i else 0.0
        m = dict(shared)
        m.update({"xh": xhh, "mcol": _mcol(i), "flags": fl, "xf": xf, "xfh": xfh, "fflags": ffl})
        in_maps.append(m)
    return in_maps


_NC_CACHE = {}


def kernel(**inputs):
    in_maps = make_in_maps(**inputs)
    if "nc" not in _NC_CACHE:
        _NC_CACHE["nc"] = build_program()
    res = run_bass_kernel_spmd(_NC_CACHE["nc"], in_maps, core_ids=list(range(NCORES)))
    out = np.concatenate([r["y"] for r in res.results], axis=0)
    return out.reshape(1, T, D).astype(np.float32)
```

```python
import numpy as np
from contextlib import ExitStack
import concourse.bass as bass
import concourse.mybir as mybir
from concourse.bass_utils import run_bass_kernel_spmd

F32 = mybir.dt.float32
BF16 = mybir.dt.bfloat16
AF = mybir.ActivationFunctionType
ALU = mybir.AluOpType

NCORES = 8
D = 1024
T = 16384
TOK = T // NCORES
NT = TOK // 128
HT = NT + 4
NA_HEADS = 8
ML_HEADS = 4
EPS = 1e-6
MLW = 644
NA_COLS = 2048
IN_WP = NA_COLS + ML_HEADS * MLW
UNI_WORDS = 23424
NG = 28


class _Stop(Exception):
    pass


class Lane:
    def __init__(self, nc, name):
        self.sem = nc.alloc_semaphore(name=name)
        self.count = 0
        self.name = name


class Trk:
    def __init__(self, nc):
        self.nc = nc
        self.engs = {"pe": nc.tensor, "act": nc.scalar, "dve": nc.vector, "pool": nc.gpsimd, "sp": nc.sync}
        self.lanes = {k: Lane(nc, "sem_" + k) for k in ("pe", "act", "dve", "pool")}
        self.seen = {k: {} for k in self.engs}
        self.last_w = {}
        self.reads = {}
        self.dma_lanes = {}
        self.n_inst = 0
        self.n_wait = 0

    def dma_lane(self, name):
        if name not in self.dma_lanes:
            self.dma_lanes[name] = Lane(self.nc, "dsem_" + name)
        return self.dma_lanes[name]

    def _wait(self, eng, lane, val):
        s = self.seen[eng]
        if s.get(lane.name, 0) >= val:
            return
        if eng == "pe" and lane is self.lanes["pe"]:
            return
        self.engs[eng].wait_ge(lane.sem, val)
        s[lane.name] = val
        self.n_wait += 1

    def _deps(self, eng, reads, writes):
        for k in reads:
            if k in self.last_w:
                self._wait(eng, *self.last_w[k])
        for k in writes:
            if k in self.last_w:
                self._wait(eng, *self.last_w[k])
            for (l, v) in self.reads.get(k, ()):
                self._wait(eng, l, v)

    def _record(self, lane, reads, writes):
        v = lane.count
        for k in reads:
            if k in writes:
                continue
            lst = self.reads.setdefault(k, [])
            lst[:] = [(l, x) for (l, x) in lst if l is not lane]
            lst.append((lane, v))
        for k in writes:
            self.last_w[k] = (lane, v)
            self.reads[k] = []

    def op(self, eng, fn, reads=(), writes=()):
        lane = self.lanes[eng]
        self._deps(eng, reads, writes)
        ins = fn()
        lane.count += 1
        ins.then_inc(lane.sem, 1)
        self._record(lane, reads, writes)
        self.n_inst += 1
        return ins

    def dma(self, q, lane_name, out, in_, reads=(), writes=(), **kw):
        if lane_name in ("c0", "c1") and q == "pool":
            self._plshot = getattr(self, "_plshot", 0) + 1
            lane_name = f"pl{self._plshot}"
        elif lane_name in ("c0", "c1"):
            self._oneshot = getattr(self, "_oneshot", 0) + 1
            lane_name = f"os{self._oneshot % 24}"
            lane = self.dma_lane(lane_name)
            if lane.count > 0:
                self._wait(q, lane, lane.count)
        lane = self.dma_lane(lane_name)
        self._deps(q, reads, writes)
        ins = self.engs[q].dma_start(out=out, in_=in_, **kw)
        lane.count += 16
        ins.then_inc(lane.sem, 16)
        self._record(lane, reads, writes)
        self.n_inst += 1
        return ins

    def wait_keys(self, eng, keys):
        for k in keys:
            if k in self.last_w:
                self._wait(eng, *self.last_w[k])
            for (l, v) in self.reads.get(k, ()):
                self._wait(eng, l, v)

    def barrier(self):
        all_lanes = list(self.lanes.values()) + list(self.dma_lanes.values())
        for eng in self.engs:
            for l in all_lanes:
                if l.count > 0:
                    self._wait(eng, l, l.count)


def _col_perm():
    NAW = 512
    cols = list(range(0, 4 * NAW))
    base_ml = 4 * NAW
    gate0 = 4 * NAW + 5 * 512
    for h in range(ML_HEADS):
        for blk in range(5):
            s = base_ml + blk * 512 + h * 128
            cols += list(range(s, s + 128))
        cols += [gate0 + 0 + h, gate0 + 4 + h, gate0 + 8 + h, gate0 + 12 + h]
    return np.array(cols, dtype=np.int64)


def _consts():
    s = np.arange(128)[:, None]
    t = np.arange(128)[None, :]
    same = (s // 64) == (t // 64)
    triF = (same & (s <= t)).astype(np.float32)
    triB = (same & (s >= t)).astype(np.float32)
    blk = same.astype(np.float32)
    ones = np.ones((128, 128), np.float32)
    ident = np.eye(128, dtype=np.float32)
    return np.stack([ident, triF, triB, blk, ones], 0)


def _tri2():
    s = np.arange(128)[:, None]
    t = np.arange(128)[None, :]
    return np.stack([(s > t).astype(np.float32), (s < t).astype(np.float32)], 0)


def _rpb_tables(rpb):
    p = np.arange(128)
    a = p // 64
    k = p % 64
    c = np.arange(64)
    e = np.arange(14)
    dyi = 13 - e
    dy = dyi[None, :] + a[:, None]
    dyv = (dy >= 0) & (dy <= 14)
    dx = np.clip(k[:, None] - c[None, :], -15, 15) + 15
    cs = np.clip(c - 8, 0, 48)
    cv = (k[:, None] >= cs[None, :]) & (k[:, None] < cs[None, :] + 16)
    valid = dyv[:, :, None] & cv[:, None, :]
    dyc = np.clip(dy, 0, 14)
    g = rpb[:, dyc[:, :, None], dx[:, None, :]]
    g = np.where(valid[None], g, np.float32(0.0)).astype(np.float32)
    return np.ascontiguousarray(g), valid.astype(np.float32)


def _na_base(m):
    return 14 if m == 15 else m


def _na_nj(m):
    return 6 if m in (0, 15) else 5


def _mcol(core):
    out = np.zeros((128, 16, 6, 2), np.float32)
    for m in range(16):
        for j in range(_na_nj(m)):
            for b in range(2):
                r = 32 * core + 2 * m + b
                start = min(max(r - 4, 0), 248)
                for a in range(2):
                    kr = 32 * core - 4 + 2 * (_na_base(m) + j) + a
                    ok = (start <= kr < start + 8)
                    out[a * 64:(a + 1) * 64, m, j, b] = 1.0 if ok else 0.0
    return out.reshape(128, 192)


def build_program(dbg=None):
    nc = bass.Bass("TRN2", target_bir_lowering=False)
    try:
        _build_body(nc, dbg)
    except _Stop:
        pass
    return nc


def _build_body(nc, dbg):

    def din(name, shape):
        return nc.dram_tensor(name, list(shape), F32, kind="ExternalInput").ap()

    xh = din("xh", [HT * 128, D])
    w_ada = din("w_ada", [D, 3 * D])
    w_in = din("w_in", [D, IN_WP])
    w_out = din("w_out", [D, D])
    consts = din("consts", [5, 128, 128])
    c_col = din("c_col", [128, 8])
    bada_col = din("bada_col", [128, 24])
    bada_gate = din("bada_gate", [D])
    normw_col = din("normw_col", [128, 8])
    fnw_row = din("fnw_row", [D])
    mlnw_row = din("mlnw_row", [512])
    bin_col = din("bin_col", [128, 16])
    bin_row = din("bin_row", [IN_WP])
    convw = din("convw", [128, 8, 5])
    convb = din("convb", [128, 8])
    rpbA = din("rpbA", [8, 128, 14 * 64])
    cmask = din("cmask", [128, 14 * 64])
    mcol_d = din("mcol", [128, 192])
    flags = din("flags", [128, 18])
    xf = din("xf", [NG * 512, D])
    xfh = din("xfh", [NG * 4, D])
    fflags = din("fflags", [128, NG * 4])
    tri2 = din("tri2", [2, 128, 128])
    w_gates = din("w_gates", [D, 16])
    b_gates = din("b_gates", [16])
    y = nc.dram_tensor("y", [TOK, D], F32, kind="ExternalOutput").ap()
    dbg_out = None
    if dbg:
        dbg_out = nc.dram_tensor("dbg", [128, 8 * TOK], F32, kind="ExternalOutput").ap()

    es = ExitStack()
    with es:
        def sb(name, shape, dt=F32):
            return es.enter_context(nc.sbuf_tensor(name, list(shape), dt))

        def ps(name, shape, dt=F32):
            return es.enter_context(nc.psum_tensor(name, list(shape), dt))

        t = Trk(nc)
        V, A_, P_, G_ = nc.vector, nc.scalar, nc.tensor, nc.gpsimd

        xT = sb("xT", [128, 8, TOK], BF16)
        xTh = sb("xTh", [128, 8, 512], BF16)
        mixT = sb("mixT", [128, 4, TOK], BF16)
        gate_bc = sb("gate_bc", [128, D])
        cst = sb("cst", [128, 5, 128])
        identb = sb("identb", [128, 128], BF16)
        gT = sb("gT", [128, 8])
        shiftT = sb("shiftT", [128, 8])
        bcol = sb("bcol", [128, 16])
        bcolq = sb("bcolq", [128, 4])
        flg = sb("flg", [128, 18])
        cw = sb("cw", [128, 8, 5])
        cb = sb("cb", [128, 8])
        kcol = sb("kcol", [128, 4])
        wst = [sb(f"wst{i}", [128, 8, 256]) for i in range(2)]
        wbf = [sb(f"wbf{i}", [128, 8, 512], BF16) for i in range(2)]
        Cacc = sb("Cacc", [128, 2, 4, 129])
        UNI = sb("UNI", [128, UNI_WORDS])

        PA = ps("PA", [128, 1024])
        PB = ps("PB", [128, 1024])
        P4 = ps("P4", [128, 512])
        P5 = ps("P5", [128, 512])
        P6 = ps("P6", [128, 512])
        P7 = ps("P7", [128, 512])
        IDENT, TRIF, TRIB, BLK, ONES = range(5)

        class Carver:
            def __init__(self):
                self.off = 0

            def take(self, shape, dt=F32):
                n = int(np.prod(shape[1:]))
                words = n if dt == F32 else (n + 1) // 2
                ap = UNI[:, self.off:self.off + words]
                self.off += words
                assert self.off <= UNI_WORDS, self.off
                if dt != F32:
                    ap = ap.bitcast(BF16)[:, 0:n]
                if len(shape) > 2:
                    names = " ".join(f"d{i}" for i in range(1, len(shape)))
                    kw = {f"d{i}": shape[i] for i in range(1, len(shape))}
                    ap = ap.rearrange(f"p ({names}) -> p {names}", **kw)
                return ap

        def dbg_dump(stage, items):
            if dbg != stage:
                return
            t.barrier()
            off = 0
            dt_ = UNI[:, UNI_WORDS - 2048:UNI_WORDS]
            for ap, n in items:
                o2 = 0
                while o2 < n:
                    w = min(2048, n - o2)
                    t.op("dve", lambda ap=ap, o2=o2, w=w: V.tensor_copy(out=dt_[:, 0:w], in_=ap[:, o2:o2 + w]), reads=[], writes=["dbgtmp"])
                    t.dma("sp", "dbg", dbg_out[:, off:off + w], dt_[:, 0:w], reads=["dbgtmp"], writes=["dbgout"])
                    off += w
                    o2 += w
            nc.sync.wait_ge(t.dma_lanes["dbg"].sem, t.dma_lanes["dbg"].count)
            print("dbg stop at", stage, "instructions", t.n_inst, "waits", t.n_wait)
            raise _Stop()

        t.dma("sp", "c0", cst[:], consts.rearrange("n p f -> p n f"), writes=["cst"])
        t.dma("sp", "c0", flg[:], flags, writes=["flg"])
        t.dma("sp", "c0", bcol[:], bin_col, writes=["bcol"])
        t.dma("sp", "c0", cw[:], convw, writes=["cw"])
        t.dma("sp", "c0", cb[:], convb, writes=["cb"])
        t.op("dve", lambda: V.tensor_copy(out=identb[:], in_=cst[:, IDENT, :]), reads=["cst"], writes=["identb"])
        t.op("pool", lambda: G_.memset(kcol[:, 0:1], 1.0), writes=["kcol"])
        t.op("pool", lambda: G_.memset(kcol[:, 1:2], float(np.log(128.0 ** -0.5))), writes=["kcol"])
        t.op("pool", lambda: G_.memset(kcol[:, 2:3], EPS), writes=["kcol"])
        t.op("pool", lambda: G_.memset(kcol[:, 3:4], 0.0), writes=["kcol"])
        t.op("dve", lambda: V.tensor_scalar(out=bcolq[:], in0=bcol[:, 0:4], scalar1=0.125, scalar2=None, op0=ALU.mult),
             reads=["bcol"], writes=["bcolq"])

        cv = Carver()
        ccol = cv.take([128, 8])
        cact = cv.take([128, 8])
        cbc = cv.take([128, 8, 128])
        badac = cv.take([128, 24])
        nwc = cv.take([128, 8])
        bgate = cv.take([128, D])
        modT = cv.take([128, 16])
        wada_sb = [cv.take([128, 8, 512]) for _ in range(2)]
        t.dma("sp", "c0", ccol, c_col, writes=["ccol"])
        t.dma("sp", "c0", badac, bada_col, writes=["badac"])
        t.dma("sp", "c0", nwc, normw_col, writes=["nwc"])
        t.dma("pool", "c1", bgate, bada_gate.partition_broadcast(128), writes=["bgate"])
        t.op("act", lambda: A_.activation(out=cact, in_=ccol, func=AF.Silu), reads=["ccol"], writes=["cact"])
        for k in range(8):
            t.op("dve", lambda k=k: V.tensor_copy(out=cbc[:, k, :], in_=cact[:, k:k + 1].broadcast_to([128, 128])),
                 reads=["cact"], writes=[f"cbc{k}"])
        wada_v = w_ada.rearrange("(k p) n -> p k n", p=128)
        for ch in range(6):
            slot = ch % 2
            t.dma("sp", f"wada{slot}", wada_sb[slot], wada_v[:, :, ch * 512:(ch + 1) * 512], writes=[f"wada{slot}"])
            if ch < 4:
                for jn in range(4):
                    col = ch * 4 + jn
                    for k in range(8):
                        t.op("pe", lambda k=k, jn=jn, col=col, slot=slot: P_.matmul(
                            P4[:, col:col + 1], lhsT=wada_sb[slot][:, k, jn * 128:(jn + 1) * 128],
                            rhs=cact[:, k:k + 1], start=(k == 0), stop=(k == 7)),
                            reads=[f"wada{slot}", "cact"], writes=["P4"])
            else:
                nb = ch - 4
                pt = P5 if nb == 0 else P6
                for k in range(8):
                    t.op("pe", lambda k=k, pt=pt, slot=slot: P_.matmul(
                        pt[:, :], lhsT=cbc[:, k, :], rhs=wada_sb[slot][:, k, :], start=(k == 0), stop=(k == 7)),
                        reads=[f"wada{slot}", f"cbc{k}"], writes=["P5" if nb == 0 else "P6"])
                t.op("dve", lambda nb=nb, pt=pt: V.tensor_tensor(out=gate_bc[:, nb * 512:(nb + 1) * 512], in0=pt[:, :],
                                                                in1=bgate[:, nb * 512:(nb + 1) * 512], op=ALU.add),
                     reads=["P5" if nb == 0 else "P6", "bgate"], writes=["gate_bc"])
        t.op("dve", lambda: V.tensor_tensor(out=modT, in0=P4[:, 0:16], in1=badac[:, 0:16], op=ALU.add),
             reads=["P4", "badac"], writes=["modT"])
        t.op("dve", lambda: V.tensor_copy(out=shiftT[:], in_=modT[:, 0:8]), reads=["modT"], writes=["shiftT"])
        t.op("dve", lambda: V.scalar_tensor_tensor(out=gT[:], in0=modT[:, 8:16], scalar=1.0, in1=nwc, op0=ALU.add, op1=ALU.mult),
             reads=["modT", "nwc"], writes=["gT"])
        t.barrier()

        cv = Carver()
        xin = [cv.take([128, D]) for _ in range(3)]
        xnb = [cv.take([128, D], BF16) for _ in range(2)]
        xtmp = [cv.take([128, D]) for _ in range(2)]
        sq_junk = cv.take([128, D])
        PT_x = [P4, P5]

        def xT_tile(ht):
            if 2 <= ht < 18:
                return xT[:, :, (ht - 2) * 128:(ht - 1) * 128]
            hi = ht if ht < 2 else ht - 18 + 2
            return xTh[:, :, hi * 128:(hi + 1) * 128]

        nstate = {"n": 0}
        NRS = 64
        ssq = cv.take([128, NRS])
        rstd = cv.take([128, NRS])

        def norm_tile(src, nr, dst, dkey):
            n = nstate["n"]
            nstate["n"] += 1
            s3, s2, c = n % 3, n % 2, n % NRS
            t.dma("sp", f"xin{s3}", xin[s3][0:nr, :], src, writes=[f"xin{s3}"])
            t.op("act", lambda: A_.activation(out=sq_junk[0:nr, :], in_=xin[s3][0:nr, :], func=AF.Square, accum_out=ssq[0:nr, c:c + 1]),
                 reads=[f"xin{s3}"], writes=["sq_junk", f"ssq{c}"])
            t.op("dve", lambda: V.tensor_scalar(out=rstd[0:nr, c:c + 1], in0=ssq[0:nr, c:c + 1], scalar1=1.0 / D, scalar2=EPS,
                                                op0=ALU.mult, op1=ALU.add), reads=[f"ssq{c}"], writes=[f"rstd{c}"])
            t.op("act", lambda: A_.activation(out=rstd[0:nr, c:c + 1], in_=rstd[0:nr, c:c + 1], func=AF.Sqrt),
                 reads=[f"rstd{c}"], writes=[f"rstd{c}"])
            t.op("dve", lambda: V.reciprocal(out=rstd[0:nr, c:c + 1], in_=rstd[0:nr, c:c + 1]), reads=[f"rstd{c}"], writes=[f"rstd{c}"])
            t.op("act", lambda: A_.activation(out=xnb[s2][0:nr, :], in_=xin[s3][0:nr, :], func=AF.Copy, scale=rstd[0:nr, c:c + 1]),
                 reads=[f"xin{s3}", f"rstd{c}"], writes=[f"xnb{s2}"])
            ptb = PT_x[s2].bitcast(BF16).rearrange("p (k f) -> p k f", k=8)
            pkey = "P4" if s2 == 0 else "P5"
            for k in range(8):
                t.op("pe", lambda k=k: P_.transpose(out=ptb[:, k, 0:nr], in_=xnb[s2][0:nr, k * 128:(k + 1) * 128], identity=identb[0:nr, 0:nr]),
                     reads=[f"xnb{s2}", "identb"], writes=[pkey])
            xt3 = xtmp[s2].rearrange("p (k f) -> p k f", k=8)
            t.op("dve", lambda: V.tensor_tensor(out=xt3[:, :, 0:nr], in0=ptb[:, :, 0:nr], in1=gT[:, :].unsqueeze(2).broadcast_to([128, 8, nr]), op=ALU.mult),
                 reads=[pkey, "gT"], writes=[f"xtmp{s2}"])
            t.op("pool", lambda: G_.tensor_tensor(out=dst, in0=xt3[:, :, 0:nr], in1=shiftT[:, :].unsqueeze(2).broadcast_to([128, 8, nr]), op=ALU.add),
                 reads=[f"xtmp{s2}", "shiftT"], writes=[dkey])

        for ht in range(HT):
            norm_tile(xh[ht * 128:(ht + 1) * 128, :], 128, xT_tile(ht), f"xT{ht}")

        pbanks = [(PA[:, 0:512], "PA0"), (PA[:, 512:1024], "PA1"), (PB[:, 0:512], "PB0"), (PB[:, 512:1024], "PB1")]
        pst = {"n": 0}

        def next_bank():
            b = pbanks[pst["n"] % 4]
            pst["n"] += 1
            return b

        win_v = w_in.rearrange("(k p) n -> p k n", p=128)
        wout_v = w_out.rearrange("(k p) n -> p k n", p=128)
        wstate = {"n": 0}

        def load_w(src_v, c0, ncols, slot):
            off = 0
            while off < ncols:
                n = min(256, ncols - off)
                s = wstate["n"] % 2
                wstate["n"] += 1
                t.dma("sp", f"wst{s}", wst[s][:, :, 0:n], src_v[:, :, c0 + off:c0 + off + n], writes=[f"wst{s}"])
                t.op("pool", lambda s=s, n=n, off=off: G_.tensor_copy(out=wbf[slot][:, :, off:off + n], in_=wst[s][:, :, 0:n]),
                     reads=[f"wst{s}"], writes=[f"wbf{slot}"])
                off += n

        def tok_blocks(with_halo):
            blks = []
            if with_halo:
                blks.append((xTh[:, :, 0:256], 256, 0))
            for b in range(4):
                blks.append((xT[:, :, b * 512:(b + 1) * 512], 512, 256 + b * 512))
            if with_halo:
                blks.append((xTh[:, :, 256:512], 256, 2304))
            return blks

        def xkeys(c0, n):
            return [f"xT{c0 // 128 + i}" for i in range(n // 128)]

        wFk = cv.take([128, 8, 512], BF16)
        wFv = cv.take([128, 8, 512], BF16)
        wFg = cv.take([128, 8, 16], BF16)
        xg = [cv.take([128, 8, 516], BF16) for _ in range(2)]
        preF2 = [cv.take([128, 516]) for _ in range(2)]
        accF2 = [cv.take([128, 512]) for _ in range(2)]
        kTF2 = [cv.take([128, 512], BF16) for _ in range(2)]
        xgh = [cv.take([128, 8, 4], BF16) for _ in range(2)]
        KtokF = cv.take([128, 4, 4, 128], BF16)
        Vf = cv.take([128, 4, 4, 129], BF16)
        Gf = cv.take([128, 4, 16])
        nlfF = cv.take([128, 4, 2, 4])
        tmpF = cv.take([128, 4, 2, 4])
        Wt = cv.take([128, 4, 2, 4])
        accg = cv.take([128, 2, 4])
        tmpa = cv.take([128, 2, 4])
        wVf = cv.take([128, 4, 2, 4, 129], BF16)
        bFv = cv.take([128, 512])
        bFg = cv.take([128, 16])
        ffl = cv.take([128, NG * 4])
        sgt = cv.take([128, 2, 128])
        print("phase F union words", cv.off)
        t.dma("sp", "c0", ffl, fflags, writes=["ffl"])
        t.dma("sp", "c0", sgt, tri2.rearrange("n p f -> p n f"), writes=["sgt"])
        t.dma("pool", "c1", bFg, b_gates.partition_broadcast(128), writes=["bFg"])
        for hh in range(ML_HEADS):
            c0 = NA_COLS + hh * MLW
            t.dma("pool", "c1", bFv[:, hh * 128:(hh + 1) * 128], bin_row[c0 + 256:c0 + 384].partition_broadcast(128), writes=["bFv"])
            for (dstw, cc) in ((wFk, c0 + 128), (wFv, c0 + 256)):
                sidx = wstate["n"] % 2
                wstate["n"] += 1
                t.dma("sp", f"wst{sidx}", wst[sidx][:, :, 0:128], win_v[:, :, cc:cc + 128], writes=[f"wst{sidx}"])
                t.op("pool", lambda sidx=sidx, dstw=dstw, hh=hh: G_.tensor_copy(out=dstw[:, :, hh * 128:(hh + 1) * 128], in_=wst[sidx][:, :, 0:128]),
                     reads=[f"wst{sidx}"], writes=["wF"])
        sidx = wstate["n"] % 2
        wstate["n"] += 1
        t.dma("sp", f"wst{sidx}", wst[sidx][:, :, 0:16], w_gates.rearrange("(k p) n -> p k n", p=128), writes=[f"wst{sidx}"])
        t.op("pool", lambda: G_.tensor_copy(out=wFg, in_=wst[sidx][:, :, 0:16]), reads=[f"wst{sidx}"], writes=["wF"])
        t.op("pool", lambda: G_.memset(Vf[:, :, :, 128:129], 1.0), writes=["Vfones"])
        t.op("pool", lambda: G_.memset(accg, 0.0), writes=["accg"])
        t.op("pool", lambda: G_.memset(Cacc[:], 0.0), writes=["Cacc"])
        Gv = Gf.rearrange("p t (d x h) -> p t d x h", d=2, x=2)
        SGT, SLT = 0, 1
        def norm_group(gi):
            xs = gi % 2
            norm_tile(xfh[gi * 4:gi * 4 + 4, :], 4, xgh[xs], f"xgh{xs}")
            for j in range(4):
                norm_tile(xf[(gi * 4 + j) * 128:(gi * 4 + j + 1) * 128, :], 128, xg[xs][:, :, 2 + j * 128:2 + (j + 1) * 128], f"xg{xs}")

        norm_group(0)
        for gi in range(NG):
            xs = gi % 2
            xga = xg[xs]
            for j in range(4):
                pb, pk = next_bank()
                for k in range(8):
                    t.op("pe", lambda k=k, j=j, pb=pb: P_.matmul(pb[:, :], lhsT=xga[:, k, 2 + j * 128:2 + (j + 1) * 128], rhs=wFv[:, k, :], start=(k == 0), stop=(k == 7)),
                         reads=["wF", f"xg{xs}"], writes=[pk])
                t.op("dve", lambda j=j, pb=pb: V.tensor_tensor(out=Vf[:, j, :, 0:128], in0=pb.rearrange("p (h d) -> p h d", h=4),
                                                              in1=bFv.rearrange("p (h d) -> p h d", h=4), op=ALU.add), reads=[pk, "bFv"], writes=["Vf"])
                for k in range(8):
                    t.op("pe", lambda k=k, j=j: P_.matmul(P6[:, j * 16:(j + 1) * 16], lhsT=xga[:, k, 2 + j * 128:2 + (j + 1) * 128], rhs=wFg[:, k, :], start=(k == 0), stop=(k == 7)),
                         reads=["wF", f"xg{xs}"], writes=["P6"])
            t.op("dve", lambda: V.tensor_tensor(out=Gf, in0=P6[:, 0:64].rearrange("p (t g) -> p t g", t=4), in1=bFg.unsqueeze(1).broadcast_to([128, 4, 16]), op=ALU.add),
                 reads=["P6", "bFg"], writes=["Gf"])
            t.op("act", lambda: A_.activation(out=nlfF, in_=Gv[:, :, :, 1, :], func=AF.Exp, scale=-1.0), reads=["Gf"], writes=["nlfF"])
            t.op("act", lambda: A_.activation(out=nlfF, in_=nlfF, func=AF.Ln, bias=kcol[:, 0:1], scale=1.0), reads=["nlfF", "kcol"], writes=["nlfF"])
            P7c = P7[:, 0:32].rearrange("p (t d h) -> p t d h", t=4, d=2)
            for j in range(4):
                for d in range(2):
                    others = [jj for jj in range(4) if (jj > j if d == 0 else jj < j)]
                    seq = [(j, sgt[:, SGT if d == 0 else SLT, :])] + [(jj, cst[:, ONES, :]) for jj in others]
                    for qi, (jj, lh) in enumerate(seq):
                        t.op("pe", lambda j=j, d=d, jj=jj, lh=lh, qi=qi, nq=len(seq): P_.matmul(P7c[:, j, d, :], lhsT=lh, rhs=nlfF[:, jj, d, :], start=(qi == 0), stop=(qi == nq - 1)),
                             reads=["nlfF", "sgt", "cst"], writes=["P7"])
            P7t = P7[:, 32:40].rearrange("p (d h) -> p d h", d=2)
            for d in range(2):
                for jj in range(4):
                    t.op("pe", lambda d=d, jj=jj: P_.matmul(P7t[:, d, :], lhsT=cst[:, ONES, :], rhs=nlfF[:, jj, d, :], start=(jj == 0), stop=(jj == 3)),
                         reads=["nlfF", "cst"], writes=["P7"])
            fl = ffl[:, gi * 4:gi * 4 + 2]
            t.op("dve", lambda: V.tensor_tensor(out=tmpF, in0=P7c, in1=accg.unsqueeze(1).broadcast_to([128, 4, 2, 4]), op=ALU.add), reads=["P7", "accg"], writes=["tmpF"])
            t.op("dve", lambda: V.tensor_tensor(out=tmpF, in0=Gv[:, :, :, 0, :], in1=tmpF, op=ALU.subtract), reads=["Gf", "tmpF"], writes=["tmpF"])
            t.op("act", lambda: A_.activation(out=Wt, in_=tmpF, func=AF.Exp), reads=["tmpF"], writes=["Wt"])
            t.op("dve", lambda fl=fl: V.tensor_tensor(out=Wt, in0=Wt, in1=fl.unsqueeze(1).unsqueeze(3).broadcast_to([128, 4, 2, 4]), op=ALU.mult), reads=["Wt", "ffl"], writes=["Wt"])
            t.op("dve", lambda fl=fl: V.tensor_tensor(out=tmpa, in0=P7t, in1=fl.unsqueeze(2).broadcast_to([128, 2, 4]), op=ALU.mult), reads=["P7", "ffl"], writes=["tmpa"])
            t.op("dve", lambda: V.tensor_tensor(out=accg, in0=accg, in1=tmpa, op=ALU.add), reads=["accg", "tmpa"], writes=["accg"])
            for d in range(2):
                eng = "dve" if d == 0 else "pool"
                E = V if d == 0 else G_
                t.op(eng, lambda E=E, d=d: E.tensor_tensor(out=wVf[:, :, d, :, :], in0=Vf, in1=Wt[:, :, d, :].unsqueeze(3).broadcast_to([128, 4, 4, 129]), op=ALU.mult),
                     reads=["Vf", "Vfones", "Wt"], writes=[f"wVf{d}"])
            if gi + 1 < NG:
                norm_group(gi + 1)
            def kproj(hh):
                preF = preF2[hh % 2]
                pk_ = f"preF{hh % 2}"
                pb, pk = next_bank()
                for k in range(8):
                    t.op("pe", lambda k=k: P_.matmul(pb[:, 0:512], lhsT=wFk[:, k, hh * 128:(hh + 1) * 128], rhs=xga[:, k, 2:514], start=(k == 0), stop=(k == 7)),
                         reads=["wF", f"xg{xs}"], writes=[pk])
                t.op("act", lambda: A_.activation(out=preF[:, 2:514], in_=pb[:, 0:512], func=AF.Identity, scale=1.0, bias=bcol[:, 9 + 2 * hh:10 + 2 * hh]),
                     reads=[pk, "bcol"], writes=[pk_])
                pb2, pk2 = next_bank()
                for k in range(8):
                    t.op("pe", lambda k=k: P_.matmul(pb2[:, 0:4], lhsT=wFk[:, k, hh * 128:(hh + 1) * 128], rhs=xgh[xs][:, k, :], start=(k == 0), stop=(k == 7)),
                         reads=["wF", f"xgh{xs}"], writes=[pk2])
                t.op("act", lambda: A_.activation(out=preF[:, 0:2], in_=pb2[:, 0:2], func=AF.Identity, scale=1.0, bias=bcol[:, 9 + 2 * hh:10 + 2 * hh]),
                     reads=[pk2, "bcol"], writes=[pk_])
                t.op("act", lambda: A_.activation(out=preF[:, 514:516], in_=pb2[:, 2:4], func=AF.Identity, scale=1.0, bias=bcol[:, 9 + 2 * hh:10 + 2 * hh]),
                     reads=[pk2, "bcol"], writes=[pk_])

            kproj(0)
            for hh in range(ML_HEADS):
                if hh + 1 < ML_HEADS:
                    kproj(hh + 1)
                preF, accF, kTF = preF2[hh % 2], accF2[hh % 2], kTF2[hh % 2]
                pk_, ak_, kk_ = f"preF{hh % 2}", f"accF{hh % 2}", f"kTF{hh % 2}"
                t.op("dve", lambda: V.tensor_scalar(out=preF[:, 0:2], in0=preF[:, 0:2], scalar1=ffl[:, gi * 4 + 2:gi * 4 + 3], scalar2=None, op0=ALU.mult), reads=[pk_, "ffl"], writes=[pk_])
                t.op("dve", lambda: V.tensor_scalar(out=preF[:, 514:516], in0=preF[:, 514:516], scalar1=ffl[:, gi * 4 + 3:gi * 4 + 4], scalar2=None, op0=ALU.mult), reads=[pk_, "ffl"], writes=[pk_])
                gidx = 4 + hh
                t.op("dve", lambda: V.tensor_scalar(out=accF, in0=preF[:, 0:512], scalar1=cw[:, gidx, 0:1], scalar2=cb[:, gidx:gidx + 1], op0=ALU.mult, op1=ALU.add),
                     reads=[pk_, "cw", "cb"], writes=[ak_])
                for jc in range(1, 5):
                    t.op("dve", lambda jc=jc: V.scalar_tensor_tensor(out=accF, in0=preF[:, jc:jc + 512], scalar=cw[:, gidx, jc:jc + 1], in1=accF, op0=ALU.mult, op1=ALU.add),
                         reads=[pk_, "cw", ak_], writes=[ak_])
                t.op("act", lambda: A_.activation(out=kTF, in_=accF, func=AF.Silu), reads=[ak_], writes=[kk_])
                p4b = P4.bitcast(BF16).rearrange("p (k f) -> p k f", k=8)
                for j in range(4):
                    t.op("pe", lambda j=j: P_.transpose(out=p4b[:, j, :], in_=kTF[:, j * 128:(j + 1) * 128], identity=identb[:]), reads=[kk_, "identb"], writes=["P4"])
                t.op("act", lambda: A_.activation(out=KtokF[:, hh, :, :], in_=p4b[:, 0:4, :], func=AF.Copy), reads=["P4"], writes=[f"KtokF{hh}"])
                for d in range(2):
                    pb, pk = next_bank()
                    for j in range(4):
                        t.op("pe", lambda j=j: P_.matmul(pb[:, 0:129], lhsT=KtokF[:, hh, j, :], rhs=wVf[:, j, d, hh, :], start=(j == 0), stop=(j == 3)),
                             reads=[f"KtokF{hh}", f"wVf{d}"], writes=[pk])
                    t.op("dve", lambda: V.tensor_tensor(out=Cacc[:, d, hh, :], in0=Cacc[:, d, hh, :], in1=pb[:, 0:129], op=ALU.add),
                         reads=[pk, "Cacc"], writes=["Cacc"])
        dbg_dump("f_c", [(Cacc[:].rearrange("p a b c -> p (a b c)"), 8 * 129)])
        t.barrier()

        cv = Carver()
        KT = cv.take([128, 4, HT * 128], BF16)
        QT = cv.take([128, 4, TOK], BF16)
        Vaug = cv.take([128, HT, 8, 65], BF16)
        Ar = cv.take([128, 8, 14, 64], BF16)
        mcol = cv.take([128, 192])
        bias_bc = cv.take([128, 512])
        mark = cv.off
        cmk = cv.take([128, 14 * 64])
        rtmp = [cv.take([128, 14 * 64]) for _ in range(2)]
        t.dma("sp", "c0", mcol, mcol_d, writes=["mcol"])
        t.dma("sp", "c0", cmk, cmask, writes=["cmk"])
        for h in range(8):
            s = h % 2
            t.dma("sp", f"rtmp{s}", rtmp[s], rpbA[h], writes=[f"rtmp{s}"])
            t.op("act", lambda s=s: A_.activation(out=rtmp[s], in_=rtmp[s], func=AF.Exp), reads=[f"rtmp{s}"], writes=[f"rtmp{s}"])
            t.op("dve", lambda s=s, h=h: V.tensor_tensor(out=Ar[:, h, :, :].rearrange("p e c -> p (e c)"), in0=rtmp[s], in1=cmk, op=ALU.mult),
                 reads=[f"rtmp{s}", "cmk"], writes=[f"Ar{h}"])
        t.barrier()
        cv.off = mark
        expS = [cv.take([128, 6, 128]) for _ in range(2)]
        PTt = [cv.take([128, 6, 128], BF16) for _ in range(2)]
        sz_t = cv.take([128, 512])
        sz_b = cv.take([128, 512], BF16)
        rden = cv.take([128, 8])
        otmp = cv.take([128, 512])
        ob = cv.take([128, 512], BF16)
        print("NA union words", cv.off)
        t.op("pool", lambda: G_.memset(Vaug[:, :, :, 64:65], 1.0), writes=["Vones"])

        for which in range(2):
            slot = which
            load_w(win_v, which * 512, 512, slot)
            for g in range(4):
                for (xap, n, c0) in tok_blocks(with_halo=(which == 1)):
                    pb, pk = next_bank()
                    for k in range(8):
                        t.op("pe", lambda k=k, g=g, xap=xap, n=n, pb=pb, slot=slot: P_.matmul(
                            pb[:, 0:n], lhsT=wbf[slot][:, k, g * 128:(g + 1) * 128], rhs=xap[:, k, :], start=(k == 0), stop=(k == 7)),
                            reads=[f"wbf{slot}"] + xkeys(c0, n), writes=[pk])
                    if which == 0:
                        q0 = c0 - 256
                        t.op("act", lambda g=g, pb=pb, n=n, q0=q0: A_.activation(out=QT[:, g, q0:q0 + n], in_=pb[:, 0:n], func=AF.Identity,
                                                                                 scale=0.125, bias=bcolq[:, g:g + 1]),
                             reads=[pk, "bcolq"], writes=[f"QT{g}_{q0 // 128 + i}" for i in range(n // 128)])
                    else:
                        t.op("act", lambda g=g, pb=pb, n=n, c0=c0: A_.activation(out=KT[:, g, c0:c0 + n], in_=pb[:, 0:n], func=AF.Identity,
                                                                                 scale=1.0, bias=bcol[:, 4 + g:5 + g]),
                             reads=[pk, "bcol"], writes=[f"KT{g}_{c0 // 128 + i}" for i in range(n // 128)])
        load_w(win_v, 1024, 512, 0)
        t.dma("pool", "c1", bias_bc, bin_row[1024:1536].partition_broadcast(128), reads=[], writes=["bias_bc"])
        for ht in range(HT):
            pb, pk = next_bank()
            xa = xT_tile(ht)
            for k in range(8):
                t.op("pe", lambda k=k, xa=xa, pb=pb: P_.matmul(pb[:, :], lhsT=xa[:, k, :], rhs=wbf[0][:, k, :], start=(k == 0), stop=(k == 7)),
                     reads=["wbf0", f"xT{ht}"], writes=[pk])
            t.op("dve", lambda ht=ht, pb=pb: V.tensor_tensor(out=Vaug[:, ht, :, 0:64], in0=pb.rearrange("p (h d) -> p h d", h=8),
                                                            in1=bias_bc.rearrange("p (h d) -> p h d", h=8), op=ALU.add),
                 reads=[pk, "bias_bc"], writes=[f"V{ht}"])
        load_w(win_v, 1536, 512, 1)
        t.wait_keys("pool", ["bias_bc"])
        t.dma("pool", "c1", bias_bc, bin_row[1536:2048].partition_broadcast(128), reads=[], writes=["bias_bc"])

        for m in range(NT):
            base, nj = _na_base(m), _na_nj(m)
            qt = m
            for k in range(8):
                t.op("pe", lambda k=k, m=m: P_.matmul(P7[:, :], lhsT=xT[:, k, m * 128:(m + 1) * 128], rhs=wbf[1][:, k, :], start=(k == 0), stop=(k == 7)),
                     reads=["wbf1", f"xT{m + 2}"], writes=["P7"])
            t.op("dve", lambda: V.tensor_tensor(out=sz_t, in0=P7[:, :], in1=bias_bc, op=ALU.add), reads=["P7", "bias_bc"], writes=["sz_t"])
            t.op("act", lambda: A_.activation(out=sz_b, in_=sz_t, func=AF.Silu), reads=["sz_t"], writes=["sz_b"])
            def score_mm(h):
                g, hh = h // 2, h % 2
                sl = h % 2
                PS = PA if sl == 0 else PB
                pskeys = ["PA0", "PA1"] if sl == 0 else ["PB0", "PB1"]
                PS3 = PS.rearrange("p (j q) -> p j q", q=128)
                for j in range(nj):
                    kt = base + j
                    t.op("pe", lambda j=j, kt=kt: P_.matmul(
                        PS3[:, j, :], lhsT=KT[hh * 64:(hh + 1) * 64, g, kt * 128:(kt + 1) * 128],
                        rhs=QT[hh * 64:(hh + 1) * 64, g, qt * 128:(qt + 1) * 128], start=True, stop=True),
                        reads=[f"KT{g}_{kt}", f"QT{g}_{qt}"], writes=pskeys)

            score_mm(0)
            for h in range(8):
                if h + 1 < 8:
                    score_mm(h + 1)
                g, hh = h // 2, h % 2
                sl = h % 2
                PS = PA if sl == 0 else PB
                pskeys = ["PA0", "PA1"] if sl == 0 else ["PB0", "PB1"]
                PS3 = PS.rearrange("p (j q) -> p j q", q=128)
                t.op("act", lambda sl=sl, nj=nj, PS3=PS3: A_.activation(out=expS[sl][:, 0:nj, :], in_=PS3[:, 0:nj, :], func=AF.Exp),
                     reads=pskeys, writes=[f"expS{sl}"])
                eng = "dve" if h % 2 == 0 else "pool"
                E = V if eng == "dve" else G_
                interior = 2 <= m <= 13
                for j in range(nj):
                    dyi0 = 2 * (base - m + j) + 3
                    e0 = 13 - dyi0
                    if interior and 1 <= j <= 3:
                        t.op(eng, lambda E=E, sl=sl, j=j, h=h, e0=e0: E.tensor_tensor(
                            out=PTt[sl][:, j, :], in0=expS[sl][:, j, :], in1=Ar[:, h, e0:e0 + 2, :].rearrange("p e c -> p (e c)"), op=ALU.mult),
                            reads=[f"expS{sl}", f"Ar{h}"], writes=[f"PT{sl}"])
                    else:
                        for b in range(2):
                            mc = (m * 6 + j) * 2 + b
                            t.op("dve", lambda E=V, sl=sl, j=j, h=h, e0=e0, b=b, mc=mc: E.scalar_tensor_tensor(
                                out=PTt[sl][:, j, b * 64:(b + 1) * 64], in0=expS[sl][:, j, b * 64:(b + 1) * 64], scalar=mcol[:, mc:mc + 1],
                                in1=Ar[:, h, e0 + b, :], op0=ALU.mult, op1=ALU.mult),
                                reads=[f"expS{sl}", f"Ar{h}", "mcol"], writes=[f"PT{sl}"])
                PO = P4 if h < 4 else P5
                pok = "P4" if h < 4 else "P5"
                PO3 = PO[:, 0:260].rearrange("p (h d) -> p h d", d=65)
                for j in range(nj):
                    kt = base + j
                    t.op("pe", lambda j=j, kt=kt, h=h, sl=sl, PO3=PO3, nj=nj: P_.matmul(
                        PO3[:, h % 4, :], lhsT=PTt[sl][:, j, :], rhs=Vaug[:, kt, h, :], start=(j == 0), stop=(j == nj - 1)),
                        reads=[f"PT{sl}", f"V{kt}", "Vones"], writes=[pok])
            for half in range(2):
                PO = P4 if half == 0 else P5
                pok = "P4" if half == 0 else "P5"
                PO3 = PO[:, 0:260].rearrange("p (h d) -> p h d", d=65)
                t.op("dve", lambda half=half, PO3=PO3: V.reciprocal(out=rden[:, half * 4:(half + 1) * 4], in_=PO3[:, :, 64]),
                     reads=[pok], writes=["rden"])
                t.op("dve", lambda half=half, PO3=PO3: V.tensor_tensor(
                    out=otmp[:, half * 256:(half + 1) * 256].rearrange("p (h d) -> p h d", d=64), in0=PO3[:, :, 0:64],
                    in1=rden[:, half * 4:(half + 1) * 4].unsqueeze(2).broadcast_to([128, 4, 64]), op=ALU.mult),
                    reads=[pok, "rden"], writes=["otmp"])
            t.op("dve", lambda: V.tensor_tensor(out=ob, in0=otmp, in1=sz_b, op=ALU.mult), reads=["otmp", "sz_b"], writes=["ob"])
            p6b = P6.bitcast(BF16).rearrange("p (k f) -> p k f", k=8)
            for c4 in range(4):
                t.op("pe", lambda c4=c4, p6b=p6b: P_.transpose(out=p6b[:, c4, :], in_=ob[:, c4 * 128:(c4 + 1) * 128], identity=identb[:]),
                     reads=["ob", "identb"], writes=["P6"])
            t.op("act", lambda m=m, p6b=p6b: A_.activation(out=mixT[:, 0:4, m * 128:(m + 1) * 128], in_=p6b[:, 0:4, :], func=AF.Copy),
                 reads=["P6"], writes=[f"mixNA{m}"])
        t.barrier()

        if dbg == "na":
            cv = Carver()
            dtmp = cv.take([128, TOK])
            for c8 in range(4):
                t.op("dve", lambda c8=c8: V.tensor_copy(out=dtmp, in_=mixT[:, c8, :]), reads=[f"mixNA{m}" for m in range(NT)], writes=["dtmp"])
                t.dma("sp", "dbg", dbg_out[:, c8 * TOK:(c8 + 1) * TOK], dtmp, reads=["dtmp"], writes=["dbgout"])
            t.wait_keys("sp", ["dbgout"])
            nc.sync.wait_ge(t.dma_lanes["dbg"].sem, t.dma_lanes["dbg"].count)
            print("instructions", t.n_inst, "waits", t.n_wait)
            return nc


        cv = Carver()
        mixML = cv.take([128, 4, TOK], BF16)
        mqT = cv.take([128, TOK], BF16)
        mkT = cv.take([128, TOK], BF16)
        QsT = [cv.take([128, TOK], BF16) for _ in range(2)]
        Ktok = cv.take([128, NT, 128], BF16)
        Vg = cv.take([128, NT, 129], BF16)
        wV = [cv.take([128, NT, 129], BF16) for _ in range(2)]
        so = cv.take([128, NT, 128], BF16)
        szm = cv.take([128, NT, 128], BF16)
        hF = cv.take([128, NT, 128])
        Gt = cv.take([128, NT, 4])
        nlf = cv.take([128, NT, 2])
        gw = cv.take([128, NT, 4])
        ebl = cv.take([128, NT, 2, 2])
        bias_ml = cv.take([128, 388])
        mlnw_bc = cv.take([128, 512])
        Cst = [cv.take([128, 129]) for _ in range(2)]
        Cbf = [cv.take([128, 129], BF16) for _ in range(2)]
        small = cv.take([128, 64])
        mark = cv.off
        t.dma("pool", "c1", mlnw_bc, mlnw_row.partition_broadcast(128), writes=["mlnw_bc"])
        t.op("pool", lambda: G_.memset(Vg[:, :, 128:129], 1.0), writes=["Vgones"])
        PNs = [(P6, "P6"), (P7, "P7")]

        for h in range(ML_HEADS):
            c0 = NA_COLS + h * MLW
            cv.off = mark
            pre = cv.take([128, TOK + 4])
            acc = cv.take([128, TOK])
            load_w(win_v, c0, 256, 0)
            load_w(win_v, c0 + 256, 388, 1)
            t.dma("pool", "c1", bias_ml, bin_row[c0 + 256:c0 + 644].partition_broadcast(128), writes=["bias_ml"])
            for g in range(2):
                blks = [(xTh[:, :, 254:256], 2, 0, ["xT1"])]
                for b in range(4):
                    blks.append((xT[:, :, b * 512:(b + 1) * 512], 512, 2 + b * 512, [f"xT{2 + 4 * b + i}" for i in range(4)]))
                blks.append((xTh[:, :, 256:258], 2, 2050, ["xT18"]))
                for (xap, n, p0, xk) in blks:
                    pb, pk = next_bank()
                    for k in range(8):
                        t.op("pe", lambda k=k, g=g, xap=xap, n=n, pb=pb: P_.matmul(
                            pb[:, 0:n], lhsT=wbf[0][:, k, g * 128:(g + 1) * 128], rhs=xap[:, k, :], start=(k == 0), stop=(k == 7)),
                            reads=["wbf0"] + xk, writes=[pk])
                    t.op("act", lambda g=g, pb=pb, n=n, p0=p0, h=h: A_.activation(out=pre[:, p0:p0 + n], in_=pb[:, 0:n], func=AF.Identity,
                                                                             scale=1.0, bias=bcol[:, 8 + 2 * h + g:9 + 2 * h + g]),
                         reads=[pk, "bcol"], writes=["pre"])
                t.op("dve", lambda: V.tensor_scalar(out=pre[:, 0:2], in0=pre[:, 0:2], scalar1=flg[:, 0:1], scalar2=None, op0=ALU.mult),
                     reads=["pre", "flg"], writes=["pre"])
                t.op("dve", lambda: V.tensor_scalar(out=pre[:, TOK + 2:TOK + 4], in0=pre[:, TOK + 2:TOK + 4], scalar1=flg[:, 1:2], scalar2=None, op0=ALU.mult),
                     reads=["pre", "flg"], writes=["pre"])
                gi = g * 4 + h
                t.op("dve", lambda gi=gi: V.tensor_scalar(out=acc, in0=pre[:, 0:TOK], scalar1=cw[:, gi, 0:1], scalar2=cb[:, gi:gi + 1],
                                                          op0=ALU.mult, op1=ALU.add), reads=["pre", "cw", "cb"], writes=["acc"])
                for j in range(1, 5):
                    t.op("dve", lambda gi=gi, j=j: V.scalar_tensor_tensor(out=acc, in0=pre[:, j:j + TOK], scalar=cw[:, gi, j:j + 1], in1=acc,
                                                                          op0=ALU.mult, op1=ALU.add), reads=["pre", "cw", "acc"], writes=["acc"])
                dstT = mqT if g == 0 else mkT
                t.op("act", lambda dstT=dstT: A_.activation(out=dstT, in_=acc, func=AF.Silu), reads=["acc"], writes=["mqT" if g == 0 else "mkT"])
            t.barrier()
            if h == 0:
                dbg_dump("ml_a", [(mqT, TOK), (mkT, TOK)])
            cv.off = mark
            tmpv = [cv.take([128, 388]) for _ in range(2)]
            Rfb = [cv.take([128, 2, 128]) for _ in range(2)]
            ebt = [cv.take([128, 2, 128]) for _ in range(2)]
            tmp4 = [cv.take([128, 4]) for _ in range(2)]
            sdT = [cv.take([128, 128], BF16) for _ in range(2)]
            hm = [cv.take([128, 128]) for _ in range(2)]
            hjunk = cv.take([128, 128])
            hB = cv.take([128, NT, 128])
            mo = [cv.take([128, 128], BF16) for _ in range(2)]
            st = [cv.take([128, 8]) for _ in range(2)]
            for tl in range(NT):
                pb, pk = next_bank()
                s2 = tl % 2
                for k in range(8):
                    t.op("pe", lambda k=k, tl=tl, pb=pb: P_.matmul(pb[:, 0:388], lhsT=xT[:, k, tl * 128:(tl + 1) * 128], rhs=wbf[1][:, k, 0:388],
                                                                  start=(k == 0), stop=(k == 7)), reads=["wbf1", f"xT{tl + 2}"], writes=[pk])
                t.op("dve", lambda pb=pb, s2=s2: V.tensor_tensor(out=tmpv[s2], in0=pb[:, 0:388], in1=bias_ml, op=ALU.add),
                     reads=[pk, "bias_ml"], writes=[f"tmpv{s2}"])
                t.op("pool", lambda tl=tl, s2=s2: G_.tensor_copy(out=Vg[:, tl, 0:128], in_=tmpv[s2][:, 0:128]), reads=[f"tmpv{s2}"], writes=[f"Vg{tl}"])
                t.op("act", lambda tl=tl, s2=s2: A_.activation(out=so[:, tl, :], in_=tmpv[s2][:, 128:256], func=AF.Sigmoid), reads=[f"tmpv{s2}"], writes=[f"so{tl}"])
                t.op("act", lambda tl=tl, s2=s2: A_.activation(out=szm[:, tl, :], in_=tmpv[s2][:, 256:384], func=AF.Silu), reads=[f"tmpv{s2}"], writes=[f"szm{tl}"])
                t.op("pool", lambda tl=tl, s2=s2: G_.tensor_copy(out=Gt[:, tl, :], in_=tmpv[s2][:, 384:388]), reads=[f"tmpv{s2}"], writes=["Gt"])
            Gf = Gt.rearrange("p t (d two) -> p t d two", two=2)[:, :, :, 1]
            Gi = Gt.rearrange("p t (d two) -> p t d two", two=2)[:, :, :, 0]
            t.op("act", lambda: A_.activation(out=nlf, in_=Gf, func=AF.Exp, scale=-1.0), reads=["Gt"], writes=["nlf"])
            t.op("act", lambda: A_.activation(out=nlf, in_=nlf, func=AF.Ln, bias=kcol[:, 0:1], scale=1.0), reads=["nlf", "kcol"], writes=["nlf"])
            for tl in range(NT):
                s2 = tl % 2
                t.op("pe", lambda tl=tl: P_.matmul(P6[:, 0:1], lhsT=cst[:, TRIF, :], rhs=nlf[:, tl, 0:1], start=True, stop=True), reads=["cst", "nlf"], writes=["P6"])
                t.op("pe", lambda tl=tl: P_.matmul(P6[:, 1:2], lhsT=cst[:, TRIB, :], rhs=nlf[:, tl, 1:2], start=True, stop=True), reads=["cst", "nlf"], writes=["P6"])
                t.op("pe", lambda tl=tl: P_.matmul(P6[:, 2:4], lhsT=cst[:, BLK, :], rhs=nlf[:, tl, 0:2], start=True, stop=True), reads=["cst", "nlf"], writes=["P6"])
                t.op("dve", lambda tl=tl, s2=s2: V.tensor_tensor(out=tmp4[s2][:, 0:2], in0=Gi[:, tl, :], in1=P6[:, 0:2], op=ALU.add),
                     reads=["Gt", "P6"], writes=[f"tmp4{s2}"])
                t.op("dve", lambda s2=s2: V.tensor_tensor(out=tmp4[s2][:, 2:4], in0=tmp4[s2][:, 0:2], in1=P6[:, 2:4], op=ALU.subtract),
                     reads=["P6", f"tmp4{s2}"], writes=[f"tmp4{s2}"])
                t.op("act", lambda tl=tl, s2=s2: A_.activation(out=gw[:, tl, :], in_=tmp4[s2], func=AF.Exp), reads=[f"tmp4{s2}"], writes=[f"gw{tl}"])
                t.op("dve", lambda tl=tl, s2=s2: V.tensor_scalar(out=Rfb[s2][:, 0, :], in0=cst[:, TRIF, :], scalar1=nlf[:, tl, 0:1], scalar2=None, op0=ALU.mult),
                     reads=["cst", "nlf"], writes=[f"Rfb{s2}"])
                t.op("dve", lambda tl=tl, s2=s2: V.tensor_scalar(out=Rfb[s2][:, 1, :], in0=cst[:, TRIB, :], scalar1=nlf[:, tl, 1:2], scalar2=None, op0=ALU.mult),
                     reads=["cst", "nlf"], writes=[f"Rfb{s2}"])
                t.op("pe", lambda s2=s2: P_.matmul(P7[:, 0:256], lhsT=cst[:, ONES, :], rhs=Rfb[s2].rearrange("p d t -> p (d t)"), start=True, stop=True),
                     reads=["cst", f"Rfb{s2}"], writes=["P7"])
                P7v = P7[:, 0:256].rearrange("p (d t) -> p d t", d=2)
                t.op("act", lambda s2=s2, P7v=P7v: A_.activation(out=ebt[s2], in_=P7v, func=AF.Exp, scale=-1.0, bias=kcol[:, 1:2]),
                     reads=["P7", "kcol"], writes=[f"ebt{s2}"])
                t.op("act", lambda tl=tl: A_.activation(out=ebl[:, tl, 0, :], in_=P7[:, 63:128:64], func=AF.Exp, scale=-1.0), reads=["P7"], writes=[f"ebl{tl}"])
                t.op("act", lambda tl=tl: A_.activation(out=ebl[:, tl, 1, :], in_=P7[:, 128:256:64], func=AF.Exp, scale=-1.0), reads=["P7"], writes=[f"ebl{tl}"])
                for d in range(2):
                    eng = "dve" if d == 0 else "pool"
                    E = V if d == 0 else G_
                    t.op(eng, lambda E=E, d=d, tl=tl, s2=s2: E.tensor_tensor(out=QsT[d][:, tl * 128:(tl + 1) * 128], in0=mqT[:, tl * 128:(tl + 1) * 128],
                                                                           in1=ebt[s2][:, d, :], op=ALU.mult), reads=["mqT", f"ebt{s2}"], writes=[f"Qs{d}_{tl}"])
                    t.op("pool", lambda d=d, tl=tl: G_.tensor_scalar(out=wV[d][:, tl, :], in0=Vg[:, tl, :], scalar1=gw[:, tl, 2 + d:3 + d], scalar2=None, op0=ALU.mult),
                         reads=[f"Vg{tl}", "Vgones", f"gw{tl}"], writes=[f"wV{d}_{tl}"])
                p4b = P4.bitcast(BF16)
                t.op("pe", lambda tl=tl, p4b=p4b, s2=s2: P_.transpose(out=p4b[:, s2 * 128:(s2 + 1) * 128], in_=mkT[:, tl * 128:(tl + 1) * 128], identity=identb[:]),
                     reads=["mkT", "identb"], writes=["P4"])
                t.op("act", lambda tl=tl, p4b=p4b, s2=s2: A_.activation(out=Ktok[:, tl, :], in_=p4b[:, s2 * 128:(s2 + 1) * 128], func=AF.Copy),
                     reads=["P4"], writes=[f"Ktok{tl}"])

            if h == 0:
                dbg_dump("ml_b", [(gw.rearrange("p a b -> p (a b)"), NT * 4), (ebl.rearrange("p a b c -> p (a b c)"), NT * 4),
                                  (nlf.rearrange("p a b -> p (a b)"), NT * 2), (QsT[0], TOK), (QsT[1], TOK),
                                  (Ktok.rearrange("p a b -> p (a b)"), TOK), (Vg.rearrange("p a b -> p (a b)"), NT * 129),
                                  (wV[0].rearrange("p a b -> p (a b)"), NT * 129)])
            def chunks(d):
                order = list(range(2 * NT))
                return order if d == 0 else order[::-1]

            def state_update(d, c, to_bf):
                tl, c2 = c // 2, c % 2
                pb, pk = next_bank()
                lo, hi = c2 * 64, c2 * 64 + 64
                t.op("pe", lambda: P_.matmul(pb[:, 0:129], lhsT=Ktok[lo:hi, tl, :], rhs=wV[d][lo:hi, tl, :], start=True, stop=True),
                     reads=[f"Ktok{tl}", f"wV{d}_{tl}"], writes=[pk])
                t.op("dve", lambda: V.scalar_tensor_tensor(out=Cst[d], in0=Cst[d], scalar=ebl[:, tl, d, c2:c2 + 1], in1=pb[:, 0:129],
                                                           op0=ALU.mult, op1=ALU.add), reads=[pk, f"ebl{tl}", f"C{d}"], writes=[f"C{d}"])
                if to_bf:
                    t.op("act", lambda: A_.activation(out=Cbf[d], in_=Cst[d], func=AF.Copy), reads=[f"C{d}"], writes=[f"Cb{d}"])

            for d in range(2):
                t.op("dve", lambda d=d, h=h: V.tensor_copy(out=Cst[d], in_=Cacc[:, d, h, :]), reads=["Cacc"], writes=[f"C{d}"])
                t.op("act", lambda d=d: A_.activation(out=Cbf[d], in_=Cst[d], func=AF.Copy), reads=[f"C{d}"], writes=[f"Cb{d}"])
            if h == 0:
                dbg_dump("ml_d", [(Cst[0], 129), (Cst[1], 129)])
            def scan_tile(d, tl):
                PN, pnk = PNs[d]
                pb, pk = next_bank()
                tok = slice(tl * 128, (tl + 1) * 128)
                t.op("pe", lambda: P_.matmul(pb[:, 0:128], lhsT=mkT[:, tok], rhs=QsT[d][:, tok], start=True, stop=True),
                     reads=["mkT", f"Qs{d}_{tl}"], writes=[pk])
                msk = cst[:, TRIF, :] if d == 0 else cst[:, TRIB, :]
                t.op("dve", lambda: V.scalar_tensor_tensor(out=sdT[d], in0=pb[:, 0:128], scalar=gw[:, tl, d:d + 1], in1=msk, op0=ALU.mult, op1=ALU.mult),
                     reads=[pk, f"gw{tl}", "cst"], writes=[f"sdT{d}"])
                t.op("pe", lambda: P_.matmul(PN[:, 0:129], lhsT=sdT[d], rhs=Vg[:, tl, :], start=True, stop=False),
                     reads=[f"sdT{d}", f"Vg{tl}", "Vgones"], writes=[pnk])
                order = (0, 1) if d == 0 else (1, 0)
                for ci, c2 in enumerate(order):
                    lo = tl * 128 + c2 * 64
                    t.op("pe", lambda c2=c2, lo=lo, ci=ci: P_.matmul(PN[c2 * 64:(c2 + 1) * 64, 0:129], lhsT=QsT[d][:, lo:lo + 64], rhs=Cbf[d],
                                                                 start=False, stop=True), reads=[f"Qs{d}_{tl}", f"Cb{d}"], writes=[pnk])
                    c = tl * 2 + c2
                    last = (c == (2 * NT - 1 if d == 0 else 0))
                    if not last:
                        state_update(d, c, True)
                s2 = d
                t.op("act", lambda: A_.activation(out=st[s2][:, 0:1], in_=PN[:, 128:129], func=AF.Abs), reads=[pnk], writes=[f"st{s2}"])
                t.op("dve", lambda: V.tensor_scalar(out=st[s2][:, 0:1], in0=st[s2][:, 0:1], scalar1=1.0, scalar2=None, op0=ALU.max),
                     reads=[f"st{s2}"], writes=[f"st{s2}"])
                t.op("dve", lambda: V.reciprocal(out=st[s2][:, 1:2], in_=st[s2][:, 0:1]), reads=[f"st{s2}"], writes=[f"st{s2}"])
                return PN, pnk

            def finalize(i):
                s2 = i % 2
                t.op("pool", lambda: G_.tensor_tensor(out=hm[s2], in0=hF[:, i, :], in1=hB[:, i, :], op=ALU.add),
                     reads=[f"hF{i}", f"hB{i}"], writes=[f"hm{s2}"])
                fin_rest(i, s2)

            def fin_rest(i, s2):
                t.op("dve", lambda i=i, s2=s2: V.tensor_tensor(out=hm[s2], in0=hm[s2], in1=so[:, i, :], op=ALU.mult), reads=[f"hm{s2}", f"so{i}"], writes=[f"hm{s2}"])
                t.op("act", lambda s2=s2: A_.activation(out=hjunk, in_=hm[s2], func=AF.Identity, accum_out=st[1][:, 2:3]), reads=[f"hm{s2}"], writes=["hjunk", "st1b"])
                t.op("act", lambda s2=s2: A_.activation(out=hjunk, in_=hm[s2], func=AF.Square, accum_out=st[1][:, 3:4]), reads=[f"hm{s2}"], writes=["hjunk", "st1c"])
                t.op("dve", lambda: V.tensor_scalar(out=st[1][:, 4:5], in0=st[1][:, 2:3], scalar1=1.0 / 128, scalar2=None, op0=ALU.mult), reads=["st1b"], writes=["st1d"])
                t.op("dve", lambda: V.tensor_tensor(out=st[1][:, 5:6], in0=st[1][:, 4:5], in1=st[1][:, 4:5], op=ALU.mult), reads=["st1d"], writes=["st1e"])
                t.op("dve", lambda: V.scalar_tensor_tensor(out=st[1][:, 6:7], in0=st[1][:, 3:4], scalar=1.0 / 128, in1=st[1][:, 5:6], op0=ALU.mult, op1=ALU.subtract),
                     reads=["st1c", "st1e"], writes=["st1f"])
                t.op("dve", lambda: V.tensor_scalar(out=st[1][:, 6:7], in0=st[1][:, 6:7], scalar1=EPS, scalar2=None, op0=ALU.add), reads=["st1f"], writes=["st1f"])
                t.op("act", lambda: A_.activation(out=st[1][:, 6:7], in_=st[1][:, 6:7], func=AF.Sqrt), reads=["st1f"], writes=["st1f"])
                t.op("dve", lambda: V.reciprocal(out=st[1][:, 7:8], in_=st[1][:, 6:7]), reads=["st1f"], writes=["st1g"])
                t.op("dve", lambda s2=s2: V.tensor_scalar(out=hm[s2], in0=hm[s2], scalar1=st[1][:, 4:5], scalar2=st[1][:, 7:8], op0=ALU.subtract, op1=ALU.mult),
                     reads=[f"hm{s2}", "st1d", "st1g"], writes=[f"hm{s2}"])
                t.op("pool", lambda s2=s2, h=h: G_.tensor_tensor(out=hm[s2], in0=hm[s2], in1=mlnw_bc[:, h * 128:(h + 1) * 128], op=ALU.mult),
                     reads=[f"hm{s2}", "mlnw_bc"], writes=[f"hm{s2}"])
                t.op("pool", lambda s2=s2, i=i: G_.tensor_tensor(out=mo[s2], in0=hm[s2], in1=szm[:, i, :], op=ALU.mult), reads=[f"hm{s2}", f"szm{i}"], writes=[f"mo{s2}"])
                p4b = P4.bitcast(BF16)
                t.op("pe", lambda s2=s2, p4b=p4b: P_.transpose(out=p4b[:, s2 * 128:(s2 + 1) * 128], in_=mo[s2], identity=identb[:]),
                     reads=[f"mo{s2}", "identb"], writes=["P4"])
                t.op("act", lambda s2=s2, p4b=p4b, i=i, h=h: A_.activation(out=mixML[:, h, i * 128:(i + 1) * 128], in_=p4b[:, s2 * 128:(s2 + 1) * 128], func=AF.Copy),
                     reads=["P4"], writes=[f"mixML{h}_{i}"])

            for i in range(NT):
                PN, pnk = scan_tile(0, i)
                t.op("dve", lambda PN=PN, i=i: V.tensor_scalar(out=hF[:, i, :], in0=PN[:, 0:128], scalar1=st[0][:, 1:2], scalar2=None, op0=ALU.mult),
                     reads=[pnk, "st0"], writes=[f"hF{i}"])
                j = NT - 1 - i
                PN, pnk = scan_tile(1, j)
                t.op("dve", lambda PN=PN, j=j: V.tensor_scalar(out=hB[:, j, :], in0=PN[:, 0:128], scalar1=st[1][:, 1:2], scalar2=None, op0=ALU.mult),
                     reads=[pnk, "st1"], writes=[f"hB{j}"])
                if i >= NT // 2:
                    finalize(i)
                    finalize(j)
            t.barrier()
            if h == 0:
                dbg_dump("ml_f", [(mixML[:, 0, :], TOK)])

        if dbg == "ml":
            cv.off = mark
            dtmp = cv.take([128, TOK])
            for c8 in range(4):
                t.op("dve", lambda c8=c8: V.tensor_copy(out=dtmp, in_=mixML[:, c8, :]), reads=[], writes=["dtmp"])
                t.dma("sp", "dbg", dbg_out[:, (4 + c8) * TOK:(5 + c8) * TOK], dtmp, reads=["dtmp"], writes=["dbgout"])
            nc.sync.wait_ge(t.dma_lanes["dbg"].sem, t.dma_lanes["dbg"].count)
            print("instructions", t.n_inst, "waits", t.n_wait)
            return nc

        cv.off = 4 * TOK // 2
        fnw_bc = cv.take([128, D])
        woutb = cv.take([128, 8, D], BF16)
        xres = [cv.take([128, D]) for _ in range(2)]
        hres = [cv.take([128, D]) for _ in range(2)]
        ojunk = cv.take([128, D])
        ost = cv.take([128, 2 * NT])
        t.dma("pool", "c1", fnw_bc, fnw_row.partition_broadcast(128), writes=["fnw_bc"])
        for q4 in range(4):
            s = wstate["n"] % 2
            wstate["n"] += 1
            t.dma("sp", f"wst{s}", wst[s][:, :, 0:256], wout_v[:, :, q4 * 256:(q4 + 1) * 256], writes=[f"wst{s}"])
            t.op("pool", lambda s=s, q4=q4: G_.tensor_copy(out=woutb[:, :, q4 * 256:(q4 + 1) * 256], in_=wst[s][:, :, 0:256]), reads=[f"wst{s}"], writes=["woutb"])
        for tl in range(NT):
            s2 = tl % 2
            t.dma("sp", f"xres{s2}", xres[s2], xh[(tl + 2) * 128:(tl + 3) * 128, :], writes=[f"xres{s2}"])
            for nb in range(2):
                pb, pk = next_bank()
                for mc in range(8):
                    src = mixT[:, mc, tl * 128:(tl + 1) * 128] if mc < 4 else mixML[:, mc - 4, tl * 128:(tl + 1) * 128]
                    t.op("pe", lambda mc=mc, src=src, pb=pb, nb=nb: P_.matmul(pb[:, :], lhsT=src, rhs=woutb[:, mc, nb * 512:(nb + 1) * 512], start=(mc == 0), stop=(mc == 7)),
                         reads=["woutb"], writes=[pk])
                t.op("dve", lambda pb=pb, nb=nb, s2=s2: V.tensor_tensor(out=hres[s2][:, nb * 512:(nb + 1) * 512], in0=pb[:, :], in1=gate_bc[:, nb * 512:(nb + 1) * 512], op=ALU.mult),
                     reads=[pk, "gate_bc"], writes=[f"hres{s2}_{nb}"])
                t.op("pool", lambda nb=nb, s2=s2: G_.tensor_tensor(out=hres[s2][:, nb * 512:(nb + 1) * 512], in0=hres[s2][:, nb * 512:(nb + 1) * 512],
                                                                 in1=xres[s2][:, nb * 512:(nb + 1) * 512], op=ALU.add),
                     reads=[f"hres{s2}_{nb}", f"xres{s2}"], writes=[f"hres{s2}_{nb}"])
            hk = [f"hres{s2}_0", f"hres{s2}_1"]
            t.op("act", lambda s2=s2, tl=tl: A_.activation(out=ojunk, in_=hres[s2], func=AF.Square, accum_out=ost[:, tl:tl + 1]), reads=hk, writes=["ojunk", f"ost{tl}"])
            t.op("dve", lambda tl=tl: V.tensor_scalar(out=ost[:, tl:tl + 1], in0=ost[:, tl:tl + 1], scalar1=1.0 / D, scalar2=EPS, op0=ALU.mult, op1=ALU.add),
                 reads=[f"ost{tl}"], writes=[f"ost{tl}"])
            t.op("act", lambda tl=tl: A_.activation(out=ost[:, tl:tl + 1], in_=ost[:, tl:tl + 1], func=AF.Sqrt), reads=[f"ost{tl}"], writes=[f"ost{tl}"])
            t.op("dve", lambda tl=tl: V.reciprocal(out=ost[:, NT + tl:NT + tl + 1], in_=ost[:, tl:tl + 1]), reads=[f"ost{tl}"], writes=[f"ost{tl}"])
            t.op("dve", lambda tl=tl, s2=s2: V.scalar_tensor_tensor(out=hres[s2], in0=hres[s2], scalar=ost[:, NT + tl:NT + tl + 1], in1=fnw_bc, op0=ALU.mult, op1=ALU.mult),
                 reads=hk + [f"ost{tl}", "fnw_bc"], writes=hk)
            t.dma("sp", "yout", y[tl * 128:(tl + 1) * 128, :], hres[s2], reads=hk, writes=["yout"])
        yl = t.dma_lanes["yout"]
        nc.sync.wait_ge(yl.sem, yl.count)
        print("instructions", t.n_inst, "waits", t.n_wait)

    return nc


def make_in_maps(x, c, w_ada, b_ada, norm_w, w_in, b_in, conv_w, conv_b, rpb, ml_norm_w, w_out, final_norm_w):
    f = lambda a: np.ascontiguousarray(np.asarray(a, dtype=np.float32))
    x = f(x)[0]
    perm = _col_perm()
    w_in_p = f(f(w_in)[0][:, perm])
    b_in_p = f(f(b_in)[0][perm])
    groups = [b_in_p[g * 128:(g + 1) * 128] for g in range(8)]
    for h in range(ML_HEADS):
        o = NA_COLS + h * MLW
        groups.append(b_in_p[o:o + 128])
        groups.append(b_in_p[o + 128:o + 256])
    bin_col = f(np.stack(groups, 1))
    col8 = lambda v: f(np.asarray(v, np.float32).reshape(-1, 128).T)
    rpbA, cmask = _rpb_tables(f(rpb)[0])
    cw = f(conv_w)[0]
    convw = f(cw.T.reshape(8, 128, 5).transpose(1, 0, 2))
    shared = {
        "w_ada": f(w_ada)[0], "w_in": w_in_p, "w_out": f(w_out)[0], "consts": _consts(),
        "c_col": col8(f(c)[0]), "bada_col": col8(f(b_ada)[0]), "bada_gate": f(f(b_ada)[0][2 * D:]),
        "normw_col": col8(f(norm_w)[0]), "fnw_row": f(final_norm_w), "mlnw_row": f(ml_norm_w)[0],
        "bin_col": bin_col, "bin_row": b_in_p, "convw": convw, "convb": col8(f(conv_b)[0]),
        "rpbA": f(rpbA.reshape(8, 128, 14 * 64)), "cmask": f(cmask.reshape(128, 14 * 64)),
        "tri2": _tri2(), "w_gates": f(f(w_in)[0][:, 4608:4624]), "b_gates": f(f(b_in)[0][4608:4624]),
    }
    in_maps = []
    for i in range(NCORES):
        xhh = np.zeros((HT * 128, D), np.float32)
        lo, hi = i * TOK - 256, i * TOK + TOK + 256
        slo, shi = max(lo, 0), min(hi, T)
        xhh[slo - lo:shi - lo] = x[slo:shi]
        fl = np.zeros((128, 18), np.float32)
        fl[:, 0] = 1.0 if i > 0 else 0.0
        fl[:, 1] = 1.0 if i < NCORES - 1 else 0.0
        for j in range(NCORES):
            fl[:, 2 + j] = 1.0 if j < i else 0.0
            fl[:, 10 + j] = 1.0 if j > i else 0.0
        gb = list(range(4 * i - 1, -1, -1))
        ga = list(range(4 * i + 4, T // 512))
        order = gb + ga
        assert len(order) == NG
        xf = np.concatenate([x[g * 512:(g + 1) * 512] for g in order], 0)
        xfh = np.zeros((NG * 4, D), np.float32)
        ffl = np.zeros((128, NG * 4), np.float32)
        for p, g in enumerate(order):
            if g > 0:
                xfh[p * 4:p * 4 + 2] = x[g * 512 - 2:g * 512]
                ffl[:, p * 4 + 2] = 1.0
            if g < T // 512 - 1:
                xfh[p * 4 + 2:p * 4 + 4] = x[(g + 1) * 512:(g + 1) * 512 + 2]
                ffl[:, p * 4 + 3] = 1.0
            ffl[:, p * 4 + 0] = 1.0 if g < 4 * i else 0.0
            ffl[:, p * 4 + 1] = 1.0 if g > 4 * i else 0.0
        m = dict(shared)
        m.update({"xh": xhh, "mcol": _mcol(i), "flags": fl, "xf": xf, "xfh": xfh, "fflags": ffl})
        in_maps.append(m)
    return in_maps


_NC_CACHE = {}


def kernel(**inputs):
    in_maps = make_in_maps(**inputs)
    if "nc" not in _NC_CACHE:
        _NC_CACHE["nc"] = build_program()
    res = run_bass_kernel_spmd(_NC_CACHE["nc"], in_maps, core_ids=list(range(NCORES)))
    out = np.concatenate([r["y"] for r in res.results], axis=0)
    return out.reshape(1, T, D).astype(np.float32)
```
